# Optimizing a Trainium2 kernel written in Bass

```python
import math
import jax, jax.numpy as jnp
from jax import lax
import numpy as np

D_MODEL = 1024
BATCH = 4
SEQ = 8192
DEPTH = 2

GRID_W = 64
CTX_LEN = 256
Q_BLOCK = 128
ROPE_BASE = 10000.0
LN_EPS = 1e-6
RMS_EPS = 1e-6
DN_ALPHA = (2 * DEPTH) ** 0.25
DN_BETA = (8 * DEPTH) ** -0.25
FOURIER_GROUPS = 4
FOURIER_GROUP_DIM = D_MODEL // 8
FOURIER_WIDTH = FOURIER_GROUPS * FOURIER_GROUP_DIM
MLA_HEADS = 8
MLA_Q_LORA = D_MODEL // 4
MLA_KV_LORA = D_MODEL // 4
MLA_NOPE = 64
MLA_ROPE = 32
MLA_V = 64
MLA_SCALE = (MLA_NOPE + MLA_ROPE) ** -0.5
MLA_IN_WIDTH = FOURIER_WIDTH + MLA_Q_LORA + MLA_KV_LORA + MLA_ROPE
DIFF_HEADS = 8
DIFF_HEAD_DIM = 64
DIFF_QK_WIDTH = DIFF_HEADS * 2 * DIFF_HEAD_DIM
DIFF_V_WIDTH = DIFF_HEADS * 2 * DIFF_HEAD_DIM
DIFF_SCALE = DIFF_HEAD_DIM ** -0.5
FF_HIDDEN = int(math.ceil(8 * D_MODEL / 3 / 256)) * 256

kernel_name = "hybrid_fourier_mla_diffattn_prefix_dit"


def layer_norm(x, g=None, b=None):
    xf = x.astype(jnp.float32)
    mu = jnp.mean(xf, axis=-1, keepdims=True)
    var = jnp.mean(jnp.square(xf - mu), axis=-1, keepdims=True)
    y = (xf - mu) * lax.rsqrt(var + LN_EPS)
    if g is not None:
        y = y * g.astype(jnp.float32) + b.astype(jnp.float32)
    return y.astype(x.dtype)


def rms_norm(x, g):
    xf = x.astype(jnp.float32)
    y = xf * lax.rsqrt(jnp.mean(jnp.square(xf), axis=-1, keepdims=True) + RMS_EPS) * g.astype(jnp.float32)
    return y.astype(x.dtype)


def axial_rope(x, rows, cols):
    quarter = x.shape[-1] // 4
    inv = 1.0 / (ROPE_BASE ** (jnp.arange(quarter, dtype=jnp.float32) / quarter))
    bshape = (x.shape[1],) + (1,) * (x.ndim - 3) + (quarter,)
    xf = x.astype(jnp.float32)
    outs = []
    for pos, xh in ((rows, xf[..., :2 * quarter]), (cols, xf[..., 2 * quarter:])):
        ang = (pos.astype(jnp.float32)[:, None] * inv).reshape(bshape)
        cos, sin = jnp.cos(ang), jnp.sin(ang)
        x1, x2 = xh[..., :quarter], xh[..., quarter:]
        outs += [x1 * cos - x2 * sin, x1 * sin + x2 * cos]
    return jnp.concatenate(outs, axis=-1).astype(x.dtype)


def to_blocks(a):
    b, s = a.shape[0], a.shape[1]
    return jnp.moveaxis(a.reshape((b, s // Q_BLOCK, Q_BLOCK) + a.shape[2:]), 1, 0)


def from_blocks(o):
    nb, b, blk = o.shape[0], o.shape[1], o.shape[2]
    return jnp.moveaxis(o, 0, 1).reshape((b, nb * blk) + o.shape[3:])


def fourier_mix(u):
    b, n, _ = u.shape
    z = u.astype(jnp.float32).reshape(b, n, FOURIER_GROUPS, FOURIER_GROUP_DIM)
    z = jnp.fft.fft2(z, axes=(1, 3), norm='ortho').real
    return z.reshape(b, n, FOURIER_WIDTH).astype(u.dtype)


def mla_attend(qn, qr, kn, kr, v):
    s = jnp.einsum('bqhd,bkhd->bhqk', qn, kn) + jnp.einsum('bqhr,bkr->bhqk', qr, kr)
    p = jax.nn.softmax(s.astype(jnp.float32) * MLA_SCALE, axis=-1).astype(v.dtype)
    return jnp.einsum('bhqk,bkhd->bqhd', p, v)


def diff_attend(q, k, v, lam):
    s = jnp.einsum('bqhcd,bkhcd->bhcqk', q, k).astype(jnp.float32) * DIFF_SCALE
    p = jax.nn.softmax(s, axis=-1)
    a = (p[:, :, 0] - lam * p[:, :, 1]).astype(v.dtype)
    return jnp.einsum('bhqk,bkhe->bqhe', a, v)


def swiglu(h, w_gate, w_up, w_down):
    return (jax.nn.silu(h @ w_gate) * (h @ w_up)) @ w_down


def mixer_fourier_mla(h, hc, rows, cols, w_in, q_norm, w_uq, kv_norm, w_ukv, w_out, need_ctx):
    def project(z):
        b, n, _ = z.shape
        u = z @ w_in
        o1 = FOURIER_WIDTH
        o2 = o1 + MLA_Q_LORA
        o3 = o2 + MLA_KV_LORA
        f, cq, ckv, kr = u[..., :o1], u[..., o1:o2], u[..., o2:o3], u[..., o3:]
        q = (rms_norm(cq, q_norm) @ w_uq).reshape(b, n, MLA_HEADS, MLA_NOPE + MLA_ROPE)
        kv = (rms_norm(ckv, kv_norm) @ w_ukv).reshape(b, n, MLA_HEADS, MLA_NOPE + MLA_V)
        return f, q[..., :MLA_NOPE], q[..., MLA_NOPE:], kv[..., :MLA_NOPE], kv[..., MLA_NOPE:], kr

    b, s, _ = h.shape
    f, qn, qr, kn, v, kr = project(h)
    qr = axial_rope(qr, rows, cols)
    kr = axial_rope(kr, rows, cols)
    fc, qnc, qrc, knc, vc, krc = project(hc)
    kn_all = jnp.concatenate([knc, kn], axis=1)
    kr_all = jnp.concatenate([krc, kr], axis=1)
    v_all = jnp.concatenate([vc, v], axis=1)
    att = from_blocks(lax.map(lambda qb: mla_attend(qb[0], qb[1], kn_all, kr_all, v_all),
                              (to_blocks(qn), to_blocks(qr))))
    y = jnp.concatenate([fourier_mix(f), att.reshape(b, s, MLA_HEADS * MLA_V)], axis=-1) @ w_out
    yc = None
    if need_ctx:
        attc = mla_attend(qnc, qrc, knc, krc, vc)
        yc = jnp.concatenate([fourier_mix(fc), attc.reshape(b, hc.shape[1], MLA_HEADS * MLA_V)], axis=-1) @ w_out
    return y, yc


def mixer_diff(h, hc, rows, cols, w_in, lambda_q1, lambda_k1, lambda_q2, lambda_k2, subln, w_out,
               lambda_init, need_ctx):
    def project(z):
        b, n, _ = z.shape
        u = z @ w_in
        q = u[..., :DIFF_QK_WIDTH].reshape(b, n, DIFF_HEADS, 2, DIFF_HEAD_DIM)
        k = u[..., DIFF_QK_WIDTH:2 * DIFF_QK_WIDTH].reshape(b, n, DIFF_HEADS, 2, DIFF_HEAD_DIM)
        v = u[..., 2 * DIFF_QK_WIDTH:].reshape(b, n, DIFF_HEADS, 2 * DIFF_HEAD_DIM)
        return q, k, v

    def finish(o):
        b, n = o.shape[0], o.shape[1]
        o = rms_norm(o, subln) * (1.0 - lambda_init)
        return o.reshape(b, n, DIFF_V_WIDTH) @ w_out

    f32 = jnp.float32
    lam = (jnp.exp(jnp.sum(lambda_q1.astype(f32) * lambda_k1.astype(f32)))
           - jnp.exp(jnp.sum(lambda_q2.astype(f32) * lambda_k2.astype(f32))) + lambda_init)
    q, k, v = project(h)
    q = axial_rope(q, rows, cols)
    k = axial_rope(k, rows, cols)
    qc, kc, vc = project(hc)
    k_all = jnp.concatenate([kc, k], axis=1)
    v_all = jnp.concatenate([vc, v], axis=1)
    att = from_blocks(lax.map(lambda qb: diff_attend(qb, k_all, v_all, lam), to_blocks(q)))
    y = finish(att)
    yc = finish(diff_attend(qc, kc, vc, lam)) if need_ctx else None
    return y, yc


def setup_inputs(seed: int = 0) -> dict:
    key = jax.random.key(seed)
    ks = iter(jax.random.split(key, 48))

    def nrm(shape, scale):
        return jax.random.normal(next(ks), shape, jnp.float32) * scale

    def gain(n):
        return 1.0 + nrm((n,), 0.02)

    d = D_MODEL
    inp = {}
    inp['x'] = nrm((BATCH, SEQ, d), 1.0)
    inp['c'] = nrm((BATCH, d), 1.0)
    inp['ctx'] = nrm((BATCH, CTX_LEN, d), 1.0)
    inp['c_ctx'] = nrm((d,), 1.0)
    inp['l0_w_mod'] = nrm((d, 6 * d), 0.5 * d ** -0.5)
    inp['l0_b_mod'] = nrm((6 * d,), 0.02)
    inp['l0_w_in'] = nrm((d, MLA_IN_WIDTH), d ** -0.5)
    inp['l0_q_norm'] = gain(MLA_Q_LORA)
    inp['l0_w_uq'] = nrm((MLA_Q_LORA, MLA_HEADS * (MLA_NOPE + MLA_ROPE)), MLA_Q_LORA ** -0.5)
    inp['l0_kv_norm'] = gain(MLA_KV_LORA)
    inp['l0_w_ukv'] = nrm((MLA_KV_LORA, MLA_HEADS * (MLA_NOPE + MLA_V)), MLA_KV_LORA ** -0.5)
    inp['l0_w_out'] = nrm((FOURIER_WIDTH + MLA_HEADS * MLA_V, d), DN_BETA * (FOURIER_WIDTH + MLA_HEADS * MLA_V) ** -0.5)
    inp['l0_ln1_g'] = gain(d)
    inp['l0_ln1_b'] = nrm((d,), 0.02)
    inp['l0_w_gate'] = nrm((d, FF_HIDDEN), d ** -0.5)
    inp['l0_w_up'] = nrm((d, FF_HIDDEN), d ** -0.5)
    inp['l0_w_down'] = nrm((FF_HIDDEN, d), DN_BETA * FF_HIDDEN ** -0.5)
    inp['l0_ln2_g'] = gain(d)
    inp['l0_ln2_b'] = nrm((d,), 0.02)
    inp['l1_w_mod'] = nrm((d, 6 * d), 0.5 * d ** -0.5)
    inp['l1_b_mod'] = nrm((6 * d,), 0.02)
    inp['l1_w_in'] = nrm((d, 2 * DIFF_QK_WIDTH + DIFF_V_WIDTH), d ** -0.5)
    inp['l1_lambda_q1'] = nrm((DIFF_HEAD_DIM,), 0.1)
    inp['l1_lambda_k1'] = nrm((DIFF_HEAD_DIM,), 0.1)
    inp['l1_lambda_q2'] = nrm((DIFF_HEAD_DIM,), 0.1)
    inp['l1_lambda_k2'] = nrm((DIFF_HEAD_DIM,), 0.1)
    inp['l1_subln'] = gain(2 * DIFF_HEAD_DIM)
    inp['l1_w_out'] = nrm((DIFF_V_WIDTH, d), DN_BETA * DIFF_V_WIDTH ** -0.5)
    inp['l1_ln1_g'] = gain(d)
    inp['l1_ln1_b'] = nrm((d,), 0.02)
    inp['l1_w_gate'] = nrm((d, FF_HIDDEN), d ** -0.5)
    inp['l1_w_up'] = nrm((d, FF_HIDDEN), d ** -0.5)
    inp['l1_w_down'] = nrm((FF_HIDDEN, d), DN_BETA * FF_HIDDEN ** -0.5)
    inp['l1_ln2_g'] = gain(d)
    inp['l1_ln2_b'] = nrm((d,), 0.02)
    return inp


def reference(x, c, ctx, c_ctx,
              l0_w_mod, l0_b_mod, l0_w_in, l0_q_norm, l0_w_uq, l0_kv_norm, l0_w_ukv, l0_w_out,
              l0_ln1_g, l0_ln1_b, l0_w_gate, l0_w_up, l0_w_down, l0_ln2_g, l0_ln2_b,
              l1_w_mod, l1_b_mod, l1_w_in, l1_lambda_q1, l1_lambda_k1, l1_lambda_q2, l1_lambda_k2,
              l1_subln, l1_w_out, l1_ln1_g, l1_ln1_b, l1_w_gate, l1_w_up, l1_w_down, l1_ln2_g, l1_ln2_b):
    s = x.shape[1]
    ROWS = s // GRID_W
    rows = jnp.repeat(jnp.arange(ROWS, dtype=jnp.int32), GRID_W)
    cols = jnp.tile(jnp.arange(GRID_W, dtype=jnp.int32), ROWS)

    layers = [
        dict(w_mod=l0_w_mod, b_mod=l0_b_mod, ln1_g=l0_ln1_g, ln1_b=l0_ln1_b, w_gate=l0_w_gate,
             w_up=l0_w_up, w_down=l0_w_down, ln2_g=l0_ln2_g, ln2_b=l0_ln2_b,
             mix=(l0_w_in, l0_q_norm, l0_w_uq, l0_kv_norm, l0_w_ukv, l0_w_out)),
        dict(w_mod=l1_w_mod, b_mod=l1_b_mod, ln1_g=l1_ln1_g, ln1_b=l1_ln1_b, w_gate=l1_w_gate,
             w_up=l1_w_up, w_down=l1_w_down, ln2_g=l1_ln2_g, ln2_b=l1_ln2_b,
             mix=(l1_w_in, l1_lambda_q1, l1_lambda_k1, l1_lambda_q2, l1_lambda_k2, l1_subln, l1_w_out)),
    ]

    xc = ctx
    for i in range(DEPTH):
        p = layers[i]
        need_ctx = i < DEPTH - 1
        sh1, sc1, g1, sh2, sc2, g2 = jnp.split(jax.nn.silu(c)[:, None, :] @ p['w_mod'] + p['b_mod'], 6, axis=-1)
        csh1, csc1, cg1, csh2, csc2, cg2 = jnp.split(jax.nn.silu(c_ctx) @ p['w_mod'] + p['b_mod'], 6, axis=-1)
        h = layer_norm(x) * (1.0 + sc1) + sh1
        hc = layer_norm(xc) * (1.0 + csc1) + csh1
        if i % 2 == 0:
            y, yc = mixer_fourier_mla(h, hc, rows, cols, *p['mix'], need_ctx)
        else:
            lambda_init = 0.8 - 0.6 * math.exp(-0.3 * i)
            y, yc = mixer_diff(h, hc, rows, cols, *p['mix'], lambda_init, need_ctx)
        x = layer_norm(DN_ALPHA * x + g1 * y, p['ln1_g'], p['ln1_b'])
        h = layer_norm(x) * (1.0 + sc2) + sh2
        x = layer_norm(DN_ALPHA * x + g2 * swiglu(h, p['w_gate'], p['w_up'], p['w_down']), p['ln2_g'], p['ln2_b'])
        if need_ctx:
            xc = layer_norm(DN_ALPHA * xc + cg1 * yc, p['ln1_g'], p['ln1_b'])
            hc = layer_norm(xc) * (1.0 + csc2) + csh2
            xc = layer_norm(DN_ALPHA * xc + cg2 * swiglu(hc, p['w_gate'], p['w_up'], p['w_down']), p['ln2_g'], p['ln2_b'])
    return x
```

```python
import math
import numpy as np
from contextlib import ExitStack
import concourse.bass as bass
import concourse.mybir as mybir
from concourse.bass_utils import run_bass_kernel_spmd

F32 = mybir.dt.float32
BF16 = mybir.dt.bfloat16
AF = mybir.ActivationFunctionType
ALU = mybir.AluOpType

P = 128
D = 1024
SEQ = 8192
SH = 4096
CTX = 256
NK = SEQ + CTX
FF = 2816
NF = 22
ALPHA = 4.0 ** 0.25
EPS = 1e-6
MLA_SCALE = 96.0 ** -0.5
DIFF_SCALE = 64.0 ** -0.5
LAMBDA_INIT1 = 0.8 - 0.6 * math.exp(-0.3)

ENGS = ["pe", "act", "dve", "pool", "sp"]
DEBUG_SCR = False
NO_COLL = False


class Buf:
    __slots__ = ("name", "w", "r", "dsem", "dcnt")

    def __init__(self, name):
        self.name = name
        self.w = None
        self.r = []
        self.dsem = None
        self.dcnt = 0


class Sched:
    def __init__(self, nc, es):
        self.nc = nc
        self.es = es
        self.ops = {e: [] for e in ENGS}
        self.sem = {e: es.enter_context(nc.semaphore("s_" + e)) for e in ENGS}
        self.cnt = {e: 0 for e in ENGS}
        self.seen = {e: {} for e in ENGS}
        self.sem_pool = {"sp": [], "pool": [], "act": []}
        self.phase_bufs = []
        self.nsem = 0
        self.pbig = es.enter_context(nc.psum_tensor("pbig", [P, 4096], F32))
        self.pb = [self.pbig[:, i * 512:(i + 1) * 512] for i in range(8)]
        self.bpb = [Buf("pb%d" % i) for i in range(8)]
        self.pbi = 0

    def bank(self, lo=0, hi=8):
        i = self.pbi
        if i < lo or i >= hi:
            i = lo
        self.pbi = i + 1
        return self.pb[i], self.bpb[i]

    def _need(self, e, ev, waits):
        if ev is None:
            return
        sem, val = ev
        if sem is self.sem[e] and e in ("pe", "sp"):
            return
        k = id(sem)
        if self.seen[e].get(k, 0) >= val:
            return
        self.seen[e][k] = val
        waits.append((sem, val))

    def _deps(self, e, reads, writes):
        waits = []
        for b in reads:
            self._need(e, b.w, waits)
        for b in writes:
            self._need(e, b.w, waits)
            for ev in b.r:
                self._need(e, ev, waits)
        return waits

    def op(self, e, fn, reads=(), writes=()):
        waits = self._deps(e, reads, writes)
        self.cnt[e] += 1
        ev = (self.sem[e], self.cnt[e])
        self.ops[e].append((waits, fn, (self.sem[e], 1)))
        for b in reads:
            b.r.append(ev)
        for b in writes:
            b.w = ev
            b.r = []
        return ev

    def dma(self, q, out_ap, in_ap, src, dst, **kw):
        srcs = src if isinstance(src, (list, tuple)) else [src]
        dsts = dst if isinstance(dst, (list, tuple)) else [dst]
        waits = self._deps(q, srcs, dsts)
        d0 = dsts[0]
        if d0.dsem is None:
            if self.sem_pool[q]:
                d0.dsem, d0.dcnt = self.sem_pool[q].pop()
            else:
                d0.dsem = self.es.enter_context(self.nc.semaphore("d%d" % self.nsem))
                self.nsem += 1
                d0.dcnt = 0
            self.phase_bufs.append((d0, q))
        else:
            assert any(b is d0 and qq == q for b, qq in self.phase_bufs), "buffer %s written by DMAs of two queues" % d0.name
        d0.dcnt += 16
        ev = (d0.dsem, d0.dcnt)
        self.ops[q].append(
            (waits, lambda eng: eng.dma_start(out=out_ap, in_=in_ap, **kw), (d0.dsem, 16)))
        for b in srcs:
            b.r.append(ev)
        for b in dsts:
            b.w = ev
            b.r = []
        return ev

    def coll(self, kind, groups, in_ap, out_ap, src, dst):
        waits = self._deps("pool", [src], [dst])
        if not hasattr(self, "cc_sem"):
            self.cc_sem = self.es.enter_context(self.nc.semaphore("cc_sem"))
            self.cc_cnt = 0
        self.cc_cnt += 1
        ev = (self.cc_sem, self.cc_cnt)
        sem = self.cc_sem
        self.ops["pool"].append(
            (waits, lambda eng: eng.collective_compute(kind, ALU.bypass, replica_groups=groups,
                                                       ins=[in_ap.opt()], outs=[out_ap.opt()]), (sem, 1)))
        src.r.append(ev)
        dst.w = ev
        dst.r = []
        return ev

    def end_phase(self):
        for b, _q in self.phase_bufs:
            waits = []
            self._need("sp", (b.dsem, b.dcnt), waits)
            if waits:
                self.ops["sp"].append((waits, None, None))
        nc = self.nc
        ops = self.ops

        def replay(e, eng):
            for waits, fn, inc in ops[e]:
                for sem, val in waits:
                    eng.wait_ge(sem, val)
                if fn is not None:
                    fn(eng).then_inc(inc[0], inc[1])

        with nc.Block() as block:
            @block.tensor
            def _(eng):
                replay("pe", eng)

            @block.scalar
            def _(eng):
                replay("act", eng)

            @block.vector
            def _(eng):
                replay("dve", eng)

            @block.gpsimd
            def _(eng):
                replay("pool", eng)

            @block.sync
            def _(eng):
                replay("sp", eng)
        self.ops = {e: [] for e in ENGS}
        for b, q in self.phase_bufs:
            self.sem_pool[q].append((b.dsem, b.dcnt))
            b.dsem = None
        self.phase_bufs = []


_UID = [0]


def _uname(name):
    _UID[0] += 1
    return "sb%d_%s" % (_UID[0], name)


class Ring:
    def __init__(self, nc, ph, name, n, shape, dt):
        self.t = [ph.enter_context(nc.sbuf_tensor(_uname("%s%d" % (name, i)), list(shape), dt)) for i in range(n)]
        self.b = [Buf("%s%d" % (name, i)) for i in range(n)]
        self.i = 0

    def next(self):
        i = self.i
        self.i = (i + 1) % len(self.t)
        return self.t[i], self.b[i]


def sbt(nc, ph, name, shape, dt):
    return ph.enter_context(nc.sbuf_tensor(_uname(name), list(shape), dt)), Buf(name)


def rope_tables(dim, positions_rc):
    q = dim // 4
    inv = 1.0 / (10000.0 ** (np.arange(q, dtype=np.float64) / q))
    n = positions_rc.shape[0]
    cos = np.ones((dim, n), np.float64)
    sin = np.zeros((dim, n), np.float64)
    valid = positions_rc[:, 0] >= 0
    for d in range(dim):
        a = d // (2 * q)
        w = d % (2 * q)
        fi = w % q
        first = w < q
        ang = positions_rc[:, a].astype(np.float64) * inv[fi]
        cos[d] = np.where(valid, np.cos(ang), 1.0)
        s = np.sin(ang)
        sin[d] = np.where(valid, -s if first else s, 0.0)
    return cos, sin


def rope_swap_index(dim):
    q = dim // 4
    idx = np.zeros(dim, np.int64)
    for d in range(dim):
        w = d % (2 * q)
        idx[d] = d + q if w < q else d - q
    return idx


def pos_rc(tokens):
    tokens = np.asarray(tokens)
    return np.stack([tokens // 64, tokens % 64], 1)


_TABLE_CACHE = {}


def get_tables(half):
    if half in _TABLE_CACHE:
        return _TABLE_CACHE[half]
    t = {}
    neg = -np.ones((CTX, 2), np.int64)
    key_pos = np.concatenate([neg, pos_rc(np.arange(SEQ))], 0)
    own_pos = pos_rc(half * SH + np.arange(SH))
    ck, sk = rope_tables(32, key_pos)
    t["cosK0"], t["sinK0"] = ck.astype(np.float32), sk.astype(np.float32)
    cq, sq = rope_tables(32, np.concatenate([own_pos, neg], 0))
    cq96 = np.ones((96, SH + CTX)); sq96 = np.zeros((96, SH + CTX))
    cq96[64:] = cq; sq96[64:] = sq
    t["cosQ0"], t["sinQ0"] = cq96.astype(np.float32), sq96.astype(np.float32)
    ck, sk = rope_tables(64, key_pos)
    t["cosK1"] = np.concatenate([ck, ck], 0).astype(np.float32)
    t["sinK1"] = np.concatenate([sk, sk], 0).astype(np.float32)
    cq, sq = rope_tables(64, own_pos)
    t["cosQ1"] = np.concatenate([cq, cq], 0).astype(np.float32)
    t["sinQ1"] = np.concatenate([sq, sq], 0).astype(np.float32)
    n2 = np.arange(64)[:, None]; k2 = np.arange(64)[None, :]
    a = 2 * np.pi * n2 * k2 / 64
    t["F64"] = np.concatenate([np.cos(a), -np.sin(a)], 1).astype(np.float32)
    n1 = np.arange(128)[:, None, None]
    kk = (64 * (64 * half + np.arange(64))[None, None, :] + np.arange(64)[None, :, None])
    a = 2 * np.pi * ((n1 * kk) % SEQ) / SEQ
    Gr, Gi = np.cos(a), -np.sin(a)
    GA = np.concatenate([Gr, Gi], 2); GB = np.concatenate([-Gi, Gr], 2)
    t["GAB"] = np.stack([GA, GB], 2).astype(np.float32)
    ch = np.arange(128)[:, None]; ch2 = np.arange(128)[None, :]
    a = 2 * np.pi * ch * ch2 / 128
    cds = np.stack([np.cos(a), np.sin(a)], 1)
    t["CdSl"] = (cds / math.sqrt(SEQ * 128)).astype(np.float32)
    t["CdSc"] = (cds / math.sqrt(CTX * 128)).astype(np.float32)
    n = (np.arange(2)[None, :, None] * 128 + np.arange(128)[:, None, None])
    k = np.arange(256)[None, None, :]
    a = 2 * np.pi * ((n * k) % 256) / 256
    t["F256"] = np.concatenate([np.cos(a), -np.sin(a)], 2).astype(np.float32)
    t["ident"] = np.eye(128, dtype=np.float32)
    _TABLE_CACHE[half] = t
    return t


class LayerIO:
    pass


def declare_layer(nc, L, pfx, x_from_dram=None):
    io = {}

    def inp(name, shape, dt=F32):
        io[name] = nc.dram_tensor(pfx + name, list(shape), dt, kind="ExternalInput").ap()

    def scr(name, shape, dt):
        kind = "ExternalOutput" if (DEBUG_SCR and name in ("Fdl", "Fdc", "KT", "Vs", "QT", "cat")) else "Internal"
        io[name] = nc.dram_tensor(pfx + name, list(shape), dt, kind=kind).ap()

    if x_from_dram is None:
        inp("xf", [SEQ, D]); inp("xo", [SH, D]); inp("xc", [CTX, D])
    else:
        io.update(x_from_dram)
    inp("cv", [P, 8, 2])
    inp("w_mod", [D, 6 * D]); inp("b_mod", [1, 6 * D])
    inp("w_out", [D, D]); inp("w_gate", [D, FF]); inp("w_up", [D, FF]); inp("w_down", [FF, D])
    inp("ln", [4, D])
    if "ident" not in io:
        inp("ident", [P, P])
    if L == 0:
        inp("w_in", [D, 1056]); inp("w_kr_sw", [D, 32])
        inp("qn", [P, 2]); inp("kvn", [P, 2])
        inp("w_uq", [256, 768]); inp("w_uq_sw", [256, 768]); inp("w_ukv_kn", [256, 512]); inp("w_ukv_v", [256, 512])
        inp("cosK", [32, NK]); inp("sinK", [32, NK]); inp("cosQ", [96, SH + CTX]); inp("sinQ", [96, SH + CTX])
        inp("F64", [64, 128]); inp("GAB", [P, 64, 2, 128]); inp("CdSl", [P, 2, 128]); inp("CdSc", [P, 2, 128])
        inp("F256", [P, 2, 512])
        scr("Fdl", [SEQ, 512], F32); scr("Fdc", [CTX, 512], F32)
        scr("KT", [8, 96, NK], BF16); scr("Vs", [NK, 8, 65], BF16); scr("QT", [8, 96, SH + CTX], BF16)
        scr("cat", [SH + CTX, D], BF16)
    else:
        inp("w_in", [D, 3072]); inp("w_qk_sw", [D, 2048])
        inp("lam", [4, 64]); inp("subln", [1, 128])
        inp("cosK", [P, NK]); inp("sinK", [P, NK]); inp("cosQ", [P, SH]); inp("sinQ", [P, SH])
        scr("KT", [8, 128, NK], BF16); scr("Vs", [NK, 8, 129], BF16); scr("QT", [8, 128, SH], BF16)
        scr("cat", [SH, D], BF16)
    scr("wg_bf", [11, P, 8, 256], BF16); scr("wu_bf", [11, P, 8, 256], BF16); scr("wd_bf", [FF, D], BF16)
    return io


def build_layer(nc, S, L, io, xout, xcout, consts, xbufs, b_out, b_outc):
    need_ctx = (L == 0)
    ident_f, b_idf, ident_b, b_idb, ones_f, b_ones = consts
    bin_ = Buf("ext_in")
    b_wbf = Buf("wbf%d" % L)

    for c in range(11):
        S.dma("pool", io["wg_bf"][c], io["w_gate"][:, c * 256:(c + 1) * 256].rearrange("(j p) n -> p j n", p=P),
              bin_, b_wbf)
        S.dma("pool", io["wu_bf"][c], io["w_up"][:, c * 256:(c + 1) * 256].rearrange("(j p) n -> p j n", p=P),
              bin_, b_wbf)
    S.dma("pool", io["wd_bf"], io["w_down"], bin_, b_wbf, max_dma_last_dim=4096)

    with ExitStack() as lay:
        fcol, b_fcol = sbt(nc, lay, "fcol", [P, 4, 8, 2], F32)
        gbc, b_gbc = sbt(nc, lay, "gbc", [P, 2, 2, D], F32)
        lnbc, b_lnbc = sbt(nc, lay, "lnbc", [P, 4, D], F32)
        S.dma("sp", lnbc[:], io["ln"].partition_broadcast(P), bin_, b_lnbc)

        with ExitStack() as ph:
            cv, b_cv = sbt(nc, ph, "cv", [P, 8, 2], F32)
            sl, b_sl = sbt(nc, ph, "sl", [P, 8, 2], F32)
            rep, b_rep = sbt(nc, ph, "rep", [P, 2, 8, P], F32)
            wm = Ring(nc, ph, "wm", 4, [P, 8, 512], F32)
            bm = Ring(nc, ph, "bm", 4, [1, 512], F32)
            S.dma("sp", cv[:], io["cv"], bin_, b_cv)
            S.op("act", lambda e: e.activation(out=sl[:], in_=cv[:], func=AF.Silu), [b_cv], [b_sl])
            for m in range(2):
                for j in range(8):
                    S.op("dve", lambda e, m=m, j=j: e.tensor_scalar(
                        out=rep[:, m, j, :], in0=ones_f[:, :], scalar1=sl[:, j, m:m + 1], scalar2=None,
                        op0=ALU.mult), [b_sl, b_ones], [b_rep])
            vmap = {1: 0, 0: 1, 4: 2, 3: 3}
            for cb in range(12):
                c6, hf = cb // 2, cb % 2
                wt, bw = wm.next()
                bt, bb = bm.next()
                for jh in range(2):
                    S.dma("sp", wt[:, jh * 4:(jh + 1) * 4, :],
                          io["w_mod"][jh * 512:(jh + 1) * 512, cb * 512:(cb + 1) * 512].rearrange("(j p) n -> p j n", p=P),
                          bin_, bw)
                S.dma("sp", bt[:], io["b_mod"][:, cb * 512:(cb + 1) * 512], bin_, bb)
                if c6 in (2, 5):
                    gi = 0 if c6 == 2 else 1
                    for m in range(2):
                        pb, bp = S.bank()
                        for j in range(8):
                            S.op("pe", lambda e, pb=pb, m=m, j=j, wt=wt: e.matmul(
                                pb[:, :], lhsT=rep[:, m, j, :], rhs=wt[:, j, :], start=(j == 0), stop=False),
                                [b_rep, bw], [bp])
                        S.op("pe", lambda e, pb=pb, bt=bt: e.matmul(
                            pb[:, :], lhsT=ones_f[0:1, :], rhs=bt[0:1, :], start=False, stop=True),
                            [b_ones, bb], [bp])
                        S.op("dve", lambda e, pb=pb, m=m, gi=gi, hf=hf: e.tensor_copy(
                            out=gbc[:, m, gi, hf * 512:(hf + 1) * 512], in_=pb[:, :]), [bp], [b_gbc])
                else:
                    v = vmap[c6]
                    pb, bp = S.bank()
                    for q in range(4):
                        for j in range(8):
                            S.op("pe", lambda e, pb=pb, q=q, j=j, wt=wt: e.matmul(
                                pb[:, 2 * q:2 * q + 2], lhsT=wt[:, j, q * P:(q + 1) * P], rhs=sl[:, j, :],
                                start=(j == 0), stop=False), [b_sl, bw], [bp])
                        S.op("pe", lambda e, pb=pb, q=q, bt=bt: e.matmul(
                            pb[:, 2 * q:2 * q + 2], lhsT=bt[0:1, q * P:(q + 1) * P], rhs=ones_f[0:1, 0:2],
                            start=False, stop=True), [b_ones, bb], [bp])
                    S.op("dve", lambda e, pb=pb, v=v, hf=hf: e.tensor_copy(
                        out=fcol[:, v, hf * 4:hf * 4 + 4, :], in_=pb[:, 0:8].rearrange("p (q m) -> p q m", m=2)),
                        [bp], [b_fcol])
            for v in (0, 2):
                S.op("dve", lambda e, v=v: e.tensor_scalar(
                    out=fcol[:, v, :, :], in0=fcol[:, v, :, :], scalar1=1.0, scalar2=None, op0=ALU.add),
                    [b_fcol], [b_fcol])
            S.end_phase()

        def ln_rows(st_ring, xin, bx, out, bo, reads_extra=(), on_act=True):
            stt, bst = st_ring.next()
            for i in range(2):
                S.op("dve", lambda e, i=i, stt=stt: e.bn_stats(out=stt[:, 6 * i:6 * i + 6],
                                                              in_=xin[:, 512 * i:512 * i + 512]),
                     [bx] + list(reads_extra), [bst])
            S.op("dve", lambda e, stt=stt: e.bn_aggr(out=stt[:, 12:14], in_=stt[:, 0:12]), [bst], [bst])
            S.op("act", lambda e, stt=stt: e.activation(out=stt[:, 14:15], in_=stt[:, 13:14], func=AF.Sqrt,
                                                       bias=EPS, scale=1.0), [bst], [bst])
            S.op("dve", lambda e, stt=stt: e.reciprocal(out=stt[:, 14:15], in_=stt[:, 14:15]), [bst], [bst])
            if on_act:
                S.op("dve", lambda e, stt=stt: e.tensor_scalar(out=stt[:, 15:16], in0=stt[:, 12:13], scalar1=stt[:, 14:15],
                                                              scalar2=-1.0, op0=ALU.mult, op1=ALU.mult), [bst], [bst])
                S.op("act", lambda e, stt=stt: e.activation(out=out, in_=xin, func=AF.Identity, scale=stt[:, 14:15],
                                                           bias=stt[:, 15:16]), [bx, bst], [bo])
            else:
                S.op("dve", lambda e, stt=stt: e.tensor_scalar(out=out, in0=xin, scalar1=stt[:, 12:13],
                                                              scalar2=stt[:, 14:15], op0=ALU.subtract, op1=ALU.mult),
                     [bx, bst], [bo])

        def ln_block(st_ring, xb, bxb, xn_, bxn_, nt):
            stt, bst = st_ring.next()
            for t in range(nt):
                for i in range(2):
                    S.op("dve", lambda e, i=i, t=t, stt=stt: e.bn_stats(out=stt[:, 12 * t + 6 * i:12 * t + 6 * i + 6],
                                                                       in_=xb[:, t, 512 * i:512 * i + 512]), [bxb], [bst])
            for t in range(nt):
                S.op("dve", lambda e, t=t, stt=stt: e.bn_aggr(out=stt[:, 48 + 2 * t:50 + 2 * t], in_=stt[:, 12 * t:12 * t + 12]),
                     [bst], [bst])
            mv = stt[:, 48:56].rearrange("p (t c) -> p t c", c=2)
            S.op("act", lambda e, stt=stt: e.activation(out=stt[:, 56:56 + nt], in_=mv[:, 0:nt, 1], func=AF.Sqrt,
                                                       bias=EPS, scale=1.0), [bst], [bst])
            S.op("dve", lambda e, stt=stt: e.reciprocal(out=stt[:, 56:56 + nt], in_=stt[:, 56:56 + nt]), [bst], [bst])
            S.op("dve", lambda e, stt=stt: e.scalar_tensor_tensor(out=stt[:, 60:60 + nt], in0=mv[:, 0:nt, 0], scalar=-1.0,
                                                                 in1=stt[:, 56:56 + nt], op0=ALU.mult, op1=ALU.mult),
                 [bst], [bst])
            for t in range(nt):
                S.op("act", lambda e, t=t, stt=stt: e.activation(out=xn_[:, t, :], in_=xb[:, t, :], func=AF.Identity,
                                                                scale=stt[:, 56 + t:57 + t], bias=stt[:, 60 + t:61 + t]),
                     [bxb, bst], [bxn_])

        def make_hT(xn, bxn, nt, hT, bhT, vs, vb, m):
            for j in range(8):
                pb, bp = S.bank()
                for t in range(nt):
                    S.op("pe", lambda e, pb=pb, t=t, j=j: e.transpose(
                        out=pb[:, t * P:(t + 1) * P], in_=xn[:, t, j * P:(j + 1) * P], identity=ident_f[:, :]),
                        [bxn, b_idf], [bp])
                if j % 4 != 3:
                    S.op("act", lambda e, pb=pb, j=j: e.activation(
                        out=hT[:, j, 0:nt * P], in_=pb[:, 0:nt * P], func=AF.Identity,
                        scale=fcol[:, vs, j, m:m + 1], bias=fcol[:, vb, j, m:m + 1]), [bp, b_fcol], [bhT])
                else:
                    S.op("dve", lambda e, pb=pb, j=j: e.tensor_scalar(
                        out=hT[:, j, 0:nt * P], in0=pb[:, 0:nt * P], scalar1=fcol[:, vs, j, m:m + 1],
                        scalar2=fcol[:, vb, j, m:m + 1], op0=ALU.mult, op1=ALU.add), [bp, b_fcol], [bhT])

        def rms_T(src_banks, nrm, b_nrm, ntok, sq, b_sq, rb, b_rb, outT, b_out):
            for m2 in range(2):
                pbk, bpk = src_banks[m2]
                S.op("act", lambda e, pbk=pbk, m2=m2: e.activation(out=sq[:, m2, 0:ntok], in_=pbk[:, 0:ntok],
                                                                 func=AF.Square), [bpk], [b_sq])
            pb, bp = S.bank()
            for m2 in range(2):
                S.op("pe", lambda e, pb=pb, m2=m2: e.matmul(pb[:, 0:ntok], lhsT=ones_f[:, :], rhs=sq[:, m2, 0:ntok],
                                                           start=(m2 == 0), stop=(m2 == 1)), [b_sq, b_ones], [bp])
            S.op("act", lambda e, pb=pb: e.activation(out=rb[:, 0:ntok], in_=pb[:, 0:ntok], func=AF.Sqrt,
                                                     bias=EPS, scale=1.0 / 256.0), [bp], [b_rb])
            S.op("dve", lambda e: e.reciprocal(out=rb[:, 0:ntok], in_=rb[:, 0:ntok]), [b_rb], [b_rb])
            for m2 in range(2):
                pbk, bpk = src_banks[m2]
                S.op("dve", lambda e, pbk=pbk, m2=m2: e.scalar_tensor_tensor(
                    out=outT[:, m2, 0:ntok], in0=pbk[:, 0:ntok], scalar=nrm[:, m2:m2 + 1], in1=rb[:, 0:ntok],
                    op0=ALU.mult, op1=ALU.mult), [bpk, b_nrm, b_rb], [b_out])

        def rope_out(pa, bpa, pbk, bpbk, rows, ntok, cs, b_cs, sn, b_sn, t1, b_t1, t2, b_t2, out, b_o):
            S.op("dve", lambda e: e.tensor_tensor(out=t1[0:rows, 0:ntok], in0=pa[0:rows, 0:ntok],
                                                  in1=cs[0:rows, 0:ntok], op=ALU.mult), [bpa, b_cs], [b_t1])
            S.op("dve", lambda e: e.tensor_tensor(out=t2[0:rows, 0:ntok], in0=pbk[0:rows, 0:ntok],
                                                  in1=sn[0:rows, 0:ntok], op=ALU.mult), [bpbk, b_sn], [b_t2])
            S.op("pool", lambda e: e.tensor_tensor(out=out, in0=t1[0:rows, 0:ntok], in1=t2[0:rows, 0:ntok],
                                                   op=ALU.add), [b_t1, b_t2], [b_o])

        kv_blocks = [(io["xc"], 2, 0, 1, xbufs["xc"])] + [(xbufs["xf_blk"](b), 4, CTX + b * 512, 0, xbufs["xf_buf"](b)) for b in range(16)]
        q_blocks = [(xbufs["xo_blk"](b), 4, b * 512, 0, xbufs["xo_buf"](b)) for b in range(8)]
        if need_ctx:
            q_blocks.append((io["xc"], 2, SH, 1, xbufs["xc"]))
        b_KT, b_Vs, b_QT, b_cat = Buf("KT"), Buf("Vs"), Buf("QT"), Buf("cat")
        b_Fdl, b_Fdc = Buf("Fdl"), Buf("Fdc")
        KT, Vs, QT, cat = io["KT"], io["Vs"], io["QT"], io["cat"]
        vw = 65 if L == 0 else 129
        R = 96 if L == 0 else 128

        with ExitStack() as ph:
            xr = Ring(nc, ph, "xr", 1, [P, 4, D], F32)
            xnr = Ring(nc, ph, "xn", 2, [P, 4, D], F32)
            stb_ring = Ring(nc, ph, "stb", 2, [P, 64], F32)
            hTr = Ring(nc, ph, "hT", 2, [P, 8, 512], BF16)
            st_ring = Ring(nc, ph, "st", 4, [P, 16], F32)
            csr = Ring(nc, ph, "csk", 2, [R if L == 1 else 32, 512], F32)
            snr = Ring(nc, ph, "snk", 2, [R if L == 1 else 32, 512], F32)
            t1, b_t1 = sbt(nc, ph, "t1", [P, 512], F32)
            t2, b_t2 = sbt(nc, ph, "t2", [P, 512], F32)
            vsr = Ring(nc, ph, "vsb", 2, [P, 4, 8, vw], BF16)
            for i in range(2):
                S.op("pool", lambda e, i=i: e.memset(vsr.t[i][:], 1.0), [], [vsr.b[i]])
            if L == 0:
                win, b_win = sbt(nc, ph, "win", [P, 8, 1056], BF16)
                wkr, b_wkr = sbt(nc, ph, "wkr", [P, 8, 32], BF16)
                wkn, b_wkn = sbt(nc, ph, "wkn", [P, 2, 512], BF16)
                wv, b_wv = sbt(nc, ph, "wv", [P, 2, 512], BF16)
                kvn, b_kvn = sbt(nc, ph, "kvn", [P, 2], F32)
                S.dma("pool", win[:], io["w_in"].rearrange("(j p) n -> p j n", p=P), bin_, b_win)
                S.dma("pool", wkr[:], io["w_kr_sw"].rearrange("(j p) n -> p j n", p=P), bin_, b_wkr)
                S.dma("pool", wkn[:], io["w_ukv_kn"].rearrange("(j p) n -> p j n", p=P), bin_, b_wkn)
                S.dma("pool", wv[:], io["w_ukv_v"].rearrange("(j p) n -> p j n", p=P), bin_, b_wv)
                S.dma("sp", kvn[:], io["kvn"], bin_, b_kvn)
                fsr = Ring(nc, ph, "fsb", 2, [P, 4, 512], F32)
                sq, b_sq = sbt(nc, ph, "sq", [P, 2, 512], F32)
                rb, b_rb = sbt(nc, ph, "rb", [P, 512], F32)
                ckvn, b_ckvn = sbt(nc, ph, "ckvn", [P, 2, 512], BF16)
                knr = Ring(nc, ph, "knT", 2, [P, 4, 512], BF16)
                krr = Ring(nc, ph, "krT", 2, [32, 512], BF16)
            else:
                win, b_win = sbt(nc, ph, "win", [P, 8, 2048], BF16)
                wsw, b_wsw = sbt(nc, ph, "wsw", [P, 8, 1024], BF16)
                for kvh in range(2):
                    S.dma("pool", win[:, :, kvh * 1024:(kvh + 1) * 1024],
                          io["w_in"][:, 1024 + kvh * 1024:2048 + kvh * 1024].rearrange("(j p) n -> p j n", p=P), bin_, b_win)
                S.dma("pool", wsw[:], io["w_qk_sw"][:, 1024:2048].rearrange("(j p) n -> p j n", p=P), bin_, b_wsw)
                knr = Ring(nc, ph, "knT", 2, [P, 8, 512], BF16)

            def kv_a(src, nt, koff, m, sbuf_src):
                ntok = nt * P
                xb, bxb = xr.next()
                S.dma("sp", xb[:, 0:nt, :], src.rearrange("(t p) d -> p t d", p=P), sbuf_src, bxb)
                cs, b_cs = csr.next()
                sn, b_sn = snr.next()
                S.dma("sp", cs[:, 0:ntok], io["cosK"][:, koff:koff + ntok], bin_, b_cs)
                S.dma("sp", sn[:, 0:ntok], io["sinK"][:, koff:koff + ntok], bin_, b_sn)
                xn_, bxn_ = xnr.next()
                ln_block(stb_ring, xb, bxb, xn_, bxn_, nt)
                return dict(nt=nt, koff=koff, m=m, cs=cs, b_cs=b_cs, sn=sn, b_sn=b_sn, xn=xn_, bxn=bxn_)

            def kv_b(st):
                hT, bhT = hTr.next()
                make_hT(st["xn"], st["bxn"], st["nt"], hT, bhT, 0, 1, st["m"])
                st["hT"], st["bhT"] = hT, bhT

            def kv_body(st):
                nt, koff, m = st["nt"], st["koff"], st["m"]
                cs, b_cs, sn, b_sn, hT, bhT = st["cs"], st["b_cs"], st["sn"], st["b_sn"], st["hT"], st["bhT"]
                ntok = nt * P
                vsb, bvs = vsr.next()
                if L == 0:
                    fsb, bfs = fsr.next()
                    for t in range(nt):
                        pb, bp = S.bank()
                        for j in range(8):
                            S.op("pe", lambda e, pb=pb, t=t, j=j, hT=hT: e.matmul(
                                pb[:, :], lhsT=hT[:, j, t * P:(t + 1) * P], rhs=win[:, j, 0:512],
                                start=(j == 0), stop=(j == 7)), [bhT, b_win], [bp])
                        S.op("act", lambda e, pb=pb, t=t, fsb=fsb: e.copy(out=fsb[:, t, :], in_=pb[:, :]), [bp], [bfs])
                    if m == 1:
                        S.dma("pool", io["Fdc"].rearrange("(t p) c -> p t c", p=P), fsb[:, 0:nt, :], bfs, b_Fdc)
                    else:
                        n0 = koff - CTX
                        S.dma("pool", io["Fdl"][n0:n0 + ntok, :].rearrange("(t p) c -> p t c", p=P), fsb[:, 0:nt, :],
                              bfs, b_Fdl)
                    banks = []
                    for m2 in range(2):
                        pb, bp = S.bank()
                        banks.append((pb, bp))
                        for j in range(8):
                            S.op("pe", lambda e, pb=pb, m2=m2, j=j, hT=hT: e.matmul(
                                pb[:, 0:ntok], lhsT=win[:, j, 768 + m2 * P:768 + (m2 + 1) * P], rhs=hT[:, j, 0:ntok],
                                start=(j == 0), stop=(j == 7)), [bhT, b_win], [bp])
                    rms_T(banks, kvn, b_kvn, ntok, sq, b_sq, rb, b_rb, ckvn, b_ckvn)
                    knT, bkn = knr.next()
                    for hp in range(4):
                        pb, bp = S.bank()
                        for m2 in range(2):
                            S.op("pe", lambda e, pb=pb, hp=hp, m2=m2: e.matmul(
                                pb[:, 0:ntok],
                                lhsT=wkn[:, m2, hp * P:(hp + 1) * P],
                                rhs=ckvn[:, m2, 0:ntok], start=(m2 == 0), stop=(m2 == 1)), [b_ckvn, b_wkn], [bp])
                        S.op("act", lambda e, pb=pb, hp=hp, knT=knT: e.copy(out=knT[:, hp, 0:ntok], in_=pb[:, 0:ntok]),
                             [bp], [bkn])
                    for hh in range(2):
                        S.dma("pool", KT.rearrange("(hp hh) r n -> hh r hp n", hh=2)[hh, 0:64, :, koff:koff + ntok],
                              knT[hh * 64:(hh + 1) * 64, :, 0:ntok], bkn, b_KT)
                    for t in range(nt):
                        pb, bp = S.bank()
                        for m2 in range(2):
                            S.op("pe", lambda e, pb=pb, t=t, m2=m2: e.matmul(
                                pb[:, :], lhsT=ckvn[:, m2, t * P:(t + 1) * P],
                                rhs=wv[:, m2, :],
                                start=(m2 == 0), stop=(m2 == 1)), [b_ckvn, b_wv], [bp])
                        S.op("dve", lambda e, pb=pb, t=t, vsb=vsb: e.tensor_copy(
                            out=vsb[:, t, :, 0:64], in_=pb[:, :].rearrange("p (h c) -> p h c", c=64)), [bp], [bvs])
                    pa, bpa = S.bank()
                    pbk, bpbk = S.bank()
                    for j in range(8):
                        S.op("pe", lambda e, pa=pa, j=j, hT=hT: e.matmul(
                            pa[0:32, 0:ntok], lhsT=win[:, j, 1024:1056], rhs=hT[:, j, 0:ntok],
                            start=(j == 0), stop=(j == 7)), [bhT, b_win], [bpa])
                    for j in range(8):
                        S.op("pe", lambda e, pbk=pbk, j=j, hT=hT: e.matmul(
                            pbk[0:32, 0:ntok], lhsT=wkr[:, j, :], rhs=hT[:, j, 0:ntok],
                            start=(j == 0), stop=(j == 7)), [bhT, b_wkr], [bpbk])
                    krT, bkr = krr.next()
                    rope_out(pa, bpa, pbk, bpbk, 32, ntok, cs, b_cs, sn, b_sn, t1, b_t1, t2, b_t2,
                             krT[0:32, 0:ntok], bkr)
                    for h in range(8):
                        S.dma("pool", KT[h, 64:96, koff:koff + ntok], krT[0:32, 0:ntok], bkr, b_KT)
                else:
                    knT, bkn = knr.next()
                    for h in range(8):
                        pa, bpa = S.bank()
                        pbk, bpbk = S.bank()
                        for j in range(8):
                            S.op("pe", lambda e, pa=pa, j=j, h=h, hT=hT: e.matmul(
                                pa[:, 0:ntok], lhsT=win[:, j, h * P:(h + 1) * P], rhs=hT[:, j, 0:ntok],
                                start=(j == 0), stop=(j == 7)), [bhT, b_win], [bpa])
                        for j in range(8):
                            S.op("pe", lambda e, pbk=pbk, j=j, h=h, hT=hT: e.matmul(
                                pbk[:, 0:ntok], lhsT=wsw[:, j, h * P:(h + 1) * P], rhs=hT[:, j, 0:ntok],
                                start=(j == 0), stop=(j == 7)), [bhT, b_wsw], [bpbk])
                        rope_out(pa, bpa, pbk, bpbk, P, ntok, cs, b_cs, sn, b_sn, t1, b_t1, t2, b_t2,
                                 knT[:, h, 0:ntok], bkn)
                    S.dma("pool", KT[:, :, koff:koff + ntok].rearrange("h r n -> r h n"), knT[:, :, 0:ntok], bkn, b_KT)
                    for t in range(nt):
                        for hf in range(2):
                            pb, bp = S.bank()
                            for j in range(8):
                                S.op("pe", lambda e, pb=pb, t=t, j=j, hf=hf, hT=hT: e.matmul(
                                    pb[:, :], lhsT=hT[:, j, t * P:(t + 1) * P],
                                    rhs=win[:, j, 1024 + hf * 512:1024 + (hf + 1) * 512],
                                    start=(j == 0), stop=(j == 7)), [bhT, b_win], [bp])
                            S.op("dve", lambda e, pb=pb, t=t, hf=hf, vsb=vsb: e.tensor_copy(
                                out=vsb[:, t, hf * 4:(hf + 1) * 4, 0:128], in_=pb[:, :].rearrange("p (h c) -> p h c", c=128)),
                                [bp], [bvs])
                S.dma("pool", Vs[koff:koff + ntok].rearrange("(t p) h c -> p t (h c)", p=P),
                      vsb[:, 0:nt].rearrange("p t h c -> p t (h c)"), bvs, b_Vs)
            sts = [None] * len(kv_blocks)
            sts[0] = kv_a(*kv_blocks[0])
            kv_b(sts[0])
            for i in range(len(kv_blocks)):
                if i + 1 < len(kv_blocks):
                    sts[i + 1] = kv_a(*kv_blocks[i + 1])
                kv_body(sts[i])
                if i + 1 < len(kv_blocks):
                    kv_b(sts[i + 1])
            S.end_phase()

        with ExitStack() as ph:
            xr = Ring(nc, ph, "xr", 1, [P, 4, D], F32)
            xnr = Ring(nc, ph, "xn", 2, [P, 4, D], F32)
            stb_ring = Ring(nc, ph, "stb", 2, [P, 64], F32)
            hTr = Ring(nc, ph, "hT", 2, [P, 8, 512], BF16)
            st_ring = Ring(nc, ph, "st", 4, [P, 16], F32)
            csr = Ring(nc, ph, "csq", 2, [R, 512], F32)
            snr = Ring(nc, ph, "snq", 2, [R, 512], F32)
            t1r = Ring(nc, ph, "t1", 2, [P, 512], F32)
            t2r = Ring(nc, ph, "t2", 2, [P, 512], F32)
            qsr = Ring(nc, ph, "qsb", 2, [R, 8, 512], BF16)
            if L == 0:
                win, b_win = sbt(nc, ph, "win", [P, 8, 256], BF16)
                wuq, b_wuq = sbt(nc, ph, "wuq", [P, 2, 768], BF16)
                wuqs, b_wuqs = sbt(nc, ph, "wuqs", [P, 2, 768], BF16)
                qn, b_qn = sbt(nc, ph, "qn", [P, 2], F32)
                S.dma("pool", win[:], io["w_in"][:, 512:768].rearrange("(j p) n -> p j n", p=P), bin_, b_win)
                S.dma("pool", wuq[:], io["w_uq"].rearrange("(j p) n -> p j n", p=P), bin_, b_wuq)
                S.dma("pool", wuqs[:], io["w_uq_sw"].rearrange("(j p) n -> p j n", p=P), bin_, b_wuqs)
                S.dma("sp", qn[:], io["qn"], bin_, b_qn)
                sq, b_sq = sbt(nc, ph, "sq", [P, 2, 512], F32)
                rb, b_rb = sbt(nc, ph, "rb", [P, 512], F32)
                cqn, b_cqn = sbt(nc, ph, "cqn", [P, 2, 512], BF16)
            else:
                win, b_win = sbt(nc, ph, "win", [P, 8, 1024], BF16)
                wsw, b_wsw = sbt(nc, ph, "wsw", [P, 8, 1024], BF16)
                S.dma("pool", win[:], io["w_in"][:, 0:1024].rearrange("(j p) n -> p j n", p=P), bin_, b_win)
                S.dma("pool", wsw[:], io["w_qk_sw"][:, 0:1024].rearrange("(j p) n -> p j n", p=P), bin_, b_wsw)
            def q_a(src, nt, qoff, m, sbuf_src):
                ntok = nt * P
                xb, bxb = xr.next()
                S.dma("sp", xb[:, 0:nt, :], src.rearrange("(t p) d -> p t d", p=P), sbuf_src, bxb)
                cs, b_cs = csr.next()
                sn, b_sn = snr.next()
                S.dma("sp", cs[:, 0:ntok], io["cosQ"][:, qoff:qoff + ntok], bin_, b_cs)
                S.dma("sp", sn[:, 0:ntok], io["sinQ"][:, qoff:qoff + ntok], bin_, b_sn)
                xn_, bxn_ = xnr.next()
                ln_block(stb_ring, xb, bxb, xn_, bxn_, nt)
                return dict(nt=nt, qoff=qoff, m=m, cs=cs, b_cs=b_cs, sn=sn, b_sn=b_sn, xn=xn_, bxn=bxn_)

            def q_b(st):
                hT, bhT = hTr.next()
                make_hT(st["xn"], st["bxn"], st["nt"], hT, bhT, 0, 1, st["m"])
                st["hT"], st["bhT"] = hT, bhT

            def q_body(st):
                nt, qoff, m = st["nt"], st["qoff"], st["m"]
                cs, b_cs, sn, b_sn, hT, bhT = st["cs"], st["b_cs"], st["sn"], st["b_sn"], st["hT"], st["bhT"]
                ntok = nt * P
                qsb, bqs = qsr.next()
                if L == 0:
                    banks = []
                    for m2 in range(2):
                        pb, bp = S.bank()
                        banks.append((pb, bp))
                        for j in range(8):
                            S.op("pe", lambda e, pb=pb, m2=m2, j=j, hT=hT: e.matmul(
                                pb[:, 0:ntok], lhsT=win[:, j, m2 * P:(m2 + 1) * P], rhs=hT[:, j, 0:ntok],
                                start=(j == 0), stop=(j == 7)), [bhT, b_win], [bp])
                    rms_T(banks, qn, b_qn, ntok, sq, b_sq, rb, b_rb, cqn, b_cqn)
                for h in range(8):
                    pa, bpa = S.bank()
                    pbk, bpbk = S.bank()
                    if L == 0:
                        for m2 in range(2):
                            S.op("pe", lambda e, pa=pa, m2=m2, h=h: e.matmul(
                                pa[0:96, 0:ntok], lhsT=wuq[:, m2, h * 96:(h + 1) * 96], rhs=cqn[:, m2, 0:ntok],
                                start=(m2 == 0), stop=(m2 == 1)), [b_cqn, b_wuq], [bpa])
                        for m2 in range(2):
                            S.op("pe", lambda e, pbk=pbk, m2=m2, h=h: e.matmul(
                                pbk[0:96, 0:ntok], lhsT=wuqs[:, m2, h * 96:(h + 1) * 96], rhs=cqn[:, m2, 0:ntok],
                                start=(m2 == 0), stop=(m2 == 1)), [b_cqn, b_wuqs], [bpbk])
                    else:
                        for j in range(8):
                            S.op("pe", lambda e, pa=pa, j=j, h=h, hT=hT: e.matmul(
                                pa[:, 0:ntok], lhsT=win[:, j, h * P:(h + 1) * P], rhs=hT[:, j, 0:ntok],
                                start=(j == 0), stop=(j == 7)), [bhT, b_win], [bpa])
                        for j in range(8):
                            S.op("pe", lambda e, pbk=pbk, j=j, h=h, hT=hT: e.matmul(
                                pbk[:, 0:ntok], lhsT=wsw[:, j, h * P:(h + 1) * P], rhs=hT[:, j, 0:ntok],
                                start=(j == 0), stop=(j == 7)), [bhT, b_wsw], [bpbk])
                    t1, b_t1 = t1r.next()
                    t2, b_t2 = t2r.next()
                    rope_out(pa, bpa, pbk, bpbk, R, ntok, cs, b_cs, sn, b_sn, t1, b_t1, t2, b_t2,
                             qsb[0:R, h, 0:ntok], bqs)
                S.dma("pool", QT[:, :, qoff:qoff + ntok].rearrange("h r n -> r h n"), qsb[0:R, :, 0:ntok], bqs, b_QT)
            sts = [None] * len(q_blocks)
            sts[0] = q_a(*q_blocks[0])
            q_b(sts[0])
            for i in range(len(q_blocks)):
                if i + 1 < len(q_blocks):
                    sts[i + 1] = q_a(*q_blocks[i + 1])
                q_body(sts[i])
                if i + 1 < len(q_blocks):
                    q_b(sts[i + 1])
            S.end_phase()

        if L == 0:
            with ExitStack() as ph:
                f64, b_f64 = sbt(nc, ph, "f64", [64, 128], F32)
                cdl, b_cdl = sbt(nc, ph, "cdl", [P, 2, 128], F32)
                cdc, b_cdc = sbt(nc, ph, "cdc", [P, 2, 128], F32)
                f256, b_f256 = sbt(nc, ph, "f256", [P, 2, 512], F32)
                S.dma("sp", f64[:], io["F64"], bin_, b_f64)
                S.dma("sp", cdl[:], io["CdSl"], bin_, b_cdl)
                S.dma("sp", cdc[:], io["CdSc"], bin_, b_cdc)
                S.dma("sp", f256[:], io["F256"], bin_, b_f256)
                Xr = Ring(nc, ph, "Xh", 2, [64, 128, 32], F32)
                T, b_T = sbt(nc, ph, "T", [P, 128, 128], F32)
                ZT, b_ZT = sbt(nc, ph, "ZT", [P, 2, 64, 64], F32)
                Gr_ = Ring(nc, ph, "G", 2, [P, 8, 2, 128], F32)
                ysr = Ring(nc, ph, "ysb", 2, [P, 4, 128], BF16)
                Fd3 = io["Fdl"].rearrange("(a b) c -> a b c", b=128)
                ZTf = ZT[:, :, :, :].rearrange("p r i q -> p r (i q)")
                for g in range(4):
                    for xq in range(4):
                        X, bX = Xr.next()
                        c0 = g * 128 + xq * 32
                        S.dma("sp", X[:], Fd3[:, :, c0:c0 + 32], b_Fdl, bX)
                        for cg in range(8):
                            pb, bp = S.bank()
                            for cc in range(4):
                                c = cg * 4 + cc
                                S.op("pe", lambda e, pb=pb, cc=cc, c=c, X=X: e.matmul(
                                    pb[:, cc * 128:(cc + 1) * 128], lhsT=X[:, :, c], rhs=f64[:, :],
                                    start=True, stop=True), [bX, b_f64], [bp])
                            ch0 = xq * 32 + cg * 4
                            if cg % 2 == 0:
                                S.op("act", lambda e, pb=pb, ch0=ch0: e.copy(
                                    out=T[:, ch0:ch0 + 4, :], in_=pb[:, :].rearrange("p (c k) -> p c k", k=128)),
                                    [bp], [b_T])
                            else:
                                S.op("dve", lambda e, pb=pb, ch0=ch0: e.tensor_copy(
                                    out=T[:, ch0:ch0 + 4, :], in_=pb[:, :].rearrange("p (c k) -> p c k", k=128)),
                                    [bp], [b_T])
                    for kc in range(8):
                        G, bG = Gr_.next()
                        S.dma("sp", G[:], io["GAB"][:, kc * 8:(kc + 1) * 8], bin_, bG)
                        for kq in range(2):
                            pb, bp = S.bank()
                            for q in range(4):
                                kl = kq * 4 + q
                                k2 = kc * 8 + kl
                                S.op("pe", lambda e, pb=pb, q=q, kl=kl, k2=k2, G=G: e.matmul(
                                    pb[:, q * 128:(q + 1) * 128], lhsT=T[:, :, k2],
                                    rhs=G[:, kl, 0, :], start=True, stop=False), [b_T, bG], [bp])
                                S.op("pe", lambda e, pb=pb, q=q, kl=kl, k2=k2, G=G: e.matmul(
                                    pb[:, q * 128:(q + 1) * 128], lhsT=T[:, :, 64 + k2],
                                    rhs=G[:, kl, 1, :], start=False, stop=True), [b_T, bG], [bp])
                            k20 = kc * 8 + kq * 4
                            S.op("dve", lambda e, pb=pb, k20=k20: e.tensor_copy(
                                out=ZT[:, :, :, k20:k20 + 4].rearrange("p r i q -> p q r i"),
                                in_=pb[:, :].rearrange("p (q r i) -> p q r i", q=4, r=2)),
                                [bp], [b_ZT])
                    for tq in range(8):
                        pb, bp = S.bank()
                        for q in range(4):
                            tt = tq * 4 + q
                            for r in range(2):
                                S.op("pe", lambda e, pb=pb, q=q, tt=tt, r=r: e.matmul(
                                    pb[:, q * 128:(q + 1) * 128], lhsT=ZTf[:, r, tt * 128:(tt + 1) * 128],
                                    rhs=cdl[:, r, :], start=(r == 0), stop=(r == 1)), [b_ZT, b_cdl], [bp])
                        ysb, bys = ysr.next()
                        S.op("act", lambda e, pb=pb, ysb=ysb: e.copy(
                            out=ysb[:, :, :], in_=pb[:, :].rearrange("p (q c) -> p q c", c=128)), [bp], [bys])
                        S.dma("pool", cat[tq * 512:(tq + 1) * 512, g * 128:(g + 1) * 128].rearrange("(t p) c -> p t c", p=P),
                              ysb[:, :, :], bys, b_cat)
                fcx, b_fcx = sbt(nc, ph, "fcx", [P, 2, 512], F32)
                zc, b_zc = sbt(nc, ph, "zc", [P, 512], F32)
                S.dma("sp", fcx[:], io["Fdc"].rearrange("(t p) c -> p t c", p=P), b_Fdc, b_fcx)
                for g in range(4):
                    pb, bp = S.bank()
                    for tl in range(2):
                        S.op("pe", lambda e, pb=pb, tl=tl, g=g: e.matmul(
                            pb[:, :], lhsT=fcx[:, tl, g * 128:(g + 1) * 128], rhs=f256[:, tl, :],
                            start=(tl == 0), stop=(tl == 1)), [b_fcx, b_f256], [bp])
                    S.op("dve", lambda e, pb=pb: e.tensor_copy(out=zc[:, :], in_=pb[:, :]), [bp], [b_zc])
                    pb2, bp2 = S.bank()
                    for tl in range(2):
                        for r in range(2):
                            S.op("pe", lambda e, pb2=pb2, tl=tl, r=r: e.matmul(
                                pb2[:, tl * 128:(tl + 1) * 128], lhsT=zc[:, r * 256 + tl * 128:r * 256 + (tl + 1) * 128],
                                rhs=cdc[:, r, :], start=(r == 0), stop=(r == 1)), [b_zc, b_cdc], [bp2])
                    ysb, bys = ysr.next()
                    S.op("act", lambda e, pb2=pb2, ysb=ysb: e.copy(
                        out=ysb[:, 0:2, :], in_=pb2[:, 0:256].rearrange("p (q c) -> p q c", c=128)), [bp2], [bys])
                    S.dma("pool", cat[SH:SH + CTX, g * 128:(g + 1) * 128].rearrange("(t p) c -> p t c", p=P),
                          ysb[:, 0:2, :], bys, b_cat)
                S.end_phase()

        with ExitStack() as ph:
            ktr = Ring(nc, ph, "kts", 2, [R, NK], BF16)
            vr = Ring(nc, ph, "vs", 2, [P, 66, vw], BF16)
            if L == 0:
                qtr = Ring(nc, ph, "qts", 2, [R, SH + CTX], BF16)
            else:
                qtr = Ring(nc, ph, "qts", 2, [P, 2, SH], BF16)
                for i in range(2):
                    S.op("pool", lambda e, i=i: e.memset(qtr.t[i][:], 0.0), [], [qtr.b[i]])
            ptr = Ring(nc, ph, "pt", 3, [P, 1024], BF16)
            rcr = Ring(nc, ph, "rc", 4, [P, 8], F32)
            asr = Ring(nc, ph, "asb", 2, [P, 4, 128 if L == 1 else 64], BF16)
            if L == 1:
                o0r = Ring(nc, ph, "o0", 2, [P, 4, 128], F32)
                o1r = Ring(nc, ph, "o1", 2, [P, 128], F32)
                sqr = Ring(nc, ph, "sqr", 2, [P, 128], F32)
                lamt, b_lam = sbt(nc, ph, "lamt", [P, 4, 64], F32)
                lamw, b_lamw = sbt(nc, ph, "lamw", [P, 8], F32)
                lamj, b_lamj = sbt(nc, ph, "lamj", [P, 64], F32)
                subl, b_subl = sbt(nc, ph, "subl", [P, 128], F32)
                epsc, b_epsc = sbt(nc, ph, "epsc", [P, 1], F32)
                S.op("dve", lambda e: e.memset(epsc[:], EPS), [], [b_epsc])
                S.dma("sp", lamt[:], io["lam"].partition_broadcast(P), bin_, b_lam)
                S.dma("sp", subl[:], io["subln"].partition_broadcast(P), bin_, b_subl)
                for i in range(2):
                    S.op("dve", lambda e, i=i: e.scalar_tensor_tensor(
                        out=lamj[:, :], in0=lamt[:, 2 * i, :], scalar=1.0, in1=lamt[:, 2 * i + 1, :],
                        op0=ALU.mult, op1=ALU.mult, accum_out=lamw[:, i:i + 1]), [b_lam], [b_lamj, b_lamw])
                S.op("act", lambda e: e.activation(out=lamw[:, 2:4], in_=lamw[:, 0:2], func=AF.Exp), [b_lamw], [b_lamw])
                S.op("dve", lambda e: e.tensor_tensor(out=lamw[:, 4:5], in0=lamw[:, 3:4], in1=lamw[:, 2:3],
                                                      op=ALU.subtract), [b_lamw], [b_lamw])
                S.op("dve", lambda e: e.tensor_scalar(out=lamw[:, 5:6], in0=lamw[:, 4:5], scalar1=-LAMBDA_INIT1,
                                                      scalar2=None, op0=ALU.add), [b_lamw], [b_lamw])
                S.op("dve", lambda e: e.tensor_scalar(out=subl[:, :], in0=subl[:, :], scalar1=1.0 - LAMBDA_INIT1,
                                                      scalar2=None, op0=ALU.mult), [b_subl], [b_subl])
            scale = MLA_SCALE if L == 0 else DIFF_SCALE
            nq_tot = SH + (CTX if need_ctx else 0)
            qblocks = [(b * 512, 512, 66) for b in range(8)]
            if need_ctx:
                qblocks.append((SH, 256, 2))
            nmaps = 1 if L == 0 else 2
            accr = Ring(nc, ph, "accs", 4, [P, 4, 132], F32)
            rc8r = Ring(nc, ph, "rc8", 4, [P, 16], F32)
            o1br = Ring(nc, ph, "o1b", 2, [P, 4, 128], F32)
            items = []
            for h in range(8):
                for qi in range(len(qblocks)):
                    for ci in range(nmaps):
                        for kp in range(qblocks[qi][2] // 2):
                            items.append((h, qi, ci, kp))
            heads = {}

            def load_head(h):
                kts, bkt = ktr.next()
                vs_, bv = vr.next()
                qts, bqt = qtr.next()
                S.dma("sp", kts[:, :], KT[h], b_KT, bkt)
                S.dma("sp", vs_[:, :, :], Vs[:, h, :].rearrange("(t p) c -> p t c", p=P), b_Vs, bv)
                if L == 0:
                    S.dma("sp", qts[:, 0:nq_tot], QT[h, :, 0:nq_tot], b_QT, bqt)
                else:
                    S.dma("sp", qts[0:64, 0, :], QT[h, 0:64, :], b_QT, bqt)
                    S.dma("sp", qts[64:128, 1, :], QT[h, 64:128, :], b_QT, bqt)
                heads[h] = (kts, bkt, vs_, bv, qts, bqt)

            qstate = {}

            def finalize(h, qi, ci):
                q0, nq, nkt = qblocks[qi]
                nj = nq // P
                stq = qstate.setdefault((h, qi), {})
                acs, bacs = accr.next()
                for j in range(nj):
                    S.op("dve", lambda e, j=j, acs=acs: e.tensor_copy(out=acs[:, j, 0:vw], in_=S.pb[4 + j][:, 0:vw]),
                         [S.bpb[4 + j]], [bacs])
                if ci == 0:
                    rc, brc = rc8r.next()
                    stq["rc"], stq["brc"] = rc, brc
                    stq["a0"], stq["ba0"] = acs, bacs
                else:
                    rc, brc = stq["rc"], stq["brc"]
                S.op("dve", lambda e, acs=acs, rc=rc, ci=ci: e.reciprocal(
                    out=rc[:, 4 * ci:4 * ci + nj], in_=acs[:, 0:nj, vw - 1]), [bacs], [brc])
                if L == 0:
                    asb, bas = asr.next()
                    for j in range(nj):
                        S.op("dve", lambda e, j=j, acs=acs, rc=rc, asb=asb: e.tensor_scalar(
                            out=asb[:, j, :], in0=acs[:, j, 0:64], scalar1=rc[:, j:j + 1], scalar2=None, op0=ALU.mult),
                            [bacs, brc], [bas])
                    S.dma("pool", cat[q0:q0 + nq, 512 + h * 64:512 + (h + 1) * 64].rearrange("(t p) c -> p t c", p=P),
                          asb[:, 0:nj, :], bas, b_cat)
                elif ci == 1:
                    a0, ba0 = stq["a0"], stq["ba0"]
                    asb, bas = asr.next()
                    o1, b_o1 = o1br.next()
                    sqt, bsqt = sqr.next()
                    S.op("dve", lambda e, rc=rc: e.tensor_scalar(out=rc[:, 4:8], in0=rc[:, 4:8], scalar1=lamw[:, 5:6],
                                                                 scalar2=None, op0=ALU.mult), [brc, b_lamw], [brc])
                    for j in range(nj):
                        S.op("dve", lambda e, j=j, a0=a0, rc=rc, o1=o1: e.tensor_scalar(
                            out=o1[:, j, :], in0=a0[:, j, 0:128], scalar1=rc[:, j:j + 1], scalar2=None, op0=ALU.mult),
                            [ba0, brc], [b_o1])
                        S.op("dve", lambda e, j=j, acs=acs, rc=rc, o1=o1: e.scalar_tensor_tensor(
                            out=o1[:, j, :], in0=acs[:, j, 0:128], scalar=rc[:, 4 + j:5 + j], in1=o1[:, j, :],
                            op0=ALU.mult, op1=ALU.add), [bacs, brc, b_o1], [b_o1])
                        S.op("dve", lambda e, j=j, o1=o1, rc=rc, sqt=sqt: e.scalar_tensor_tensor(
                            out=sqt[:, :], in0=o1[:, j, :], scalar=1.0, in1=o1[:, j, :],
                            op0=ALU.mult, op1=ALU.mult, accum_out=rc[:, 8 + j:9 + j]), [b_o1], [brc, bsqt])
                    S.op("act", lambda e, rc=rc: e.activation(out=rc[:, 12:16], in_=rc[:, 8:12], func=AF.Ln,
                                                             bias=epsc[:, 0:1], scale=1.0 / 128.0), [brc, b_epsc], [brc])
                    S.op("act", lambda e, rc=rc: e.activation(out=rc[:, 12:16], in_=rc[:, 12:16], func=AF.Exp,
                                                             scale=-0.5), [brc], [brc])
                    for j in range(nj):
                        S.op("dve", lambda e, j=j, o1=o1, rc=rc, asb=asb: e.scalar_tensor_tensor(
                            out=asb[:, j, :], in0=o1[:, j, :], scalar=rc[:, 12 + j:13 + j], in1=subl[:, :],
                            op0=ALU.mult, op1=ALU.mult), [b_o1, brc, b_subl], [bas])
                    S.dma("pool", cat[q0:q0 + nq, h * 128:(h + 1) * 128].rearrange("(t p) c -> p t c", p=P),
                          asb[:, 0:nj, :], bas, b_cat)

            DEPTH = 1
            pts = {}
            load_head(0)
            spair = [(S.pbig[:, 0:1024], [S.bpb[0], S.bpb[1]]), (S.pbig[:, 1024:2048], [S.bpb[2], S.bpb[3]])]
            for n in range(len(items) + DEPTH):
                if n < len(items):
                    h, qi, ci, kp = items[n]
                    q0, nq, nkt = qblocks[qi]
                    kts, bkt, vs_, bv, qts, bqt = heads[h]
                    sp2, bsp2 = spair[n % 2]
                    for u in range(2):
                        kt = 2 * kp + u
                        if L == 0:
                            S.op("pe", lambda e, sp2=sp2, u=u, kt=kt, kts=kts, qts=qts, q0=q0, nq=nq: e.matmul(
                                sp2[:, u * 512:u * 512 + nq], lhsT=kts[0:96, kt * P:(kt + 1) * P], rhs=qts[0:96, q0:q0 + nq],
                                start=True, stop=True), [bkt, bqt], [bsp2[u]])
                        else:
                            S.op("pe", lambda e, sp2=sp2, u=u, kt=kt, kts=kts, qts=qts, ci=ci, q0=q0, nq=nq: e.matmul(
                                sp2[:, u * 512:u * 512 + nq], lhsT=kts[:, kt * P:(kt + 1) * P], rhs=qts[:, ci, q0:q0 + nq],
                                start=True, stop=True), [bkt, bqt], [bsp2[u]])
                    pt, bpt = ptr.next()
                    S.op("act", lambda e, sp2=sp2, pt=pt, nq=nq: e.activation(
                        out=pt[:, :].rearrange("p (u n) -> p u n", u=2)[:, :, 0:nq],
                        in_=sp2.rearrange("p (u n) -> p u n", u=2)[:, :, 0:nq], func=AF.Exp, scale=scale), bsp2, [bpt])
                    pts[n] = (pt, bpt)
                if n >= DEPTH:
                    h, qi, ci, kp = items[n - DEPTH]
                    q0, nq, nkt = qblocks[qi]
                    if qi == 0 and ci == 0 and kp == 0 and h + 1 < 8:
                        load_head(h + 1)
                    kts, bkt, vs_, bv, qts, bqt = heads[h]
                    pt, bpt = pts.pop(n - DEPTH)
                    for u in range(2):
                        kt = 2 * kp + u
                        for j in range(nq // P):
                            S.op("pe", lambda e, j=j, u=u, pt=pt, kt=kt, vs_=vs_, nkt=nkt: e.matmul(
                                S.pb[4 + j][:, 0:vw], lhsT=pt[:, u * 512 + j * P:u * 512 + (j + 1) * P], rhs=vs_[:, kt, :],
                                start=(kt == 0), stop=(kt == nkt - 1)), [bpt, bv], [S.bpb[4 + j]])
                    if 2 * kp + 1 == nkt - 1:
                        finalize(h, qi, ci)
            S.end_phase()

        with ExitStack() as ph:
            wout, b_wout = sbt(nc, ph, "wout", [P, 8, D], BF16)
            S.dma("pool", wout[:], io["w_out"].rearrange("(j p) n -> p j n", p=P), bin_, b_wout)
            xr = Ring(nc, ph, "xr", 2, [P, D], F32)
            ctr = Ring(nc, ph, "ct", 2, [P, D], BF16)
            cTr = Ring(nc, ph, "cT", 1, [P, 8, P], BF16)
            rr = Ring(nc, ph, "rr", 2, [P, D], F32)
            x1r = Ring(nc, ph, "x1", 2, [P, 4, D], F32)
            xn2r = Ring(nc, ph, "xn2", 1, [P, 1, D], F32)
            h2r = Ring(nc, ph, "h2T", 2, [P, 8, 512], BF16)
            actT, b_actT = sbt(nc, ph, "actT", [P, NF, 512], BF16)
            gur = Ring(nc, ph, "gu", 3, [P, 2, 8, 256], BF16)
            wdr = Ring(nc, ph, "wd", 3, [P, 2, 512], BF16)
            sgr = Ring(nc, ph, "sg", 2, [P, 512], F32)
            ygr = Ring(nc, ph, "yg", 2, [P, 512], F32)
            outr = Ring(nc, ph, "ot", 2, [P, D], F32)
            st_ring = Ring(nc, ph, "st", 4, [P, 16], F32)
            o_blocks = [(xbufs["xo_blk"](b), 4, b * 512, 0, xout(b), xbufs["xo_buf"](b), b_out(b)) for b in range(8)]
            if need_ctx:
                o_blocks.append((io["xc"], 2, SH, 1, xcout, xbufs["xc"], b_outc))

            def ln_affine(tmp, btmp, dst, bdst, g_row, b_row):
                ln_rows(st_ring, tmp, btmp, tmp, btmp)
                S.op("dve", lambda e: e.tensor_tensor(out=tmp, in0=tmp, in1=lnbc[:, g_row, :], op=ALU.mult),
                     [btmp, b_lnbc], [btmp])
                S.op("pool", lambda e: e.tensor_tensor(out=dst, in0=tmp, in1=lnbc[:, b_row, :], op=ALU.add),
                     [btmp, b_lnbc], [bdst])

            brange = [4, 8]

            def stage_a(src, nt, coff, m, dst, sbuf_src, bdst_out):
                x1s, b_x1s = x1r.next()
                h2T, b_h2T = h2r.next()
                st = dict(nt=nt, m=m, dst=dst, bdst_out=bdst_out, x1=x1s, b_x1=b_x1s, h2T=h2T, b_h2T=b_h2T)
                yield st
                for t in range(nt):
                    xt, bxt = xr.next()
                    ct, bct = ctr.next()
                    S.dma("sp", xt[:, :], src[t * P:(t + 1) * P, :], sbuf_src, bxt)
                    S.dma("sp", ct[:, :], cat[coff + t * P:coff + (t + 1) * P, :], b_cat, bct)
                    pb, bp = S.bank(*brange)
                    pbv = pb[:, :].bitcast(BF16)
                    for j in range(8):
                        S.op("pe", lambda e, pbv=pbv, j=j, ct=ct: e.transpose(
                            out=pbv[:, j * P:(j + 1) * P], in_=ct[:, j * P:(j + 1) * P], identity=ident_b[:, :]),
                            [bct, b_idb], [bp])
                    cT, bcT = cTr.next()
                    S.op("act", lambda e, pbv=pbv, cT=cT: e.copy(out=cT[:, :, :], in_=pbv.rearrange("p (j c) -> p j c", c=P)),
                         [bp], [bcT])
                    yield None
                    tmp, btmp = rr.next()
                    for hf in range(2):
                        pb, bp = S.bank(*brange)
                        for j in range(8):
                            S.op("pe", lambda e, pb=pb, j=j, hf=hf, cT=cT: e.matmul(
                                pb[:, :], lhsT=cT[:, j, :], rhs=wout[:, j, hf * 512:(hf + 1) * 512],
                                start=(j == 0), stop=(j == 7)), [bcT, b_wout], [bp])
                        S.op("dve", lambda e, pb=pb, hf=hf, tmp=tmp: e.tensor_tensor(
                            out=tmp[:, hf * 512:(hf + 1) * 512], in0=pb[:, :], in1=gbc[:, m, 0, hf * 512:(hf + 1) * 512],
                            op=ALU.mult), [bp, b_gbc], [btmp])
                    yield None
                    S.op("dve", lambda e, tmp=tmp, xt=xt: e.scalar_tensor_tensor(
                        out=tmp[:, :], in0=xt[:, :], scalar=ALPHA, in1=tmp[:, :], op0=ALU.mult, op1=ALU.add),
                        [bxt, btmp], [btmp])
                    ln_affine(tmp[:, :], btmp, x1s[:, t, :], b_x1s, 0, 1)
                    yield None
                    xn2, bxn2 = xn2r.next()
                    ln_rows(st_ring, x1s[:, t, :], b_x1s, xn2[:, 0, :], bxn2)
                    for jh in range(2):
                        pb, bp = S.bank(*brange)
                        for jj in range(4):
                            j = jh * 4 + jj
                            S.op("pe", lambda e, pb=pb, jj=jj, j=j, xn2=xn2: e.transpose(
                                out=pb[:, jj * P:(jj + 1) * P], in_=xn2[:, 0, j * P:(j + 1) * P], identity=ident_f[:, :]),
                                [bxn2, b_idf], [bp])
                        for jj in range(4):
                            j = jh * 4 + jj
                            if jj % 2 == 0:
                                S.op("act", lambda e, pb=pb, jj=jj, j=j, t=t: e.activation(
                                    out=h2T[:, j, t * P:(t + 1) * P], in_=pb[:, jj * P:(jj + 1) * P], func=AF.Identity,
                                    scale=fcol[:, 2, j, m:m + 1], bias=fcol[:, 3, j, m:m + 1]), [bp, b_fcol], [b_h2T])
                            else:
                                S.op("dve", lambda e, pb=pb, jj=jj, j=j, t=t: e.tensor_scalar(
                                    out=h2T[:, j, t * P:(t + 1) * P], in0=pb[:, jj * P:(jj + 1) * P],
                                    scalar1=fcol[:, 2, j, m:m + 1], scalar2=fcol[:, 3, j, m:m + 1],
                                    op0=ALU.mult, op1=ALU.add), [bp, b_fcol], [b_h2T])
                    yield None

            def stage_c(st):
                nt, x1s, b_x1s = st["nt"], st["x1"], st["b_x1"]
                for t in range(nt):
                    ot, bot = outr.next()
                    ln_affine(x1s[:, t, :], b_x1s, ot[:, :], bot, 2, 3)
                    S.dma("pool", st["dst"][t * P:(t + 1) * P, :], ot[:, :], bot, st["bdst_out"])
                    yield None

            def step(bg, n=1):
                for _ in range(n):
                    if bg is None:
                        return
                    try:
                        next(bg)
                    except StopIteration:
                        return

            def stage_b(st, bg):
                nt, m, x1s, b_x1s, h2T, b_h2T = st["nt"], st["m"], st["x1"], st["b_x1"], st["h2T"], st["b_h2T"]
                ntok = nt * P
                brange[0] = 0
                for c in range(11):
                    gu, bgu = gur.next()
                    S.dma("sp", gu[:, 0, :, :], io["wg_bf"][c], b_wbf, bgu)
                    S.dma("sp", gu[:, 1, :, :], io["wu_bf"][c], b_wbf, bgu)
                    for fc in range(2):
                        f = c * 2 + fc
                        pg, bpg = S.bank(*brange)
                        pu, bpu = S.bank(*brange)
                        for j in range(8):
                            S.op("pe", lambda e, pg=pg, j=j, fc=fc, gu=gu: e.matmul(
                                pg[:, 0:ntok], lhsT=gu[:, 0, j, fc * P:(fc + 1) * P], rhs=h2T[:, j, 0:ntok],
                                start=(j == 0), stop=(j == 7)), [bgu, b_h2T], [bpg])
                        for j in range(8):
                            S.op("pe", lambda e, pu=pu, j=j, fc=fc, gu=gu: e.matmul(
                                pu[:, 0:ntok], lhsT=gu[:, 1, j, fc * P:(fc + 1) * P], rhs=h2T[:, j, 0:ntok],
                                start=(j == 0), stop=(j == 7)), [bgu, b_h2T], [bpu])
                        sg, bsg = sgr.next()
                        S.op("act", lambda e, pg=pg, sg=sg: e.activation(out=sg[:, 0:ntok], in_=pg[:, 0:ntok], func=AF.Silu),
                             [bpg], [bsg])
                        S.op("dve", lambda e, pu=pu, sg=sg, f=f: e.tensor_tensor(
                            out=actT[:, f, 0:ntok], in0=pu[:, 0:ntok], in1=sg[:, 0:ntok], op=ALU.mult),
                            [bpu, bsg], [b_actT])
                    step(bg)
                brange[0] = 4
                for hf in range(2):
                    for c in range(11):
                        wd, bwd = wdr.next()
                        S.dma("sp", wd[:, :, :], io["wd_bf"][c * 256:(c + 1) * 256, hf * 512:(hf + 1) * 512]
                              .rearrange("(f p) n -> p f n", p=P), b_wbf, bwd)
                        for fc in range(2):
                            f = c * 2 + fc
                            for t in range(nt):
                                S.op("pe", lambda e, f=f, fc=fc, t=t, wd=wd: e.matmul(
                                    S.pb[t][:, :], lhsT=actT[:, f, t * P:(t + 1) * P], rhs=wd[:, fc, :],
                                    start=(f == 0), stop=(f == NF - 1)), [b_actT, bwd], [S.bpb[t]])
                        step(bg)
                    for t in range(nt):
                        yg, byg = ygr.next()
                        S.op("dve", lambda e, t=t, yg=yg, hf=hf: e.tensor_tensor(
                            out=yg[:, :], in0=S.pb[t][:, :], in1=gbc[:, m, 1, hf * 512:(hf + 1) * 512], op=ALU.mult),
                            [S.bpb[t], b_gbc], [byg])
                        S.op("dve", lambda e, t=t, yg=yg, hf=hf: e.scalar_tensor_tensor(
                            out=x1s[:, t, hf * 512:(hf + 1) * 512], in0=x1s[:, t, hf * 512:(hf + 1) * 512], scalar=ALPHA,
                            in1=yg[:, :], op0=ALU.mult, op1=ALU.add), [b_x1s, byg], [b_x1s])

            def chain(*gens):
                for g_ in gens:
                    if g_ is not None:
                        for _ in g_:
                            yield None

            nb = len(o_blocks)
            ga = stage_a(*o_blocks[0])
            st_cur = next(ga)
            for _ in ga:
                pass
            prev_c = None
            for i in range(nb):
                if i + 1 < nb:
                    ga = stage_a(*o_blocks[i + 1])
                    st_next = next(ga)
                else:
                    ga, st_next = None, None
                bg = chain(prev_c, ga)
                stage_b(st_cur, bg)
                for _ in bg:
                    pass
                prev_c = stage_c(st_cur)
                st_cur = st_next
            for _ in prev_c:
                pass
            S.end_phase()


def build_program():
    nc = bass.Bass("TRN2", target_bir_lowering=False)
    io0 = declare_layer(nc, 0, "l0_")
    x1own = [nc.dram_tensor("x1own%d" % c, [512, D], F32, kind="Internal").ap() for c in range(8)]
    x1gat = [nc.dram_tensor("x1gat%d" % c, [1024, D], F32, kind="Internal").ap() for c in range(8)]
    xc1 = nc.dram_tensor("xc1", [CTX, D], F32, kind="Internal").ap()
    io1 = declare_layer(nc, 1, "l1_", x_from_dram={"xc": xc1, "ident": io0["ident"]})
    xout = nc.dram_tensor("xout", [SH, D], F32, kind="ExternalOutput").ap()
    with ExitStack() as es:
        S = Sched(nc, es)
        ident_f = es.enter_context(nc.sbuf_tensor("ident_f", [P, P], F32)); b_idf = Buf("idf")
        ident_b = es.enter_context(nc.sbuf_tensor("ident_b", [P, P], BF16)); b_idb = Buf("idb")
        ones_f = es.enter_context(nc.sbuf_tensor("ones_f", [P, P], F32)); b_ones = Buf("ones")
        bin_ = Buf("cin")
        S.dma("sp", ident_f[:], io0["ident"], bin_, b_idf)
        S.op("dve", lambda e: e.tensor_copy(out=ident_b[:], in_=ident_f[:]), [b_idf], [b_idb])
        S.op("dve", lambda e: e.memset(ones_f[:], 1.0), [], [b_ones])
        consts = (ident_f, b_idf, ident_b, b_idb, ones_f, b_ones)
        b_xc1, b_xout = Buf("xc1"), Buf("xout")
        b_own = [Buf("x1own%d" % c) for c in range(8)]
        b_gat = [Buf("x1gat%d" % c) for c in range(8)]
        xb0 = {"xc": bin_,
               "xf_blk": lambda b: io0["xf"][b * 512:(b + 1) * 512, :], "xf_buf": lambda b: bin_,
               "xo_blk": lambda b: io0["xo"][b * 512:(b + 1) * 512, :], "xo_buf": lambda b: bin_}
        build_layer(nc, S, 0, io0, lambda b: x1own[b], xc1, consts, xb0, lambda b: b_own[b], b_xc1)
        for c in range(8):
            if NO_COLL:
                S.dma("pool", x1gat[c][0:512], x1own[c], b_own[c], b_gat[c])
                S.dma("pool", x1gat[c][512:1024], x1own[c], b_own[c], b_gat[c])
            else:
                S.coll("AllGather", [[0, 1], [2, 3], [4, 5], [6, 7]], x1own[c], x1gat[c], b_own[c], b_gat[c])
        xb1 = {"xc": b_xc1,
               "xf_blk": lambda b: x1gat[b % 8][(b // 8) * 512:(b // 8 + 1) * 512, :], "xf_buf": lambda b: b_gat[b % 8],
               "xo_blk": lambda b: x1own[b], "xo_buf": lambda b: b_own[b]}
        build_layer(nc, S, 1, io1, lambda b: xout[b * 512:(b + 1) * 512, :], None, consts, xb1, lambda b: b_xout, None)
    return nc


L0_NAMES = ["l0_w_mod", "l0_b_mod", "l0_w_in", "l0_q_norm", "l0_w_uq", "l0_kv_norm", "l0_w_ukv", "l0_w_out",
            "l0_ln1_g", "l0_ln1_b", "l0_w_gate", "l0_w_up", "l0_w_down", "l0_ln2_g", "l0_ln2_b"]
L1_NAMES = ["l1_w_mod", "l1_b_mod", "l1_w_in", "l1_lambda_q1", "l1_lambda_k1", "l1_lambda_q2", "l1_lambda_k2",
            "l1_subln", "l1_w_out", "l1_ln1_g", "l1_ln1_b", "l1_w_gate", "l1_w_up", "l1_w_down", "l1_ln2_g", "l1_ln2_b"]


def layer_inputs(L, inputs, b, half):
    pre = "l%d_" % L
    names = L0_NAMES if L == 0 else L1_NAMES
    w = {n[3:]: np.ascontiguousarray(np.asarray(inputs[n], dtype=np.float32)) for n in names}
    g = lambda n: w[n]
    t = get_tables(half)
    m = {}
    c = np.asarray(inputs["c"], np.float32)[b]
    cc = np.asarray(inputs["c_ctx"], np.float32)
    m["cv"] = np.ascontiguousarray(np.stack([c.reshape(8, P).T, cc.reshape(8, P).T], 2))
    m["w_mod"] = g("w_mod"); m["b_mod"] = g("b_mod").reshape(1, -1)
    m["w_out"] = g("w_out"); m["w_gate"] = g("w_gate"); m["w_up"] = g("w_up"); m["w_down"] = g("w_down")
    m["ln"] = np.ascontiguousarray(np.stack([g("ln1_g"), g("ln1_b"), g("ln2_g"), g("ln2_b")], 0))
    w_in = g("w_in")
    if L == 0:
        m["ident"] = t["ident"]
        m["w_in"] = w_in
        sw = rope_swap_index(32)
        m["w_kr_sw"] = np.ascontiguousarray(w_in[:, 1024 + sw])
        m["qn"] = np.ascontiguousarray(g("q_norm").reshape(2, P).T)
        m["kvn"] = np.ascontiguousarray(g("kv_norm").reshape(2, P).T)
        w_uq = g("w_uq")
        perm = np.arange(768)
        for h in range(8):
            perm[h * 96 + 64:h * 96 + 96] = h * 96 + 64 + sw
        m["w_uq"] = w_uq
        m["w_uq_sw"] = np.ascontiguousarray(w_uq[:, perm])
        w_ukv = g("w_ukv").reshape(256, 8, 128)
        m["w_ukv_kn"] = np.ascontiguousarray(w_ukv[:, :, 0:64].reshape(256, 512))
        m["w_ukv_v"] = np.ascontiguousarray(w_ukv[:, :, 64:128].reshape(256, 512))
        m["cosK"], m["sinK"], m["cosQ"], m["sinQ"] = t["cosK0"], t["sinK0"], t["cosQ0"], t["sinQ0"]
        for k in ("F64", "GAB", "CdSl", "CdSc", "F256"):
            m[k] = t[k]
    else:
        m["w_in"] = w_in
        sw = rope_swap_index(64)
        perm = np.arange(2048)
        for blk in range(32):
            perm[blk * 64:(blk + 1) * 64] = blk * 64 + sw
        m["w_qk_sw"] = np.ascontiguousarray(w_in[:, perm])
        m["lam"] = np.ascontiguousarray(np.stack([g("lambda_q1"), g("lambda_k1"), g("lambda_q2"), g("lambda_k2")], 0))
        m["subln"] = g("subln").reshape(1, 128)
        m["cosK"], m["sinK"], m["cosQ"], m["sinQ"] = t["cosK1"], t["sinK1"], t["cosQ1"], t["sinQ1"]
    return {pre + k: v for k, v in m.items()}


_PROG = {}


def kernel(**inputs):
    x = np.asarray(inputs["x"], np.float32)
    xc = np.asarray(inputs["ctx"], np.float32)
    if "nc" not in _PROG:
        _PROG["nc"] = build_program()
    nc = _PROG["nc"]
    in_maps = []
    for k in range(8):
        b, half = k // 2, k % 2
        m = {}
        m.update(layer_inputs(0, inputs, b, half))
        m.update(layer_inputs(1, inputs, b, half))
        m["l0_xf"] = np.ascontiguousarray(x[b])
        m["l0_xo"] = np.ascontiguousarray(x[b, half * SH:(half + 1) * SH])
        m["l0_xc"] = np.ascontiguousarray(xc[b])
        in_maps.append(m)
    res = run_bass_kernel_spmd(nc, in_maps, core_ids=list(range(8)))
    r = res.results
    out = np.stack([np.concatenate([r[2 * b]["xout"], r[2 * b + 1]["xout"]], 0) for b in range(4)], 0)
    return out.astype(np.float32)
```

```python
import math
import numpy as np
from contextlib import ExitStack
import concourse.bass as bass
import concourse.mybir as mybir
from concourse.bass_utils import run_bass_kernel_spmd

F32 = mybir.dt.float32
BF16 = mybir.dt.bfloat16
AF = mybir.ActivationFunctionType
ALU = mybir.AluOpType

P = 128
D = 1024
SEQ = 8192
SH = 4096
CTX = 256
NK = SEQ + CTX
FF = 2816
NF = 22
ALPHA = 4.0 ** 0.25
EPS = 1e-6
MLA_SCALE = 96.0 ** -0.5
DIFF_SCALE = 64.0 ** -0.5
LAMBDA_INIT1 = 0.8 - 0.6 * math.exp(-0.3)

ENGS = ["pe", "act", "dve", "pool", "sp"]
DEBUG_SCR = False
NO_COLL = False


class Buf:
    __slots__ = ("name", "w", "r", "dsem", "dcnt")

    def __init__(self, name):
        self.name = name
        self.w = None
        self.r = []
        self.dsem = None
        self.dcnt = 0


class Sched:
    def __init__(self, nc, es):
        self.nc = nc
        self.es = es
        self.ops = {e: [] for e in ENGS}
        self.sem = {e: es.enter_context(nc.semaphore("s_" + e)) for e in ENGS}
        self.cnt = {e: 0 for e in ENGS}
        self.seen = {e: {} for e in ENGS}
        self.sem_pool = {"sp": [], "pool": [], "act": []}
        self.phase_bufs = []
        self.nsem = 0
        self.pbig = es.enter_context(nc.psum_tensor("pbig", [P, 4096], F32))
        self.pb = [self.pbig[:, i * 512:(i + 1) * 512] for i in range(8)]
        self.bpb = [Buf("pb%d" % i) for i in range(8)]
        self.pbi = 0

    def bank(self, lo=0, hi=8):
        i = self.pbi
        if i < lo or i >= hi:
            i = lo
        self.pbi = i + 1
        return self.pb[i], self.bpb[i]

    def _need(self, e, ev, waits):
        if ev is None:
            return
        sem, val = ev
        if sem is self.sem[e] and e in ("pe", "sp"):
            return
        k = id(sem)
        if self.seen[e].get(k, 0) >= val:
            return
        self.seen[e][k] = val
        waits.append((sem, val))

    def _deps(self, e, reads, writes):
        waits = []
        for b in reads:
            self._need(e, b.w, waits)
        for b in writes:
            self._need(e, b.w, waits)
            for ev in b.r:
                self._need(e, ev, waits)
        return waits

    def op(self, e, fn, reads=(), writes=()):
        waits = self._deps(e, reads, writes)
        self.cnt[e] += 1
        ev = (self.sem[e], self.cnt[e])
        self.ops[e].append((waits, fn, (self.sem[e], 1)))
        for b in reads:
            b.r.append(ev)
        for b in writes:
            b.w = ev
            b.r = []
        return ev

    def dma(self, q, out_ap, in_ap, src, dst, **kw):
        srcs = src if isinstance(src, (list, tuple)) else [src]
        dsts = dst if isinstance(dst, (list, tuple)) else [dst]
        waits = self._deps(q, srcs, dsts)
        d0 = dsts[0]
        if d0.dsem is None:
            if self.sem_pool[q]:
                d0.dsem, d0.dcnt = self.sem_pool[q].pop()
            else:
                d0.dsem = self.es.enter_context(self.nc.semaphore("d%d" % self.nsem))
                self.nsem += 1
                d0.dcnt = 0
            self.phase_bufs.append((d0, q))
        else:
            assert any(b is d0 and qq == q for b, qq in self.phase_bufs), "buffer %s written by DMAs of two queues" % d0.name
        d0.dcnt += 16
        ev = (d0.dsem, d0.dcnt)
        self.ops[q].append(
            (waits, lambda eng: eng.dma_start(out=out_ap, in_=in_ap, **kw), (d0.dsem, 16)))
        for b in srcs:
            b.r.append(ev)
        for b in dsts:
            b.w = ev
            b.r = []
        return ev

    def coll(self, kind, groups, in_ap, out_ap, src, dst):
        waits = self._deps("pool", [src], [dst])
        if not hasattr(self, "cc_sem"):
            self.cc_sem = self.es.enter_context(self.nc.semaphore("cc_sem"))
            self.cc_cnt = 0
        self.cc_cnt += 1
        ev = (self.cc_sem, self.cc_cnt)
        sem = self.cc_sem
        self.ops["pool"].append(
            (waits, lambda eng: eng.collective_compute(kind, ALU.bypass, replica_groups=groups,
                                                       ins=[in_ap.opt()], outs=[out_ap.opt()]), (sem, 1)))
        src.r.append(ev)
        dst.w = ev
        dst.r = []
        return ev

    def end_phase(self):
        for b, _q in self.phase_bufs:
            waits = []
            self._need("sp", (b.dsem, b.dcnt), waits)
            if waits:
                self.ops["sp"].append((waits, None, None))
        nc = self.nc
        ops = self.ops

        def replay(e, eng):
            for waits, fn, inc in ops[e]:
                for sem, val in waits:
                    eng.wait_ge(sem, val)
                if fn is not None:
                    fn(eng).then_inc(inc[0], inc[1])

        with nc.Block() as block:
            @block.tensor
            def _(eng):
                replay("pe", eng)

            @block.scalar
            def _(eng):
                replay("act", eng)

            @block.vector
            def _(eng):
                replay("dve", eng)

            @block.gpsimd
            def _(eng):
                replay("pool", eng)

            @block.sync
            def _(eng):
                replay("sp", eng)
        self.ops = {e: [] for e in ENGS}
        for b, q in self.phase_bufs:
            self.sem_pool[q].append((b.dsem, b.dcnt))
            b.dsem = None
        self.phase_bufs = []


_UID = [0]


def _uname(name):
    _UID[0] += 1
    return "sb%d_%s" % (_UID[0], name)


class Ring:
    def __init__(self, nc, ph, name, n, shape, dt):
        self.t = [ph.enter_context(nc.sbuf_tensor(_uname("%s%d" % (name, i)), list(shape), dt)) for i in range(n)]
        self.b = [Buf("%s%d" % (name, i)) for i in range(n)]
        self.i = 0

    def next(self):
        i = self.i
        self.i = (i + 1) % len(self.t)
        return self.t[i], self.b[i]


def sbt(nc, ph, name, shape, dt):
    return ph.enter_context(nc.sbuf_tensor(_uname(name), list(shape), dt)), Buf(name)


def rope_tables(dim, positions_rc):
    q = dim // 4
    inv = 1.0 / (10000.0 ** (np.arange(q, dtype=np.float64) / q))
    n = positions_rc.shape[0]
    cos = np.ones((dim, n), np.float64)
    sin = np.zeros((dim, n), np.float64)
    valid = positions_rc[:, 0] >= 0
    for d in range(dim):
        a = d // (2 * q)
        w = d % (2 * q)
        fi = w % q
        first = w < q
        ang = positions_rc[:, a].astype(np.float64) * inv[fi]
        cos[d] = np.where(valid, np.cos(ang), 1.0)
        s = np.sin(ang)
        sin[d] = np.where(valid, -s if first else s, 0.0)
    return cos, sin


def rope_swap_index(dim):
    q = dim // 4
    idx = np.zeros(dim, np.int64)
    for d in range(dim):
        w = d % (2 * q)
        idx[d] = d + q if w < q else d - q
    return idx


def pos_rc(tokens):
    tokens = np.asarray(tokens)
    return np.stack([tokens // 64, tokens % 64], 1)


_TABLE_CACHE = {}


def get_tables(half):
    if half in _TABLE_CACHE:
        return _TABLE_CACHE[half]
    t = {}
    neg = -np.ones((CTX, 2), np.int64)
    key_pos = np.concatenate([neg, pos_rc(np.arange(SEQ))], 0)
    own_pos = pos_rc(half * SH + np.arange(SH))
    ck, sk = rope_tables(32, key_pos)
    t["cosK0"], t["sinK0"] = ck.astype(np.float32), sk.astype(np.float32)
    cq, sq = rope_tables(32, np.concatenate([own_pos, neg], 0))
    cq96 = np.ones((96, SH + CTX)); sq96 = np.zeros((96, SH + CTX))
    cq96[64:] = cq; sq96[64:] = sq
    t["cosQ0"], t["sinQ0"] = cq96.astype(np.float32), sq96.astype(np.float32)
    ck, sk = rope_tables(64, key_pos)
    t["cosK1"] = np.concatenate([ck, ck], 0).astype(np.float32)
    t["sinK1"] = np.concatenate([sk, sk], 0).astype(np.float32)
    cq, sq = rope_tables(64, own_pos)
    t["cosQ1"] = np.concatenate([cq, cq], 0).astype(np.float32)
    t["sinQ1"] = np.concatenate([sq, sq], 0).astype(np.float32)
    n2 = np.arange(64)[:, None]; k2 = np.arange(64)[None, :]
    a = 2 * np.pi * n2 * k2 / 64
    t["F64"] = np.concatenate([np.cos(a), -np.sin(a)], 1).astype(np.float32)
    n1 = np.arange(128)[:, None, None]
    kk = (64 * (64 * half + np.arange(64))[None, None, :] + np.arange(64)[None, :, None])
    a = 2 * np.pi * ((n1 * kk) % SEQ) / SEQ
    Gr, Gi = np.cos(a), -np.sin(a)
    GA = np.concatenate([Gr, Gi], 2); GB = np.concatenate([-Gi, Gr], 2)
    t["GAB"] = np.stack([GA, GB], 2).astype(np.float32)
    ch = np.arange(128)[:, None]; ch2 = np.arange(128)[None, :]
    a = 2 * np.pi * ch * ch2 / 128
    cds = np.stack([np.cos(a), np.sin(a)], 1)
    t["CdSl"] = (cds / math.sqrt(SEQ * 128)).astype(np.float32)
    t["CdSc"] = (cds / math.sqrt(CTX * 128)).astype(np.float32)
    n = (np.arange(2)[None, :, None] * 128 + np.arange(128)[:, None, None])
    k = np.arange(256)[None, None, :]
    a = 2 * np.pi * ((n * k) % 256) / 256
    t["F256"] = np.concatenate([np.cos(a), -np.sin(a)], 2).astype(np.float32)
    t["ident"] = np.eye(128, dtype=np.float32)
    _TABLE_CACHE[half] = t
    return t


class LayerIO:
    pass


def declare_layer(nc, L, pfx, x_from_dram=None):
    io = {}

    def inp(name, shape, dt=F32):
        io[name] = nc.dram_tensor(pfx + name, list(shape), dt, kind="ExternalInput").ap()

    def scr(name, shape, dt):
        kind = "ExternalOutput" if (DEBUG_SCR and name in ("Fdl", "Fdc", "KT", "Vs", "QT", "cat")) else "Internal"
        io[name] = nc.dram_tensor(pfx + name, list(shape), dt, kind=kind).ap()

    if x_from_dram is None:
        inp("xf", [SEQ, D]); inp("xo", [SH, D]); inp("xc", [CTX, D])
    else:
        io.update(x_from_dram)
    inp("cv", [P, 8, 2])
    inp("w_mod", [D, 6 * D]); inp("b_mod", [1, 6 * D])
    inp("w_out", [D, D]); inp("w_gate", [D, FF]); inp("w_up", [D, FF]); inp("w_down", [FF, D])
    inp("ln", [4, D])
    if "ident" not in io:
        inp("ident", [P, P])
    if L == 0:
        inp("w_in", [D, 1056]); inp("w_kr_sw", [D, 32])
        inp("qn", [P, 2]); inp("kvn", [P, 2])
        inp("w_uq", [256, 768]); inp("w_uq_sw", [256, 768]); inp("w_ukv_kn", [256, 512]); inp("w_ukv_v", [256, 512])
        inp("cosK", [32, NK]); inp("sinK", [32, NK]); inp("cosQ", [96, SH + CTX]); inp("sinQ", [96, SH + CTX])
        inp("F64", [64, 128]); inp("GAB", [P, 64, 2, 128]); inp("CdSl", [P, 2, 128]); inp("CdSc", [P, 2, 128])
        inp("F256", [P, 2, 512])
        scr("Fdl", [SEQ, 512], F32); scr("Fdc", [CTX, 512], F32)
        scr("KT", [8, 96, NK], BF16); scr("Vs", [NK, 8, 65], BF16); scr("QT", [8, 96, SH + CTX], BF16)
        scr("cat", [SH + CTX, D], BF16)
    else:
        inp("w_in", [D, 3072]); inp("w_qk_sw", [D, 2048])
        inp("lam", [4, 64]); inp("subln", [1, 128])
        inp("cosK", [P, NK]); inp("sinK", [P, NK]); inp("cosQ", [P, SH]); inp("sinQ", [P, SH])
        scr("KT", [8, 128, NK], BF16); scr("Vs", [NK, 8, 129], BF16); scr("QT", [8, 128, SH], BF16)
        scr("cat", [SH, D], BF16)
    scr("wg_bf", [11, P, 8, 256], BF16); scr("wu_bf", [11, P, 8, 256], BF16); scr("wd_bf", [FF, D], BF16)
    return io


def build_layer(nc, S, L, io, xout, xcout, consts, xbufs, b_out, b_outc):
    need_ctx = (L == 0)
    ident_f, b_idf, ident_b, b_idb, ones_f, b_ones = consts
    bin_ = Buf("ext_in")
    b_wbf = Buf("wbf%d" % L)

    for c in range(11):
        S.dma("pool", io["wg_bf"][c], io["w_gate"][:, c * 256:(c + 1) * 256].rearrange("(j p) n -> p j n", p=P),
              bin_, b_wbf)
        S.dma("pool", io["wu_bf"][c], io["w_up"][:, c * 256:(c + 1) * 256].rearrange("(j p) n -> p j n", p=P),
              bin_, b_wbf)
    S.dma("pool", io["wd_bf"], io["w_down"], bin_, b_wbf, max_dma_last_dim=4096)

    with ExitStack() as lay:
        fcol, b_fcol = sbt(nc, lay, "fcol", [P, 4, 8, 2], F32)
        gbc, b_gbc = sbt(nc, lay, "gbc", [P, 2, 2, D], F32)
        lnbc, b_lnbc = sbt(nc, lay, "lnbc", [P, 4, D], F32)
        S.dma("sp", lnbc[:], io["ln"].partition_broadcast(P), bin_, b_lnbc)

        with ExitStack() as ph:
            cv, b_cv = sbt(nc, ph, "cv", [P, 8, 2], F32)
            sl, b_sl = sbt(nc, ph, "sl", [P, 8, 2], F32)
            rep, b_rep = sbt(nc, ph, "rep", [P, 2, 8, P], F32)
            wm = Ring(nc, ph, "wm", 4, [P, 8, 512], F32)
            bm = Ring(nc, ph, "bm", 4, [1, 512], F32)
            S.dma("sp", cv[:], io["cv"], bin_, b_cv)
            S.op("act", lambda e: e.activation(out=sl[:], in_=cv[:], func=AF.Silu), [b_cv], [b_sl])
            for m in range(2):
                for j in range(8):
                    S.op("dve", lambda e, m=m, j=j: e.tensor_scalar(
                        out=rep[:, m, j, :], in0=ones_f[:, :], scalar1=sl[:, j, m:m + 1], scalar2=None,
                        op0=ALU.mult), [b_sl, b_ones], [b_rep])
            vmap = {1: 0, 0: 1, 4: 2, 3: 3}
            for cb in range(12):
                c6, hf = cb // 2, cb % 2
                wt, bw = wm.next()
                bt, bb = bm.next()
                for jh in range(2):
                    S.dma("sp", wt[:, jh * 4:(jh + 1) * 4, :],
                          io["w_mod"][jh * 512:(jh + 1) * 512, cb * 512:(cb + 1) * 512].rearrange("(j p) n -> p j n", p=P),
                          bin_, bw)
                S.dma("sp", bt[:], io["b_mod"][:, cb * 512:(cb + 1) * 512], bin_, bb)
                if c6 in (2, 5):
                    gi = 0 if c6 == 2 else 1
                    for m in range(2):
                        pb, bp = S.bank()
                        for j in range(8):
                            S.op("pe", lambda e, pb=pb, m=m, j=j, wt=wt: e.matmul(
                                pb[:, :], lhsT=rep[:, m, j, :], rhs=wt[:, j, :], start=(j == 0), stop=False),
                                [b_rep, bw], [bp])
                        S.op("pe", lambda e, pb=pb, bt=bt: e.matmul(
                            pb[:, :], lhsT=ones_f[0:1, :], rhs=bt[0:1, :], start=False, stop=True),
                            [b_ones, bb], [bp])
                        S.op("dve", lambda e, pb=pb, m=m, gi=gi, hf=hf: e.tensor_copy(
                            out=gbc[:, m, gi, hf * 512:(hf + 1) * 512], in_=pb[:, :]), [bp], [b_gbc])
                else:
                    v = vmap[c6]
                    pb, bp = S.bank()
                    for q in range(4):
                        for j in range(8):
                            S.op("pe", lambda e, pb=pb, q=q, j=j, wt=wt: e.matmul(
                                pb[:, 2 * q:2 * q + 2], lhsT=wt[:, j, q * P:(q + 1) * P], rhs=sl[:, j, :],
                                start=(j == 0), stop=False), [b_sl, bw], [bp])
                        S.op("pe", lambda e, pb=pb, q=q, bt=bt: e.matmul(
                            pb[:, 2 * q:2 * q + 2], lhsT=bt[0:1, q * P:(q + 1) * P], rhs=ones_f[0:1, 0:2],
                            start=False, stop=True), [b_ones, bb], [bp])
                    S.op("dve", lambda e, pb=pb, v=v, hf=hf: e.tensor_copy(
                        out=fcol[:, v, hf * 4:hf * 4 + 4, :], in_=pb[:, 0:8].rearrange("p (q m) -> p q m", m=2)),
                        [bp], [b_fcol])
            for v in (0, 2):
                S.op("dve", lambda e, v=v: e.tensor_scalar(
                    out=fcol[:, v, :, :], in0=fcol[:, v, :, :], scalar1=1.0, scalar2=None, op0=ALU.add),
                    [b_fcol], [b_fcol])
            S.end_phase()

        def ln_rows(st_ring, xin, bx, out, bo, reads_extra=(), on_act=True):
            stt, bst = st_ring.next()
            for i in range(2):
                S.op("dve", lambda e, i=i, stt=stt: e.bn_stats(out=stt[:, 6 * i:6 * i + 6],
                                                              in_=xin[:, 512 * i:512 * i + 512]),
                     [bx] + list(reads_extra), [bst])
            S.op("dve", lambda e, stt=stt: e.bn_aggr(out=stt[:, 12:14], in_=stt[:, 0:12]), [bst], [bst])
            S.op("act", lambda e, stt=stt: e.activation(out=stt[:, 14:15], in_=stt[:, 13:14], func=AF.Sqrt,
                                                       bias=EPS, scale=1.0), [bst], [bst])
            S.op("dve", lambda e, stt=stt: e.reciprocal(out=stt[:, 14:15], in_=stt[:, 14:15]), [bst], [bst])
            if on_act:
                S.op("dve", lambda e, stt=stt: e.tensor_scalar(out=stt[:, 15:16], in0=stt[:, 12:13], scalar1=stt[:, 14:15],
                                                              scalar2=-1.0, op0=ALU.mult, op1=ALU.mult), [bst], [bst])
                S.op("act", lambda e, stt=stt: e.activation(out=out, in_=xin, func=AF.Identity, scale=stt[:, 14:15],
                                                           bias=stt[:, 15:16]), [bx, bst], [bo])
            else:
                S.op("dve", lambda e, stt=stt: e.tensor_scalar(out=out, in0=xin, scalar1=stt[:, 12:13],
                                                              scalar2=stt[:, 14:15], op0=ALU.subtract, op1=ALU.mult),
                     [bx, bst], [bo])

        def ln_block(st_ring, xb, bxb, xn_, bxn_, nt):
            stt, bst = st_ring.next()
            for t in range(nt):
                for i in range(2):
                    S.op("dve", lambda e, i=i, t=t, stt=stt: e.bn_stats(out=stt[:, 12 * t + 6 * i:12 * t + 6 * i + 6],
                                                                       in_=xb[:, t, 512 * i:512 * i + 512]), [bxb], [bst])
            for t in range(nt):
                S.op("dve", lambda e, t=t, stt=stt: e.bn_aggr(out=stt[:, 48 + 2 * t:50 + 2 * t], in_=stt[:, 12 * t:12 * t + 12]),
                     [bst], [bst])
            mv = stt[:, 48:56].rearrange("p (t c) -> p t c", c=2)
            S.op("act", lambda e, stt=stt: e.activation(out=stt[:, 56:56 + nt], in_=mv[:, 0:nt, 1], func=AF.Sqrt,
                                                       bias=EPS, scale=1.0), [bst], [bst])
            S.op("dve", lambda e, stt=stt: e.reciprocal(out=stt[:, 56:56 + nt], in_=stt[:, 56:56 + nt]), [bst], [bst])
            S.op("dve", lambda e, stt=stt: e.scalar_tensor_tensor(out=stt[:, 60:60 + nt], in0=mv[:, 0:nt, 0], scalar=-1.0,
                                                                 in1=stt[:, 56:56 + nt], op0=ALU.mult, op1=ALU.mult),
                 [bst], [bst])
            for t in range(nt):
                S.op("act", lambda e, t=t, stt=stt: e.activation(out=xn_[:, t, :], in_=xb[:, t, :], func=AF.Identity,
                                                                scale=stt[:, 56 + t:57 + t], bias=stt[:, 60 + t:61 + t]),
                     [bxb, bst], [bxn_])

        def make_hT(xn, bxn, nt, hT, bhT, vs, vb, m):
            for j in range(8):
                pb, bp = S.bank()
                for t in range(nt):
                    S.op("pe", lambda e, pb=pb, t=t, j=j: e.transpose(
                        out=pb[:, t * P:(t + 1) * P], in_=xn[:, t, j * P:(j + 1) * P], identity=ident_f[:, :]),
                        [bxn, b_idf], [bp])
                if j % 4 != 3:
                    S.op("act", lambda e, pb=pb, j=j: e.activation(
                        out=hT[:, j, 0:nt * P], in_=pb[:, 0:nt * P], func=AF.Identity,
                        scale=fcol[:, vs, j, m:m + 1], bias=fcol[:, vb, j, m:m + 1]), [bp, b_fcol], [bhT])
                else:
                    S.op("dve", lambda e, pb=pb, j=j: e.tensor_scalar(
                        out=hT[:, j, 0:nt * P], in0=pb[:, 0:nt * P], scalar1=fcol[:, vs, j, m:m + 1],
                        scalar2=fcol[:, vb, j, m:m + 1], op0=ALU.mult, op1=ALU.add), [bp, b_fcol], [bhT])

        def rms_T(src_banks, nrm, b_nrm, ntok, sq, b_sq, rb, b_rb, outT, b_out):
            for m2 in range(2):
                pbk, bpk = src_banks[m2]
                S.op("act", lambda e, pbk=pbk, m2=m2: e.activation(out=sq[:, m2, 0:ntok], in_=pbk[:, 0:ntok],
                                                                 func=AF.Square), [bpk], [b_sq])
            pb, bp = S.bank()
            for m2 in range(2):
                S.op("pe", lambda e, pb=pb, m2=m2: e.matmul(pb[:, 0:ntok], lhsT=ones_f[:, :], rhs=sq[:, m2, 0:ntok],
                                                           start=(m2 == 0), stop=(m2 == 1)), [b_sq, b_ones], [bp])
            S.op("act", lambda e, pb=pb: e.activation(out=rb[:, 0:ntok], in_=pb[:, 0:ntok], func=AF.Sqrt,
                                                     bias=EPS, scale=1.0 / 256.0), [bp], [b_rb])
            S.op("dve", lambda e: e.reciprocal(out=rb[:, 0:ntok], in_=rb[:, 0:ntok]), [b_rb], [b_rb])
            for m2 in range(2):
                pbk, bpk = src_banks[m2]
                S.op("dve", lambda e, pbk=pbk, m2=m2: e.scalar_tensor_tensor(
                    out=outT[:, m2, 0:ntok], in0=pbk[:, 0:ntok], scalar=nrm[:, m2:m2 + 1], in1=rb[:, 0:ntok],
                    op0=ALU.mult, op1=ALU.mult), [bpk, b_nrm, b_rb], [b_out])

        def rope_out(pa, bpa, pbk, bpbk, rows, ntok, cs, b_cs, sn, b_sn, t1, b_t1, t2, b_t2, out, b_o):
            S.op("dve", lambda e: e.tensor_tensor(out=t1[0:rows, 0:ntok], in0=pa[0:rows, 0:ntok],
                                                  in1=cs[0:rows, 0:ntok], op=ALU.mult), [bpa, b_cs], [b_t1])
            S.op("dve", lambda e: e.tensor_tensor(out=t2[0:rows, 0:ntok], in0=pbk[0:rows, 0:ntok],
                                                  in1=sn[0:rows, 0:ntok], op=ALU.mult), [bpbk, b_sn], [b_t2])
            S.op("pool", lambda e: e.tensor_tensor(out=out, in0=t1[0:rows, 0:ntok], in1=t2[0:rows, 0:ntok],
                                                   op=ALU.add), [b_t1, b_t2], [b_o])

        kv_blocks = [(io["xc"], 2, 0, 1, xbufs["xc"])] + [(xbufs["xf_blk"](b), 4, CTX + b * 512, 0, xbufs["xf_buf"](b)) for b in range(16)]
        q_blocks = [(xbufs["xo_blk"](b), 4, b * 512, 0, xbufs["xo_buf"](b)) for b in range(8)]
        if need_ctx:
            q_blocks.append((io["xc"], 2, SH, 1, xbufs["xc"]))
        b_KT, b_Vs, b_QT, b_cat = Buf("KT"), Buf("Vs"), Buf("QT"), Buf("cat")
        b_Fdl, b_Fdc = Buf("Fdl"), Buf("Fdc")
        KT, Vs, QT, cat = io["KT"], io["Vs"], io["QT"], io["cat"]
        vw = 65 if L == 0 else 129
        R = 96 if L == 0 else 128

        with ExitStack() as ph:
            xr = Ring(nc, ph, "xr", 1, [P, 4, D], F32)
            xnr = Ring(nc, ph, "xn", 2, [P, 4, D], F32)
            stb_ring = Ring(nc, ph, "stb", 2, [P, 64], F32)
            hTr = Ring(nc, ph, "hT", 2, [P, 8, 512], BF16)
            st_ring = Ring(nc, ph, "st", 4, [P, 16], F32)
            csr = Ring(nc, ph, "csk", 2, [R if L == 1 else 32, 512], F32)
            snr = Ring(nc, ph, "snk", 2, [R if L == 1 else 32, 512], F32)
            t1, b_t1 = sbt(nc, ph, "t1", [P, 512], F32)
            t2, b_t2 = sbt(nc, ph, "t2", [P, 512], F32)
            vsr = Ring(nc, ph, "vsb", 2, [P, 4, 8, vw], BF16)
            for i in range(2):
                S.op("pool", lambda e, i=i: e.memset(vsr.t[i][:], 1.0), [], [vsr.b[i]])
            if L == 0:
                win, b_win = sbt(nc, ph, "win", [P, 8, 1056], BF16)
                wkr, b_wkr = sbt(nc, ph, "wkr", [P, 8, 32], BF16)
                wkn, b_wkn = sbt(nc, ph, "wkn", [P, 2, 512], BF16)
                wv, b_wv = sbt(nc, ph, "wv", [P, 2, 512], BF16)
                kvn, b_kvn = sbt(nc, ph, "kvn", [P, 2], F32)
                S.dma("pool", win[:], io["w_in"].rearrange("(j p) n -> p j n", p=P), bin_, b_win)
                S.dma("pool", wkr[:], io["w_kr_sw"].rearrange("(j p) n -> p j n", p=P), bin_, b_wkr)
                S.dma("pool", wkn[:], io["w_ukv_kn"].rearrange("(j p) n -> p j n", p=P), bin_, b_wkn)
                S.dma("pool", wv[:], io["w_ukv_v"].rearrange("(j p) n -> p j n", p=P), bin_, b_wv)
                S.dma("sp", kvn[:], io["kvn"], bin_, b_kvn)
                fsr = Ring(nc, ph, "fsb", 2, [P, 4, 512], F32)
                sq, b_sq = sbt(nc, ph, "sq", [P, 2, 512], F32)
                rb, b_rb = sbt(nc, ph, "rb", [P, 512], F32)
                ckvn, b_ckvn = sbt(nc, ph, "ckvn", [P, 2, 512], BF16)
                knr = Ring(nc, ph, "knT", 2, [P, 4, 512], BF16)
                krr = Ring(nc, ph, "krT", 2, [32, 512], BF16)
            else:
                win, b_win = sbt(nc, ph, "win", [P, 8, 2048], BF16)
                wsw, b_wsw = sbt(nc, ph, "wsw", [P, 8, 1024], BF16)
                for kvh in range(2):
                    S.dma("pool", win[:, :, kvh * 1024:(kvh + 1) * 1024],
                          io["w_in"][:, 1024 + kvh * 1024:2048 + kvh * 1024].rearrange("(j p) n -> p j n", p=P), bin_, b_win)
                S.dma("pool", wsw[:], io["w_qk_sw"][:, 1024:2048].rearrange("(j p) n -> p j n", p=P), bin_, b_wsw)
                knr = Ring(nc, ph, "knT", 2, [P, 8, 512], BF16)

            def kv_a(src, nt, koff, m, sbuf_src):
                ntok = nt * P
                xb, bxb = xr.next()
                S.dma("sp", xb[:, 0:nt, :], src.rearrange("(t p) d -> p t d", p=P), sbuf_src, bxb)
                cs, b_cs = csr.next()
                sn, b_sn = snr.next()
                S.dma("sp", cs[:, 0:ntok], io["cosK"][:, koff:koff + ntok], bin_, b_cs)
                S.dma("sp", sn[:, 0:ntok], io["sinK"][:, koff:koff + ntok], bin_, b_sn)
                xn_, bxn_ = xnr.next()
                ln_block(stb_ring, xb, bxb, xn_, bxn_, nt)
                return dict(nt=nt, koff=koff, m=m, cs=cs, b_cs=b_cs, sn=sn, b_sn=b_sn, xn=xn_, bxn=bxn_)

            def kv_b(st):
                hT, bhT = hTr.next()
                make_hT(st["xn"], st["bxn"], st["nt"], hT, bhT, 0, 1, st["m"])
                st["hT"], st["bhT"] = hT, bhT

            def kv_body(st):
                nt, koff, m = st["nt"], st["koff"], st["m"]
                cs, b_cs, sn, b_sn, hT, bhT = st["cs"], st["b_cs"], st["sn"], st["b_sn"], st["hT"], st["bhT"]
                ntok = nt * P
                vsb, bvs = vsr.next()
                if L == 0:
                    fsb, bfs = fsr.next()
                    for t in range(nt):
                        pb, bp = S.bank()
                        for j in range(8):
                            S.op("pe", lambda e, pb=pb, t=t, j=j, hT=hT: e.matmul(
                                pb[:, :], lhsT=hT[:, j, t * P:(t + 1) * P], rhs=win[:, j, 0:512],
                                start=(j == 0), stop=(j == 7)), [bhT, b_win], [bp])
                        S.op("act", lambda e, pb=pb, t=t, fsb=fsb: e.copy(out=fsb[:, t, :], in_=pb[:, :]), [bp], [bfs])
                    if m == 1:
                        S.dma("pool", io["Fdc"].rearrange("(t p) c -> p t c", p=P), fsb[:, 0:nt, :], bfs, b_Fdc)
                    else:
                        n0 = koff - CTX
                        S.dma("pool", io["Fdl"][n0:n0 + ntok, :].rearrange("(t p) c -> p t c", p=P), fsb[:, 0:nt, :],
                              bfs, b_Fdl)
                    banks = []
                    for m2 in range(2):
                        pb, bp = S.bank()
                        banks.append((pb, bp))
                        for j in range(8):
                            S.op("pe", lambda e, pb=pb, m2=m2, j=j, hT=hT: e.matmul(
                                pb[:, 0:ntok], lhsT=win[:, j, 768 + m2 * P:768 + (m2 + 1) * P], rhs=hT[:, j, 0:ntok],
                                start=(j == 0), stop=(j == 7)), [bhT, b_win], [bp])
                    rms_T(banks, kvn, b_kvn, ntok, sq, b_sq, rb, b_rb, ckvn, b_ckvn)
                    knT, bkn = knr.next()
                    for hp in range(4):
                        pb, bp = S.bank()
                        for m2 in range(2):
                            S.op("pe", lambda e, pb=pb, hp=hp, m2=m2: e.matmul(
                                pb[:, 0:ntok],
                                lhsT=wkn[:, m2, hp * P:(hp + 1) * P],
                                rhs=ckvn[:, m2, 0:ntok], start=(m2 == 0), stop=(m2 == 1)), [b_ckvn, b_wkn], [bp])
                        S.op("act", lambda e, pb=pb, hp=hp, knT=knT: e.copy(out=knT[:, hp, 0:ntok], in_=pb[:, 0:ntok]),
                             [bp], [bkn])
                    for hh in range(2):
                        S.dma("pool", KT.rearrange("(hp hh) r n -> hh r hp n", hh=2)[hh, 0:64, :, koff:koff + ntok],
                              knT[hh * 64:(hh + 1) * 64, :, 0:ntok], bkn, b_KT)
                    for t in range(nt):
                        pb, bp = S.bank()
                        for m2 in range(2):
                            S.op("pe", lambda e, pb=pb, t=t, m2=m2: e.matmul(
                                pb[:, :], lhsT=ckvn[:, m2, t * P:(t + 1) * P],
                                rhs=wv[:, m2, :],
                                start=(m2 == 0), stop=(m2 == 1)), [b_ckvn, b_wv], [bp])
                        S.op("dve", lambda e, pb=pb, t=t, vsb=vsb: e.tensor_copy(
                            out=vsb[:, t, :, 0:64], in_=pb[:, :].rearrange("p (h c) -> p h c", c=64)), [bp], [bvs])
                    pa, bpa = S.bank()
                    pbk, bpbk = S.bank()
                    for j in range(8):
                        S.op("pe", lambda e, pa=pa, j=j, hT=hT: e.matmul(
                            pa[0:32, 0:ntok], lhsT=win[:, j, 1024:1056], rhs=hT[:, j, 0:ntok],
                            start=(j == 0), stop=(j == 7)), [bhT, b_win], [bpa])
                    for j in range(8):
                        S.op("pe", lambda e, pbk=pbk, j=j, hT=hT: e.matmul(
                            pbk[0:32, 0:ntok], lhsT=wkr[:, j, :], rhs=hT[:, j, 0:ntok],
                            start=(j == 0), stop=(j == 7)), [bhT, b_wkr], [bpbk])
                    krT, bkr = krr.next()
                    rope_out(pa, bpa, pbk, bpbk, 32, ntok, cs, b_cs, sn, b_sn, t1, b_t1, t2, b_t2,
                             krT[0:32, 0:ntok], bkr)
                    for h in range(8):
                        S.dma("pool", KT[h, 64:96, koff:koff + ntok], krT[0:32, 0:ntok], bkr, b_KT)
                else:
                    knT, bkn = knr.next()
                    for h in range(8):
                        pa, bpa = S.bank()
                        pbk, bpbk = S.bank()
                        for j in range(8):
                            S.op("pe", lambda e, pa=pa, j=j, h=h, hT=hT: e.matmul(
                                pa[:, 0:ntok], lhsT=win[:, j, h * P:(h + 1) * P], rhs=hT[:, j, 0:ntok],
                                start=(j == 0), stop=(j == 7)), [bhT, b_win], [bpa])
                        for j in range(8):
                            S.op("pe", lambda e, pbk=pbk, j=j, h=h, hT=hT: e.matmul(
                                pbk[:, 0:ntok], lhsT=wsw[:, j, h * P:(h + 1) * P], rhs=hT[:, j, 0:ntok],
                                start=(j == 0), stop=(j == 7)), [bhT, b_wsw], [bpbk])
                        rope_out(pa, bpa, pbk, bpbk, P, ntok, cs, b_cs, sn, b_sn, t1, b_t1, t2, b_t2,
                                 knT[:, h, 0:ntok], bkn)
                    S.dma("pool", KT[:, :, koff:koff + ntok].rearrange("h r n -> r h n"), knT[:, :, 0:ntok], bkn, b_KT)
                    for t in range(nt):
                        for hf in range(2):
                            pb, bp = S.bank()
                            for j in range(8):
                                S.op("pe", lambda e, pb=pb, t=t, j=j, hf=hf, hT=hT: e.matmul(
                                    pb[:, :], lhsT=hT[:, j, t * P:(t + 1) * P],
                                    rhs=win[:, j, 1024 + hf * 512:1024 + (hf + 1) * 512],
                                    start=(j == 0), stop=(j == 7)), [bhT, b_win], [bp])
                            S.op("dve", lambda e, pb=pb, t=t, hf=hf, vsb=vsb: e.tensor_copy(
                                out=vsb[:, t, hf * 4:(hf + 1) * 4, 0:128], in_=pb[:, :].rearrange("p (h c) -> p h c", c=128)),
                                [bp], [bvs])
                S.dma("pool", Vs[koff:koff + ntok].rearrange("(t p) h c -> p t (h c)", p=P),
                      vsb[:, 0:nt].rearrange("p t h c -> p t (h c)"), bvs, b_Vs)
            sts = [None] * len(kv_blocks)
            sts[0] = kv_a(*kv_blocks[0])
            kv_b(sts[0])
            for i in range(len(kv_blocks)):
                if i + 1 < len(kv_blocks):
                    sts[i + 1] = kv_a(*kv_blocks[i + 1])
                kv_body(sts[i])
                if i + 1 < len(kv_blocks):
                    kv_b(sts[i + 1])
            S.end_phase()

        with ExitStack() as ph:
            xr = Ring(nc, ph, "xr", 1, [P, 4, D], F32)
            xnr = Ring(nc, ph, "xn", 2, [P, 4, D], F32)
            stb_ring = Ring(nc, ph, "stb", 2, [P, 64], F32)
            hTr = Ring(nc, ph, "hT", 2, [P, 8, 512], BF16)
            st_ring = Ring(nc, ph, "st", 4, [P, 16], F32)
            csr = Ring(nc, ph, "csq", 2, [R, 512], F32)
            snr = Ring(nc, ph, "snq", 2, [R, 512], F32)
            t1r = Ring(nc, ph, "t1", 2, [P, 512], F32)
            t2r = Ring(nc, ph, "t2", 2, [P, 512], F32)
            qsr = Ring(nc, ph, "qsb", 2, [R, 8, 512], BF16)
            if L == 0:
                win, b_win = sbt(nc, ph, "win", [P, 8, 256], BF16)
                wuq, b_wuq = sbt(nc, ph, "wuq", [P, 2, 768], BF16)
                wuqs, b_wuqs = sbt(nc, ph, "wuqs", [P, 2, 768], BF16)
                qn, b_qn = sbt(nc, ph, "qn", [P, 2], F32)
                S.dma("pool", win[:], io["w_in"][:, 512:768].rearrange("(j p) n -> p j n", p=P), bin_, b_win)
                S.dma("pool", wuq[:], io["w_uq"].rearrange("(j p) n -> p j n", p=P), bin_, b_wuq)
                S.dma("pool", wuqs[:], io["w_uq_sw"].rearrange("(j p) n -> p j n", p=P), bin_, b_wuqs)
                S.dma("sp", qn[:], io["qn"], bin_, b_qn)
                sq, b_sq = sbt(nc, ph, "sq", [P, 2, 512], F32)
                rb, b_rb = sbt(nc, ph, "rb", [P, 512], F32)
                cqn, b_cqn = sbt(nc, ph, "cqn", [P, 2, 512], BF16)
            else:
                win, b_win = sbt(nc, ph, "win", [P, 8, 1024], BF16)
                wsw, b_wsw = sbt(nc, ph, "wsw", [P, 8, 1024], BF16)
                S.dma("pool", win[:], io["w_in"][:, 0:1024].rearrange("(j p) n -> p j n", p=P), bin_, b_win)
                S.dma("pool", wsw[:], io["w_qk_sw"][:, 0:1024].rearrange("(j p) n -> p j n", p=P), bin_, b_wsw)
            def q_a(src, nt, qoff, m, sbuf_src):
                ntok = nt * P
                xb, bxb = xr.next()
                S.dma("sp", xb[:, 0:nt, :], src.rearrange("(t p) d -> p t d", p=P), sbuf_src, bxb)
                cs, b_cs = csr.next()
                sn, b_sn = snr.next()
                S.dma("sp", cs[:, 0:ntok], io["cosQ"][:, qoff:qoff + ntok], bin_, b_cs)
                S.dma("sp", sn[:, 0:ntok], io["sinQ"][:, qoff:qoff + ntok], bin_, b_sn)
                xn_, bxn_ = xnr.next()
                ln_block(stb_ring, xb, bxb, xn_, bxn_, nt)
                return dict(nt=nt, qoff=qoff, m=m, cs=cs, b_cs=b_cs, sn=sn, b_sn=b_sn, xn=xn_, bxn=bxn_)

            def q_b(st):
                hT, bhT = hTr.next()
                make_hT(st["xn"], st["bxn"], st["nt"], hT, bhT, 0, 1, st["m"])
                st["hT"], st["bhT"] = hT, bhT

            def q_body(st):
                nt, qoff, m = st["nt"], st["qoff"], st["m"]
                cs, b_cs, sn, b_sn, hT, bhT = st["cs"], st["b_cs"], st["sn"], st["b_sn"], st["hT"], st["bhT"]
                ntok = nt * P
                qsb, bqs = qsr.next()
                if L == 0:
                    banks = []
                    for m2 in range(2):
                        pb, bp = S.bank()
                        banks.append((pb, bp))
                        for j in range(8):
                            S.op("pe", lambda e, pb=pb, m2=m2, j=j, hT=hT: e.matmul(
                                pb[:, 0:ntok], lhsT=win[:, j, m2 * P:(m2 + 1) * P], rhs=hT[:, j, 0:ntok],
                                start=(j == 0), stop=(j == 7)), [bhT, b_win], [bp])
                    rms_T(banks, qn, b_qn, ntok, sq, b_sq, rb, b_rb, cqn, b_cqn)
                for h in range(8):
                    pa, bpa = S.bank()
                    pbk, bpbk = S.bank()
                    if L == 0:
                        for m2 in range(2):
                            S.op("pe", lambda e, pa=pa, m2=m2, h=h: e.matmul(
                                pa[0:96, 0:ntok], lhsT=wuq[:, m2, h * 96:(h + 1) * 96], rhs=cqn[:, m2, 0:ntok],
                                start=(m2 == 0), stop=(m2 == 1)), [b_cqn, b_wuq], [bpa])
                        for m2 in range(2):
                            S.op("pe", lambda e, pbk=pbk, m2=m2, h=h: e.matmul(
                                pbk[0:96, 0:ntok], lhsT=wuqs[:, m2, h * 96:(h + 1) * 96], rhs=cqn[:, m2, 0:ntok],
                                start=(m2 == 0), stop=(m2 == 1)), [b_cqn, b_wuqs], [bpbk])
                    else:
                        for j in range(8):
                            S.op("pe", lambda e, pa=pa, j=j, h=h, hT=hT: e.matmul(
                                pa[:, 0:ntok], lhsT=win[:, j, h * P:(h + 1) * P], rhs=hT[:, j, 0:ntok],
                                start=(j == 0), stop=(j == 7)), [bhT, b_win], [bpa])
                        for j in range(8):
                            S.op("pe", lambda e, pbk=pbk, j=j, h=h, hT=hT: e.matmul(
                                pbk[:, 0:ntok], lhsT=wsw[:, j, h * P:(h + 1) * P], rhs=hT[:, j, 0:ntok],
                                start=(j == 0), stop=(j == 7)), [bhT, b_wsw], [bpbk])
                    t1, b_t1 = t1r.next()
                    t2, b_t2 = t2r.next()
                    rope_out(pa, bpa, pbk, bpbk, R, ntok, cs, b_cs, sn, b_sn, t1, b_t1, t2, b_t2,
                             qsb[0:R, h, 0:ntok], bqs)
                S.dma("pool", QT[:, :, qoff:qoff + ntok].rearrange("h r n -> r h n"), qsb[0:R, :, 0:ntok], bqs, b_QT)
            sts = [None] * len(q_blocks)
            sts[0] = q_a(*q_blocks[0])
            q_b(sts[0])
            for i in range(len(q_blocks)):
                if i + 1 < len(q_blocks):
                    sts[i + 1] = q_a(*q_blocks[i + 1])
                q_body(sts[i])
                if i + 1 < len(q_blocks):
                    q_b(sts[i + 1])
            S.end_phase()

        if L == 0:
            with ExitStack() as ph:
                f64, b_f64 = sbt(nc, ph, "f64", [64, 128], F32)
                cdl, b_cdl = sbt(nc, ph, "cdl", [P, 2, 128], F32)
                cdc, b_cdc = sbt(nc, ph, "cdc", [P, 2, 128], F32)
                f256, b_f256 = sbt(nc, ph, "f256", [P, 2, 512], F32)
                S.dma("sp", f64[:], io["F64"], bin_, b_f64)
                S.dma("sp", cdl[:], io["CdSl"], bin_, b_cdl)
                S.dma("sp", cdc[:], io["CdSc"], bin_, b_cdc)
                S.dma("sp", f256[:], io["F256"], bin_, b_f256)
                Xr = Ring(nc, ph, "Xh", 2, [64, 128, 32], F32)
                T, b_T = sbt(nc, ph, "T", [P, 128, 128], F32)
                ZT, b_ZT = sbt(nc, ph, "ZT", [P, 2, 64, 64], F32)
                Gr_ = Ring(nc, ph, "G", 2, [P, 8, 2, 128], F32)
                ysr = Ring(nc, ph, "ysb", 2, [P, 4, 128], BF16)
                Fd3 = io["Fdl"].rearrange("(a b) c -> a b c", b=128)
                ZTf = ZT[:, :, :, :].rearrange("p r i q -> p r (i q)")
                for g in range(4):
                    for xq in range(4):
                        X, bX = Xr.next()
                        c0 = g * 128 + xq * 32
                        S.dma("sp", X[:], Fd3[:, :, c0:c0 + 32], b_Fdl, bX)
                        for cg in range(8):
                            pb, bp = S.bank()
                            for cc in range(4):
                                c = cg * 4 + cc
                                S.op("pe", lambda e, pb=pb, cc=cc, c=c, X=X: e.matmul(
                                    pb[:, cc * 128:(cc + 1) * 128], lhsT=X[:, :, c], rhs=f64[:, :],
                                    start=True, stop=True), [bX, b_f64], [bp])
                            ch0 = xq * 32 + cg * 4
                            if cg % 2 == 0:
                                S.op("act", lambda e, pb=pb, ch0=ch0: e.copy(
                                    out=T[:, ch0:ch0 + 4, :], in_=pb[:, :].rearrange("p (c k) -> p c k", k=128)),
                                    [bp], [b_T])
                            else:
                                S.op("dve", lambda e, pb=pb, ch0=ch0: e.tensor_copy(
                                    out=T[:, ch0:ch0 + 4, :], in_=pb[:, :].rearrange("p (c k) -> p c k", k=128)),
                                    [bp], [b_T])
                    for kc in range(8):
                        G, bG = Gr_.next()
                        S.dma("sp", G[:], io["GAB"][:, kc * 8:(kc + 1) * 8], bin_, bG)
                        for kq in range(2):
                            pb, bp = S.bank()
                            for q in range(4):
                                kl = kq * 4 + q
                                k2 = kc * 8 + kl
                                S.op("pe", lambda e, pb=pb, q=q, kl=kl, k2=k2, G=G: e.matmul(
                                    pb[:, q * 128:(q + 1) * 128], lhsT=T[:, :, k2],
                                    rhs=G[:, kl, 0, :], start=True, stop=False), [b_T, bG], [bp])
                                S.op("pe", lambda e, pb=pb, q=q, kl=kl, k2=k2, G=G: e.matmul(
                                    pb[:, q * 128:(q + 1) * 128], lhsT=T[:, :, 64 + k2],
                                    rhs=G[:, kl, 1, :], start=False, stop=True), [b_T, bG], [bp])
                            k20 = kc * 8 + kq * 4
                            S.op("dve", lambda e, pb=pb, k20=k20: e.tensor_copy(
                                out=ZT[:, :, :, k20:k20 + 4].rearrange("p r i q -> p q r i"),
                                in_=pb[:, :].rearrange("p (q r i) -> p q r i", q=4, r=2)),
                                [bp], [b_ZT])
                    for tq in range(8):
                        pb, bp = S.bank()
                        for q in range(4):
                            tt = tq * 4 + q
                            for r in range(2):
                                S.op("pe", lambda e, pb=pb, q=q, tt=tt, r=r: e.matmul(
                                    pb[:, q * 128:(q + 1) * 128], lhsT=ZTf[:, r, tt * 128:(tt + 1) * 128],
                                    rhs=cdl[:, r, :], start=(r == 0), stop=(r == 1)), [b_ZT, b_cdl], [bp])
                        ysb, bys = ysr.next()
                        S.op("act", lambda e, pb=pb, ysb=ysb: e.copy(
                            out=ysb[:, :, :], in_=pb[:, :].rearrange("p (q c) -> p q c", c=128)), [bp], [bys])
                        S.dma("pool", cat[tq * 512:(tq + 1) * 512, g * 128:(g + 1) * 128].rearrange("(t p) c -> p t c", p=P),
                              ysb[:, :, :], bys, b_cat)
                fcx, b_fcx = sbt(nc, ph, "fcx", [P, 2, 512], F32)
                zc, b_zc = sbt(nc, ph, "zc", [P, 512], F32)
                S.dma("sp", fcx[:], io["Fdc"].rearrange("(t p) c -> p t c", p=P), b_Fdc, b_fcx)
                for g in range(4):
                    pb, bp = S.bank()
                    for tl in range(2):
                        S.op("pe", lambda e, pb=pb, tl=tl, g=g: e.matmul(
                            pb[:, :], lhsT=fcx[:, tl, g * 128:(g + 1) * 128], rhs=f256[:, tl, :],
                            start=(tl == 0), stop=(tl == 1)), [b_fcx, b_f256], [bp])
                    S.op("dve", lambda e, pb=pb: e.tensor_copy(out=zc[:, :], in_=pb[:, :]), [bp], [b_zc])
                    pb2, bp2 = S.bank()
                    for tl in range(2):
                        for r in range(2):
                            S.op("pe", lambda e, pb2=pb2, tl=tl, r=r: e.matmul(
                                pb2[:, tl * 128:(tl + 1) * 128], lhsT=zc[:, r * 256 + tl * 128:r * 256 + (tl + 1) * 128],
                                rhs=cdc[:, r, :], start=(r == 0), stop=(r == 1)), [b_zc, b_cdc], [bp2])
                    ysb, bys = ysr.next()
                    S.op("act", lambda e, pb2=pb2, ysb=ysb: e.copy(
                        out=ysb[:, 0:2, :], in_=pb2[:, 0:256].rearrange("p (q c) -> p q c", c=128)), [bp2], [bys])
                    S.dma("pool", cat[SH:SH + CTX, g * 128:(g + 1) * 128].rearrange("(t p) c -> p t c", p=P),
                          ysb[:, 0:2, :], bys, b_cat)
                S.end_phase()

        with ExitStack() as ph:
            ktr = Ring(nc, ph, "kts", 2, [R, NK], BF16)
            vr = Ring(nc, ph, "vs", 2, [P, 66, vw], BF16)
            if L == 0:
                qtr = Ring(nc, ph, "qts", 2, [R, SH + CTX], BF16)
            else:
                qtr = Ring(nc, ph, "qts", 2, [P, 2, SH], BF16)
                for i in range(2):
                    S.op("pool", lambda e, i=i: e.memset(qtr.t[i][:], 0.0), [], [qtr.b[i]])
            ptr = Ring(nc, ph, "pt", 4, [P, 1024], BF16)
            rcr = Ring(nc, ph, "rc", 4, [P, 8], F32)
            asr = Ring(nc, ph, "asb", 2, [P, 4, 128 if L == 1 else 64], BF16)
            if L == 1:
                o0r = Ring(nc, ph, "o0", 2, [P, 4, 128], F32)
                o1r = Ring(nc, ph, "o1", 2, [P, 128], F32)
                sqr = Ring(nc, ph, "sqr", 2, [P, 128], F32)
                lamt, b_lam = sbt(nc, ph, "lamt", [P, 4, 64], F32)
                lamw, b_lamw = sbt(nc, ph, "lamw", [P, 8], F32)
                lamj, b_lamj = sbt(nc, ph, "lamj", [P, 64], F32)
                subl, b_subl = sbt(nc, ph, "subl", [P, 128], F32)
                epsc, b_epsc = sbt(nc, ph, "epsc", [P, 1], F32)
                S.op("dve", lambda e: e.memset(epsc[:], EPS), [], [b_epsc])
                S.dma("sp", lamt[:], io["lam"].partition_broadcast(P), bin_, b_lam)
                S.dma("sp", subl[:], io["subln"].partition_broadcast(P), bin_, b_subl)
                for i in range(2):
                    S.op("dve", lambda e, i=i: e.scalar_tensor_tensor(
                        out=lamj[:, :], in0=lamt[:, 2 * i, :], scalar=1.0, in1=lamt[:, 2 * i + 1, :],
                        op0=ALU.mult, op1=ALU.mult, accum_out=lamw[:, i:i + 1]), [b_lam], [b_lamj, b_lamw])
                S.op("act", lambda e: e.activation(out=lamw[:, 2:4], in_=lamw[:, 0:2], func=AF.Exp), [b_lamw], [b_lamw])
                S.op("dve", lambda e: e.tensor_tensor(out=lamw[:, 4:5], in0=lamw[:, 3:4], in1=lamw[:, 2:3],
                                                      op=ALU.subtract), [b_lamw], [b_lamw])
                S.op("dve", lambda e: e.tensor_scalar(out=lamw[:, 5:6], in0=lamw[:, 4:5], scalar1=-LAMBDA_INIT1,
                                                      scalar2=None, op0=ALU.add), [b_lamw], [b_lamw])
                S.op("dve", lambda e: e.tensor_scalar(out=subl[:, :], in0=subl[:, :], scalar1=1.0 - LAMBDA_INIT1,
                                                      scalar2=None, op0=ALU.mult), [b_subl], [b_subl])
            scale = MLA_SCALE if L == 0 else DIFF_SCALE
            nq_tot = SH + (CTX if need_ctx else 0)
            qblocks = [(b * 512, 512, 66) for b in range(8)]
            if need_ctx:
                qblocks.append((SH, 256, 2))
            nmaps = 1 if L == 0 else 2
            accr = Ring(nc, ph, "accs", 4, [P, 4, 132], F32)
            rc8r = Ring(nc, ph, "rc8", 4, [P, 16], F32)
            o1br = Ring(nc, ph, "o1b", 2, [P, 4, 128], F32)
            items = []
            for h in range(8):
                for qi in range(len(qblocks)):
                    for ci in range(nmaps):
                        for kp in range(qblocks[qi][2] // 2):
                            items.append((h, qi, ci, kp))
            heads = {}

            def load_head(h):
                kts, bkt = ktr.next()
                vs_, bv = vr.next()
                qts, bqt = qtr.next()
                S.dma("sp", kts[:, :], KT[h], b_KT, bkt)
                S.dma("sp", vs_[:, :, :], Vs[:, h, :].rearrange("(t p) c -> p t c", p=P), b_Vs, bv)
                if L == 0:
                    S.dma("sp", qts[:, 0:nq_tot], QT[h, :, 0:nq_tot], b_QT, bqt)
                else:
                    S.dma("sp", qts[0:64, 0, :], QT[h, 0:64, :], b_QT, bqt)
                    S.dma("sp", qts[64:128, 1, :], QT[h, 64:128, :], b_QT, bqt)
                heads[h] = (kts, bkt, vs_, bv, qts, bqt)

            qstate = {}

            def finalize(h, qi, ci):
                q0, nq, nkt = qblocks[qi]
                nj = nq // P
                stq = qstate.setdefault((h, qi), {})
                acs, bacs = accr.next()
                for j in range(nj):
                    S.op("dve", lambda e, j=j, acs=acs: e.tensor_copy(out=acs[:, j, 0:vw], in_=S.pb[4 + j][:, 0:vw]),
                         [S.bpb[4 + j]], [bacs])
                if ci == 0:
                    rc, brc = rc8r.next()
                    stq["rc"], stq["brc"] = rc, brc
                    stq["a0"], stq["ba0"] = acs, bacs
                else:
                    rc, brc = stq["rc"], stq["brc"]
                S.op("dve", lambda e, acs=acs, rc=rc, ci=ci: e.reciprocal(
                    out=rc[:, 4 * ci:4 * ci + nj], in_=acs[:, 0:nj, vw - 1]), [bacs], [brc])
                if L == 0:
                    asb, bas = asr.next()
                    for j in range(nj):
                        S.op("dve", lambda e, j=j, acs=acs, rc=rc, asb=asb: e.tensor_scalar(
                            out=asb[:, j, :], in0=acs[:, j, 0:64], scalar1=rc[:, j:j + 1], scalar2=None, op0=ALU.mult),
                            [bacs, brc], [bas])
                    S.dma("pool", cat[q0:q0 + nq, 512 + h * 64:512 + (h + 1) * 64].rearrange("(t p) c -> p t c", p=P),
                          asb[:, 0:nj, :], bas, b_cat)
                elif ci == 1:
                    a0, ba0 = stq["a0"], stq["ba0"]
                    asb, bas = asr.next()
                    o1, b_o1 = o1br.next()
                    sqt, bsqt = sqr.next()
                    S.op("dve", lambda e, rc=rc: e.tensor_scalar(out=rc[:, 4:8], in0=rc[:, 4:8], scalar1=lamw[:, 5:6],
                                                                 scalar2=None, op0=ALU.mult), [brc, b_lamw], [brc])
                    for j in range(nj):
                        S.op("dve", lambda e, j=j, a0=a0, rc=rc, o1=o1: e.tensor_scalar(
                            out=o1[:, j, :], in0=a0[:, j, 0:128], scalar1=rc[:, j:j + 1], scalar2=None, op0=ALU.mult),
                            [ba0, brc], [b_o1])
                        S.op("dve", lambda e, j=j, acs=acs, rc=rc, o1=o1: e.scalar_tensor_tensor(
                            out=o1[:, j, :], in0=acs[:, j, 0:128], scalar=rc[:, 4 + j:5 + j], in1=o1[:, j, :],
                            op0=ALU.mult, op1=ALU.add), [bacs, brc, b_o1], [b_o1])
                        S.op("dve", lambda e, j=j, o1=o1, rc=rc, sqt=sqt: e.scalar_tensor_tensor(
                            out=sqt[:, :], in0=o1[:, j, :], scalar=1.0, in1=o1[:, j, :],
                            op0=ALU.mult, op1=ALU.mult, accum_out=rc[:, 8 + j:9 + j]), [b_o1], [brc, bsqt])
                    S.op("act", lambda e, rc=rc: e.activation(out=rc[:, 12:16], in_=rc[:, 8:12], func=AF.Ln,
                                                             bias=epsc[:, 0:1], scale=1.0 / 128.0), [brc, b_epsc], [brc])
                    S.op("act", lambda e, rc=rc: e.activation(out=rc[:, 12:16], in_=rc[:, 12:16], func=AF.Exp,
                                                             scale=-0.5), [brc], [brc])
                    for j in range(nj):
                        S.op("dve", lambda e, j=j, o1=o1, rc=rc, asb=asb: e.scalar_tensor_tensor(
                            out=asb[:, j, :], in0=o1[:, j, :], scalar=rc[:, 12 + j:13 + j], in1=subl[:, :],
                            op0=ALU.mult, op1=ALU.mult), [b_o1, brc, b_subl], [bas])
                    S.dma("pool", cat[q0:q0 + nq, h * 128:(h + 1) * 128].rearrange("(t p) c -> p t c", p=P),
                          asb[:, 0:nj, :], bas, b_cat)

            DEPTH = 2
            pts = {}
            load_head(0)
            spair = [(S.pbig[:, 0:1024], [S.bpb[0], S.bpb[1]]), (S.pbig[:, 1024:2048], [S.bpb[2], S.bpb[3]])]
            for n in range(len(items) + DEPTH):
                if n < len(items):
                    h, qi, ci, kp = items[n]
                    q0, nq, nkt = qblocks[qi]
                    kts, bkt, vs_, bv, qts, bqt = heads[h]
                    sp2, bsp2 = spair[n % 2]
                    for u in range(2):
                        kt = 2 * kp + u
                        if L == 0:
                            S.op("pe", lambda e, sp2=sp2, u=u, kt=kt, kts=kts, qts=qts, q0=q0, nq=nq: e.matmul(
                                sp2[:, u * 512:u * 512 + nq], lhsT=kts[0:96, kt * P:(kt + 1) * P], rhs=qts[0:96, q0:q0 + nq],
                                start=True, stop=True), [bkt, bqt], [bsp2[u]])
                        else:
                            S.op("pe", lambda e, sp2=sp2, u=u, kt=kt, kts=kts, qts=qts, ci=ci, q0=q0, nq=nq: e.matmul(
                                sp2[:, u * 512:u * 512 + nq], lhsT=kts[:, kt * P:(kt + 1) * P], rhs=qts[:, ci, q0:q0 + nq],
                                start=True, stop=True), [bkt, bqt], [bsp2[u]])
                    pt, bpt = ptr.next()
                    S.op("act", lambda e, sp2=sp2, pt=pt, nq=nq: e.activation(
                        out=pt[:, :].rearrange("p (u n) -> p u n", u=2)[:, :, 0:nq],
                        in_=sp2.rearrange("p (u n) -> p u n", u=2)[:, :, 0:nq], func=AF.Exp, scale=scale), bsp2, [bpt])
                    pts[n] = (pt, bpt)
                if n >= DEPTH:
                    h, qi, ci, kp = items[n - DEPTH]
                    q0, nq, nkt = qblocks[qi]
                    if qi == 0 and ci == 0 and kp == 0 and h + 1 < 8:
                        load_head(h + 1)
                    kts, bkt, vs_, bv, qts, bqt = heads[h]
                    pt, bpt = pts.pop(n - DEPTH)
                    for u in range(2):
                        kt = 2 * kp + u
                        for j in range(nq // P):
                            S.op("pe", lambda e, j=j, u=u, pt=pt, kt=kt, vs_=vs_, nkt=nkt: e.matmul(
                                S.pb[4 + j][:, 0:vw], lhsT=pt[:, u * 512 + j * P:u * 512 + (j + 1) * P], rhs=vs_[:, kt, :],
                                start=(kt == 0), stop=(kt == nkt - 1)), [bpt, bv], [S.bpb[4 + j]])
                    if 2 * kp + 1 == nkt - 1:
                        finalize(h, qi, ci)
            S.end_phase()

        with ExitStack() as ph:
            wout, b_wout = sbt(nc, ph, "wout", [P, 8, D], BF16)
            S.dma("pool", wout[:], io["w_out"].rearrange("(j p) n -> p j n", p=P), bin_, b_wout)
            xr = Ring(nc, ph, "xr", 2, [P, D], F32)
            ctr = Ring(nc, ph, "ct", 2, [P, D], BF16)
            cTr = Ring(nc, ph, "cT", 1, [P, 8, P], BF16)
            rr = Ring(nc, ph, "rr", 2, [P, D], F32)
            x1r = Ring(nc, ph, "x1", 2, [P, 4, D], F32)
            xn2r = Ring(nc, ph, "xn2", 1, [P, 1, D], F32)
            h2r = Ring(nc, ph, "h2T", 2, [P, 8, 512], BF16)
            actT, b_actT = sbt(nc, ph, "actT", [P, NF, 512], BF16)
            gur = Ring(nc, ph, "gu", 3, [P, 2, 8, 256], BF16)
            wdr = Ring(nc, ph, "wd", 3, [P, 2, 512], BF16)
            sgr = Ring(nc, ph, "sg", 2, [P, 512], F32)
            ygr = Ring(nc, ph, "yg", 2, [P, 512], F32)
            outr = Ring(nc, ph, "ot", 2, [P, D], F32)
            st_ring = Ring(nc, ph, "st", 4, [P, 16], F32)
            o_blocks = [(xbufs["xo_blk"](b), 4, b * 512, 0, xout(b), xbufs["xo_buf"](b), b_out(b)) for b in range(8)]
            if need_ctx:
                o_blocks.append((io["xc"], 2, SH, 1, xcout, xbufs["xc"], b_outc))

            def ln_affine(tmp, btmp, dst, bdst, g_row, b_row):
                ln_rows(st_ring, tmp, btmp, tmp, btmp)
                S.op("dve", lambda e: e.tensor_tensor(out=tmp, in0=tmp, in1=lnbc[:, g_row, :], op=ALU.mult),
                     [btmp, b_lnbc], [btmp])
                S.op("pool", lambda e: e.tensor_tensor(out=dst, in0=tmp, in1=lnbc[:, b_row, :], op=ALU.add),
                     [btmp, b_lnbc], [bdst])

            brange = [4, 8]

            def stage_a(src, nt, coff, m, dst, sbuf_src, bdst_out):
                x1s, b_x1s = x1r.next()
                h2T, b_h2T = h2r.next()
                st = dict(nt=nt, m=m, dst=dst, bdst_out=bdst_out, x1=x1s, b_x1=b_x1s, h2T=h2T, b_h2T=b_h2T)
                yield st
                for t in range(nt):
                    xt, bxt = xr.next()
                    ct, bct = ctr.next()
                    S.dma("sp", xt[:, :], src[t * P:(t + 1) * P, :], sbuf_src, bxt)
                    S.dma("sp", ct[:, :], cat[coff + t * P:coff + (t + 1) * P, :], b_cat, bct)
                    pb, bp = S.bank(*brange)
                    pbv = pb[:, :].bitcast(BF16)
                    for j in range(8):
                        S.op("pe", lambda e, pbv=pbv, j=j, ct=ct: e.transpose(
                            out=pbv[:, j * P:(j + 1) * P], in_=ct[:, j * P:(j + 1) * P], identity=ident_b[:, :]),
                            [bct, b_idb], [bp])
                    cT, bcT = cTr.next()
                    S.op("act", lambda e, pbv=pbv, cT=cT: e.copy(out=cT[:, :, :], in_=pbv.rearrange("p (j c) -> p j c", c=P)),
                         [bp], [bcT])
                    yield None
                    tmp, btmp = rr.next()
                    for hf in range(2):
                        pb, bp = S.bank(*brange)
                        for j in range(8):
                            S.op("pe", lambda e, pb=pb, j=j, hf=hf, cT=cT: e.matmul(
                                pb[:, :], lhsT=cT[:, j, :], rhs=wout[:, j, hf * 512:(hf + 1) * 512],
                                start=(j == 0), stop=(j == 7)), [bcT, b_wout], [bp])
                        S.op("dve", lambda e, pb=pb, hf=hf, tmp=tmp: e.tensor_tensor(
                            out=tmp[:, hf * 512:(hf + 1) * 512], in0=pb[:, :], in1=gbc[:, m, 0, hf * 512:(hf + 1) * 512],
                            op=ALU.mult), [bp, b_gbc], [btmp])
                    yield None
                    S.op("dve", lambda e, tmp=tmp, xt=xt: e.scalar_tensor_tensor(
                        out=tmp[:, :], in0=xt[:, :], scalar=ALPHA, in1=tmp[:, :], op0=ALU.mult, op1=ALU.add),
                        [bxt, btmp], [btmp])
                    ln_affine(tmp[:, :], btmp, x1s[:, t, :], b_x1s, 0, 1)
                    yield None
                    xn2, bxn2 = xn2r.next()
                    ln_rows(st_ring, x1s[:, t, :], b_x1s, xn2[:, 0, :], bxn2)
                    for jh in range(2):
                        pb, bp = S.bank(*brange)
                        for jj in range(4):
                            j = jh * 4 + jj
                            S.op("pe", lambda e, pb=pb, jj=jj, j=j, xn2=xn2: e.transpose(
                                out=pb[:, jj * P:(jj + 1) * P], in_=xn2[:, 0, j * P:(j + 1) * P], identity=ident_f[:, :]),
                                [bxn2, b_idf], [bp])
                        for jj in range(4):
                            j = jh * 4 + jj
                            if jj % 2 == 0:
                                S.op("act", lambda e, pb=pb, jj=jj, j=j, t=t: e.activation(
                                    out=h2T[:, j, t * P:(t + 1) * P], in_=pb[:, jj * P:(jj + 1) * P], func=AF.Identity,
                                    scale=fcol[:, 2, j, m:m + 1], bias=fcol[:, 3, j, m:m + 1]), [bp, b_fcol], [b_h2T])
                            else:
                                S.op("dve", lambda e, pb=pb, jj=jj, j=j, t=t: e.tensor_scalar(
                                    out=h2T[:, j, t * P:(t + 1) * P], in0=pb[:, jj * P:(jj + 1) * P],
                                    scalar1=fcol[:, 2, j, m:m + 1], scalar2=fcol[:, 3, j, m:m + 1],
                                    op0=ALU.mult, op1=ALU.add), [bp, b_fcol], [b_h2T])
                    yield None

            def stage_c(st):
                nt, x1s, b_x1s = st["nt"], st["x1"], st["b_x1"]
                for t in range(nt):
                    ot, bot = outr.next()
                    ln_affine(x1s[:, t, :], b_x1s, ot[:, :], bot, 2, 3)
                    S.dma("pool", st["dst"][t * P:(t + 1) * P, :], ot[:, :], bot, st["bdst_out"])
                    yield None

            def step(bg, n=1):
                for _ in range(n):
                    if bg is None:
                        return
                    try:
                        next(bg)
                    except StopIteration:
                        return

            def stage_b(st, bg):
                nt, m, x1s, b_x1s, h2T, b_h2T = st["nt"], st["m"], st["x1"], st["b_x1"], st["h2T"], st["b_h2T"]
                ntok = nt * P
                brange[0] = 0
                for c in range(11):
                    gu, bgu = gur.next()
                    S.dma("sp", gu[:, 0, :, :], io["wg_bf"][c], b_wbf, bgu)
                    S.dma("sp", gu[:, 1, :, :], io["wu_bf"][c], b_wbf, bgu)
                    for fc in range(2):
                        f = c * 2 + fc
                        pg, bpg = S.bank(*brange)
                        pu, bpu = S.bank(*brange)
                        for j in range(8):
                            S.op("pe", lambda e, pg=pg, j=j, fc=fc, gu=gu: e.matmul(
                                pg[:, 0:ntok], lhsT=gu[:, 0, j, fc * P:(fc + 1) * P], rhs=h2T[:, j, 0:ntok],
                                start=(j == 0), stop=(j == 7)), [bgu, b_h2T], [bpg])
                        for j in range(8):
                            S.op("pe", lambda e, pu=pu, j=j, fc=fc, gu=gu: e.matmul(
                                pu[:, 0:ntok], lhsT=gu[:, 1, j, fc * P:(fc + 1) * P], rhs=h2T[:, j, 0:ntok],
                                start=(j == 0), stop=(j == 7)), [bgu, b_h2T], [bpu])
                        sg, bsg = sgr.next()
                        S.op("act", lambda e, pg=pg, sg=sg: e.activation(out=sg[:, 0:ntok], in_=pg[:, 0:ntok], func=AF.Silu),
                             [bpg], [bsg])
                        S.op("dve", lambda e, pu=pu, sg=sg, f=f: e.tensor_tensor(
                            out=actT[:, f, 0:ntok], in0=pu[:, 0:ntok], in1=sg[:, 0:ntok], op=ALU.mult),
                            [bpu, bsg], [b_actT])
                    step(bg)
                brange[0] = 4
                for hf in range(2):
                    for c in range(11):
                        wd, bwd = wdr.next()
                        S.dma("sp", wd[:, :, :], io["wd_bf"][c * 256:(c + 1) * 256, hf * 512:(hf + 1) * 512]
                              .rearrange("(f p) n -> p f n", p=P), b_wbf, bwd)
                        for fc in range(2):
                            f = c * 2 + fc
                            for t in range(nt):
                                S.op("pe", lambda e, f=f, fc=fc, t=t, wd=wd: e.matmul(
                                    S.pb[t][:, :], lhsT=actT[:, f, t * P:(t + 1) * P], rhs=wd[:, fc, :],
                                    start=(f == 0), stop=(f == NF - 1)), [b_actT, bwd], [S.bpb[t]])
                        step(bg)
                    for t in range(nt):
                        yg, byg = ygr.next()
                        S.op("dve", lambda e, t=t, yg=yg, hf=hf: e.tensor_tensor(
                            out=yg[:, :], in0=S.pb[t][:, :], in1=gbc[:, m, 1, hf * 512:(hf + 1) * 512], op=ALU.mult),
                            [S.bpb[t], b_gbc], [byg])
                        S.op("dve", lambda e, t=t, yg=yg, hf=hf: e.scalar_tensor_tensor(
                            out=x1s[:, t, hf * 512:(hf + 1) * 512], in0=x1s[:, t, hf * 512:(hf + 1) * 512], scalar=ALPHA,
                            in1=yg[:, :], op0=ALU.mult, op1=ALU.add), [b_x1s, byg], [b_x1s])

            def chain(*gens):
                for g_ in gens:
                    if g_ is not None:
                        for _ in g_:
                            yield None

            nb = len(o_blocks)
            ga = stage_a(*o_blocks[0])
            st_cur = next(ga)
            for _ in ga:
                pass
            prev_c = None
            for i in range(nb):
                if i + 1 < nb:
                    ga = stage_a(*o_blocks[i + 1])
                    st_next = next(ga)
                else:
                    ga, st_next = None, None
                bg = chain(prev_c, ga)
                stage_b(st_cur, bg)
                for _ in bg:
                    pass
                prev_c = stage_c(st_cur)
                st_cur = st_next
            for _ in prev_c:
                pass
            S.end_phase()


def build_program():
    nc = bass.Bass("TRN2", target_bir_lowering=False)
    io0 = declare_layer(nc, 0, "l0_")
    x1own = [nc.dram_tensor("x1own%d" % c, [512, D], F32, kind="Internal").ap() for c in range(8)]
    x1gat = [nc.dram_tensor("x1gat%d" % c, [1024, D], F32, kind="Internal").ap() for c in range(8)]
    xc1 = nc.dram_tensor("xc1", [CTX, D], F32, kind="Internal").ap()
    io1 = declare_layer(nc, 1, "l1_", x_from_dram={"xc": xc1, "ident": io0["ident"]})
    xout = nc.dram_tensor("xout", [SH, D], F32, kind="ExternalOutput").ap()
    with ExitStack() as es:
        S = Sched(nc, es)
        ident_f = es.enter_context(nc.sbuf_tensor("ident_f", [P, P], F32)); b_idf = Buf("idf")
        ident_b = es.enter_context(nc.sbuf_tensor("ident_b", [P, P], BF16)); b_idb = Buf("idb")
        ones_f = es.enter_context(nc.sbuf_tensor("ones_f", [P, P], F32)); b_ones = Buf("ones")
        bin_ = Buf("cin")
        S.dma("sp", ident_f[:], io0["ident"], bin_, b_idf)
        S.op("dve", lambda e: e.tensor_copy(out=ident_b[:], in_=ident_f[:]), [b_idf], [b_idb])
        S.op("dve", lambda e: e.memset(ones_f[:], 1.0), [], [b_ones])
        consts = (ident_f, b_idf, ident_b, b_idb, ones_f, b_ones)
        b_xc1, b_xout = Buf("xc1"), Buf("xout")
        b_own = [Buf("x1own%d" % c) for c in range(8)]
        b_gat = [Buf("x1gat%d" % c) for c in range(8)]
        xb0 = {"xc": bin_,
               "xf_blk": lambda b: io0["xf"][b * 512:(b + 1) * 512, :], "xf_buf": lambda b: bin_,
               "xo_blk": lambda b: io0["xo"][b * 512:(b + 1) * 512, :], "xo_buf": lambda b: bin_}
        build_layer(nc, S, 0, io0, lambda b: x1own[b], xc1, consts, xb0, lambda b: b_own[b], b_xc1)
        for c in range(8):
            if NO_COLL:
                S.dma("pool", x1gat[c][0:512], x1own[c], b_own[c], b_gat[c])
                S.dma("pool", x1gat[c][512:1024], x1own[c], b_own[c], b_gat[c])
            else:
                S.coll("AllGather", [[0, 1], [2, 3], [4, 5], [6, 7]], x1own[c], x1gat[c], b_own[c], b_gat[c])
        xb1 = {"xc": b_xc1,
               "xf_blk": lambda b: x1gat[b % 8][(b // 8) * 512:(b // 8 + 1) * 512, :], "xf_buf": lambda b: b_gat[b % 8],
               "xo_blk": lambda b: x1own[b], "xo_buf": lambda b: b_own[b]}
        build_layer(nc, S, 1, io1, lambda b: xout[b * 512:(b + 1) * 512, :], None, consts, xb1, lambda b: b_xout, None)
    return nc


L0_NAMES = ["l0_w_mod", "l0_b_mod", "l0_w_in", "l0_q_norm", "l0_w_uq", "l0_kv_norm", "l0_w_ukv", "l0_w_out",
            "l0_ln1_g", "l0_ln1_b", "l0_w_gate", "l0_w_up", "l0_w_down", "l0_ln2_g", "l0_ln2_b"]
L1_NAMES = ["l1_w_mod", "l1_b_mod", "l1_w_in", "l1_lambda_q1", "l1_lambda_k1", "l1_lambda_q2", "l1_lambda_k2",
            "l1_subln", "l1_w_out", "l1_ln1_g", "l1_ln1_b", "l1_w_gate", "l1_w_up", "l1_w_down", "l1_ln2_g", "l1_ln2_b"]


def layer_inputs(L, inputs, b, half):
    pre = "l%d_" % L
    names = L0_NAMES if L == 0 else L1_NAMES
    w = {n[3:]: np.ascontiguousarray(np.asarray(inputs[n], dtype=np.float32)) for n in names}
    g = lambda n: w[n]
    t = get_tables(half)
    m = {}
    c = np.asarray(inputs["c"], np.float32)[b]
    cc = np.asarray(inputs["c_ctx"], np.float32)
    m["cv"] = np.ascontiguousarray(np.stack([c.reshape(8, P).T, cc.reshape(8, P).T], 2))
    m["w_mod"] = g("w_mod"); m["b_mod"] = g("b_mod").reshape(1, -1)
    m["w_out"] = g("w_out"); m["w_gate"] = g("w_gate"); m["w_up"] = g("w_up"); m["w_down"] = g("w_down")
    m["ln"] = np.ascontiguousarray(np.stack([g("ln1_g"), g("ln1_b"), g("ln2_g"), g("ln2_b")], 0))
    w_in = g("w_in")
    if L == 0:
        m["ident"] = t["ident"]
        m["w_in"] = w_in
        sw = rope_swap_index(32)
        m["w_kr_sw"] = np.ascontiguousarray(w_in[:, 1024 + sw])
        m["qn"] = np.ascontiguousarray(g("q_norm").reshape(2, P).T)
        m["kvn"] = np.ascontiguousarray(g("kv_norm").reshape(2, P).T)
        w_uq = g("w_uq")
        perm = np.arange(768)
        for h in range(8):
            perm[h * 96 + 64:h * 96 + 96] = h * 96 + 64 + sw
        m["w_uq"] = w_uq
        m["w_uq_sw"] = np.ascontiguousarray(w_uq[:, perm])
        w_ukv = g("w_ukv").reshape(256, 8, 128)
        m["w_ukv_kn"] = np.ascontiguousarray(w_ukv[:, :, 0:64].reshape(256, 512))
        m["w_ukv_v"] = np.ascontiguousarray(w_ukv[:, :, 64:128].reshape(256, 512))
        m["cosK"], m["sinK"], m["cosQ"], m["sinQ"] = t["cosK0"], t["sinK0"], t["cosQ0"], t["sinQ0"]
        for k in ("F64", "GAB", "CdSl", "CdSc", "F256"):
            m[k] = t[k]
    else:
        m["w_in"] = w_in
        sw = rope_swap_index(64)
        perm = np.arange(2048)
        for blk in range(32):
            perm[blk * 64:(blk + 1) * 64] = blk * 64 + sw
        m["w_qk_sw"] = np.ascontiguousarray(w_in[:, perm])
        m["lam"] = np.ascontiguousarray(np.stack([g("lambda_q1"), g("lambda_k1"), g("lambda_q2"), g("lambda_k2")], 0))
        m["subln"] = g("subln").reshape(1, 128)
        m["cosK"], m["sinK"], m["cosQ"], m["sinQ"] = t["cosK1"], t["sinK1"], t["cosQ1"], t["sinQ1"]
    return {pre + k: v for k, v in m.items()}


_PROG = {}


def kernel(**inputs):
    x = np.asarray(inputs["x"], np.float32)
    xc = np.asarray(inputs["ctx"], np.float32)
    if "nc" not in _PROG:
        _PROG["nc"] = build_program()
    nc = _PROG["nc"]
    in_maps = []
    for k in range(8):
        b, half = k // 2, k % 2
        m = {}
        m.update(layer_inputs(0, inputs, b, half))
        m.update(layer_inputs(1, inputs, b, half))
        m["l0_xf"] = np.ascontiguousarray(x[b])
        m["l0_xo"] = np.ascontiguousarray(x[b, half * SH:(half + 1) * SH])
        m["l0_xc"] = np.ascontiguousarray(xc[b])
        in_maps.append(m)
    res = run_bass_kernel_spmd(nc, in_maps, core_ids=list(range(8)))
    r = res.results
    out = np.stack([np.concatenate([r[2 * b]["xout"], r[2 * b + 1]["xout"]], 0) for b in range(4)], 0)
    return out.astype(np.float32)
```

```python
import math
import numpy as np
from contextlib import ExitStack
import concourse.bass as bass
import concourse.mybir as mybir
from concourse.bass_utils import run_bass_kernel_spmd

F32 = mybir.dt.float32
BF16 = mybir.dt.bfloat16
AF = mybir.ActivationFunctionType
ALU = mybir.AluOpType

P = 128
D = 1024
SEQ = 8192
SH = 4096
CTX = 256
NK = SEQ + CTX
FF = 2816
NF = 22
ALPHA = 4.0 ** 0.25
EPS = 1e-6
MLA_SCALE = 96.0 ** -0.5
DIFF_SCALE = 64.0 ** -0.5
LAMBDA_INIT1 = 0.8 - 0.6 * math.exp(-0.3)

ENGS = ["pe", "act", "dve", "pool", "sp"]
DEBUG_SCR = False
NO_COLL = False


class Buf:
    __slots__ = ("name", "w", "r", "dsem", "dcnt")

    def __init__(self, name):
        self.name = name
        self.w = None
        self.r = []
        self.dsem = None
        self.dcnt = 0


class Sched:
    def __init__(self, nc, es):
        self.nc = nc
        self.es = es
        self.ops = {e: [] for e in ENGS}
        self.sem = {e: es.enter_context(nc.semaphore("s_" + e)) for e in ENGS}
        self.cnt = {e: 0 for e in ENGS}
        self.seen = {e: {} for e in ENGS}
        self.sem_pool = {"sp": [], "pool": [], "act": []}
        self.phase_bufs = []
        self.nsem = 0
        self.pbig = es.enter_context(nc.psum_tensor("pbig", [P, 4096], F32))
        self.pb = [self.pbig[:, i * 512:(i + 1) * 512] for i in range(8)]
        self.bpb = [Buf("pb%d" % i) for i in range(8)]
        self.pbi = 0

    def bank(self, lo=0, hi=8):
        i = self.pbi
        if i < lo or i >= hi:
            i = lo
        self.pbi = i + 1
        return self.pb[i], self.bpb[i]

    def _need(self, e, ev, waits):
        if ev is None:
            return
        sem, val = ev
        if sem is self.sem[e] and e in ("pe", "sp"):
            return
        k = id(sem)
        if self.seen[e].get(k, 0) >= val:
            return
        self.seen[e][k] = val
        waits.append((sem, val))

    def _deps(self, e, reads, writes):
        waits = []
        for b in reads:
            self._need(e, b.w, waits)
        for b in writes:
            self._need(e, b.w, waits)
            for ev in b.r:
                self._need(e, ev, waits)
        return waits

    def op(self, e, fn, reads=(), writes=()):
        waits = self._deps(e, reads, writes)
        self.cnt[e] += 1
        ev = (self.sem[e], self.cnt[e])
        self.ops[e].append((waits, fn, (self.sem[e], 1)))
        for b in reads:
            b.r.append(ev)
        for b in writes:
            b.w = ev
            b.r = []
        return ev

    def dma(self, q, out_ap, in_ap, src, dst, **kw):
        srcs = src if isinstance(src, (list, tuple)) else [src]
        dsts = dst if isinstance(dst, (list, tuple)) else [dst]
        waits = self._deps(q, srcs, dsts)
        d0 = dsts[0]
        if d0.dsem is None:
            if self.sem_pool[q]:
                d0.dsem, d0.dcnt = self.sem_pool[q].pop()
            else:
                d0.dsem = self.es.enter_context(self.nc.semaphore("d%d" % self.nsem))
                self.nsem += 1
                d0.dcnt = 0
            self.phase_bufs.append((d0, q))
        else:
            assert any(b is d0 and qq == q for b, qq in self.phase_bufs), "buffer %s written by DMAs of two queues" % d0.name
        d0.dcnt += 16
        ev = (d0.dsem, d0.dcnt)
        self.ops[q].append(
            (waits, lambda eng: eng.dma_start(out=out_ap, in_=in_ap, **kw), (d0.dsem, 16)))
        for b in srcs:
            b.r.append(ev)
        for b in dsts:
            b.w = ev
            b.r = []
        return ev

    def coll(self, kind, groups, in_ap, out_ap, src, dst):
        waits = self._deps("pool", [src], [dst])
        if not hasattr(self, "cc_sem"):
            self.cc_sem = self.es.enter_context(self.nc.semaphore("cc_sem"))
            self.cc_cnt = 0
        self.cc_cnt += 1
        ev = (self.cc_sem, self.cc_cnt)
        sem = self.cc_sem
        self.ops["pool"].append(
            (waits, lambda eng: eng.collective_compute(kind, ALU.bypass, replica_groups=groups,
                                                       ins=[in_ap.opt()], outs=[out_ap.opt()]), (sem, 1)))
        src.r.append(ev)
        dst.w = ev
        dst.r = []
        return ev

    def end_phase(self):
        for b, _q in self.phase_bufs:
            waits = []
            self._need("sp", (b.dsem, b.dcnt), waits)
            if waits:
                self.ops["sp"].append((waits, None, None))
        nc = self.nc
        ops = self.ops

        def replay(e, eng):
            for waits, fn, inc in ops[e]:
                for sem, val in waits:
                    eng.wait_ge(sem, val)
                if fn is not None:
                    fn(eng).then_inc(inc[0], inc[1])

        with nc.Block() as block:
            @block.tensor
            def _(eng):
                replay("pe", eng)

            @block.scalar
            def _(eng):
                replay("act", eng)

            @block.vector
            def _(eng):
                replay("dve", eng)

            @block.gpsimd
            def _(eng):
                replay("pool", eng)

            @block.sync
            def _(eng):
                replay("sp", eng)
        self.ops = {e: [] for e in ENGS}
        for b, q in self.phase_bufs:
            self.sem_pool[q].append((b.dsem, b.dcnt))
            b.dsem = None
        self.phase_bufs = []


_UID = [0]


def _uname(name):
    _UID[0] += 1
    return "sb%d_%s" % (_UID[0], name)


class Ring:
    def __init__(self, nc, ph, name, n, shape, dt):
        self.t = [ph.enter_context(nc.sbuf_tensor(_uname("%s%d" % (name, i)), list(shape), dt)) for i in range(n)]
        self.b = [Buf("%s%d" % (name, i)) for i in range(n)]
        self.i = 0

    def next(self):
        i = self.i
        self.i = (i + 1) % len(self.t)
        return self.t[i], self.b[i]


def sbt(nc, ph, name, shape, dt):
    return ph.enter_context(nc.sbuf_tensor(_uname(name), list(shape), dt)), Buf(name)


def rope_tables(dim, positions_rc):
    q = dim // 4
    inv = 1.0 / (10000.0 ** (np.arange(q, dtype=np.float64) / q))
    n = positions_rc.shape[0]
    cos = np.ones((dim, n), np.float64)
    sin = np.zeros((dim, n), np.float64)
    valid = positions_rc[:, 0] >= 0
    for d in range(dim):
        a = d // (2 * q)
        w = d % (2 * q)
        fi = w % q
        first = w < q
        ang = positions_rc[:, a].astype(np.float64) * inv[fi]
        cos[d] = np.where(valid, np.cos(ang), 1.0)
        s = np.sin(ang)
        sin[d] = np.where(valid, -s if first else s, 0.0)
    return cos, sin


def rope_swap_index(dim):
    q = dim // 4
    idx = np.zeros(dim, np.int64)
    for d in range(dim):
        w = d % (2 * q)
        idx[d] = d + q if w < q else d - q
    return idx


def pos_rc(tokens):
    tokens = np.asarray(tokens)
    return np.stack([tokens // 64, tokens % 64], 1)


_TABLE_CACHE = {}


def get_tables(half):
    if half in _TABLE_CACHE:
        return _TABLE_CACHE[half]
    t = {}
    neg = -np.ones((CTX, 2), np.int64)
    key_pos = np.concatenate([neg, pos_rc(np.arange(SEQ))], 0)
    own_pos = pos_rc(half * SH + np.arange(SH))
    ck, sk = rope_tables(32, key_pos)
    t["cosK0"], t["sinK0"] = ck.astype(np.float32), sk.astype(np.float32)
    cq, sq = rope_tables(32, np.concatenate([own_pos, neg], 0))
    cq96 = np.ones((96, SH + CTX)); sq96 = np.zeros((96, SH + CTX))
    cq96[64:] = cq; sq96[64:] = sq
    t["cosQ0"], t["sinQ0"] = cq96.astype(np.float32), sq96.astype(np.float32)
    ck, sk = rope_tables(64, key_pos)
    t["cosK1"] = np.concatenate([ck, ck], 0).astype(np.float32)
    t["sinK1"] = np.concatenate([sk, sk], 0).astype(np.float32)
    cq, sq = rope_tables(64, own_pos)
    t["cosQ1"] = np.concatenate([cq, cq], 0).astype(np.float32)
    t["sinQ1"] = np.concatenate([sq, sq], 0).astype(np.float32)
    n2 = np.arange(64)[:, None]; k2 = np.arange(64)[None, :]
    a = 2 * np.pi * n2 * k2 / 64
    t["F64"] = np.concatenate([np.cos(a), -np.sin(a)], 1).astype(np.float32)
    n1 = np.arange(128)[:, None, None]
    kk = (64 * (64 * half + np.arange(64))[None, None, :] + np.arange(64)[None, :, None])
    a = 2 * np.pi * ((n1 * kk) % SEQ) / SEQ
    Gr, Gi = np.cos(a), -np.sin(a)
    GA = np.concatenate([Gr, Gi], 2); GB = np.concatenate([-Gi, Gr], 2)
    t["GAB"] = np.stack([GA, GB], 2).astype(np.float32)
    ch = np.arange(128)[:, None]; ch2 = np.arange(128)[None, :]
    a = 2 * np.pi * ch * ch2 / 128
    cds = np.stack([np.cos(a), np.sin(a)], 1)
    t["CdSl"] = (cds / math.sqrt(SEQ * 128)).astype(np.float32)
    t["CdSc"] = (cds / math.sqrt(CTX * 128)).astype(np.float32)
    n = (np.arange(2)[None, :, None] * 128 + np.arange(128)[:, None, None])
    k = np.arange(256)[None, None, :]
    a = 2 * np.pi * ((n * k) % 256) / 256
    t["F256"] = np.concatenate([np.cos(a), -np.sin(a)], 2).astype(np.float32)
    t["ident"] = np.eye(128, dtype=np.float32)
    _TABLE_CACHE[half] = t
    return t


class LayerIO:
    pass


def declare_layer(nc, L, pfx, x_from_dram=None):
    io = {}

    def inp(name, shape, dt=F32):
        io[name] = nc.dram_tensor(pfx + name, list(shape), dt, kind="ExternalInput").ap()

    def scr(name, shape, dt):
        kind = "ExternalOutput" if (DEBUG_SCR and name in ("Fdl", "Fdc", "KT", "Vs", "QT", "cat")) else "Internal"
        io[name] = nc.dram_tensor(pfx + name, list(shape), dt, kind=kind).ap()

    if x_from_dram is None:
        inp("xf", [SEQ, D]); inp("xo", [SH, D]); inp("xc", [CTX, D])
    else:
        io.update(x_from_dram)
    inp("cv", [P, 8, 2])
    inp("w_mod", [D, 6 * D]); inp("b_mod", [1, 6 * D])
    inp("w_out", [D, D]); inp("w_gate", [D, FF]); inp("w_up", [D, FF]); inp("w_down", [FF, D])
    inp("ln", [4, D])
    if "ident" not in io:
        inp("ident", [P, P])
    if L == 0:
        inp("w_in", [D, 1056]); inp("w_kr_sw", [D, 32])
        inp("qn", [P, 2]); inp("kvn", [P, 2])
        inp("w_uq", [256, 768]); inp("w_uq_sw", [256, 768]); inp("w_ukv_kn", [256, 512]); inp("w_ukv_v", [256, 512])
        inp("cosK", [32, NK]); inp("sinK", [32, NK]); inp("cosQ", [96, SH + CTX]); inp("sinQ", [96, SH + CTX])
        inp("F64", [64, 128]); inp("GAB", [P, 64, 2, 128]); inp("CdSl", [P, 2, 128]); inp("CdSc", [P, 2, 128])
        inp("F256", [P, 2, 512])
        scr("Fdl", [SEQ, 512], F32); scr("Fdc", [CTX, 512], F32)
        scr("KT", [8, 96, NK], BF16); scr("Vs", [NK, 8, 65], BF16); scr("QT", [8, 96, SH + CTX], BF16)
        scr("cat", [SH + CTX, D], BF16)
    else:
        inp("w_in", [D, 3072]); inp("w_qk_sw", [D, 2048])
        inp("lam", [4, 64]); inp("subln", [1, 128])
        inp("cosK", [P, NK]); inp("sinK", [P, NK]); inp("cosQ", [P, SH]); inp("sinQ", [P, SH])
        scr("KT", [8, 128, NK], BF16); scr("Vs", [NK, 8, 129], BF16); scr("QT", [8, 128, SH], BF16)
        scr("cat", [SH, D], BF16)
    scr("wg_bf", [11, P, 8, 256], BF16); scr("wu_bf", [11, P, 8, 256], BF16); scr("wd_bf", [FF, D], BF16)
    return io


def build_layer(nc, S, L, io, xout, xcout, consts, xbufs, b_out, b_outc):
    need_ctx = (L == 0)
    ident_f, b_idf, ident_b, b_idb, ones_f, b_ones = consts
    bin_ = Buf("ext_in")
    b_wbf = Buf("wbf%d" % L)

    with ExitStack() as lay:
        fcol, b_fcol = sbt(nc, lay, "fcol", [P, 4, 8, 2], F32)
        gbc, b_gbc = sbt(nc, lay, "gbc", [P, 2, 2, D], F32)
        lnbc, b_lnbc = sbt(nc, lay, "lnbc", [P, 4, D], F32)
        S.dma("sp", lnbc[:], io["ln"].partition_broadcast(P), bin_, b_lnbc)

        with ExitStack() as ph:
            cv, b_cv = sbt(nc, ph, "cv", [P, 8, 2], F32)
            sl, b_sl = sbt(nc, ph, "sl", [P, 8, 2], F32)
            rep, b_rep = sbt(nc, ph, "rep", [P, 2, 8, P], F32)
            wm = Ring(nc, ph, "wm", 4, [P, 8, 512], F32)
            bm = Ring(nc, ph, "bm", 4, [1, 512], F32)
            S.dma("sp", cv[:], io["cv"], bin_, b_cv)
            S.op("act", lambda e: e.activation(out=sl[:], in_=cv[:], func=AF.Silu), [b_cv], [b_sl])
            for m in range(2):
                for j in range(8):
                    S.op("dve", lambda e, m=m, j=j: e.tensor_scalar(
                        out=rep[:, m, j, :], in0=ones_f[:, :], scalar1=sl[:, j, m:m + 1], scalar2=None,
                        op0=ALU.mult), [b_sl, b_ones], [b_rep])
            vmap = {1: 0, 0: 1, 4: 2, 3: 3}
            for cb in range(12):
                c6, hf = cb // 2, cb % 2
                wt, bw = wm.next()
                bt, bb = bm.next()
                for jh in range(2):
                    S.dma("sp", wt[:, jh * 4:(jh + 1) * 4, :],
                          io["w_mod"][jh * 512:(jh + 1) * 512, cb * 512:(cb + 1) * 512].rearrange("(j p) n -> p j n", p=P),
                          bin_, bw)
                S.dma("sp", bt[:], io["b_mod"][:, cb * 512:(cb + 1) * 512], bin_, bb)
                if c6 in (2, 5):
                    gi = 0 if c6 == 2 else 1
                    for m in range(2):
                        pb, bp = S.bank()
                        for j in range(8):
                            S.op("pe", lambda e, pb=pb, m=m, j=j, wt=wt: e.matmul(
                                pb[:, :], lhsT=rep[:, m, j, :], rhs=wt[:, j, :], start=(j == 0), stop=False),
                                [b_rep, bw], [bp])
                        S.op("pe", lambda e, pb=pb, bt=bt: e.matmul(
                            pb[:, :], lhsT=ones_f[0:1, :], rhs=bt[0:1, :], start=False, stop=True),
                            [b_ones, bb], [bp])
                        S.op("dve", lambda e, pb=pb, m=m, gi=gi, hf=hf: e.tensor_copy(
                            out=gbc[:, m, gi, hf * 512:(hf + 1) * 512], in_=pb[:, :]), [bp], [b_gbc])
                else:
                    v = vmap[c6]
                    pb, bp = S.bank()
                    for q in range(4):
                        for j in range(8):
                            S.op("pe", lambda e, pb=pb, q=q, j=j, wt=wt: e.matmul(
                                pb[:, 2 * q:2 * q + 2], lhsT=wt[:, j, q * P:(q + 1) * P], rhs=sl[:, j, :],
                                start=(j == 0), stop=False), [b_sl, bw], [bp])
                        S.op("pe", lambda e, pb=pb, q=q, bt=bt: e.matmul(
                            pb[:, 2 * q:2 * q + 2], lhsT=bt[0:1, q * P:(q + 1) * P], rhs=ones_f[0:1, 0:2],
                            start=False, stop=True), [b_ones, bb], [bp])
                    S.op("dve", lambda e, pb=pb, v=v, hf=hf: e.tensor_copy(
                        out=fcol[:, v, hf * 4:hf * 4 + 4, :], in_=pb[:, 0:8].rearrange("p (q m) -> p q m", m=2)),
                        [bp], [b_fcol])
            for v in (0, 2):
                S.op("dve", lambda e, v=v: e.tensor_scalar(
                    out=fcol[:, v, :, :], in0=fcol[:, v, :, :], scalar1=1.0, scalar2=None, op0=ALU.add),
                    [b_fcol], [b_fcol])
            S.end_phase()

        def ln_rows(st_ring, xin, bx, out, bo, reads_extra=(), on_act=True):
            stt, bst = st_ring.next()
            for i in range(2):
                S.op("dve", lambda e, i=i, stt=stt: e.bn_stats(out=stt[:, 6 * i:6 * i + 6],
                                                              in_=xin[:, 512 * i:512 * i + 512]),
                     [bx] + list(reads_extra), [bst])
            S.op("dve", lambda e, stt=stt: e.bn_aggr(out=stt[:, 12:14], in_=stt[:, 0:12]), [bst], [bst])
            S.op("act", lambda e, stt=stt: e.activation(out=stt[:, 14:15], in_=stt[:, 13:14], func=AF.Sqrt,
                                                       bias=EPS, scale=1.0), [bst], [bst])
            S.op("dve", lambda e, stt=stt: e.reciprocal(out=stt[:, 14:15], in_=stt[:, 14:15]), [bst], [bst])
            if on_act:
                S.op("dve", lambda e, stt=stt: e.tensor_scalar(out=stt[:, 15:16], in0=stt[:, 12:13], scalar1=stt[:, 14:15],
                                                              scalar2=-1.0, op0=ALU.mult, op1=ALU.mult), [bst], [bst])
                S.op("act", lambda e, stt=stt: e.activation(out=out, in_=xin, func=AF.Identity, scale=stt[:, 14:15],
                                                           bias=stt[:, 15:16]), [bx, bst], [bo])
            else:
                S.op("dve", lambda e, stt=stt: e.tensor_scalar(out=out, in0=xin, scalar1=stt[:, 12:13],
                                                              scalar2=stt[:, 14:15], op0=ALU.subtract, op1=ALU.mult),
                     [bx, bst], [bo])

        def ln_block(st_ring, xb, bxb, xn_, bxn_, nt):
            stt, bst = st_ring.next()
            for t in range(nt):
                for i in range(2):
                    S.op("dve", lambda e, i=i, t=t, stt=stt: e.bn_stats(out=stt[:, 12 * t + 6 * i:12 * t + 6 * i + 6],
                                                                       in_=xb[:, t, 512 * i:512 * i + 512]), [bxb], [bst])
            for t in range(nt):
                S.op("dve", lambda e, t=t, stt=stt: e.bn_aggr(out=stt[:, 48 + 2 * t:50 + 2 * t], in_=stt[:, 12 * t:12 * t + 12]),
                     [bst], [bst])
            mv = stt[:, 48:56].rearrange("p (t c) -> p t c", c=2)
            S.op("act", lambda e, stt=stt: e.activation(out=stt[:, 56:56 + nt], in_=mv[:, 0:nt, 1], func=AF.Sqrt,
                                                       bias=EPS, scale=1.0), [bst], [bst])
            S.op("dve", lambda e, stt=stt: e.reciprocal(out=stt[:, 56:56 + nt], in_=stt[:, 56:56 + nt]), [bst], [bst])
            S.op("dve", lambda e, stt=stt: e.scalar_tensor_tensor(out=stt[:, 60:60 + nt], in0=mv[:, 0:nt, 0], scalar=-1.0,
                                                                 in1=stt[:, 56:56 + nt], op0=ALU.mult, op1=ALU.mult),
                 [bst], [bst])
            for t in range(nt):
                S.op("act", lambda e, t=t, stt=stt: e.activation(out=xn_[:, t, :], in_=xb[:, t, :], func=AF.Identity,
                                                                scale=stt[:, 56 + t:57 + t], bias=stt[:, 60 + t:61 + t]),
                     [bxb, bst], [bxn_])

        def make_hT(xn, bxn, nt, hT, bhT, vs, vb, m):
            for j in range(8):
                pb, bp = S.bank()
                for t in range(nt):
                    S.op("pe", lambda e, pb=pb, t=t, j=j: e.transpose(
                        out=pb[:, t * P:(t + 1) * P], in_=xn[:, t, j * P:(j + 1) * P], identity=ident_f[:, :]),
                        [bxn, b_idf], [bp])
                if j % 4 != 3:
                    S.op("act", lambda e, pb=pb, j=j: e.activation(
                        out=hT[:, j, 0:nt * P], in_=pb[:, 0:nt * P], func=AF.Identity,
                        scale=fcol[:, vs, j, m:m + 1], bias=fcol[:, vb, j, m:m + 1]), [bp, b_fcol], [bhT])
                else:
                    S.op("dve", lambda e, pb=pb, j=j: e.tensor_scalar(
                        out=hT[:, j, 0:nt * P], in0=pb[:, 0:nt * P], scalar1=fcol[:, vs, j, m:m + 1],
                        scalar2=fcol[:, vb, j, m:m + 1], op0=ALU.mult, op1=ALU.add), [bp, b_fcol], [bhT])

        def rms_T(src_banks, nrm, b_nrm, ntok, sq, b_sq, rb, b_rb, outT, b_out):
            for m2 in range(2):
                pbk, bpk = src_banks[m2]
                S.op("act", lambda e, pbk=pbk, m2=m2: e.activation(out=sq[:, m2, 0:ntok], in_=pbk[:, 0:ntok],
                                                                 func=AF.Square), [bpk], [b_sq])
            pb, bp = S.bank()
            for m2 in range(2):
                S.op("pe", lambda e, pb=pb, m2=m2: e.matmul(pb[:, 0:ntok], lhsT=ones_f[:, :], rhs=sq[:, m2, 0:ntok],
                                                           start=(m2 == 0), stop=(m2 == 1)), [b_sq, b_ones], [bp])
            S.op("act", lambda e, pb=pb: e.activation(out=rb[:, 0:ntok], in_=pb[:, 0:ntok], func=AF.Sqrt,
                                                     bias=EPS, scale=1.0 / 256.0), [bp], [b_rb])
            S.op("dve", lambda e: e.reciprocal(out=rb[:, 0:ntok], in_=rb[:, 0:ntok]), [b_rb], [b_rb])
            for m2 in range(2):
                pbk, bpk = src_banks[m2]
                S.op("dve", lambda e, pbk=pbk, m2=m2: e.scalar_tensor_tensor(
                    out=outT[:, m2, 0:ntok], in0=pbk[:, 0:ntok], scalar=nrm[:, m2:m2 + 1], in1=rb[:, 0:ntok],
                    op0=ALU.mult, op1=ALU.mult), [bpk, b_nrm, b_rb], [b_out])

        def rope_out(pa, bpa, pbk, bpbk, rows, ntok, cs, b_cs, sn, b_sn, t1, b_t1, t2, b_t2, out, b_o):
            S.op("dve", lambda e: e.tensor_tensor(out=t1[0:rows, 0:ntok], in0=pa[0:rows, 0:ntok],
                                                  in1=cs[0:rows, 0:ntok], op=ALU.mult), [bpa, b_cs], [b_t1])
            S.op("dve", lambda e: e.tensor_tensor(out=t2[0:rows, 0:ntok], in0=pbk[0:rows, 0:ntok],
                                                  in1=sn[0:rows, 0:ntok], op=ALU.mult), [bpbk, b_sn], [b_t2])
            S.op("pool", lambda e: e.tensor_tensor(out=out, in0=t1[0:rows, 0:ntok], in1=t2[0:rows, 0:ntok],
                                                   op=ALU.add), [b_t1, b_t2], [b_o])

        kv_blocks = [(io["xc"], 2, 0, 1, xbufs["xc"])] + [(xbufs["xf_blk"](b), 4, CTX + b * 512, 0, xbufs["xf_buf"](b)) for b in range(16)]
        q_blocks = [(xbufs["xo_blk"](b), 4, b * 512, 0, xbufs["xo_buf"](b)) for b in range(8)]
        if need_ctx:
            q_blocks.append((io["xc"], 2, SH, 1, xbufs["xc"]))
        b_KT, b_Vs, b_QT, b_cat = Buf("KT"), Buf("Vs"), Buf("QT"), Buf("cat")
        b_Fdl, b_Fdc = Buf("Fdl"), Buf("Fdc")
        KT, Vs, QT, cat = io["KT"], io["Vs"], io["QT"], io["cat"]
        vw = 65 if L == 0 else 129
        R = 96 if L == 0 else 128

        with ExitStack() as ph:
            xr = Ring(nc, ph, "xr", 1, [P, 4, D], F32)
            xnr = Ring(nc, ph, "xn", 2, [P, 4, D], F32)
            stb_ring = Ring(nc, ph, "stb", 2, [P, 64], F32)
            hTr = Ring(nc, ph, "hT", 2, [P, 8, 512], BF16)
            st_ring = Ring(nc, ph, "st", 4, [P, 16], F32)
            csr = Ring(nc, ph, "csk", 2, [R if L == 1 else 32, 512], F32)
            snr = Ring(nc, ph, "snk", 2, [R if L == 1 else 32, 512], F32)
            t1, b_t1 = sbt(nc, ph, "t1", [P, 512], F32)
            t2, b_t2 = sbt(nc, ph, "t2", [P, 512], F32)
            vsr = Ring(nc, ph, "vsb", 2, [P, 4, 8, vw], BF16)
            for i in range(2):
                S.op("pool", lambda e, i=i: e.memset(vsr.t[i][:], 1.0), [], [vsr.b[i]])
            if L == 0:
                win, b_win = sbt(nc, ph, "win", [P, 8, 1056], BF16)
                wkr, b_wkr = sbt(nc, ph, "wkr", [P, 8, 32], BF16)
                wkn, b_wkn = sbt(nc, ph, "wkn", [P, 2, 512], BF16)
                wv, b_wv = sbt(nc, ph, "wv", [P, 2, 512], BF16)
                kvn, b_kvn = sbt(nc, ph, "kvn", [P, 2], F32)
                S.dma("pool", win[:], io["w_in"].rearrange("(j p) n -> p j n", p=P), bin_, b_win)
                S.dma("pool", wkr[:], io["w_kr_sw"].rearrange("(j p) n -> p j n", p=P), bin_, b_wkr)
                S.dma("pool", wkn[:], io["w_ukv_kn"].rearrange("(j p) n -> p j n", p=P), bin_, b_wkn)
                S.dma("pool", wv[:], io["w_ukv_v"].rearrange("(j p) n -> p j n", p=P), bin_, b_wv)
                S.dma("sp", kvn[:], io["kvn"], bin_, b_kvn)
                fsr = Ring(nc, ph, "fsb", 2, [P, 4, 512], F32)
                sq, b_sq = sbt(nc, ph, "sq", [P, 2, 512], F32)
                rb, b_rb = sbt(nc, ph, "rb", [P, 512], F32)
                ckvn, b_ckvn = sbt(nc, ph, "ckvn", [P, 2, 512], BF16)
                knr = Ring(nc, ph, "knT", 2, [P, 4, 512], BF16)
                krr = Ring(nc, ph, "krT", 2, [32, 512], BF16)
            else:
                win, b_win = sbt(nc, ph, "win", [P, 8, 2048], BF16)
                wsw, b_wsw = sbt(nc, ph, "wsw", [P, 8, 1024], BF16)
                for kvh in range(2):
                    S.dma("pool", win[:, :, kvh * 1024:(kvh + 1) * 1024],
                          io["w_in"][:, 1024 + kvh * 1024:2048 + kvh * 1024].rearrange("(j p) n -> p j n", p=P), bin_, b_win)
                S.dma("pool", wsw[:], io["w_qk_sw"][:, 1024:2048].rearrange("(j p) n -> p j n", p=P), bin_, b_wsw)
                knr = Ring(nc, ph, "knT", 2, [P, 8, 512], BF16)

            def kv_a(src, nt, koff, m, sbuf_src):
                ntok = nt * P
                xb, bxb = xr.next()
                S.dma("sp", xb[:, 0:nt, :], src.rearrange("(t p) d -> p t d", p=P), sbuf_src, bxb)
                cs, b_cs = csr.next()
                sn, b_sn = snr.next()
                S.dma("sp", cs[:, 0:ntok], io["cosK"][:, koff:koff + ntok], bin_, b_cs)
                S.dma("sp", sn[:, 0:ntok], io["sinK"][:, koff:koff + ntok], bin_, b_sn)
                xn_, bxn_ = xnr.next()
                ln_block(stb_ring, xb, bxb, xn_, bxn_, nt)
                return dict(nt=nt, koff=koff, m=m, cs=cs, b_cs=b_cs, sn=sn, b_sn=b_sn, xn=xn_, bxn=bxn_)

            def kv_b(st):
                hT, bhT = hTr.next()
                make_hT(st["xn"], st["bxn"], st["nt"], hT, bhT, 0, 1, st["m"])
                st["hT"], st["bhT"] = hT, bhT

            def kv_body(st):
                nt, koff, m = st["nt"], st["koff"], st["m"]
                cs, b_cs, sn, b_sn, hT, bhT = st["cs"], st["b_cs"], st["sn"], st["b_sn"], st["hT"], st["bhT"]
                ntok = nt * P
                vsb, bvs = vsr.next()
                if L == 0:
                    fsb, bfs = fsr.next()
                    for t in range(nt):
                        pb, bp = S.bank()
                        for j in range(8):
                            S.op("pe", lambda e, pb=pb, t=t, j=j, hT=hT: e.matmul(
                                pb[:, :], lhsT=hT[:, j, t * P:(t + 1) * P], rhs=win[:, j, 0:512],
                                start=(j == 0), stop=(j == 7)), [bhT, b_win], [bp])
                        S.op("act", lambda e, pb=pb, t=t, fsb=fsb: e.copy(out=fsb[:, t, :], in_=pb[:, :]), [bp], [bfs])
                    if m == 1:
                        S.dma("pool", io["Fdc"].rearrange("(t p) c -> p t c", p=P), fsb[:, 0:nt, :], bfs, b_Fdc)
                    else:
                        n0 = koff - CTX
                        S.dma("pool", io["Fdl"][n0:n0 + ntok, :].rearrange("(t p) c -> p t c", p=P), fsb[:, 0:nt, :],
                              bfs, b_Fdl)
                    banks = []
                    for m2 in range(2):
                        pb, bp = S.bank()
                        banks.append((pb, bp))
                        for j in range(8):
                            S.op("pe", lambda e, pb=pb, m2=m2, j=j, hT=hT: e.matmul(
                                pb[:, 0:ntok], lhsT=win[:, j, 768 + m2 * P:768 + (m2 + 1) * P], rhs=hT[:, j, 0:ntok],
                                start=(j == 0), stop=(j == 7)), [bhT, b_win], [bp])
                    rms_T(banks, kvn, b_kvn, ntok, sq, b_sq, rb, b_rb, ckvn, b_ckvn)
                    knT, bkn = knr.next()
                    for hp in range(4):
                        pb, bp = S.bank()
                        for m2 in range(2):
                            S.op("pe", lambda e, pb=pb, hp=hp, m2=m2: e.matmul(
                                pb[:, 0:ntok],
                                lhsT=wkn[:, m2, hp * P:(hp + 1) * P],
                                rhs=ckvn[:, m2, 0:ntok], start=(m2 == 0), stop=(m2 == 1)), [b_ckvn, b_wkn], [bp])
                        S.op("act", lambda e, pb=pb, hp=hp, knT=knT: e.copy(out=knT[:, hp, 0:ntok], in_=pb[:, 0:ntok]),
                             [bp], [bkn])
                    for hh in range(2):
                        S.dma("pool", KT.rearrange("(hp hh) r n -> hh r hp n", hh=2)[hh, 0:64, :, koff:koff + ntok],
                              knT[hh * 64:(hh + 1) * 64, :, 0:ntok], bkn, b_KT)
                    for t in range(nt):
                        pb, bp = S.bank()
                        for m2 in range(2):
                            S.op("pe", lambda e, pb=pb, t=t, m2=m2: e.matmul(
                                pb[:, :], lhsT=ckvn[:, m2, t * P:(t + 1) * P],
                                rhs=wv[:, m2, :],
                                start=(m2 == 0), stop=(m2 == 1)), [b_ckvn, b_wv], [bp])
                        S.op("dve", lambda e, pb=pb, t=t, vsb=vsb: e.tensor_copy(
                            out=vsb[:, t, :, 0:64], in_=pb[:, :].rearrange("p (h c) -> p h c", c=64)), [bp], [bvs])
                    pa, bpa = S.bank()
                    pbk, bpbk = S.bank()
                    for j in range(8):
                        S.op("pe", lambda e, pa=pa, j=j, hT=hT: e.matmul(
                            pa[0:32, 0:ntok], lhsT=win[:, j, 1024:1056], rhs=hT[:, j, 0:ntok],
                            start=(j == 0), stop=(j == 7)), [bhT, b_win], [bpa])
                    for j in range(8):
                        S.op("pe", lambda e, pbk=pbk, j=j, hT=hT: e.matmul(
                            pbk[0:32, 0:ntok], lhsT=wkr[:, j, :], rhs=hT[:, j, 0:ntok],
                            start=(j == 0), stop=(j == 7)), [bhT, b_wkr], [bpbk])
                    krT, bkr = krr.next()
                    rope_out(pa, bpa, pbk, bpbk, 32, ntok, cs, b_cs, sn, b_sn, t1, b_t1, t2, b_t2,
                             krT[0:32, 0:ntok], bkr)
                    for h in range(8):
                        S.dma("pool", KT[h, 64:96, koff:koff + ntok], krT[0:32, 0:ntok], bkr, b_KT)
                else:
                    knT, bkn = knr.next()
                    for h in range(8):
                        pa, bpa = S.bank()
                        pbk, bpbk = S.bank()
                        for j in range(8):
                            S.op("pe", lambda e, pa=pa, j=j, h=h, hT=hT: e.matmul(
                                pa[:, 0:ntok], lhsT=win[:, j, h * P:(h + 1) * P], rhs=hT[:, j, 0:ntok],
                                start=(j == 0), stop=(j == 7)), [bhT, b_win], [bpa])
                        for j in range(8):
                            S.op("pe", lambda e, pbk=pbk, j=j, h=h, hT=hT: e.matmul(
                                pbk[:, 0:ntok], lhsT=wsw[:, j, h * P:(h + 1) * P], rhs=hT[:, j, 0:ntok],
                                start=(j == 0), stop=(j == 7)), [bhT, b_wsw], [bpbk])
                        rope_out(pa, bpa, pbk, bpbk, P, ntok, cs, b_cs, sn, b_sn, t1, b_t1, t2, b_t2,
                                 knT[:, h, 0:ntok], bkn)
                    S.dma("pool", KT[:, :, koff:koff + ntok].rearrange("h r n -> r h n"), knT[:, :, 0:ntok], bkn, b_KT)
                    for t in range(nt):
                        for hf in range(2):
                            pb, bp = S.bank()
                            for j in range(8):
                                S.op("pe", lambda e, pb=pb, t=t, j=j, hf=hf, hT=hT: e.matmul(
                                    pb[:, :], lhsT=hT[:, j, t * P:(t + 1) * P],
                                    rhs=win[:, j, 1024 + hf * 512:1024 + (hf + 1) * 512],
                                    start=(j == 0), stop=(j == 7)), [bhT, b_win], [bp])
                            S.op("dve", lambda e, pb=pb, t=t, hf=hf, vsb=vsb: e.tensor_copy(
                                out=vsb[:, t, hf * 4:(hf + 1) * 4, 0:128], in_=pb[:, :].rearrange("p (h c) -> p h c", c=128)),
                                [bp], [bvs])
                S.dma("pool", Vs[koff:koff + ntok].rearrange("(t p) h c -> p t (h c)", p=P),
                      vsb[:, 0:nt].rearrange("p t h c -> p t (h c)"), bvs, b_Vs)
            sts = [None] * len(kv_blocks)
            sts[0] = kv_a(*kv_blocks[0])
            kv_b(sts[0])
            for i in range(len(kv_blocks)):
                if i + 1 < len(kv_blocks):
                    sts[i + 1] = kv_a(*kv_blocks[i + 1])
                kv_body(sts[i])
                if i + 1 < len(kv_blocks):
                    kv_b(sts[i + 1])
            S.end_phase()

        with ExitStack() as ph:
            xr = Ring(nc, ph, "xr", 1, [P, 4, D], F32)
            xnr = Ring(nc, ph, "xn", 2, [P, 4, D], F32)
            stb_ring = Ring(nc, ph, "stb", 2, [P, 64], F32)
            hTr = Ring(nc, ph, "hT", 2, [P, 8, 512], BF16)
            st_ring = Ring(nc, ph, "st", 4, [P, 16], F32)
            csr = Ring(nc, ph, "csq", 2, [R, 512], F32)
            snr = Ring(nc, ph, "snq", 2, [R, 512], F32)
            t1r = Ring(nc, ph, "t1", 2, [P, 512], F32)
            t2r = Ring(nc, ph, "t2", 2, [P, 512], F32)
            qsr = Ring(nc, ph, "qsb", 2, [R, 8, 512], BF16)
            if L == 0:
                win, b_win = sbt(nc, ph, "win", [P, 8, 256], BF16)
                wuq, b_wuq = sbt(nc, ph, "wuq", [P, 2, 768], BF16)
                wuqs, b_wuqs = sbt(nc, ph, "wuqs", [P, 2, 768], BF16)
                qn, b_qn = sbt(nc, ph, "qn", [P, 2], F32)
                S.dma("pool", win[:], io["w_in"][:, 512:768].rearrange("(j p) n -> p j n", p=P), bin_, b_win)
                S.dma("pool", wuq[:], io["w_uq"].rearrange("(j p) n -> p j n", p=P), bin_, b_wuq)
                S.dma("pool", wuqs[:], io["w_uq_sw"].rearrange("(j p) n -> p j n", p=P), bin_, b_wuqs)
                S.dma("sp", qn[:], io["qn"], bin_, b_qn)
                sq, b_sq = sbt(nc, ph, "sq", [P, 2, 512], F32)
                rb, b_rb = sbt(nc, ph, "rb", [P, 512], F32)
                cqn, b_cqn = sbt(nc, ph, "cqn", [P, 2, 512], BF16)
            else:
                win, b_win = sbt(nc, ph, "win", [P, 8, 1024], BF16)
                wsw, b_wsw = sbt(nc, ph, "wsw", [P, 8, 1024], BF16)
                S.dma("pool", win[:], io["w_in"][:, 0:1024].rearrange("(j p) n -> p j n", p=P), bin_, b_win)
                S.dma("pool", wsw[:], io["w_qk_sw"][:, 0:1024].rearrange("(j p) n -> p j n", p=P), bin_, b_wsw)
            def q_a(src, nt, qoff, m, sbuf_src):
                ntok = nt * P
                xb, bxb = xr.next()
                S.dma("sp", xb[:, 0:nt, :], src.rearrange("(t p) d -> p t d", p=P), sbuf_src, bxb)
                cs, b_cs = csr.next()
                sn, b_sn = snr.next()
                S.dma("sp", cs[:, 0:ntok], io["cosQ"][:, qoff:qoff + ntok], bin_, b_cs)
                S.dma("sp", sn[:, 0:ntok], io["sinQ"][:, qoff:qoff + ntok], bin_, b_sn)
                xn_, bxn_ = xnr.next()
                ln_block(stb_ring, xb, bxb, xn_, bxn_, nt)
                return dict(nt=nt, qoff=qoff, m=m, cs=cs, b_cs=b_cs, sn=sn, b_sn=b_sn, xn=xn_, bxn=bxn_)

            def q_b(st):
                hT, bhT = hTr.next()
                make_hT(st["xn"], st["bxn"], st["nt"], hT, bhT, 0, 1, st["m"])
                st["hT"], st["bhT"] = hT, bhT

            def q_body(st):
                nt, qoff, m = st["nt"], st["qoff"], st["m"]
                cs, b_cs, sn, b_sn, hT, bhT = st["cs"], st["b_cs"], st["sn"], st["b_sn"], st["hT"], st["bhT"]
                ntok = nt * P
                qsb, bqs = qsr.next()
                if L == 0:
                    banks = []
                    for m2 in range(2):
                        pb, bp = S.bank()
                        banks.append((pb, bp))
                        for j in range(8):
                            S.op("pe", lambda e, pb=pb, m2=m2, j=j, hT=hT: e.matmul(
                                pb[:, 0:ntok], lhsT=win[:, j, m2 * P:(m2 + 1) * P], rhs=hT[:, j, 0:ntok],
                                start=(j == 0), stop=(j == 7)), [bhT, b_win], [bp])
                    rms_T(banks, qn, b_qn, ntok, sq, b_sq, rb, b_rb, cqn, b_cqn)
                for h in range(8):
                    pa, bpa = S.bank()
                    pbk, bpbk = S.bank()
                    if L == 0:
                        for m2 in range(2):
                            S.op("pe", lambda e, pa=pa, m2=m2, h=h: e.matmul(
                                pa[0:96, 0:ntok], lhsT=wuq[:, m2, h * 96:(h + 1) * 96], rhs=cqn[:, m2, 0:ntok],
                                start=(m2 == 0), stop=(m2 == 1)), [b_cqn, b_wuq], [bpa])
                        for m2 in range(2):
                            S.op("pe", lambda e, pbk=pbk, m2=m2, h=h: e.matmul(
                                pbk[0:96, 0:ntok], lhsT=wuqs[:, m2, h * 96:(h + 1) * 96], rhs=cqn[:, m2, 0:ntok],
                                start=(m2 == 0), stop=(m2 == 1)), [b_cqn, b_wuqs], [bpbk])
                    else:
                        for j in range(8):
                            S.op("pe", lambda e, pa=pa, j=j, h=h, hT=hT: e.matmul(
                                pa[:, 0:ntok], lhsT=win[:, j, h * P:(h + 1) * P], rhs=hT[:, j, 0:ntok],
                                start=(j == 0), stop=(j == 7)), [bhT, b_win], [bpa])
                        for j in range(8):
                            S.op("pe", lambda e, pbk=pbk, j=j, h=h, hT=hT: e.matmul(
                                pbk[:, 0:ntok], lhsT=wsw[:, j, h * P:(h + 1) * P], rhs=hT[:, j, 0:ntok],
                                start=(j == 0), stop=(j == 7)), [bhT, b_wsw], [bpbk])
                    t1, b_t1 = t1r.next()
                    t2, b_t2 = t2r.next()
                    rope_out(pa, bpa, pbk, bpbk, R, ntok, cs, b_cs, sn, b_sn, t1, b_t1, t2, b_t2,
                             qsb[0:R, h, 0:ntok], bqs)
                S.dma("pool", QT[:, :, qoff:qoff + ntok].rearrange("h r n -> r h n"), qsb[0:R, :, 0:ntok], bqs, b_QT)
            sts = [None] * len(q_blocks)
            sts[0] = q_a(*q_blocks[0])
            q_b(sts[0])
            for i in range(len(q_blocks)):
                if i + 1 < len(q_blocks):
                    sts[i + 1] = q_a(*q_blocks[i + 1])
                q_body(sts[i])
                if i + 1 < len(q_blocks):
                    q_b(sts[i + 1])
            S.end_phase()

        if L == 0:
            with ExitStack() as ph:
                f64, b_f64 = sbt(nc, ph, "f64", [64, 128], F32)
                cdl, b_cdl = sbt(nc, ph, "cdl", [P, 2, 128], F32)
                cdc, b_cdc = sbt(nc, ph, "cdc", [P, 2, 128], F32)
                f256, b_f256 = sbt(nc, ph, "f256", [P, 2, 512], F32)
                S.dma("sp", f64[:], io["F64"], bin_, b_f64)
                S.dma("sp", cdl[:], io["CdSl"], bin_, b_cdl)
                S.dma("sp", cdc[:], io["CdSc"], bin_, b_cdc)
                S.dma("sp", f256[:], io["F256"], bin_, b_f256)
                Xr = Ring(nc, ph, "Xh", 2, [64, 128, 32], F32)
                T, b_T = sbt(nc, ph, "T", [P, 128, 128], F32)
                ZT, b_ZT = sbt(nc, ph, "ZT", [P, 2, 64, 64], F32)
                Gr_ = Ring(nc, ph, "G", 2, [P, 8, 2, 128], F32)
                ysr = Ring(nc, ph, "ysb", 2, [P, 4, 128], BF16)
                Fd3 = io["Fdl"].rearrange("(a b) c -> a b c", b=128)
                ZTf = ZT[:, :, :, :].rearrange("p r i q -> p r (i q)")
                for g in range(4):
                    for xq in range(4):
                        X, bX = Xr.next()
                        c0 = g * 128 + xq * 32
                        S.dma("sp", X[:], Fd3[:, :, c0:c0 + 32], b_Fdl, bX)
                        for cg in range(8):
                            pb, bp = S.bank()
                            for cc in range(4):
                                c = cg * 4 + cc
                                S.op("pe", lambda e, pb=pb, cc=cc, c=c, X=X: e.matmul(
                                    pb[:, cc * 128:(cc + 1) * 128], lhsT=X[:, :, c], rhs=f64[:, :],
                                    start=True, stop=True), [bX, b_f64], [bp])
                            ch0 = xq * 32 + cg * 4
                            if cg % 2 == 0:
                                S.op("act", lambda e, pb=pb, ch0=ch0: e.copy(
                                    out=T[:, ch0:ch0 + 4, :], in_=pb[:, :].rearrange("p (c k) -> p c k", k=128)),
                                    [bp], [b_T])
                            else:
                                S.op("dve", lambda e, pb=pb, ch0=ch0: e.tensor_copy(
                                    out=T[:, ch0:ch0 + 4, :], in_=pb[:, :].rearrange("p (c k) -> p c k", k=128)),
                                    [bp], [b_T])
                    for kc in range(8):
                        G, bG = Gr_.next()
                        S.dma("sp", G[:], io["GAB"][:, kc * 8:(kc + 1) * 8], bin_, bG)
                        for kq in range(2):
                            pb, bp = S.bank()
                            for q in range(4):
                                kl = kq * 4 + q
                                k2 = kc * 8 + kl
                                S.op("pe", lambda e, pb=pb, q=q, kl=kl, k2=k2, G=G: e.matmul(
                                    pb[:, q * 128:(q + 1) * 128], lhsT=T[:, :, k2],
                                    rhs=G[:, kl, 0, :], start=True, stop=False), [b_T, bG], [bp])
                                S.op("pe", lambda e, pb=pb, q=q, kl=kl, k2=k2, G=G: e.matmul(
                                    pb[:, q * 128:(q + 1) * 128], lhsT=T[:, :, 64 + k2],
                                    rhs=G[:, kl, 1, :], start=False, stop=True), [b_T, bG], [bp])
                            k20 = kc * 8 + kq * 4
                            S.op("dve", lambda e, pb=pb, k20=k20: e.tensor_copy(
                                out=ZT[:, :, :, k20:k20 + 4].rearrange("p r i q -> p q r i"),
                                in_=pb[:, :].rearrange("p (q r i) -> p q r i", q=4, r=2)),
                                [bp], [b_ZT])
                    for tq in range(8):
                        pb, bp = S.bank()
                        for q in range(4):
                            tt = tq * 4 + q
                            for r in range(2):
                                S.op("pe", lambda e, pb=pb, q=q, tt=tt, r=r: e.matmul(
                                    pb[:, q * 128:(q + 1) * 128], lhsT=ZTf[:, r, tt * 128:(tt + 1) * 128],
                                    rhs=cdl[:, r, :], start=(r == 0), stop=(r == 1)), [b_ZT, b_cdl], [bp])
                        ysb, bys = ysr.next()
                        S.op("act", lambda e, pb=pb, ysb=ysb: e.copy(
                            out=ysb[:, :, :], in_=pb[:, :].rearrange("p (q c) -> p q c", c=128)), [bp], [bys])
                        S.dma("pool", cat[tq * 512:(tq + 1) * 512, g * 128:(g + 1) * 128].rearrange("(t p) c -> p t c", p=P),
                              ysb[:, :, :], bys, b_cat)
                fcx, b_fcx = sbt(nc, ph, "fcx", [P, 2, 512], F32)
                zc, b_zc = sbt(nc, ph, "zc", [P, 512], F32)
                S.dma("sp", fcx[:], io["Fdc"].rearrange("(t p) c -> p t c", p=P), b_Fdc, b_fcx)
                for g in range(4):
                    pb, bp = S.bank()
                    for tl in range(2):
                        S.op("pe", lambda e, pb=pb, tl=tl, g=g: e.matmul(
                            pb[:, :], lhsT=fcx[:, tl, g * 128:(g + 1) * 128], rhs=f256[:, tl, :],
                            start=(tl == 0), stop=(tl == 1)), [b_fcx, b_f256], [bp])
                    S.op("dve", lambda e, pb=pb: e.tensor_copy(out=zc[:, :], in_=pb[:, :]), [bp], [b_zc])
                    pb2, bp2 = S.bank()
                    for tl in range(2):
                        for r in range(2):
                            S.op("pe", lambda e, pb2=pb2, tl=tl, r=r: e.matmul(
                                pb2[:, tl * 128:(tl + 1) * 128], lhsT=zc[:, r * 256 + tl * 128:r * 256 + (tl + 1) * 128],
                                rhs=cdc[:, r, :], start=(r == 0), stop=(r == 1)), [b_zc, b_cdc], [bp2])
                    ysb, bys = ysr.next()
                    S.op("act", lambda e, pb2=pb2, ysb=ysb: e.copy(
                        out=ysb[:, 0:2, :], in_=pb2[:, 0:256].rearrange("p (q c) -> p q c", c=128)), [bp2], [bys])
                    S.dma("pool", cat[SH:SH + CTX, g * 128:(g + 1) * 128].rearrange("(t p) c -> p t c", p=P),
                          ysb[:, 0:2, :], bys, b_cat)
                S.end_phase()

        with ExitStack() as ph:
            ktr = Ring(nc, ph, "kts", 2, [R, NK], BF16)
            vr = Ring(nc, ph, "vs", 2, [P, 66, vw], BF16)
            if L == 0:
                qtr = Ring(nc, ph, "qts", 2, [R, SH + CTX], BF16)
            else:
                qtr = Ring(nc, ph, "qts", 2, [P, 2, SH], BF16)
                for i in range(2):
                    S.op("pool", lambda e, i=i: e.memset(qtr.t[i][:], 0.0), [], [qtr.b[i]])
            ptr = Ring(nc, ph, "pt", 4, [P, 1024], BF16)
            rcr = Ring(nc, ph, "rc", 4, [P, 8], F32)
            asr = Ring(nc, ph, "asb", 2, [P, 4, 128 if L == 1 else 64], BF16)
            if L == 1:
                o0r = Ring(nc, ph, "o0", 2, [P, 4, 128], F32)
                o1r = Ring(nc, ph, "o1", 2, [P, 128], F32)
                sqr = Ring(nc, ph, "sqr", 2, [P, 128], F32)
                lamt, b_lam = sbt(nc, ph, "lamt", [P, 4, 64], F32)
                lamw, b_lamw = sbt(nc, ph, "lamw", [P, 8], F32)
                lamj, b_lamj = sbt(nc, ph, "lamj", [P, 64], F32)
                subl, b_subl = sbt(nc, ph, "subl", [P, 128], F32)
                epsc, b_epsc = sbt(nc, ph, "epsc", [P, 1], F32)
                S.op("dve", lambda e: e.memset(epsc[:], EPS), [], [b_epsc])
                S.dma("sp", lamt[:], io["lam"].partition_broadcast(P), bin_, b_lam)
                S.dma("sp", subl[:], io["subln"].partition_broadcast(P), bin_, b_subl)
                for i in range(2):
                    S.op("dve", lambda e, i=i: e.scalar_tensor_tensor(
                        out=lamj[:, :], in0=lamt[:, 2 * i, :], scalar=1.0, in1=lamt[:, 2 * i + 1, :],
                        op0=ALU.mult, op1=ALU.mult, accum_out=lamw[:, i:i + 1]), [b_lam], [b_lamj, b_lamw])
                S.op("act", lambda e: e.activation(out=lamw[:, 2:4], in_=lamw[:, 0:2], func=AF.Exp), [b_lamw], [b_lamw])
                S.op("dve", lambda e: e.tensor_tensor(out=lamw[:, 4:5], in0=lamw[:, 3:4], in1=lamw[:, 2:3],
                                                      op=ALU.subtract), [b_lamw], [b_lamw])
                S.op("dve", lambda e: e.tensor_scalar(out=lamw[:, 5:6], in0=lamw[:, 4:5], scalar1=-LAMBDA_INIT1,
                                                      scalar2=None, op0=ALU.add), [b_lamw], [b_lamw])
                S.op("dve", lambda e: e.tensor_scalar(out=subl[:, :], in0=subl[:, :], scalar1=1.0 - LAMBDA_INIT1,
                                                      scalar2=None, op0=ALU.mult), [b_subl], [b_subl])
            casts = []
            for c in range(11):
                casts.append(lambda c=c: S.dma("pool", io["wg_bf"][c],
                                               io["w_gate"][:, c * 256:(c + 1) * 256].rearrange("(j p) n -> p j n", p=P),
                                               bin_, b_wbf))
                casts.append(lambda c=c: S.dma("pool", io["wu_bf"][c],
                                               io["w_up"][:, c * 256:(c + 1) * 256].rearrange("(j p) n -> p j n", p=P),
                                               bin_, b_wbf))
            for r4 in range(4):
                casts.append(lambda r4=r4: S.dma("pool", io["wd_bf"][r4 * 704:(r4 + 1) * 704, :],
                                                 io["w_down"][r4 * 704:(r4 + 1) * 704, :], bin_, b_wbf,
                                                 max_dma_last_dim=4096))
            scale = MLA_SCALE if L == 0 else DIFF_SCALE
            nq_tot = SH + (CTX if need_ctx else 0)
            qblocks = [(b * 512, 512, 66) for b in range(8)]
            if need_ctx:
                qblocks.append((SH, 256, 2))
            nmaps = 1 if L == 0 else 2
            accr = Ring(nc, ph, "accs", 4, [P, 4, 132], F32)
            rc8r = Ring(nc, ph, "rc8", 4, [P, 16], F32)
            o1br = Ring(nc, ph, "o1b", 2, [P, 4, 128], F32)
            items = []
            for h in range(8):
                for qi in range(len(qblocks)):
                    for ci in range(nmaps):
                        for kp in range(qblocks[qi][2] // 2):
                            items.append((h, qi, ci, kp))
            heads = {}

            def load_head(h):
                kts, bkt = ktr.next()
                vs_, bv = vr.next()
                qts, bqt = qtr.next()
                S.dma("sp", kts[:, :], KT[h], b_KT, bkt)
                S.dma("sp", vs_[:, :, :], Vs[:, h, :].rearrange("(t p) c -> p t c", p=P), b_Vs, bv)
                if L == 0:
                    S.dma("sp", qts[:, 0:nq_tot], QT[h, :, 0:nq_tot], b_QT, bqt)
                else:
                    S.dma("sp", qts[0:64, 0, :], QT[h, 0:64, :], b_QT, bqt)
                    S.dma("sp", qts[64:128, 1, :], QT[h, 64:128, :], b_QT, bqt)
                heads[h] = (kts, bkt, vs_, bv, qts, bqt)

            qstate = {}

            def finalize(h, qi, ci):
                q0, nq, nkt = qblocks[qi]
                nj = nq // P
                stq = qstate.setdefault((h, qi), {})
                acs, bacs = accr.next()
                for j in range(nj):
                    S.op("dve", lambda e, j=j, acs=acs: e.tensor_copy(out=acs[:, j, 0:vw], in_=S.pb[4 + j][:, 0:vw]),
                         [S.bpb[4 + j]], [bacs])
                if ci == 0:
                    rc, brc = rc8r.next()
                    stq["rc"], stq["brc"] = rc, brc
                    stq["a0"], stq["ba0"] = acs, bacs
                else:
                    rc, brc = stq["rc"], stq["brc"]
                S.op("dve", lambda e, acs=acs, rc=rc, ci=ci: e.reciprocal(
                    out=rc[:, 4 * ci:4 * ci + nj], in_=acs[:, 0:nj, vw - 1]), [bacs], [brc])
                if L == 0:
                    asb, bas = asr.next()
                    for j in range(nj):
                        S.op("dve", lambda e, j=j, acs=acs, rc=rc, asb=asb: e.tensor_scalar(
                            out=asb[:, j, :], in0=acs[:, j, 0:64], scalar1=rc[:, j:j + 1], scalar2=None, op0=ALU.mult),
                            [bacs, brc], [bas])
                    S.dma("pool", cat[q0:q0 + nq, 512 + h * 64:512 + (h + 1) * 64].rearrange("(t p) c -> p t c", p=P),
                          asb[:, 0:nj, :], bas, b_cat)
                    if casts:
                        casts.pop(0)()
                elif ci == 1:
                    a0, ba0 = stq["a0"], stq["ba0"]
                    asb, bas = asr.next()
                    o1, b_o1 = o1br.next()
                    sqt, bsqt = sqr.next()
                    S.op("dve", lambda e, rc=rc: e.tensor_scalar(out=rc[:, 4:8], in0=rc[:, 4:8], scalar1=lamw[:, 5:6],
                                                                 scalar2=None, op0=ALU.mult), [brc, b_lamw], [brc])
                    for j in range(nj):
                        S.op("dve", lambda e, j=j, a0=a0, rc=rc, o1=o1: e.tensor_scalar(
                            out=o1[:, j, :], in0=a0[:, j, 0:128], scalar1=rc[:, j:j + 1], scalar2=None, op0=ALU.mult),
                            [ba0, brc], [b_o1])
                        S.op("dve", lambda e, j=j, acs=acs, rc=rc, o1=o1: e.scalar_tensor_tensor(
                            out=o1[:, j, :], in0=acs[:, j, 0:128], scalar=rc[:, 4 + j:5 + j], in1=o1[:, j, :],
                            op0=ALU.mult, op1=ALU.add), [bacs, brc, b_o1], [b_o1])
                        S.op("dve", lambda e, j=j, o1=o1, rc=rc, sqt=sqt: e.scalar_tensor_tensor(
                            out=sqt[:, :], in0=o1[:, j, :], scalar=1.0, in1=o1[:, j, :],
                            op0=ALU.mult, op1=ALU.mult, accum_out=rc[:, 8 + j:9 + j]), [b_o1], [brc, bsqt])
                    S.op("act", lambda e, rc=rc: e.activation(out=rc[:, 12:16], in_=rc[:, 8:12], func=AF.Ln,
                                                             bias=epsc[:, 0:1], scale=1.0 / 128.0), [brc, b_epsc], [brc])
                    S.op("act", lambda e, rc=rc: e.activation(out=rc[:, 12:16], in_=rc[:, 12:16], func=AF.Exp,
                                                             scale=-0.5), [brc], [brc])
                    for j in range(nj):
                        S.op("dve", lambda e, j=j, o1=o1, rc=rc, asb=asb: e.scalar_tensor_tensor(
                            out=asb[:, j, :], in0=o1[:, j, :], scalar=rc[:, 12 + j:13 + j], in1=subl[:, :],
                            op0=ALU.mult, op1=ALU.mult), [b_o1, brc, b_subl], [bas])
                    S.dma("pool", cat[q0:q0 + nq, h * 128:(h + 1) * 128].rearrange("(t p) c -> p t c", p=P),
                          asb[:, 0:nj, :], bas, b_cat)
                    if casts:
                        casts.pop(0)()

            DEPTH = 2
            pts = {}
            load_head(0)
            spair = [(S.pbig[:, 0:1024], [S.bpb[0], S.bpb[1]]), (S.pbig[:, 1024:2048], [S.bpb[2], S.bpb[3]])]
            for n in range(len(items) + DEPTH):
                if n < len(items):
                    h, qi, ci, kp = items[n]
                    q0, nq, nkt = qblocks[qi]
                    kts, bkt, vs_, bv, qts, bqt = heads[h]
                    sp2, bsp2 = spair[n % 2]
                    for u in range(2):
                        kt = 2 * kp + u
                        if L == 0:
                            S.op("pe", lambda e, sp2=sp2, u=u, kt=kt, kts=kts, qts=qts, q0=q0, nq=nq: e.matmul(
                                sp2[:, u * 512:u * 512 + nq], lhsT=kts[0:96, kt * P:(kt + 1) * P], rhs=qts[0:96, q0:q0 + nq],
                                start=True, stop=True), [bkt, bqt], [bsp2[u]])
                        else:
                            S.op("pe", lambda e, sp2=sp2, u=u, kt=kt, kts=kts, qts=qts, ci=ci, q0=q0, nq=nq: e.matmul(
                                sp2[:, u * 512:u * 512 + nq], lhsT=kts[:, kt * P:(kt + 1) * P], rhs=qts[:, ci, q0:q0 + nq],
                                start=True, stop=True), [bkt, bqt], [bsp2[u]])
                    pt, bpt = ptr.next()
                    S.op("act", lambda e, sp2=sp2, pt=pt, nq=nq: e.activation(
                        out=pt[:, :].rearrange("p (u n) -> p u n", u=2)[:, :, 0:nq],
                        in_=sp2.rearrange("p (u n) -> p u n", u=2)[:, :, 0:nq], func=AF.Exp, scale=scale), bsp2, [bpt])
                    pts[n] = (pt, bpt)
                if n >= DEPTH:
                    h, qi, ci, kp = items[n - DEPTH]
                    q0, nq, nkt = qblocks[qi]
                    if qi == 0 and ci == 0 and kp == 0 and h + 1 < 8:
                        load_head(h + 1)
                    kts, bkt, vs_, bv, qts, bqt = heads[h]
                    pt, bpt = pts.pop(n - DEPTH)
                    for u in range(2):
                        kt = 2 * kp + u
                        for j in range(nq // P):
                            S.op("pe", lambda e, j=j, u=u, pt=pt, kt=kt, vs_=vs_, nkt=nkt: e.matmul(
                                S.pb[4 + j][:, 0:vw], lhsT=pt[:, u * 512 + j * P:u * 512 + (j + 1) * P], rhs=vs_[:, kt, :],
                                start=(kt == 0), stop=(kt == nkt - 1)), [bpt, bv], [S.bpb[4 + j]])
                    if 2 * kp + 1 == nkt - 1:
                        finalize(h, qi, ci)
            while casts:
                casts.pop(0)()
            S.end_phase()

        with ExitStack() as ph:
            wout, b_wout = sbt(nc, ph, "wout", [P, 8, D], BF16)
            S.dma("pool", wout[:], io["w_out"].rearrange("(j p) n -> p j n", p=P), bin_, b_wout)
            xr = Ring(nc, ph, "xr", 2, [P, D], F32)
            ctr = Ring(nc, ph, "ct", 2, [P, D], BF16)
            cTr = Ring(nc, ph, "cT", 1, [P, 8, P], BF16)
            rr = Ring(nc, ph, "rr", 2, [P, D], F32)
            x1r = Ring(nc, ph, "x1", 2, [P, 4, D], F32)
            xn2r = Ring(nc, ph, "xn2", 1, [P, 1, D], F32)
            h2r = Ring(nc, ph, "h2T", 2, [P, 8, 512], BF16)
            actT, b_actT = sbt(nc, ph, "actT", [P, NF, 512], BF16)
            gur = Ring(nc, ph, "gu", 3, [P, 2, 8, 256], BF16)
            wdr = Ring(nc, ph, "wd", 3, [P, 2, 512], BF16)
            sgr = Ring(nc, ph, "sg", 2, [P, 512], F32)
            ygr = Ring(nc, ph, "yg", 2, [P, 512], F32)
            outr = Ring(nc, ph, "ot", 2, [P, D], F32)
            st_ring = Ring(nc, ph, "st", 4, [P, 16], F32)
            o_blocks = [(xbufs["xo_blk"](b), 4, b * 512, 0, xout(b), xbufs["xo_buf"](b), b_out(b)) for b in range(8)]
            if need_ctx:
                o_blocks.append((io["xc"], 2, SH, 1, xcout, xbufs["xc"], b_outc))

            def ln_affine(tmp, btmp, dst, bdst, g_row, b_row):
                ln_rows(st_ring, tmp, btmp, tmp, btmp)
                S.op("dve", lambda e: e.tensor_tensor(out=tmp, in0=tmp, in1=lnbc[:, g_row, :], op=ALU.mult),
                     [btmp, b_lnbc], [btmp])
                S.op("pool", lambda e: e.tensor_tensor(out=dst, in0=tmp, in1=lnbc[:, b_row, :], op=ALU.add),
                     [btmp, b_lnbc], [bdst])

            brange = [4, 8]

            def stage_a(src, nt, coff, m, dst, sbuf_src, bdst_out):
                x1s, b_x1s = x1r.next()
                h2T, b_h2T = h2r.next()
                st = dict(nt=nt, m=m, dst=dst, bdst_out=bdst_out, x1=x1s, b_x1=b_x1s, h2T=h2T, b_h2T=b_h2T)
                yield st
                for t in range(nt):
                    xt, bxt = xr.next()
                    ct, bct = ctr.next()
                    S.dma("sp", xt[:, :], src[t * P:(t + 1) * P, :], sbuf_src, bxt)
                    S.dma("sp", ct[:, :], cat[coff + t * P:coff + (t + 1) * P, :], b_cat, bct)
                    pb, bp = S.bank(*brange)
                    pbv = pb[:, :].bitcast(BF16)
                    for j in range(8):
                        S.op("pe", lambda e, pbv=pbv, j=j, ct=ct: e.transpose(
                            out=pbv[:, j * P:(j + 1) * P], in_=ct[:, j * P:(j + 1) * P], identity=ident_b[:, :]),
                            [bct, b_idb], [bp])
                    cT, bcT = cTr.next()
                    S.op("act", lambda e, pbv=pbv, cT=cT: e.copy(out=cT[:, :, :], in_=pbv.rearrange("p (j c) -> p j c", c=P)),
                         [bp], [bcT])
                    yield None
                    tmp, btmp = rr.next()
                    for hf in range(2):
                        pb, bp = S.bank(*brange)
                        for j in range(8):
                            S.op("pe", lambda e, pb=pb, j=j, hf=hf, cT=cT: e.matmul(
                                pb[:, :], lhsT=cT[:, j, :], rhs=wout[:, j, hf * 512:(hf + 1) * 512],
                                start=(j == 0), stop=(j == 7)), [bcT, b_wout], [bp])
                        S.op("dve", lambda e, pb=pb, hf=hf, tmp=tmp: e.tensor_tensor(
                            out=tmp[:, hf * 512:(hf + 1) * 512], in0=pb[:, :], in1=gbc[:, m, 0, hf * 512:(hf + 1) * 512],
                            op=ALU.mult), [bp, b_gbc], [btmp])
                    yield None
                    S.op("dve", lambda e, tmp=tmp, xt=xt: e.scalar_tensor_tensor(
                        out=tmp[:, :], in0=xt[:, :], scalar=ALPHA, in1=tmp[:, :], op0=ALU.mult, op1=ALU.add),
                        [bxt, btmp], [btmp])
                    ln_affine(tmp[:, :], btmp, x1s[:, t, :], b_x1s, 0, 1)
                    yield None
                    xn2, bxn2 = xn2r.next()
                    ln_rows(st_ring, x1s[:, t, :], b_x1s, xn2[:, 0, :], bxn2)
                    for jh in range(2):
                        pb, bp = S.bank(*brange)
                        for jj in range(4):
                            j = jh * 4 + jj
                            S.op("pe", lambda e, pb=pb, jj=jj, j=j, xn2=xn2: e.transpose(
                                out=pb[:, jj * P:(jj + 1) * P], in_=xn2[:, 0, j * P:(j + 1) * P], identity=ident_f[:, :]),
                                [bxn2, b_idf], [bp])
                        for jj in range(4):
                            j = jh * 4 + jj
                            if jj % 2 == 0:
                                S.op("act", lambda e, pb=pb, jj=jj, j=j, t=t: e.activation(
                                    out=h2T[:, j, t * P:(t + 1) * P], in_=pb[:, jj * P:(jj + 1) * P], func=AF.Identity,
                                    scale=fcol[:, 2, j, m:m + 1], bias=fcol[:, 3, j, m:m + 1]), [bp, b_fcol], [b_h2T])
                            else:
                                S.op("dve", lambda e, pb=pb, jj=jj, j=j, t=t: e.tensor_scalar(
                                    out=h2T[:, j, t * P:(t + 1) * P], in0=pb[:, jj * P:(jj + 1) * P],
                                    scalar1=fcol[:, 2, j, m:m + 1], scalar2=fcol[:, 3, j, m:m + 1],
                                    op0=ALU.mult, op1=ALU.add), [bp, b_fcol], [b_h2T])
                    yield None

            def stage_c(st):
                nt, x1s, b_x1s = st["nt"], st["x1"], st["b_x1"]
                for t in range(nt):
                    ot, bot = outr.next()
                    ln_affine(x1s[:, t, :], b_x1s, ot[:, :], bot, 2, 3)
                    S.dma("pool", st["dst"][t * P:(t + 1) * P, :], ot[:, :], bot, st["bdst_out"])
                    yield None

            def step(bg, n=1):
                for _ in range(n):
                    if bg is None:
                        return
                    try:
                        next(bg)
                    except StopIteration:
                        return

            def stage_b(st, bg):
                nt, m, x1s, b_x1s, h2T, b_h2T = st["nt"], st["m"], st["x1"], st["b_x1"], st["h2T"], st["b_h2T"]
                ntok = nt * P
                brange[0] = 0
                for c in range(11):
                    gu, bgu = gur.next()
                    S.dma("sp", gu[:, 0, :, :], io["wg_bf"][c], b_wbf, bgu)
                    S.dma("sp", gu[:, 1, :, :], io["wu_bf"][c], b_wbf, bgu)
                    for fc in range(2):
                        f = c * 2 + fc
                        pg, bpg = S.bank(*brange)
                        pu, bpu = S.bank(*brange)
                        for j in range(8):
                            S.op("pe", lambda e, pg=pg, j=j, fc=fc, gu=gu: e.matmul(
                                pg[:, 0:ntok], lhsT=gu[:, 0, j, fc * P:(fc + 1) * P], rhs=h2T[:, j, 0:ntok],
                                start=(j == 0), stop=(j == 7)), [bgu, b_h2T], [bpg])
                        for j in range(8):
                            S.op("pe", lambda e, pu=pu, j=j, fc=fc, gu=gu: e.matmul(
                                pu[:, 0:ntok], lhsT=gu[:, 1, j, fc * P:(fc + 1) * P], rhs=h2T[:, j, 0:ntok],
                                start=(j == 0), stop=(j == 7)), [bgu, b_h2T], [bpu])
                        sg, bsg = sgr.next()
                        S.op("act", lambda e, pg=pg, sg=sg: e.activation(out=sg[:, 0:ntok], in_=pg[:, 0:ntok], func=AF.Silu),
                             [bpg], [bsg])
                        S.op("dve", lambda e, pu=pu, sg=sg, f=f: e.tensor_tensor(
                            out=actT[:, f, 0:ntok], in0=pu[:, 0:ntok], in1=sg[:, 0:ntok], op=ALU.mult),
                            [bpu, bsg], [b_actT])
                    step(bg)
                brange[0] = 4
                for hf in range(2):
                    for c in range(11):
                        wd, bwd = wdr.next()
                        S.dma("sp", wd[:, :, :], io["wd_bf"][c * 256:(c + 1) * 256, hf * 512:(hf + 1) * 512]
                              .rearrange("(f p) n -> p f n", p=P), b_wbf, bwd)
                        for fc in range(2):
                            f = c * 2 + fc
                            for t in range(nt):
                                S.op("pe", lambda e, f=f, fc=fc, t=t, wd=wd: e.matmul(
                                    S.pb[t][:, :], lhsT=actT[:, f, t * P:(t + 1) * P], rhs=wd[:, fc, :],
                                    start=(f == 0), stop=(f == NF - 1)), [b_actT, bwd], [S.bpb[t]])
                        step(bg)
                    for t in range(nt):
                        yg, byg = ygr.next()
                        S.op("dve", lambda e, t=t, yg=yg, hf=hf: e.tensor_tensor(
                            out=yg[:, :], in0=S.pb[t][:, :], in1=gbc[:, m, 1, hf * 512:(hf + 1) * 512], op=ALU.mult),
                            [S.bpb[t], b_gbc], [byg])
                        S.op("dve", lambda e, t=t, yg=yg, hf=hf: e.scalar_tensor_tensor(
                            out=x1s[:, t, hf * 512:(hf + 1) * 512], in0=x1s[:, t, hf * 512:(hf + 1) * 512], scalar=ALPHA,
                            in1=yg[:, :], op0=ALU.mult, op1=ALU.add), [b_x1s, byg], [b_x1s])

            def chain(*gens):
                for g_ in gens:
                    if g_ is not None:
                        for _ in g_:
                            yield None

            nb = len(o_blocks)
            ga = stage_a(*o_blocks[0])
            st_cur = next(ga)
            for _ in ga:
                pass
            prev_c = None
            for i in range(nb):
                if i + 1 < nb:
                    ga = stage_a(*o_blocks[i + 1])
                    st_next = next(ga)
                else:
                    ga, st_next = None, None
                bg = chain(prev_c, ga)
                stage_b(st_cur, bg)
                for _ in bg:
                    pass
                prev_c = stage_c(st_cur)
                st_cur = st_next
            for _ in prev_c:
                pass
            S.end_phase()


def build_program():
    nc = bass.Bass("TRN2", target_bir_lowering=False)
    io0 = declare_layer(nc, 0, "l0_")
    x1own = [nc.dram_tensor("x1own%d" % c, [512, D], F32, kind="Internal").ap() for c in range(8)]
    x1gat = [nc.dram_tensor("x1gat%d" % c, [1024, D], F32, kind="Internal").ap() for c in range(8)]
    xc1 = nc.dram_tensor("xc1", [CTX, D], F32, kind="Internal").ap()
    io1 = declare_layer(nc, 1, "l1_", x_from_dram={"xc": xc1, "ident": io0["ident"]})
    xout = nc.dram_tensor("xout", [SH, D], F32, kind="ExternalOutput").ap()
    with ExitStack() as es:
        S = Sched(nc, es)
        ident_f = es.enter_context(nc.sbuf_tensor("ident_f", [P, P], F32)); b_idf = Buf("idf")
        ident_b = es.enter_context(nc.sbuf_tensor("ident_b", [P, P], BF16)); b_idb = Buf("idb")
        ones_f = es.enter_context(nc.sbuf_tensor("ones_f", [P, P], F32)); b_ones = Buf("ones")
        bin_ = Buf("cin")
        S.dma("sp", ident_f[:], io0["ident"], bin_, b_idf)
        S.op("dve", lambda e: e.tensor_copy(out=ident_b[:], in_=ident_f[:]), [b_idf], [b_idb])
        S.op("dve", lambda e: e.memset(ones_f[:], 1.0), [], [b_ones])
        consts = (ident_f, b_idf, ident_b, b_idb, ones_f, b_ones)
        b_xc1, b_xout = Buf("xc1"), Buf("xout")
        b_own = [Buf("x1own%d" % c) for c in range(8)]
        b_gat = [Buf("x1gat%d" % c) for c in range(8)]
        xb0 = {"xc": bin_,
               "xf_blk": lambda b: io0["xf"][b * 512:(b + 1) * 512, :], "xf_buf": lambda b: bin_,
               "xo_blk": lambda b: io0["xo"][b * 512:(b + 1) * 512, :], "xo_buf": lambda b: bin_}
        build_layer(nc, S, 0, io0, lambda b: x1own[b], xc1, consts, xb0, lambda b: b_own[b], b_xc1)
        for c in range(8):
            if NO_COLL:
                S.dma("pool", x1gat[c][0:512], x1own[c], b_own[c], b_gat[c])
                S.dma("pool", x1gat[c][512:1024], x1own[c], b_own[c], b_gat[c])
            else:
                S.coll("AllGather", [[0, 1], [2, 3], [4, 5], [6, 7]], x1own[c], x1gat[c], b_own[c], b_gat[c])
        xb1 = {"xc": b_xc1,
               "xf_blk": lambda b: x1gat[b % 8][(b // 8) * 512:(b // 8 + 1) * 512, :], "xf_buf": lambda b: b_gat[b % 8],
               "xo_blk": lambda b: x1own[b], "xo_buf": lambda b: b_own[b]}
        build_layer(nc, S, 1, io1, lambda b: xout[b * 512:(b + 1) * 512, :], None, consts, xb1, lambda b: b_xout, None)
    return nc


L0_NAMES = ["l0_w_mod", "l0_b_mod", "l0_w_in", "l0_q_norm", "l0_w_uq", "l0_kv_norm", "l0_w_ukv", "l0_w_out",
            "l0_ln1_g", "l0_ln1_b", "l0_w_gate", "l0_w_up", "l0_w_down", "l0_ln2_g", "l0_ln2_b"]
L1_NAMES = ["l1_w_mod", "l1_b_mod", "l1_w_in", "l1_lambda_q1", "l1_lambda_k1", "l1_lambda_q2", "l1_lambda_k2",
            "l1_subln", "l1_w_out", "l1_ln1_g", "l1_ln1_b", "l1_w_gate", "l1_w_up", "l1_w_down", "l1_ln2_g", "l1_ln2_b"]


def layer_inputs(L, inputs, b, half):
    pre = "l%d_" % L
    names = L0_NAMES if L == 0 else L1_NAMES
    w = {n[3:]: np.ascontiguousarray(np.asarray(inputs[n], dtype=np.float32)) for n in names}
    g = lambda n: w[n]
    t = get_tables(half)
    m = {}
    c = np.asarray(inputs["c"], np.float32)[b]
    cc = np.asarray(inputs["c_ctx"], np.float32)
    m["cv"] = np.ascontiguousarray(np.stack([c.reshape(8, P).T, cc.reshape(8, P).T], 2))
    m["w_mod"] = g("w_mod"); m["b_mod"] = g("b_mod").reshape(1, -1)
    m["w_out"] = g("w_out"); m["w_gate"] = g("w_gate"); m["w_up"] = g("w_up"); m["w_down"] = g("w_down")
    m["ln"] = np.ascontiguousarray(np.stack([g("ln1_g"), g("ln1_b"), g("ln2_g"), g("ln2_b")], 0))
    w_in = g("w_in")
    if L == 0:
        m["ident"] = t["ident"]
        m["w_in"] = w_in
        sw = rope_swap_index(32)
        m["w_kr_sw"] = np.ascontiguousarray(w_in[:, 1024 + sw])
        m["qn"] = np.ascontiguousarray(g("q_norm").reshape(2, P).T)
        m["kvn"] = np.ascontiguousarray(g("kv_norm").reshape(2, P).T)
        w_uq = g("w_uq")
        perm = np.arange(768)
        for h in range(8):
            perm[h * 96 + 64:h * 96 + 96] = h * 96 + 64 + sw
        m["w_uq"] = w_uq
        m["w_uq_sw"] = np.ascontiguousarray(w_uq[:, perm])
        w_ukv = g("w_ukv").reshape(256, 8, 128)
        m["w_ukv_kn"] = np.ascontiguousarray(w_ukv[:, :, 0:64].reshape(256, 512))
        m["w_ukv_v"] = np.ascontiguousarray(w_ukv[:, :, 64:128].reshape(256, 512))
        m["cosK"], m["sinK"], m["cosQ"], m["sinQ"] = t["cosK0"], t["sinK0"], t["cosQ0"], t["sinQ0"]
        for k in ("F64", "GAB", "CdSl", "CdSc", "F256"):
            m[k] = t[k]
    else:
        m["w_in"] = w_in
        sw = rope_swap_index(64)
        perm = np.arange(2048)
        for blk in range(32):
            perm[blk * 64:(blk + 1) * 64] = blk * 64 + sw
        m["w_qk_sw"] = np.ascontiguousarray(w_in[:, perm])
        m["lam"] = np.ascontiguousarray(np.stack([g("lambda_q1"), g("lambda_k1"), g("lambda_q2"), g("lambda_k2")], 0))
        m["subln"] = g("subln").reshape(1, 128)
        m["cosK"], m["sinK"], m["cosQ"], m["sinQ"] = t["cosK1"], t["sinK1"], t["cosQ1"], t["sinQ1"]
    return {pre + k: v for k, v in m.items()}


_PROG = {}


def kernel(**inputs):
    x = np.asarray(inputs["x"], np.float32)
    xc = np.asarray(inputs["ctx"], np.float32)
    if "nc" not in _PROG:
        _PROG["nc"] = build_program()
    nc = _PROG["nc"]
    in_maps = []
    for k in range(8):
        b, half = k // 2, k % 2
        m = {}
        m.update(layer_inputs(0, inputs, b, half))
        m.update(layer_inputs(1, inputs, b, half))
        m["l0_xf"] = np.ascontiguousarray(x[b])
        m["l0_xo"] = np.ascontiguousarray(x[b, half * SH:(half + 1) * SH])
        m["l0_xc"] = np.ascontiguousarray(xc[b])
        in_maps.append(m)
    res = run_bass_kernel_spmd(nc, in_maps, core_ids=list(range(8)))
    r = res.results
    out = np.stack([np.concatenate([r[2 * b]["xout"], r[2 * b + 1]["xout"]], 0) for b in range(4)], 0)
    return out.astype(np.float32)
```

```python
import math
import numpy as np
from contextlib import ExitStack
import concourse.bass as bass
import concourse.mybir as mybir
from concourse.bass_utils import run_bass_kernel_spmd

F32 = mybir.dt.float32
BF16 = mybir.dt.bfloat16
AF = mybir.ActivationFunctionType
ALU = mybir.AluOpType

P = 128
D = 1024
SEQ = 8192
SH = 4096
CTX = 256
NK = SEQ + CTX
FF = 2816
NF = 22
ALPHA = 4.0 ** 0.25
EPS = 1e-6
MLA_SCALE = 96.0 ** -0.5
DIFF_SCALE = 64.0 ** -0.5
LAMBDA_INIT1 = 0.8 - 0.6 * math.exp(-0.3)

ENGS = ["pe", "act", "dve", "pool", "sp"]
DEBUG_SCR = False
NO_COLL = False


class Buf:
    __slots__ = ("name", "w", "r", "dsem", "dcnt")

    def __init__(self, name):
        self.name = name
        self.w = None
        self.r = []
        self.dsem = None
        self.dcnt = 0


class Sched:
    def __init__(self, nc, es):
        self.nc = nc
        self.es = es
        self.ops = {e: [] for e in ENGS}
        self.sem = {e: es.enter_context(nc.semaphore("s_" + e)) for e in ENGS}
        self.cnt = {e: 0 for e in ENGS}
        self.seen = {e: {} for e in ENGS}
        self.sem_pool = {"sp": [], "pool": [], "act": []}
        self.phase_bufs = []
        self.nsem = 0
        self.pbig = es.enter_context(nc.psum_tensor("pbig", [P, 4096], F32))
        self.pb = [self.pbig[:, i * 512:(i + 1) * 512] for i in range(8)]
        self.bpb = [Buf("pb%d" % i) for i in range(8)]
        self.pbi = 0

    def bank(self, lo=0, hi=8):
        i = self.pbi
        if i < lo or i >= hi:
            i = lo
        self.pbi = i + 1
        return self.pb[i], self.bpb[i]

    def _need(self, e, ev, waits):
        if ev is None:
            return
        sem, val = ev
        if sem is self.sem[e] and e in ("pe", "sp"):
            return
        k = id(sem)
        if self.seen[e].get(k, 0) >= val:
            return
        self.seen[e][k] = val
        waits.append((sem, val))

    def _deps(self, e, reads, writes):
        waits = []
        for b in reads:
            self._need(e, b.w, waits)
        for b in writes:
            self._need(e, b.w, waits)
            for ev in b.r:
                self._need(e, ev, waits)
        return waits

    def op(self, e, fn, reads=(), writes=()):
        waits = self._deps(e, reads, writes)
        self.cnt[e] += 1
        ev = (self.sem[e], self.cnt[e])
        self.ops[e].append((waits, fn, (self.sem[e], 1)))
        for b in reads:
            b.r.append(ev)
        for b in writes:
            b.w = ev
            b.r = []
        return ev

    def dma(self, q, out_ap, in_ap, src, dst, **kw):
        srcs = src if isinstance(src, (list, tuple)) else [src]
        dsts = dst if isinstance(dst, (list, tuple)) else [dst]
        waits = self._deps(q, srcs, dsts)
        d0 = dsts[0]
        if d0.dsem is None:
            if self.sem_pool[q]:
                d0.dsem, d0.dcnt = self.sem_pool[q].pop()
            else:
                d0.dsem = self.es.enter_context(self.nc.semaphore("d%d" % self.nsem))
                self.nsem += 1
                d0.dcnt = 0
            self.phase_bufs.append((d0, q))
        else:
            assert any(b is d0 and qq == q for b, qq in self.phase_bufs), "buffer %s written by DMAs of two queues" % d0.name
        d0.dcnt += 16
        ev = (d0.dsem, d0.dcnt)
        self.ops[q].append(
            (waits, lambda eng: eng.dma_start(out=out_ap, in_=in_ap, **kw), (d0.dsem, 16)))
        for b in srcs:
            b.r.append(ev)
        for b in dsts:
            b.w = ev
            b.r = []
        return ev

    def coll(self, kind, groups, in_ap, out_ap, src, dst):
        waits = self._deps("pool", [src], [dst])
        if not hasattr(self, "cc_sem"):
            self.cc_sem = self.es.enter_context(self.nc.semaphore("cc_sem"))
            self.cc_cnt = 0
        self.cc_cnt += 1
        ev = (self.cc_sem, self.cc_cnt)
        sem = self.cc_sem
        self.ops["pool"].append(
            (waits, lambda eng: eng.collective_compute(kind, ALU.bypass, replica_groups=groups,
                                                       ins=[in_ap.opt()], outs=[out_ap.opt()]), (sem, 1)))
        src.r.append(ev)
        dst.w = ev
        dst.r = []
        return ev

    def end_phase(self):
        for b, _q in self.phase_bufs:
            waits = []
            self._need("sp", (b.dsem, b.dcnt), waits)
            if waits:
                self.ops["sp"].append((waits, None, None))
        nc = self.nc
        ops = self.ops

        def replay(e, eng):
            for waits, fn, inc in ops[e]:
                for sem, val in waits:
                    eng.wait_ge(sem, val)
                if fn is not None:
                    fn(eng).then_inc(inc[0], inc[1])

        with nc.Block() as block:
            @block.tensor
            def _(eng):
                replay("pe", eng)

            @block.scalar
            def _(eng):
                replay("act", eng)

            @block.vector
            def _(eng):
                replay("dve", eng)

            @block.gpsimd
            def _(eng):
                replay("pool", eng)

            @block.sync
            def _(eng):
                replay("sp", eng)
        self.ops = {e: [] for e in ENGS}
        for b, q in self.phase_bufs:
            self.sem_pool[q].append((b.dsem, b.dcnt))
            b.dsem = None
        self.phase_bufs = []


_UID = [0]


def _uname(name):
    _UID[0] += 1
    return "sb%d_%s" % (_UID[0], name)


class Ring:
    def __init__(self, nc, ph, name, n, shape, dt):
        self.t = [ph.enter_context(nc.sbuf_tensor(_uname("%s%d" % (name, i)), list(shape), dt)) for i in range(n)]
        self.b = [Buf("%s%d" % (name, i)) for i in range(n)]
        self.i = 0

    def next(self):
        i = self.i
        self.i = (i + 1) % len(self.t)
        return self.t[i], self.b[i]


def sbt(nc, ph, name, shape, dt):
    return ph.enter_context(nc.sbuf_tensor(_uname(name), list(shape), dt)), Buf(name)


def rope_tables(dim, positions_rc):
    q = dim // 4
    inv = 1.0 / (10000.0 ** (np.arange(q, dtype=np.float64) / q))
    n = positions_rc.shape[0]
    cos = np.ones((dim, n), np.float64)
    sin = np.zeros((dim, n), np.float64)
    valid = positions_rc[:, 0] >= 0
    for d in range(dim):
        a = d // (2 * q)
        w = d % (2 * q)
        fi = w % q
        first = w < q
        ang = positions_rc[:, a].astype(np.float64) * inv[fi]
        cos[d] = np.where(valid, np.cos(ang), 1.0)
        s = np.sin(ang)
        sin[d] = np.where(valid, -s if first else s, 0.0)
    return cos, sin


def rope_swap_index(dim):
    q = dim // 4
    idx = np.zeros(dim, np.int64)
    for d in range(dim):
        w = d % (2 * q)
        idx[d] = d + q if w < q else d - q
    return idx


def pos_rc(tokens):
    tokens = np.asarray(tokens)
    return np.stack([tokens // 64, tokens % 64], 1)


_TABLE_CACHE = {}


def get_tables(half):
    if half in _TABLE_CACHE:
        return _TABLE_CACHE[half]
    t = {}
    neg = -np.ones((CTX, 2), np.int64)
    key_pos = np.concatenate([neg, pos_rc(np.arange(SEQ))], 0)
    own_pos = pos_rc(half * SH + np.arange(SH))
    ck, sk = rope_tables(32, key_pos)
    t["cosK0"], t["sinK0"] = ck.astype(np.float32), sk.astype(np.float32)
    cq, sq = rope_tables(32, np.concatenate([own_pos, neg], 0))
    cq96 = np.ones((96, SH + CTX)); sq96 = np.zeros((96, SH + CTX))
    cq96[64:] = cq; sq96[64:] = sq
    t["cosQ0"], t["sinQ0"] = cq96.astype(np.float32), sq96.astype(np.float32)
    ck, sk = rope_tables(64, key_pos)
    t["cosK1"] = np.concatenate([ck, ck], 0).astype(np.float32)
    t["sinK1"] = np.concatenate([sk, sk], 0).astype(np.float32)
    cq, sq = rope_tables(64, own_pos)
    t["cosQ1"] = np.concatenate([cq, cq], 0).astype(np.float32)
    t["sinQ1"] = np.concatenate([sq, sq], 0).astype(np.float32)
    n2 = np.arange(64)[:, None]; k2 = np.arange(64)[None, :]
    a = 2 * np.pi * n2 * k2 / 64
    t["F64"] = np.concatenate([np.cos(a), -np.sin(a)], 1).astype(np.float32)
    n1 = np.arange(128)[:, None, None]
    kk = (64 * (64 * half + np.arange(64))[None, None, :] + np.arange(64)[None, :, None])
    a = 2 * np.pi * ((n1 * kk) % SEQ) / SEQ
    Gr, Gi = np.cos(a), -np.sin(a)
    GA = np.concatenate([Gr, Gi], 2); GB = np.concatenate([-Gi, Gr], 2)
    t["GAB"] = np.stack([GA, GB], 2).astype(np.float32)
    ch = np.arange(128)[:, None]; ch2 = np.arange(128)[None, :]
    a = 2 * np.pi * ch * ch2 / 128
    cds = np.stack([np.cos(a), np.sin(a)], 1)
    t["CdSl"] = (cds / math.sqrt(SEQ * 128)).astype(np.float32)
    t["CdSc"] = (cds / math.sqrt(CTX * 128)).astype(np.float32)
    n = (np.arange(2)[None, :, None] * 128 + np.arange(128)[:, None, None])
    k = np.arange(256)[None, None, :]
    a = 2 * np.pi * ((n * k) % 256) / 256
    t["F256"] = np.concatenate([np.cos(a), -np.sin(a)], 2).astype(np.float32)
    t["ident"] = np.eye(128, dtype=np.float32)
    _TABLE_CACHE[half] = t
    return t


class LayerIO:
    pass


def declare_layer(nc, L, pfx, x_from_dram=None):
    io = {}

    def inp(name, shape, dt=F32):
        io[name] = nc.dram_tensor(pfx + name, list(shape), dt, kind="ExternalInput").ap()

    def scr(name, shape, dt):
        kind = "ExternalOutput" if (DEBUG_SCR and name in ("Fdl", "Fdc", "KT", "Vs", "QT", "cat")) else "Internal"
        io[name] = nc.dram_tensor(pfx + name, list(shape), dt, kind=kind).ap()

    if x_from_dram is None:
        inp("xf", [SEQ, D]); inp("xo", [SH, D]); inp("xc", [CTX, D])
    else:
        io.update(x_from_dram)
    inp("cv", [P, 8, 2])
    inp("w_mod", [D, 6 * D]); inp("b_mod", [1, 6 * D])
    inp("w_out", [D, D]); inp("w_gate", [D, FF]); inp("w_up", [D, FF]); inp("w_down", [FF, D])
    inp("ln", [4, D])
    if "ident" not in io:
        inp("ident", [P, P])
    if L == 0:
        inp("w_in", [D, 1056]); inp("w_kr_sw", [D, 32])
        inp("qn", [P, 2]); inp("kvn", [P, 2])
        inp("w_uq", [256, 768]); inp("w_uq_sw", [256, 768]); inp("w_ukv_kn", [256, 512]); inp("w_ukv_v", [256, 512])
        inp("cosK", [32, NK]); inp("sinK", [32, NK]); inp("cosQ", [96, SH + CTX]); inp("sinQ", [96, SH + CTX])
        inp("F64", [64, 128]); inp("GAB", [P, 64, 2, 128]); inp("CdSl", [P, 2, 128]); inp("CdSc", [P, 2, 128])
        inp("F256", [P, 2, 512])
        scr("Fdl", [SEQ, 512], F32); scr("Fdc", [CTX, 512], F32)
        scr("KT", [8, 96, NK], BF16); scr("Vs", [NK, 8, 65], BF16); scr("QT", [8, 96, SH + CTX], BF16)
        scr("cat", [SH + CTX, D], BF16)
    else:
        inp("w_in", [D, 3072]); inp("w_qk_sw", [D, 2048])
        inp("lam", [4, 64]); inp("subln", [1, 128])
        inp("cosK", [P, NK]); inp("sinK", [P, NK]); inp("cosQ", [P, SH]); inp("sinQ", [P, SH])
        scr("KT", [8, 128, NK], BF16); scr("Vs", [NK, 8, 129], BF16); scr("QT", [8, 128, SH], BF16)
        scr("cat", [SH, D], BF16)
    scr("wg_bf", [11, P, 8, 256], BF16); scr("wu_bf", [11, P, 8, 256], BF16); scr("wd_bf", [FF, D], BF16)
    return io


def build_layer(nc, S, L, io, xout, xcout, consts, xbufs, b_out, b_outc):
    need_ctx = (L == 0)
    ident_f, b_idf, ident_b, b_idb, ones_f, b_ones = consts
    bin_ = Buf("ext_in")
    b_wbf = Buf("wbf%d" % L)

    with ExitStack() as lay:
        fcol, b_fcol = sbt(nc, lay, "fcol", [P, 4, 8, 2], F32)
        gbc, b_gbc = sbt(nc, lay, "gbc", [P, 2, 2, D], F32)
        lnbc, b_lnbc = sbt(nc, lay, "lnbc", [P, 4, D], F32)
        S.dma("sp", lnbc[:], io["ln"].partition_broadcast(P), bin_, b_lnbc)

        with ExitStack() as ph:
            cv, b_cv = sbt(nc, ph, "cv", [P, 8, 2], F32)
            sl, b_sl = sbt(nc, ph, "sl", [P, 8, 2], F32)
            rep, b_rep = sbt(nc, ph, "rep", [P, 2, 8, P], F32)
            wm = Ring(nc, ph, "wm", 4, [P, 8, 512], F32)
            bm = Ring(nc, ph, "bm", 4, [1, 512], F32)
            S.dma("sp", cv[:], io["cv"], bin_, b_cv)
            S.op("act", lambda e: e.activation(out=sl[:], in_=cv[:], func=AF.Silu), [b_cv], [b_sl])
            for m in range(2):
                for j in range(8):
                    S.op("dve", lambda e, m=m, j=j: e.tensor_scalar(
                        out=rep[:, m, j, :], in0=ones_f[:, :], scalar1=sl[:, j, m:m + 1], scalar2=None,
                        op0=ALU.mult), [b_sl, b_ones], [b_rep])
            vmap = {1: 0, 0: 1, 4: 2, 3: 3}
            for cb in range(12):
                c6, hf = cb // 2, cb % 2
                wt, bw = wm.next()
                bt, bb = bm.next()
                for jh in range(2):
                    S.dma("sp", wt[:, jh * 4:(jh + 1) * 4, :],
                          io["w_mod"][jh * 512:(jh + 1) * 512, cb * 512:(cb + 1) * 512].rearrange("(j p) n -> p j n", p=P),
                          bin_, bw)
                S.dma("sp", bt[:], io["b_mod"][:, cb * 512:(cb + 1) * 512], bin_, bb)
                if c6 in (2, 5):
                    gi = 0 if c6 == 2 else 1
                    for m in range(2):
                        pb, bp = S.bank()
                        for j in range(8):
                            S.op("pe", lambda e, pb=pb, m=m, j=j, wt=wt: e.matmul(
                                pb[:, :], lhsT=rep[:, m, j, :], rhs=wt[:, j, :], start=(j == 0), stop=False),
                                [b_rep, bw], [bp])
                        S.op("pe", lambda e, pb=pb, bt=bt: e.matmul(
                            pb[:, :], lhsT=ones_f[0:1, :], rhs=bt[0:1, :], start=False, stop=True),
                            [b_ones, bb], [bp])
                        S.op("dve", lambda e, pb=pb, m=m, gi=gi, hf=hf: e.tensor_copy(
                            out=gbc[:, m, gi, hf * 512:(hf + 1) * 512], in_=pb[:, :]), [bp], [b_gbc])
                else:
                    v = vmap[c6]
                    pb, bp = S.bank()
                    for q in range(4):
                        for j in range(8):
                            S.op("pe", lambda e, pb=pb, q=q, j=j, wt=wt: e.matmul(
                                pb[:, 2 * q:2 * q + 2], lhsT=wt[:, j, q * P:(q + 1) * P], rhs=sl[:, j, :],
                                start=(j == 0), stop=False), [b_sl, bw], [bp])
                        S.op("pe", lambda e, pb=pb, q=q, bt=bt: e.matmul(
                            pb[:, 2 * q:2 * q + 2], lhsT=bt[0:1, q * P:(q + 1) * P], rhs=ones_f[0:1, 0:2],
                            start=False, stop=True), [b_ones, bb], [bp])
                    S.op("dve", lambda e, pb=pb, v=v, hf=hf: e.tensor_copy(
                        out=fcol[:, v, hf * 4:hf * 4 + 4, :], in_=pb[:, 0:8].rearrange("p (q m) -> p q m", m=2)),
                        [bp], [b_fcol])
            for v in (0, 2):
                S.op("dve", lambda e, v=v: e.tensor_scalar(
                    out=fcol[:, v, :, :], in0=fcol[:, v, :, :], scalar1=1.0, scalar2=None, op0=ALU.add),
                    [b_fcol], [b_fcol])
            S.end_phase()

        def ln_rows(st_ring, xin, bx, out, bo, reads_extra=(), on_act=True):
            stt, bst = st_ring.next()
            for i in range(2):
                S.op("dve", lambda e, i=i, stt=stt: e.bn_stats(out=stt[:, 6 * i:6 * i + 6],
                                                              in_=xin[:, 512 * i:512 * i + 512]),
                     [bx] + list(reads_extra), [bst])
            S.op("dve", lambda e, stt=stt: e.bn_aggr(out=stt[:, 12:14], in_=stt[:, 0:12]), [bst], [bst])
            S.op("act", lambda e, stt=stt: e.activation(out=stt[:, 14:15], in_=stt[:, 13:14], func=AF.Sqrt,
                                                       bias=EPS, scale=1.0), [bst], [bst])
            S.op("dve", lambda e, stt=stt: e.reciprocal(out=stt[:, 14:15], in_=stt[:, 14:15]), [bst], [bst])
            if on_act:
                S.op("dve", lambda e, stt=stt: e.tensor_scalar(out=stt[:, 15:16], in0=stt[:, 12:13], scalar1=stt[:, 14:15],
                                                              scalar2=-1.0, op0=ALU.mult, op1=ALU.mult), [bst], [bst])
                S.op("act", lambda e, stt=stt: e.activation(out=out, in_=xin, func=AF.Identity, scale=stt[:, 14:15],
                                                           bias=stt[:, 15:16]), [bx, bst], [bo])
            else:
                S.op("dve", lambda e, stt=stt: e.tensor_scalar(out=out, in0=xin, scalar1=stt[:, 12:13],
                                                              scalar2=stt[:, 14:15], op0=ALU.subtract, op1=ALU.mult),
                     [bx, bst], [bo])

        def ln_block(st_ring, xb, bxb, xn_, bxn_, nt):
            stt, bst = st_ring.next()
            for t in range(nt):
                for i in range(2):
                    S.op("dve", lambda e, i=i, t=t, stt=stt: e.bn_stats(out=stt[:, 12 * t + 6 * i:12 * t + 6 * i + 6],
                                                                       in_=xb[:, t, 512 * i:512 * i + 512]), [bxb], [bst])
            for t in range(nt):
                S.op("dve", lambda e, t=t, stt=stt: e.bn_aggr(out=stt[:, 48 + 2 * t:50 + 2 * t], in_=stt[:, 12 * t:12 * t + 12]),
                     [bst], [bst])
            mv = stt[:, 48:56].rearrange("p (t c) -> p t c", c=2)
            S.op("act", lambda e, stt=stt: e.activation(out=stt[:, 56:56 + nt], in_=mv[:, 0:nt, 1], func=AF.Sqrt,
                                                       bias=EPS, scale=1.0), [bst], [bst])
            S.op("dve", lambda e, stt=stt: e.reciprocal(out=stt[:, 56:56 + nt], in_=stt[:, 56:56 + nt]), [bst], [bst])
            S.op("dve", lambda e, stt=stt: e.scalar_tensor_tensor(out=stt[:, 60:60 + nt], in0=mv[:, 0:nt, 0], scalar=-1.0,
                                                                 in1=stt[:, 56:56 + nt], op0=ALU.mult, op1=ALU.mult),
                 [bst], [bst])
            for t in range(nt):
                S.op("act", lambda e, t=t, stt=stt: e.activation(out=xn_[:, t, :], in_=xb[:, t, :], func=AF.Identity,
                                                                scale=stt[:, 56 + t:57 + t], bias=stt[:, 60 + t:61 + t]),
                     [bxb, bst], [bxn_])

        def make_hT(xn, bxn, nt, hT, bhT, vs, vb, m):
            for j in range(8):
                pb, bp = S.bank()
                for t in range(nt):
                    S.op("pe", lambda e, pb=pb, t=t, j=j: e.transpose(
                        out=pb[:, t * P:(t + 1) * P], in_=xn[:, t, j * P:(j + 1) * P], identity=ident_f[:, :]),
                        [bxn, b_idf], [bp])
                if j % 4 != 3:
                    S.op("act", lambda e, pb=pb, j=j: e.activation(
                        out=hT[:, j, 0:nt * P], in_=pb[:, 0:nt * P], func=AF.Identity,
                        scale=fcol[:, vs, j, m:m + 1], bias=fcol[:, vb, j, m:m + 1]), [bp, b_fcol], [bhT])
                else:
                    S.op("dve", lambda e, pb=pb, j=j: e.tensor_scalar(
                        out=hT[:, j, 0:nt * P], in0=pb[:, 0:nt * P], scalar1=fcol[:, vs, j, m:m + 1],
                        scalar2=fcol[:, vb, j, m:m + 1], op0=ALU.mult, op1=ALU.add), [bp, b_fcol], [bhT])

        def rms_T(src_banks, nrm, b_nrm, ntok, sq, b_sq, rb, b_rb, outT, b_out):
            for m2 in range(2):
                pbk, bpk = src_banks[m2]
                S.op("act", lambda e, pbk=pbk, m2=m2: e.activation(out=sq[:, m2, 0:ntok], in_=pbk[:, 0:ntok],
                                                                 func=AF.Square), [bpk], [b_sq])
            pb, bp = S.bank()
            for m2 in range(2):
                S.op("pe", lambda e, pb=pb, m2=m2: e.matmul(pb[:, 0:ntok], lhsT=ones_f[:, :], rhs=sq[:, m2, 0:ntok],
                                                           start=(m2 == 0), stop=(m2 == 1)), [b_sq, b_ones], [bp])
            S.op("act", lambda e, pb=pb: e.activation(out=rb[:, 0:ntok], in_=pb[:, 0:ntok], func=AF.Sqrt,
                                                     bias=EPS, scale=1.0 / 256.0), [bp], [b_rb])
            S.op("dve", lambda e: e.reciprocal(out=rb[:, 0:ntok], in_=rb[:, 0:ntok]), [b_rb], [b_rb])
            for m2 in range(2):
                pbk, bpk = src_banks[m2]
                S.op("dve", lambda e, pbk=pbk, m2=m2: e.scalar_tensor_tensor(
                    out=outT[:, m2, 0:ntok], in0=pbk[:, 0:ntok], scalar=nrm[:, m2:m2 + 1], in1=rb[:, 0:ntok],
                    op0=ALU.mult, op1=ALU.mult), [bpk, b_nrm, b_rb], [b_out])

        def rope_out(pa, bpa, pbk, bpbk, rows, ntok, cs, b_cs, sn, b_sn, t1, b_t1, t2, b_t2, out, b_o):
            S.op("dve", lambda e: e.tensor_tensor(out=t1[0:rows, 0:ntok], in0=pa[0:rows, 0:ntok],
                                                  in1=cs[0:rows, 0:ntok], op=ALU.mult), [bpa, b_cs], [b_t1])
            S.op("dve", lambda e: e.tensor_tensor(out=t2[0:rows, 0:ntok], in0=pbk[0:rows, 0:ntok],
                                                  in1=sn[0:rows, 0:ntok], op=ALU.mult), [bpbk, b_sn], [b_t2])
            S.op("pool", lambda e: e.tensor_tensor(out=out, in0=t1[0:rows, 0:ntok], in1=t2[0:rows, 0:ntok],
                                                   op=ALU.add), [b_t1, b_t2], [b_o])

        kv_blocks = [(io["xc"], 2, 0, 1, xbufs["xc"])] + [(xbufs["xf_blk"](b), 4, CTX + b * 512, 0, xbufs["xf_buf"](b)) for b in range(16)]
        q_blocks = [(xbufs["xo_blk"](b), 4, b * 512, 0, xbufs["xo_buf"](b)) for b in range(8)]
        if need_ctx:
            q_blocks.append((io["xc"], 2, SH, 1, xbufs["xc"]))
        b_KT, b_Vs, b_QT, b_cat = Buf("KT"), Buf("Vs"), Buf("QT"), Buf("cat")
        b_Fdl, b_Fdc = Buf("Fdl"), Buf("Fdc")
        KT, Vs, QT, cat = io["KT"], io["Vs"], io["QT"], io["cat"]
        vw = 65 if L == 0 else 129
        R = 96 if L == 0 else 128

        with ExitStack() as ph:
            xr = Ring(nc, ph, "xr", 2 if L == 0 else 1, [P, 4, D], F32)
            xnr = Ring(nc, ph, "xn", 2, [P, 4, D], F32)
            stb_ring = Ring(nc, ph, "stb", 2, [P, 64], F32)
            hTr = Ring(nc, ph, "hT", 2, [P, 8, 512], BF16)
            st_ring = Ring(nc, ph, "st", 4, [P, 16], F32)
            csr = Ring(nc, ph, "csk", 2, [R if L == 1 else 32, 512], F32)
            snr = Ring(nc, ph, "snk", 2, [R if L == 1 else 32, 512], F32)
            t1, b_t1 = sbt(nc, ph, "t1", [P, 512], F32)
            t2, b_t2 = sbt(nc, ph, "t2", [P, 512], F32)
            vsr = Ring(nc, ph, "vsb", 2, [P, 4, 8, vw], BF16)
            for i in range(2):
                S.op("pool", lambda e, i=i: e.memset(vsr.t[i][:], 1.0), [], [vsr.b[i]])
            if L == 0:
                win, b_win = sbt(nc, ph, "win", [P, 8, 1056], BF16)
                wkr, b_wkr = sbt(nc, ph, "wkr", [P, 8, 32], BF16)
                wkn, b_wkn = sbt(nc, ph, "wkn", [P, 2, 512], BF16)
                wv, b_wv = sbt(nc, ph, "wv", [P, 2, 512], BF16)
                kvn, b_kvn = sbt(nc, ph, "kvn", [P, 2], F32)
                S.dma("pool", win[:], io["w_in"].rearrange("(j p) n -> p j n", p=P), bin_, b_win)
                S.dma("pool", wkr[:], io["w_kr_sw"].rearrange("(j p) n -> p j n", p=P), bin_, b_wkr)
                S.dma("pool", wkn[:], io["w_ukv_kn"].rearrange("(j p) n -> p j n", p=P), bin_, b_wkn)
                S.dma("pool", wv[:], io["w_ukv_v"].rearrange("(j p) n -> p j n", p=P), bin_, b_wv)
                S.dma("sp", kvn[:], io["kvn"], bin_, b_kvn)
                fsr = Ring(nc, ph, "fsb", 2, [P, 4, 512], F32)
                sq, b_sq = sbt(nc, ph, "sq", [P, 2, 512], F32)
                rb, b_rb = sbt(nc, ph, "rb", [P, 512], F32)
                ckvn, b_ckvn = sbt(nc, ph, "ckvn", [P, 2, 512], BF16)
                knr = Ring(nc, ph, "knT", 2, [P, 4, 512], BF16)
                krr = Ring(nc, ph, "krT", 2, [32, 512], BF16)
            else:
                win, b_win = sbt(nc, ph, "win", [P, 8, 2048], BF16)
                wsw, b_wsw = sbt(nc, ph, "wsw", [P, 8, 1024], BF16)
                for kvh in range(2):
                    S.dma("pool", win[:, :, kvh * 1024:(kvh + 1) * 1024],
                          io["w_in"][:, 1024 + kvh * 1024:2048 + kvh * 1024].rearrange("(j p) n -> p j n", p=P), bin_, b_win)
                S.dma("pool", wsw[:], io["w_qk_sw"][:, 1024:2048].rearrange("(j p) n -> p j n", p=P), bin_, b_wsw)
                knr = Ring(nc, ph, "knT", 2, [P, 8, 512], BF16)

            def kv_load(src, nt, koff, m, sbuf_src):
                xb, bxb = xr.next()
                S.dma("sp", xb[:, 0:nt, :], src.rearrange("(t p) d -> p t d", p=P), sbuf_src, bxb)
                return xb, bxb

            def kv_a(src, nt, koff, m, sbuf_src, xb, bxb):
                ntok = nt * P
                cs, b_cs = csr.next()
                sn, b_sn = snr.next()
                S.dma("sp", cs[:, 0:ntok], io["cosK"][:, koff:koff + ntok], bin_, b_cs)
                S.dma("sp", sn[:, 0:ntok], io["sinK"][:, koff:koff + ntok], bin_, b_sn)
                xn_, bxn_ = xnr.next()
                ln_block(stb_ring, xb, bxb, xn_, bxn_, nt)
                return dict(nt=nt, koff=koff, m=m, cs=cs, b_cs=b_cs, sn=sn, b_sn=b_sn, xn=xn_, bxn=bxn_)

            def kv_b(st):
                hT, bhT = hTr.next()
                make_hT(st["xn"], st["bxn"], st["nt"], hT, bhT, 0, 1, st["m"])
                st["hT"], st["bhT"] = hT, bhT

            def kv_body(st):
                nt, koff, m = st["nt"], st["koff"], st["m"]
                cs, b_cs, sn, b_sn, hT, bhT = st["cs"], st["b_cs"], st["sn"], st["b_sn"], st["hT"], st["bhT"]
                ntok = nt * P
                vsb, bvs = vsr.next()
                if L == 0:
                    fsb, bfs = fsr.next()
                    for t in range(nt):
                        pb, bp = S.bank()
                        for j in range(8):
                            S.op("pe", lambda e, pb=pb, t=t, j=j, hT=hT: e.matmul(
                                pb[:, :], lhsT=hT[:, j, t * P:(t + 1) * P], rhs=win[:, j, 0:512],
                                start=(j == 0), stop=(j == 7)), [bhT, b_win], [bp])
                        S.op("act", lambda e, pb=pb, t=t, fsb=fsb: e.copy(out=fsb[:, t, :], in_=pb[:, :]), [bp], [bfs])
                    if m == 1:
                        S.dma("pool", io["Fdc"].rearrange("(t p) c -> p t c", p=P), fsb[:, 0:nt, :], bfs, b_Fdc)
                    else:
                        n0 = koff - CTX
                        S.dma("pool", io["Fdl"][n0:n0 + ntok, :].rearrange("(t p) c -> p t c", p=P), fsb[:, 0:nt, :],
                              bfs, b_Fdl)
                    banks = []
                    for m2 in range(2):
                        pb, bp = S.bank()
                        banks.append((pb, bp))
                        for j in range(8):
                            S.op("pe", lambda e, pb=pb, m2=m2, j=j, hT=hT: e.matmul(
                                pb[:, 0:ntok], lhsT=win[:, j, 768 + m2 * P:768 + (m2 + 1) * P], rhs=hT[:, j, 0:ntok],
                                start=(j == 0), stop=(j == 7)), [bhT, b_win], [bp])
                    rms_T(banks, kvn, b_kvn, ntok, sq, b_sq, rb, b_rb, ckvn, b_ckvn)
                    knT, bkn = knr.next()
                    for hp in range(4):
                        pb, bp = S.bank()
                        for m2 in range(2):
                            S.op("pe", lambda e, pb=pb, hp=hp, m2=m2: e.matmul(
                                pb[:, 0:ntok],
                                lhsT=wkn[:, m2, hp * P:(hp + 1) * P],
                                rhs=ckvn[:, m2, 0:ntok], start=(m2 == 0), stop=(m2 == 1)), [b_ckvn, b_wkn], [bp])
                        S.op("act", lambda e, pb=pb, hp=hp, knT=knT: e.copy(out=knT[:, hp, 0:ntok], in_=pb[:, 0:ntok]),
                             [bp], [bkn])
                    for hh in range(2):
                        S.dma("pool", KT.rearrange("(hp hh) r n -> hh r hp n", hh=2)[hh, 0:64, :, koff:koff + ntok],
                              knT[hh * 64:(hh + 1) * 64, :, 0:ntok], bkn, b_KT)
                    for t in range(nt):
                        pb, bp = S.bank()
                        for m2 in range(2):
                            S.op("pe", lambda e, pb=pb, t=t, m2=m2: e.matmul(
                                pb[:, :], lhsT=ckvn[:, m2, t * P:(t + 1) * P],
                                rhs=wv[:, m2, :],
                                start=(m2 == 0), stop=(m2 == 1)), [b_ckvn, b_wv], [bp])
                        S.op("dve", lambda e, pb=pb, t=t, vsb=vsb: e.tensor_copy(
                            out=vsb[:, t, :, 0:64], in_=pb[:, :].rearrange("p (h c) -> p h c", c=64)), [bp], [bvs])
                    pa, bpa = S.bank()
                    pbk, bpbk = S.bank()
                    for j in range(8):
                        S.op("pe", lambda e, pa=pa, j=j, hT=hT: e.matmul(
                            pa[0:32, 0:ntok], lhsT=win[:, j, 1024:1056], rhs=hT[:, j, 0:ntok],
                            start=(j == 0), stop=(j == 7)), [bhT, b_win], [bpa])
                    for j in range(8):
                        S.op("pe", lambda e, pbk=pbk, j=j, hT=hT: e.matmul(
                            pbk[0:32, 0:ntok], lhsT=wkr[:, j, :], rhs=hT[:, j, 0:ntok],
                            start=(j == 0), stop=(j == 7)), [bhT, b_wkr], [bpbk])
                    krT, bkr = krr.next()
                    rope_out(pa, bpa, pbk, bpbk, 32, ntok, cs, b_cs, sn, b_sn, t1, b_t1, t2, b_t2,
                             krT[0:32, 0:ntok], bkr)
                    for h in range(8):
                        S.dma("pool", KT[h, 64:96, koff:koff + ntok], krT[0:32, 0:ntok], bkr, b_KT)
                else:
                    knT, bkn = knr.next()
                    for h in range(8):
                        pa, bpa = S.bank()
                        pbk, bpbk = S.bank()
                        for j in range(8):
                            S.op("pe", lambda e, pa=pa, j=j, h=h, hT=hT: e.matmul(
                                pa[:, 0:ntok], lhsT=win[:, j, h * P:(h + 1) * P], rhs=hT[:, j, 0:ntok],
                                start=(j == 0), stop=(j == 7)), [bhT, b_win], [bpa])
                        for j in range(8):
                            S.op("pe", lambda e, pbk=pbk, j=j, h=h, hT=hT: e.matmul(
                                pbk[:, 0:ntok], lhsT=wsw[:, j, h * P:(h + 1) * P], rhs=hT[:, j, 0:ntok],
                                start=(j == 0), stop=(j == 7)), [bhT, b_wsw], [bpbk])
                        rope_out(pa, bpa, pbk, bpbk, P, ntok, cs, b_cs, sn, b_sn, t1, b_t1, t2, b_t2,
                                 knT[:, h, 0:ntok], bkn)
                    S.dma("pool", KT[:, :, koff:koff + ntok].rearrange("h r n -> r h n"), knT[:, :, 0:ntok], bkn, b_KT)
                    for t in range(nt):
                        for hf in range(2):
                            pb, bp = S.bank()
                            for j in range(8):
                                S.op("pe", lambda e, pb=pb, t=t, j=j, hf=hf, hT=hT: e.matmul(
                                    pb[:, :], lhsT=hT[:, j, t * P:(t + 1) * P],
                                    rhs=win[:, j, 1024 + hf * 512:1024 + (hf + 1) * 512],
                                    start=(j == 0), stop=(j == 7)), [bhT, b_win], [bp])
                            S.op("dve", lambda e, pb=pb, t=t, hf=hf, vsb=vsb: e.tensor_copy(
                                out=vsb[:, t, hf * 4:(hf + 1) * 4, 0:128], in_=pb[:, :].rearrange("p (h c) -> p h c", c=128)),
                                [bp], [bvs])
                S.dma("pool", Vs[koff:koff + ntok].rearrange("(t p) h c -> p t (h c)", p=P),
                      vsb[:, 0:nt].rearrange("p t h c -> p t (h c)"), bvs, b_Vs)
            nblk = len(kv_blocks)
            dist = len(xr.t)
            xl = {}
            for i in range(min(dist, nblk)):
                xl[i] = kv_load(*kv_blocks[i])
            sts = [None] * nblk
            sts[0] = kv_a(*kv_blocks[0], *xl.pop(0))
            kv_b(sts[0])
            for i in range(nblk):
                if dist >= 2 and i + dist < nblk:
                    xl[i + dist] = kv_load(*kv_blocks[i + dist])
                if i + 1 < nblk:
                    if dist < 2:
                        xl[i + 1] = kv_load(*kv_blocks[i + 1])
                    sts[i + 1] = kv_a(*kv_blocks[i + 1], *xl.pop(i + 1))
                kv_body(sts[i])
                if i + 1 < nblk:
                    kv_b(sts[i + 1])
            S.end_phase()

        with ExitStack() as ph:
            xr = Ring(nc, ph, "xr", 2, [P, 4, D], F32)
            xnr = Ring(nc, ph, "xn", 2, [P, 4, D], F32)
            stb_ring = Ring(nc, ph, "stb", 2, [P, 64], F32)
            hTr = Ring(nc, ph, "hT", 2, [P, 8, 512], BF16)
            st_ring = Ring(nc, ph, "st", 4, [P, 16], F32)
            csr = Ring(nc, ph, "csq", 2, [R, 512], F32)
            snr = Ring(nc, ph, "snq", 2, [R, 512], F32)
            t1r = Ring(nc, ph, "t1", 2, [P, 512], F32)
            t2r = Ring(nc, ph, "t2", 2, [P, 512], F32)
            qsr = Ring(nc, ph, "qsb", 2, [R, 8, 512], BF16)
            if L == 0:
                win, b_win = sbt(nc, ph, "win", [P, 8, 256], BF16)
                wuq, b_wuq = sbt(nc, ph, "wuq", [P, 2, 768], BF16)
                wuqs, b_wuqs = sbt(nc, ph, "wuqs", [P, 2, 768], BF16)
                qn, b_qn = sbt(nc, ph, "qn", [P, 2], F32)
                S.dma("pool", win[:], io["w_in"][:, 512:768].rearrange("(j p) n -> p j n", p=P), bin_, b_win)
                S.dma("pool", wuq[:], io["w_uq"].rearrange("(j p) n -> p j n", p=P), bin_, b_wuq)
                S.dma("pool", wuqs[:], io["w_uq_sw"].rearrange("(j p) n -> p j n", p=P), bin_, b_wuqs)
                S.dma("sp", qn[:], io["qn"], bin_, b_qn)
                sq, b_sq = sbt(nc, ph, "sq", [P, 2, 512], F32)
                rb, b_rb = sbt(nc, ph, "rb", [P, 512], F32)
                cqn, b_cqn = sbt(nc, ph, "cqn", [P, 2, 512], BF16)
            else:
                win, b_win = sbt(nc, ph, "win", [P, 8, 1024], BF16)
                wsw, b_wsw = sbt(nc, ph, "wsw", [P, 8, 1024], BF16)
                S.dma("pool", win[:], io["w_in"][:, 0:1024].rearrange("(j p) n -> p j n", p=P), bin_, b_win)
                S.dma("pool", wsw[:], io["w_qk_sw"][:, 0:1024].rearrange("(j p) n -> p j n", p=P), bin_, b_wsw)
            def q_load(src, nt, qoff, m, sbuf_src):
                xb, bxb = xr.next()
                S.dma("sp", xb[:, 0:nt, :], src.rearrange("(t p) d -> p t d", p=P), sbuf_src, bxb)
                return xb, bxb

            def q_a(src, nt, qoff, m, sbuf_src, xb, bxb):
                ntok = nt * P
                cs, b_cs = csr.next()
                sn, b_sn = snr.next()
                S.dma("sp", cs[:, 0:ntok], io["cosQ"][:, qoff:qoff + ntok], bin_, b_cs)
                S.dma("sp", sn[:, 0:ntok], io["sinQ"][:, qoff:qoff + ntok], bin_, b_sn)
                xn_, bxn_ = xnr.next()
                ln_block(stb_ring, xb, bxb, xn_, bxn_, nt)
                return dict(nt=nt, qoff=qoff, m=m, cs=cs, b_cs=b_cs, sn=sn, b_sn=b_sn, xn=xn_, bxn=bxn_)

            def q_b(st):
                hT, bhT = hTr.next()
                make_hT(st["xn"], st["bxn"], st["nt"], hT, bhT, 0, 1, st["m"])
                st["hT"], st["bhT"] = hT, bhT

            def q_body(st):
                nt, qoff, m = st["nt"], st["qoff"], st["m"]
                cs, b_cs, sn, b_sn, hT, bhT = st["cs"], st["b_cs"], st["sn"], st["b_sn"], st["hT"], st["bhT"]
                ntok = nt * P
                qsb, bqs = qsr.next()
                if L == 0:
                    banks = []
                    for m2 in range(2):
                        pb, bp = S.bank()
                        banks.append((pb, bp))
                        for j in range(8):
                            S.op("pe", lambda e, pb=pb, m2=m2, j=j, hT=hT: e.matmul(
                                pb[:, 0:ntok], lhsT=win[:, j, m2 * P:(m2 + 1) * P], rhs=hT[:, j, 0:ntok],
                                start=(j == 0), stop=(j == 7)), [bhT, b_win], [bp])
                    rms_T(banks, qn, b_qn, ntok, sq, b_sq, rb, b_rb, cqn, b_cqn)
                for h in range(8):
                    pa, bpa = S.bank()
                    pbk, bpbk = S.bank()
                    if L == 0:
                        for m2 in range(2):
                            S.op("pe", lambda e, pa=pa, m2=m2, h=h: e.matmul(
                                pa[0:96, 0:ntok], lhsT=wuq[:, m2, h * 96:(h + 1) * 96], rhs=cqn[:, m2, 0:ntok],
                                start=(m2 == 0), stop=(m2 == 1)), [b_cqn, b_wuq], [bpa])
                        for m2 in range(2):
                            S.op("pe", lambda e, pbk=pbk, m2=m2, h=h: e.matmul(
                                pbk[0:96, 0:ntok], lhsT=wuqs[:, m2, h * 96:(h + 1) * 96], rhs=cqn[:, m2, 0:ntok],
                                start=(m2 == 0), stop=(m2 == 1)), [b_cqn, b_wuqs], [bpbk])
                    else:
                        for j in range(8):
                            S.op("pe", lambda e, pa=pa, j=j, h=h, hT=hT: e.matmul(
                                pa[:, 0:ntok], lhsT=win[:, j, h * P:(h + 1) * P], rhs=hT[:, j, 0:ntok],
                                start=(j == 0), stop=(j == 7)), [bhT, b_win], [bpa])
                        for j in range(8):
                            S.op("pe", lambda e, pbk=pbk, j=j, h=h, hT=hT: e.matmul(
                                pbk[:, 0:ntok], lhsT=wsw[:, j, h * P:(h + 1) * P], rhs=hT[:, j, 0:ntok],
                                start=(j == 0), stop=(j == 7)), [bhT, b_wsw], [bpbk])
                    t1, b_t1 = t1r.next()
                    t2, b_t2 = t2r.next()
                    rope_out(pa, bpa, pbk, bpbk, R, ntok, cs, b_cs, sn, b_sn, t1, b_t1, t2, b_t2,
                             qsb[0:R, h, 0:ntok], bqs)
                S.dma("pool", QT[:, :, qoff:qoff + ntok].rearrange("h r n -> r h n"), qsb[0:R, :, 0:ntok], bqs, b_QT)
            nblk = len(q_blocks)
            dist = len(xr.t)
            xl = {}
            for i in range(min(dist, nblk)):
                xl[i] = q_load(*q_blocks[i])
            sts = [None] * nblk
            sts[0] = q_a(*q_blocks[0], *xl.pop(0))
            q_b(sts[0])
            for i in range(nblk):
                if dist >= 2 and i + dist < nblk:
                    xl[i + dist] = q_load(*q_blocks[i + dist])
                if i + 1 < nblk:
                    if dist < 2:
                        xl[i + 1] = q_load(*q_blocks[i + 1])
                    sts[i + 1] = q_a(*q_blocks[i + 1], *xl.pop(i + 1))
                q_body(sts[i])
                if i + 1 < nblk:
                    q_b(sts[i + 1])
            S.end_phase()

        if L == 0:
            with ExitStack() as ph:
                f64, b_f64 = sbt(nc, ph, "f64", [64, 128], F32)
                cdl, b_cdl = sbt(nc, ph, "cdl", [P, 2, 128], F32)
                cdc, b_cdc = sbt(nc, ph, "cdc", [P, 2, 128], F32)
                f256, b_f256 = sbt(nc, ph, "f256", [P, 2, 512], F32)
                S.dma("sp", f64[:], io["F64"], bin_, b_f64)
                S.dma("sp", cdl[:], io["CdSl"], bin_, b_cdl)
                S.dma("sp", cdc[:], io["CdSc"], bin_, b_cdc)
                S.dma("sp", f256[:], io["F256"], bin_, b_f256)
                Xr = Ring(nc, ph, "Xh", 2, [64, 128, 32], F32)
                T, b_T = sbt(nc, ph, "T", [P, 128, 128], F32)
                ZT, b_ZT = sbt(nc, ph, "ZT", [P, 2, 64, 64], F32)
                Gr_ = Ring(nc, ph, "G", 2, [P, 8, 2, 128], F32)
                ysr = Ring(nc, ph, "ysb", 2, [P, 4, 128], BF16)
                Fd3 = io["Fdl"].rearrange("(a b) c -> a b c", b=128)
                ZTf = ZT[:, :, :, :].rearrange("p r i q -> p r (i q)")
                for g in range(4):
                    for xq in range(4):
                        X, bX = Xr.next()
                        c0 = g * 128 + xq * 32
                        S.dma("sp", X[:], Fd3[:, :, c0:c0 + 32], b_Fdl, bX)
                        for cg in range(8):
                            pb, bp = S.bank()
                            for cc in range(4):
                                c = cg * 4 + cc
                                S.op("pe", lambda e, pb=pb, cc=cc, c=c, X=X: e.matmul(
                                    pb[:, cc * 128:(cc + 1) * 128], lhsT=X[:, :, c], rhs=f64[:, :],
                                    start=True, stop=True), [bX, b_f64], [bp])
                            ch0 = xq * 32 + cg * 4
                            if cg % 2 == 0:
                                S.op("act", lambda e, pb=pb, ch0=ch0: e.copy(
                                    out=T[:, ch0:ch0 + 4, :], in_=pb[:, :].rearrange("p (c k) -> p c k", k=128)),
                                    [bp], [b_T])
                            else:
                                S.op("dve", lambda e, pb=pb, ch0=ch0: e.tensor_copy(
                                    out=T[:, ch0:ch0 + 4, :], in_=pb[:, :].rearrange("p (c k) -> p c k", k=128)),
                                    [bp], [b_T])
                    for kc in range(8):
                        G, bG = Gr_.next()
                        S.dma("sp", G[:], io["GAB"][:, kc * 8:(kc + 1) * 8], bin_, bG)
                        for kq in range(2):
                            pb, bp = S.bank()
                            for q in range(4):
                                kl = kq * 4 + q
                                k2 = kc * 8 + kl
                                S.op("pe", lambda e, pb=pb, q=q, kl=kl, k2=k2, G=G: e.matmul(
                                    pb[:, q * 128:(q + 1) * 128], lhsT=T[:, :, k2],
                                    rhs=G[:, kl, 0, :], start=True, stop=False), [b_T, bG], [bp])
                                S.op("pe", lambda e, pb=pb, q=q, kl=kl, k2=k2, G=G: e.matmul(
                                    pb[:, q * 128:(q + 1) * 128], lhsT=T[:, :, 64 + k2],
                                    rhs=G[:, kl, 1, :], start=False, stop=True), [b_T, bG], [bp])
                            k20 = kc * 8 + kq * 4
                            S.op("dve", lambda e, pb=pb, k20=k20: e.tensor_copy(
                                out=ZT[:, :, :, k20:k20 + 4].rearrange("p r i q -> p q r i"),
                                in_=pb[:, :].rearrange("p (q r i) -> p q r i", q=4, r=2)),
                                [bp], [b_ZT])
                    for tq in range(8):
                        pb, bp = S.bank()
                        for q in range(4):
                            tt = tq * 4 + q
                            for r in range(2):
                                S.op("pe", lambda e, pb=pb, q=q, tt=tt, r=r: e.matmul(
                                    pb[:, q * 128:(q + 1) * 128], lhsT=ZTf[:, r, tt * 128:(tt + 1) * 128],
                                    rhs=cdl[:, r, :], start=(r == 0), stop=(r == 1)), [b_ZT, b_cdl], [bp])
                        ysb, bys = ysr.next()
                        S.op("act", lambda e, pb=pb, ysb=ysb: e.copy(
                            out=ysb[:, :, :], in_=pb[:, :].rearrange("p (q c) -> p q c", c=128)), [bp], [bys])
                        S.dma("pool", cat[tq * 512:(tq + 1) * 512, g * 128:(g + 1) * 128].rearrange("(t p) c -> p t c", p=P),
                              ysb[:, :, :], bys, b_cat)
                fcx, b_fcx = sbt(nc, ph, "fcx", [P, 2, 512], F32)
                zc, b_zc = sbt(nc, ph, "zc", [P, 512], F32)
                S.dma("sp", fcx[:], io["Fdc"].rearrange("(t p) c -> p t c", p=P), b_Fdc, b_fcx)
                for g in range(4):
                    pb, bp = S.bank()
                    for tl in range(2):
                        S.op("pe", lambda e, pb=pb, tl=tl, g=g: e.matmul(
                            pb[:, :], lhsT=fcx[:, tl, g * 128:(g + 1) * 128], rhs=f256[:, tl, :],
                            start=(tl == 0), stop=(tl == 1)), [b_fcx, b_f256], [bp])
                    S.op("dve", lambda e, pb=pb: e.tensor_copy(out=zc[:, :], in_=pb[:, :]), [bp], [b_zc])
                    pb2, bp2 = S.bank()
                    for tl in range(2):
                        for r in range(2):
                            S.op("pe", lambda e, pb2=pb2, tl=tl, r=r: e.matmul(
                                pb2[:, tl * 128:(tl + 1) * 128], lhsT=zc[:, r * 256 + tl * 128:r * 256 + (tl + 1) * 128],
                                rhs=cdc[:, r, :], start=(r == 0), stop=(r == 1)), [b_zc, b_cdc], [bp2])
                    ysb, bys = ysr.next()
                    S.op("act", lambda e, pb2=pb2, ysb=ysb: e.copy(
                        out=ysb[:, 0:2, :], in_=pb2[:, 0:256].rearrange("p (q c) -> p q c", c=128)), [bp2], [bys])
                    S.dma("pool", cat[SH:SH + CTX, g * 128:(g + 1) * 128].rearrange("(t p) c -> p t c", p=P),
                          ysb[:, 0:2, :], bys, b_cat)
                S.end_phase()

        with ExitStack() as ph:
            ktr = Ring(nc, ph, "kts", 2, [R, NK], BF16)
            vr = Ring(nc, ph, "vs", 2, [P, 66, vw], BF16)
            if L == 0:
                qtr = Ring(nc, ph, "qts", 2, [R, SH + CTX], BF16)
            else:
                qtr = Ring(nc, ph, "qts", 2, [P, 2, SH], BF16)
                for i in range(2):
                    S.op("pool", lambda e, i=i: e.memset(qtr.t[i][:], 0.0), [], [qtr.b[i]])
            ptr = Ring(nc, ph, "pt", 4, [P, 1024], BF16)
            rcr = Ring(nc, ph, "rc", 4, [P, 8], F32)
            asr = Ring(nc, ph, "asb", 2, [P, 4, 128 if L == 1 else 64], BF16)
            if L == 1:
                o0r = Ring(nc, ph, "o0", 2, [P, 4, 128], F32)
                o1r = Ring(nc, ph, "o1", 2, [P, 128], F32)
                sqr = Ring(nc, ph, "sqr", 2, [P, 128], F32)
                lamt, b_lam = sbt(nc, ph, "lamt", [P, 4, 64], F32)
                lamw, b_lamw = sbt(nc, ph, "lamw", [P, 8], F32)
                lamj, b_lamj = sbt(nc, ph, "lamj", [P, 64], F32)
                subl, b_subl = sbt(nc, ph, "subl", [P, 128], F32)
                epsc, b_epsc = sbt(nc, ph, "epsc", [P, 1], F32)
                S.op("dve", lambda e: e.memset(epsc[:], EPS), [], [b_epsc])
                S.dma("sp", lamt[:], io["lam"].partition_broadcast(P), bin_, b_lam)
                S.dma("sp", subl[:], io["subln"].partition_broadcast(P), bin_, b_subl)
                for i in range(2):
                    S.op("dve", lambda e, i=i: e.scalar_tensor_tensor(
                        out=lamj[:, :], in0=lamt[:, 2 * i, :], scalar=1.0, in1=lamt[:, 2 * i + 1, :],
                        op0=ALU.mult, op1=ALU.mult, accum_out=lamw[:, i:i + 1]), [b_lam], [b_lamj, b_lamw])
                S.op("act", lambda e: e.activation(out=lamw[:, 2:4], in_=lamw[:, 0:2], func=AF.Exp), [b_lamw], [b_lamw])
                S.op("dve", lambda e: e.tensor_tensor(out=lamw[:, 4:5], in0=lamw[:, 3:4], in1=lamw[:, 2:3],
                                                      op=ALU.subtract), [b_lamw], [b_lamw])
                S.op("dve", lambda e: e.tensor_scalar(out=lamw[:, 5:6], in0=lamw[:, 4:5], scalar1=-LAMBDA_INIT1,
                                                      scalar2=None, op0=ALU.add), [b_lamw], [b_lamw])
                S.op("dve", lambda e: e.tensor_scalar(out=subl[:, :], in0=subl[:, :], scalar1=1.0 - LAMBDA_INIT1,
                                                      scalar2=None, op0=ALU.mult), [b_subl], [b_subl])
            casts = []
            for c in range(11):
                casts.append(lambda c=c: S.dma("pool", io["wg_bf"][c],
                                               io["w_gate"][:, c * 256:(c + 1) * 256].rearrange("(j p) n -> p j n", p=P),
                                               bin_, b_wbf))
                casts.append(lambda c=c: S.dma("pool", io["wu_bf"][c],
                                               io["w_up"][:, c * 256:(c + 1) * 256].rearrange("(j p) n -> p j n", p=P),
                                               bin_, b_wbf))
            for r4 in range(4):
                casts.append(lambda r4=r4: S.dma("pool", io["wd_bf"][r4 * 704:(r4 + 1) * 704, :],
                                                 io["w_down"][r4 * 704:(r4 + 1) * 704, :], bin_, b_wbf,
                                                 max_dma_last_dim=4096))
            scale = MLA_SCALE if L == 0 else DIFF_SCALE
            nq_tot = SH + (CTX if need_ctx else 0)
            qblocks = [(b * 512, 512, 66) for b in range(8)]
            if need_ctx:
                qblocks.append((SH, 256, 2))
            nmaps = 1 if L == 0 else 2
            accr = Ring(nc, ph, "accs", 4, [P, 4, 132], F32)
            rc8r = Ring(nc, ph, "rc8", 4, [P, 16], F32)
            o1br = Ring(nc, ph, "o1b", 2, [P, 4, 128], F32)
            items = []
            for h in range(8):
                for qi in range(len(qblocks)):
                    for ci in range(nmaps):
                        for kp in range(qblocks[qi][2] // 2):
                            items.append((h, qi, ci, kp))
            heads = {}

            def load_head(h):
                kts, bkt = ktr.next()
                vs_, bv = vr.next()
                qts, bqt = qtr.next()
                S.dma("sp", kts[:, :], KT[h], b_KT, bkt)
                S.dma("sp", vs_[:, :, :], Vs[:, h, :].rearrange("(t p) c -> p t c", p=P), b_Vs, bv)
                if L == 0:
                    S.dma("sp", qts[:, 0:nq_tot], QT[h, :, 0:nq_tot], b_QT, bqt)
                else:
                    S.dma("sp", qts[0:64, 0, :], QT[h, 0:64, :], b_QT, bqt)
                    S.dma("sp", qts[64:128, 1, :], QT[h, 64:128, :], b_QT, bqt)
                heads[h] = (kts, bkt, vs_, bv, qts, bqt)

            qstate = {}

            def finalize(h, qi, ci):
                q0, nq, nkt = qblocks[qi]
                nj = nq // P
                stq = qstate.setdefault((h, qi), {})
                acs, bacs = accr.next()
                for j in range(nj):
                    S.op("dve", lambda e, j=j, acs=acs: e.tensor_copy(out=acs[:, j, 0:vw], in_=S.pb[4 + j][:, 0:vw]),
                         [S.bpb[4 + j]], [bacs])
                if ci == 0:
                    rc, brc = rc8r.next()
                    stq["rc"], stq["brc"] = rc, brc
                    stq["a0"], stq["ba0"] = acs, bacs
                else:
                    rc, brc = stq["rc"], stq["brc"]
                S.op("dve", lambda e, acs=acs, rc=rc, ci=ci: e.reciprocal(
                    out=rc[:, 4 * ci:4 * ci + nj], in_=acs[:, 0:nj, vw - 1]), [bacs], [brc])
                if L == 0:
                    asb, bas = asr.next()
                    for j in range(nj):
                        S.op("dve", lambda e, j=j, acs=acs, rc=rc, asb=asb: e.tensor_scalar(
                            out=asb[:, j, :], in0=acs[:, j, 0:64], scalar1=rc[:, j:j + 1], scalar2=None, op0=ALU.mult),
                            [bacs, brc], [bas])
                    S.dma("pool", cat[q0:q0 + nq, 512 + h * 64:512 + (h + 1) * 64].rearrange("(t p) c -> p t c", p=P),
                          asb[:, 0:nj, :], bas, b_cat)
                    if casts:
                        casts.pop(0)()
                elif ci == 1:
                    a0, ba0 = stq["a0"], stq["ba0"]
                    asb, bas = asr.next()
                    o1, b_o1 = o1br.next()
                    sqt, bsqt = sqr.next()
                    S.op("dve", lambda e, rc=rc: e.tensor_scalar(out=rc[:, 4:8], in0=rc[:, 4:8], scalar1=lamw[:, 5:6],
                                                                 scalar2=None, op0=ALU.mult), [brc, b_lamw], [brc])
                    for j in range(nj):
                        S.op("dve", lambda e, j=j, a0=a0, rc=rc, o1=o1: e.tensor_scalar(
                            out=o1[:, j, :], in0=a0[:, j, 0:128], scalar1=rc[:, j:j + 1], scalar2=None, op0=ALU.mult),
                            [ba0, brc], [b_o1])
                        S.op("dve", lambda e, j=j, acs=acs, rc=rc, o1=o1: e.scalar_tensor_tensor(
                            out=o1[:, j, :], in0=acs[:, j, 0:128], scalar=rc[:, 4 + j:5 + j], in1=o1[:, j, :],
                            op0=ALU.mult, op1=ALU.add), [bacs, brc, b_o1], [b_o1])
                        S.op("dve", lambda e, j=j, o1=o1, rc=rc, sqt=sqt: e.scalar_tensor_tensor(
                            out=sqt[:, :], in0=o1[:, j, :], scalar=1.0, in1=o1[:, j, :],
                            op0=ALU.mult, op1=ALU.mult, accum_out=rc[:, 8 + j:9 + j]), [b_o1], [brc, bsqt])
                    S.op("act", lambda e, rc=rc: e.activation(out=rc[:, 12:16], in_=rc[:, 8:12], func=AF.Ln,
                                                             bias=epsc[:, 0:1], scale=1.0 / 128.0), [brc, b_epsc], [brc])
                    S.op("act", lambda e, rc=rc: e.activation(out=rc[:, 12:16], in_=rc[:, 12:16], func=AF.Exp,
                                                             scale=-0.5), [brc], [brc])
                    for j in range(nj):
                        S.op("dve", lambda e, j=j, o1=o1, rc=rc, asb=asb: e.scalar_tensor_tensor(
                            out=asb[:, j, :], in0=o1[:, j, :], scalar=rc[:, 12 + j:13 + j], in1=subl[:, :],
                            op0=ALU.mult, op1=ALU.mult), [b_o1, brc, b_subl], [bas])
                    S.dma("pool", cat[q0:q0 + nq, h * 128:(h + 1) * 128].rearrange("(t p) c -> p t c", p=P),
                          asb[:, 0:nj, :], bas, b_cat)
                    if casts:
                        casts.pop(0)()

            DEPTH = 2
            pts = {}
            load_head(0)
            spair = [(S.pbig[:, 0:1024], [S.bpb[0], S.bpb[1]]), (S.pbig[:, 1024:2048], [S.bpb[2], S.bpb[3]])]
            for n in range(len(items) + DEPTH):
                if n < len(items):
                    h, qi, ci, kp = items[n]
                    q0, nq, nkt = qblocks[qi]
                    kts, bkt, vs_, bv, qts, bqt = heads[h]
                    sp2, bsp2 = spair[n % 2]
                    for u in range(2):
                        kt = 2 * kp + u
                        if L == 0:
                            S.op("pe", lambda e, sp2=sp2, u=u, kt=kt, kts=kts, qts=qts, q0=q0, nq=nq: e.matmul(
                                sp2[:, u * 512:u * 512 + nq], lhsT=kts[0:96, kt * P:(kt + 1) * P], rhs=qts[0:96, q0:q0 + nq],
                                start=True, stop=True), [bkt, bqt], [bsp2[u]])
                        else:
                            S.op("pe", lambda e, sp2=sp2, u=u, kt=kt, kts=kts, qts=qts, ci=ci, q0=q0, nq=nq: e.matmul(
                                sp2[:, u * 512:u * 512 + nq], lhsT=kts[:, kt * P:(kt + 1) * P], rhs=qts[:, ci, q0:q0 + nq],
                                start=True, stop=True), [bkt, bqt], [bsp2[u]])
                    pt, bpt = ptr.next()
                    S.op("act", lambda e, sp2=sp2, pt=pt, nq=nq: e.activation(
                        out=pt[:, :].rearrange("p (u n) -> p u n", u=2)[:, :, 0:nq],
                        in_=sp2.rearrange("p (u n) -> p u n", u=2)[:, :, 0:nq], func=AF.Exp, scale=scale), bsp2, [bpt])
                    pts[n] = (pt, bpt)
                if n >= DEPTH:
                    h, qi, ci, kp = items[n - DEPTH]
                    q0, nq, nkt = qblocks[qi]
                    if qi == 0 and ci == 0 and kp == 0 and h + 1 < 8:
                        load_head(h + 1)
                    kts, bkt, vs_, bv, qts, bqt = heads[h]
                    pt, bpt = pts.pop(n - DEPTH)
                    for u in range(2):
                        kt = 2 * kp + u
                        for j in range(nq // P):
                            S.op("pe", lambda e, j=j, u=u, pt=pt, kt=kt, vs_=vs_, nkt=nkt: e.matmul(
                                S.pb[4 + j][:, 0:vw], lhsT=pt[:, u * 512 + j * P:u * 512 + (j + 1) * P], rhs=vs_[:, kt, :],
                                start=(kt == 0), stop=(kt == nkt - 1)), [bpt, bv], [S.bpb[4 + j]])
                    if 2 * kp + 1 == nkt - 1:
                        finalize(h, qi, ci)
            while casts:
                casts.pop(0)()
            S.end_phase()

        with ExitStack() as ph:
            wout, b_wout = sbt(nc, ph, "wout", [P, 8, D], BF16)
            S.dma("pool", wout[:], io["w_out"].rearrange("(j p) n -> p j n", p=P), bin_, b_wout)
            xr = Ring(nc, ph, "xr", 2, [P, D], F32)
            ctr = Ring(nc, ph, "ct", 2, [P, D], BF16)
            cTr = Ring(nc, ph, "cT", 1, [P, 8, P], BF16)
            rr = Ring(nc, ph, "rr", 2, [P, D], F32)
            x1r = Ring(nc, ph, "x1", 2, [P, 4, D], F32)
            xn2r = Ring(nc, ph, "xn2", 1, [P, 1, D], F32)
            h2r = Ring(nc, ph, "h2T", 2, [P, 8, 512], BF16)
            actT, b_actT = sbt(nc, ph, "actT", [P, NF, 512], BF16)
            gur = Ring(nc, ph, "gu", 3, [P, 2, 8, 256], BF16)
            wdr = Ring(nc, ph, "wd", 3, [P, 2, 512], BF16)
            sgr = Ring(nc, ph, "sg", 2, [P, 512], F32)
            ygr = Ring(nc, ph, "yg", 2, [P, 512], F32)
            outr = Ring(nc, ph, "ot", 2, [P, D], F32)
            st_ring = Ring(nc, ph, "st", 4, [P, 16], F32)
            o_blocks = [(xbufs["xo_blk"](b), 4, b * 512, 0, xout(b), xbufs["xo_buf"](b), b_out(b)) for b in range(8)]
            if need_ctx:
                o_blocks.append((io["xc"], 2, SH, 1, xcout, xbufs["xc"], b_outc))

            def ln_affine(tmp, btmp, dst, bdst, g_row, b_row):
                ln_rows(st_ring, tmp, btmp, tmp, btmp)
                S.op("dve", lambda e: e.tensor_tensor(out=tmp, in0=tmp, in1=lnbc[:, g_row, :], op=ALU.mult),
                     [btmp, b_lnbc], [btmp])
                S.op("pool", lambda e: e.tensor_tensor(out=dst, in0=tmp, in1=lnbc[:, b_row, :], op=ALU.add),
                     [btmp, b_lnbc], [bdst])

            brange = [4, 8]

            def stage_a(src, nt, coff, m, dst, sbuf_src, bdst_out):
                x1s, b_x1s = x1r.next()
                h2T, b_h2T = h2r.next()
                st = dict(nt=nt, m=m, dst=dst, bdst_out=bdst_out, x1=x1s, b_x1=b_x1s, h2T=h2T, b_h2T=b_h2T)
                yield st
                for t in range(nt):
                    xt, bxt = xr.next()
                    ct, bct = ctr.next()
                    S.dma("sp", xt[:, :], src[t * P:(t + 1) * P, :], sbuf_src, bxt)
                    S.dma("sp", ct[:, :], cat[coff + t * P:coff + (t + 1) * P, :], b_cat, bct)
                    pb, bp = S.bank(*brange)
                    pbv = pb[:, :].bitcast(BF16)
                    for j in range(8):
                        S.op("pe", lambda e, pbv=pbv, j=j, ct=ct: e.transpose(
                            out=pbv[:, j * P:(j + 1) * P], in_=ct[:, j * P:(j + 1) * P], identity=ident_b[:, :]),
                            [bct, b_idb], [bp])
                    cT, bcT = cTr.next()
                    S.op("act", lambda e, pbv=pbv, cT=cT: e.copy(out=cT[:, :, :], in_=pbv.rearrange("p (j c) -> p j c", c=P)),
                         [bp], [bcT])
                    yield None
                    tmp, btmp = rr.next()
                    for hf in range(2):
                        pb, bp = S.bank(*brange)
                        for j in range(8):
                            S.op("pe", lambda e, pb=pb, j=j, hf=hf, cT=cT: e.matmul(
                                pb[:, :], lhsT=cT[:, j, :], rhs=wout[:, j, hf * 512:(hf + 1) * 512],
                                start=(j == 0), stop=(j == 7)), [bcT, b_wout], [bp])
                        S.op("dve", lambda e, pb=pb, hf=hf, tmp=tmp: e.tensor_tensor(
                            out=tmp[:, hf * 512:(hf + 1) * 512], in0=pb[:, :], in1=gbc[:, m, 0, hf * 512:(hf + 1) * 512],
                            op=ALU.mult), [bp, b_gbc], [btmp])
                    yield None
                    S.op("dve", lambda e, tmp=tmp, xt=xt: e.scalar_tensor_tensor(
                        out=tmp[:, :], in0=xt[:, :], scalar=ALPHA, in1=tmp[:, :], op0=ALU.mult, op1=ALU.add),
                        [bxt, btmp], [btmp])
                    ln_affine(tmp[:, :], btmp, x1s[:, t, :], b_x1s, 0, 1)
                    yield None
                    xn2, bxn2 = xn2r.next()
                    ln_rows(st_ring, x1s[:, t, :], b_x1s, xn2[:, 0, :], bxn2)
                    for jh in range(2):
                        pb, bp = S.bank(*brange)
                        for jj in range(4):
                            j = jh * 4 + jj
                            S.op("pe", lambda e, pb=pb, jj=jj, j=j, xn2=xn2: e.transpose(
                                out=pb[:, jj * P:(jj + 1) * P], in_=xn2[:, 0, j * P:(j + 1) * P], identity=ident_f[:, :]),
                                [bxn2, b_idf], [bp])
                        for jj in range(4):
                            j = jh * 4 + jj
                            if jj % 2 == 0:
                                S.op("act", lambda e, pb=pb, jj=jj, j=j, t=t: e.activation(
                                    out=h2T[:, j, t * P:(t + 1) * P], in_=pb[:, jj * P:(jj + 1) * P], func=AF.Identity,
                                    scale=fcol[:, 2, j, m:m + 1], bias=fcol[:, 3, j, m:m + 1]), [bp, b_fcol], [b_h2T])
                            else:
                                S.op("dve", lambda e, pb=pb, jj=jj, j=j, t=t: e.tensor_scalar(
                                    out=h2T[:, j, t * P:(t + 1) * P], in0=pb[:, jj * P:(jj + 1) * P],
                                    scalar1=fcol[:, 2, j, m:m + 1], scalar2=fcol[:, 3, j, m:m + 1],
                                    op0=ALU.mult, op1=ALU.add), [bp, b_fcol], [b_h2T])
                    yield None

            def stage_c(st):
                nt, x1s, b_x1s = st["nt"], st["x1"], st["b_x1"]
                for t in range(nt):
                    ot, bot = outr.next()
                    ln_affine(x1s[:, t, :], b_x1s, ot[:, :], bot, 2, 3)
                    S.dma("pool", st["dst"][t * P:(t + 1) * P, :], ot[:, :], bot, st["bdst_out"])
                    yield None

            def step(bg, n=1):
                for _ in range(n):
                    if bg is None:
                        return
                    try:
                        next(bg)
                    except StopIteration:
                        return

            def stage_b(st, bg):
                nt, m, x1s, b_x1s, h2T, b_h2T = st["nt"], st["m"], st["x1"], st["b_x1"], st["h2T"], st["b_h2T"]
                ntok = nt * P
                brange[0] = 0
                for c in range(11):
                    gu, bgu = gur.next()
                    S.dma("sp", gu[:, 0, :, :], io["wg_bf"][c], b_wbf, bgu)
                    S.dma("sp", gu[:, 1, :, :], io["wu_bf"][c], b_wbf, bgu)
                    for fc in range(2):
                        f = c * 2 + fc
                        pg, bpg = S.bank(*brange)
                        pu, bpu = S.bank(*brange)
                        for j in range(8):
                            S.op("pe", lambda e, pg=pg, j=j, fc=fc, gu=gu: e.matmul(
                                pg[:, 0:ntok], lhsT=gu[:, 0, j, fc * P:(fc + 1) * P], rhs=h2T[:, j, 0:ntok],
                                start=(j == 0), stop=(j == 7)), [bgu, b_h2T], [bpg])
                        for j in range(8):
                            S.op("pe", lambda e, pu=pu, j=j, fc=fc, gu=gu: e.matmul(
                                pu[:, 0:ntok], lhsT=gu[:, 1, j, fc * P:(fc + 1) * P], rhs=h2T[:, j, 0:ntok],
                                start=(j == 0), stop=(j == 7)), [bgu, b_h2T], [bpu])
                        sg, bsg = sgr.next()
                        S.op("act", lambda e, pg=pg, sg=sg: e.activation(out=sg[:, 0:ntok], in_=pg[:, 0:ntok], func=AF.Silu),
                             [bpg], [bsg])
                        S.op("dve", lambda e, pu=pu, sg=sg, f=f: e.tensor_tensor(
                            out=actT[:, f, 0:ntok], in0=pu[:, 0:ntok], in1=sg[:, 0:ntok], op=ALU.mult),
                            [bpu, bsg], [b_actT])
                    step(bg)
                brange[0] = 4
                for hf in range(2):
                    for c in range(11):
                        wd, bwd = wdr.next()
                        S.dma("sp", wd[:, :, :], io["wd_bf"][c * 256:(c + 1) * 256, hf * 512:(hf + 1) * 512]
                              .rearrange("(f p) n -> p f n", p=P), b_wbf, bwd)
                        for fc in range(2):
                            f = c * 2 + fc
                            for t in range(nt):
                                S.op("pe", lambda e, f=f, fc=fc, t=t, wd=wd: e.matmul(
                                    S.pb[t][:, :], lhsT=actT[:, f, t * P:(t + 1) * P], rhs=wd[:, fc, :],
                                    start=(f == 0), stop=(f == NF - 1)), [b_actT, bwd], [S.bpb[t]])
                        step(bg)
                    for t in range(nt):
                        yg, byg = ygr.next()
                        S.op("dve", lambda e, t=t, yg=yg, hf=hf: e.tensor_tensor(
                            out=yg[:, :], in0=S.pb[t][:, :], in1=gbc[:, m, 1, hf * 512:(hf + 1) * 512], op=ALU.mult),
                            [S.bpb[t], b_gbc], [byg])
                        S.op("dve", lambda e, t=t, yg=yg, hf=hf: e.scalar_tensor_tensor(
                            out=x1s[:, t, hf * 512:(hf + 1) * 512], in0=x1s[:, t, hf * 512:(hf + 1) * 512], scalar=ALPHA,
                            in1=yg[:, :], op0=ALU.mult, op1=ALU.add), [b_x1s, byg], [b_x1s])

            def chain(*gens):
                for g_ in gens:
                    if g_ is not None:
                        for _ in g_:
                            yield None

            nb = len(o_blocks)
            ga = stage_a(*o_blocks[0])
            st_cur = next(ga)
            for _ in ga:
                pass
            prev_c = None
            for i in range(nb):
                if i + 1 < nb:
                    ga = stage_a(*o_blocks[i + 1])
                    st_next = next(ga)
                else:
                    ga, st_next = None, None
                bg = chain(prev_c, ga)
                stage_b(st_cur, bg)
                for _ in bg:
                    pass
                prev_c = stage_c(st_cur)
                st_cur = st_next
            for _ in prev_c:
                pass
            S.end_phase()


def build_program():
    nc = bass.Bass("TRN2", target_bir_lowering=False)
    io0 = declare_layer(nc, 0, "l0_")
    x1own = [nc.dram_tensor("x1own%d" % c, [512, D], F32, kind="Internal").ap() for c in range(8)]
    x1gat = [nc.dram_tensor("x1gat%d" % c, [1024, D], F32, kind="Internal").ap() for c in range(8)]
    xc1 = nc.dram_tensor("xc1", [CTX, D], F32, kind="Internal").ap()
    io1 = declare_layer(nc, 1, "l1_", x_from_dram={"xc": xc1, "ident": io0["ident"]})
    xout = nc.dram_tensor("xout", [SH, D], F32, kind="ExternalOutput").ap()
    with ExitStack() as es:
        S = Sched(nc, es)
        ident_f = es.enter_context(nc.sbuf_tensor("ident_f", [P, P], F32)); b_idf = Buf("idf")
        ident_b = es.enter_context(nc.sbuf_tensor("ident_b", [P, P], BF16)); b_idb = Buf("idb")
        ones_f = es.enter_context(nc.sbuf_tensor("ones_f", [P, P], F32)); b_ones = Buf("ones")
        bin_ = Buf("cin")
        S.dma("sp", ident_f[:], io0["ident"], bin_, b_idf)
        S.op("dve", lambda e: e.tensor_copy(out=ident_b[:], in_=ident_f[:]), [b_idf], [b_idb])
        S.op("dve", lambda e: e.memset(ones_f[:], 1.0), [], [b_ones])
        consts = (ident_f, b_idf, ident_b, b_idb, ones_f, b_ones)
        b_xc1, b_xout = Buf("xc1"), Buf("xout")
        b_own = [Buf("x1own%d" % c) for c in range(8)]
        b_gat = [Buf("x1gat%d" % c) for c in range(8)]
        xb0 = {"xc": bin_,
               "xf_blk": lambda b: io0["xf"][b * 512:(b + 1) * 512, :], "xf_buf": lambda b: bin_,
               "xo_blk": lambda b: io0["xo"][b * 512:(b + 1) * 512, :], "xo_buf": lambda b: bin_}
        build_layer(nc, S, 0, io0, lambda b: x1own[b], xc1, consts, xb0, lambda b: b_own[b], b_xc1)
        for c in range(8):
            if NO_COLL:
                S.dma("pool", x1gat[c][0:512], x1own[c], b_own[c], b_gat[c])
                S.dma("pool", x1gat[c][512:1024], x1own[c], b_own[c], b_gat[c])
            else:
                S.coll("AllGather", [[0, 1], [2, 3], [4, 5], [6, 7]], x1own[c], x1gat[c], b_own[c], b_gat[c])
        xb1 = {"xc": b_xc1,
               "xf_blk": lambda b: x1gat[b % 8][(b // 8) * 512:(b // 8 + 1) * 512, :], "xf_buf": lambda b: b_gat[b % 8],
               "xo_blk": lambda b: x1own[b], "xo_buf": lambda b: b_own[b]}
        build_layer(nc, S, 1, io1, lambda b: xout[b * 512:(b + 1) * 512, :], None, consts, xb1, lambda b: b_xout, None)
    return nc


L0_NAMES = ["l0_w_mod", "l0_b_mod", "l0_w_in", "l0_q_norm", "l0_w_uq", "l0_kv_norm", "l0_w_ukv", "l0_w_out",
            "l0_ln1_g", "l0_ln1_b", "l0_w_gate", "l0_w_up", "l0_w_down", "l0_ln2_g", "l0_ln2_b"]
L1_NAMES = ["l1_w_mod", "l1_b_mod", "l1_w_in", "l1_lambda_q1", "l1_lambda_k1", "l1_lambda_q2", "l1_lambda_k2",
            "l1_subln", "l1_w_out", "l1_ln1_g", "l1_ln1_b", "l1_w_gate", "l1_w_up", "l1_w_down", "l1_ln2_g", "l1_ln2_b"]


def layer_inputs(L, inputs, b, half):
    pre = "l%d_" % L
    names = L0_NAMES if L == 0 else L1_NAMES
    w = {n[3:]: np.ascontiguousarray(np.asarray(inputs[n], dtype=np.float32)) for n in names}
    g = lambda n: w[n]
    t = get_tables(half)
    m = {}
    c = np.asarray(inputs["c"], np.float32)[b]
    cc = np.asarray(inputs["c_ctx"], np.float32)
    m["cv"] = np.ascontiguousarray(np.stack([c.reshape(8, P).T, cc.reshape(8, P).T], 2))
    m["w_mod"] = g("w_mod"); m["b_mod"] = g("b_mod").reshape(1, -1)
    m["w_out"] = g("w_out"); m["w_gate"] = g("w_gate"); m["w_up"] = g("w_up"); m["w_down"] = g("w_down")
    m["ln"] = np.ascontiguousarray(np.stack([g("ln1_g"), g("ln1_b"), g("ln2_g"), g("ln2_b")], 0))
    w_in = g("w_in")
    if L == 0:
        m["ident"] = t["ident"]
        m["w_in"] = w_in
        sw = rope_swap_index(32)
        m["w_kr_sw"] = np.ascontiguousarray(w_in[:, 1024 + sw])
        m["qn"] = np.ascontiguousarray(g("q_norm").reshape(2, P).T)
        m["kvn"] = np.ascontiguousarray(g("kv_norm").reshape(2, P).T)
        w_uq = g("w_uq")
        perm = np.arange(768)
        for h in range(8):
            perm[h * 96 + 64:h * 96 + 96] = h * 96 + 64 + sw
        m["w_uq"] = w_uq
        m["w_uq_sw"] = np.ascontiguousarray(w_uq[:, perm])
        w_ukv = g("w_ukv").reshape(256, 8, 128)
        m["w_ukv_kn"] = np.ascontiguousarray(w_ukv[:, :, 0:64].reshape(256, 512))
        m["w_ukv_v"] = np.ascontiguousarray(w_ukv[:, :, 64:128].reshape(256, 512))
        m["cosK"], m["sinK"], m["cosQ"], m["sinQ"] = t["cosK0"], t["sinK0"], t["cosQ0"], t["sinQ0"]
        for k in ("F64", "GAB", "CdSl", "CdSc", "F256"):
            m[k] = t[k]
    else:
        m["w_in"] = w_in
        sw = rope_swap_index(64)
        perm = np.arange(2048)
        for blk in range(32):
            perm[blk * 64:(blk + 1) * 64] = blk * 64 + sw
        m["w_qk_sw"] = np.ascontiguousarray(w_in[:, perm])
        m["lam"] = np.ascontiguousarray(np.stack([g("lambda_q1"), g("lambda_k1"), g("lambda_q2"), g("lambda_k2")], 0))
        m["subln"] = g("subln").reshape(1, 128)
        m["cosK"], m["sinK"], m["cosQ"], m["sinQ"] = t["cosK1"], t["sinK1"], t["cosQ1"], t["sinQ1"]
    return {pre + k: v for k, v in m.items()}


_PROG = {}


def kernel(**inputs):
    x = np.asarray(inputs["x"], np.float32)
    xc = np.asarray(inputs["ctx"], np.float32)
    if "nc" not in _PROG:
        _PROG["nc"] = build_program()
    nc = _PROG["nc"]
    in_maps = []
    for k in range(8):
        b, half = k // 2, k % 2
        m = {}
        m.update(layer_inputs(0, inputs, b, half))
        m.update(layer_inputs(1, inputs, b, half))
        m["l0_xf"] = np.ascontiguousarray(x[b])
        m["l0_xo"] = np.ascontiguousarray(x[b, half * SH:(half + 1) * SH])
        m["l0_xc"] = np.ascontiguousarray(xc[b])
        in_maps.append(m)
    res = run_bass_kernel_spmd(nc, in_maps, core_ids=list(range(8)))
    r = res.results
    out = np.stack([np.concatenate([r[2 * b]["xout"], r[2 * b + 1]["xout"]], 0) for b in range(4)], 0)
    return out.astype(np.float32)
```

```python
import math
import numpy as np
from contextlib import ExitStack
import concourse.bass as bass
import concourse.mybir as mybir
from concourse.bass_utils import run_bass_kernel_spmd

F32 = mybir.dt.float32
BF16 = mybir.dt.bfloat16
AF = mybir.ActivationFunctionType
ALU = mybir.AluOpType

P = 128
D = 1024
SEQ = 8192
SH = 4096
CTX = 256
NK = SEQ + CTX
FF = 2816
NF = 22
ALPHA = 4.0 ** 0.25
EPS = 1e-6
MLA_SCALE = 96.0 ** -0.5
DIFF_SCALE = 64.0 ** -0.5
LAMBDA_INIT1 = 0.8 - 0.6 * math.exp(-0.3)

ENGS = ["pe", "act", "dve", "pool", "sp"]
DEBUG_SCR = False
NO_COLL = False


class Buf:
    __slots__ = ("name", "w", "r", "dsem", "dcnt")

    def __init__(self, name):
        self.name = name
        self.w = None
        self.r = []
        self.dsem = None
        self.dcnt = 0


class Sched:
    def __init__(self, nc, es):
        self.nc = nc
        self.es = es
        self.ops = {e: [] for e in ENGS}
        self.sem = {e: es.enter_context(nc.semaphore("s_" + e)) for e in ENGS}
        self.cnt = {e: 0 for e in ENGS}
        self.seen = {e: {} for e in ENGS}
        self.sem_pool = {"sp": [], "pool": [], "act": []}
        self.phase_bufs = []
        self.nsem = 0
        self.pbig = es.enter_context(nc.psum_tensor("pbig", [P, 4096], F32))
        self.pb = [self.pbig[:, i * 512:(i + 1) * 512] for i in range(8)]
        self.bpb = [Buf("pb%d" % i) for i in range(8)]
        self.pbi = 0

    def bank(self, lo=0, hi=8):
        i = self.pbi
        if i < lo or i >= hi:
            i = lo
        self.pbi = i + 1
        return self.pb[i], self.bpb[i]

    def _need(self, e, ev, waits):
        if ev is None:
            return
        sem, val = ev
        if sem is self.sem[e] and e in ("pe", "sp"):
            return
        k = id(sem)
        if self.seen[e].get(k, 0) >= val:
            return
        self.seen[e][k] = val
        waits.append((sem, val))

    def _deps(self, e, reads, writes):
        waits = []
        for b in reads:
            self._need(e, b.w, waits)
        for b in writes:
            self._need(e, b.w, waits)
            for ev in b.r:
                self._need(e, ev, waits)
        return waits

    def op(self, e, fn, reads=(), writes=()):
        waits = self._deps(e, reads, writes)
        self.cnt[e] += 1
        ev = (self.sem[e], self.cnt[e])
        self.ops[e].append((waits, fn, (self.sem[e], 1)))
        for b in reads:
            b.r.append(ev)
        for b in writes:
            b.w = ev
            b.r = []
        return ev

    def dma(self, q, out_ap, in_ap, src, dst, **kw):
        srcs = src if isinstance(src, (list, tuple)) else [src]
        dsts = dst if isinstance(dst, (list, tuple)) else [dst]
        waits = self._deps(q, srcs, dsts)
        d0 = dsts[0]
        if d0.dsem is None:
            if self.sem_pool[q]:
                d0.dsem, d0.dcnt = self.sem_pool[q].pop()
            else:
                d0.dsem = self.es.enter_context(self.nc.semaphore("d%d" % self.nsem))
                self.nsem += 1
                d0.dcnt = 0
            self.phase_bufs.append((d0, q))
        else:
            assert any(b is d0 and qq == q for b, qq in self.phase_bufs), "buffer %s written by DMAs of two queues" % d0.name
        d0.dcnt += 16
        ev = (d0.dsem, d0.dcnt)
        self.ops[q].append(
            (waits, lambda eng: eng.dma_start(out=out_ap, in_=in_ap, **kw), (d0.dsem, 16)))
        for b in srcs:
            b.r.append(ev)
        for b in dsts:
            b.w = ev
            b.r = []
        return ev

    def coll(self, kind, groups, in_ap, out_ap, src, dst):
        waits = self._deps("pool", [src], [dst])
        if not hasattr(self, "cc_sem"):
            self.cc_sem = self.es.enter_context(self.nc.semaphore("cc_sem"))
            self.cc_cnt = 0
        self.cc_cnt += 1
        ev = (self.cc_sem, self.cc_cnt)
        sem = self.cc_sem
        self.ops["pool"].append(
            (waits, lambda eng: eng.collective_compute(kind, ALU.bypass, replica_groups=groups,
                                                       ins=[in_ap.opt()], outs=[out_ap.opt()]), (sem, 1)))
        src.r.append(ev)
        dst.w = ev
        dst.r = []
        return ev

    def end_phase(self):
        for b, _q in self.phase_bufs:
            waits = []
            self._need("sp", (b.dsem, b.dcnt), waits)
            if waits:
                self.ops["sp"].append((waits, None, None))
        nc = self.nc
        ops = self.ops

        def replay(e, eng):
            for waits, fn, inc in ops[e]:
                for sem, val in waits:
                    eng.wait_ge(sem, val)
                if fn is not None:
                    fn(eng).then_inc(inc[0], inc[1])

        with nc.Block() as block:
            @block.tensor
            def _(eng):
                replay("pe", eng)

            @block.scalar
            def _(eng):
                replay("act", eng)

            @block.vector
            def _(eng):
                replay("dve", eng)

            @block.gpsimd
            def _(eng):
                replay("pool", eng)

            @block.sync
            def _(eng):
                replay("sp", eng)
        self.ops = {e: [] for e in ENGS}
        for b, q in self.phase_bufs:
            self.sem_pool[q].append((b.dsem, b.dcnt))
            b.dsem = None
        self.phase_bufs = []


_UID = [0]


def _uname(name):
    _UID[0] += 1
    return "sb%d_%s" % (_UID[0], name)


class Ring:
    def __init__(self, nc, ph, name, n, shape, dt):
        self.t = [ph.enter_context(nc.sbuf_tensor(_uname("%s%d" % (name, i)), list(shape), dt)) for i in range(n)]
        self.b = [Buf("%s%d" % (name, i)) for i in range(n)]
        self.i = 0

    def next(self):
        i = self.i
        self.i = (i + 1) % len(self.t)
        return self.t[i], self.b[i]


def sbt(nc, ph, name, shape, dt):
    return ph.enter_context(nc.sbuf_tensor(_uname(name), list(shape), dt)), Buf(name)


def rope_tables(dim, positions_rc):
    q = dim // 4
    inv = 1.0 / (10000.0 ** (np.arange(q, dtype=np.float64) / q))
    n = positions_rc.shape[0]
    cos = np.ones((dim, n), np.float64)
    sin = np.zeros((dim, n), np.float64)
    valid = positions_rc[:, 0] >= 0
    for d in range(dim):
        a = d // (2 * q)
        w = d % (2 * q)
        fi = w % q
        first = w < q
        ang = positions_rc[:, a].astype(np.float64) * inv[fi]
        cos[d] = np.where(valid, np.cos(ang), 1.0)
        s = np.sin(ang)
        sin[d] = np.where(valid, -s if first else s, 0.0)
    return cos, sin


def rope_swap_index(dim):
    q = dim // 4
    idx = np.zeros(dim, np.int64)
    for d in range(dim):
        w = d % (2 * q)
        idx[d] = d + q if w < q else d - q
    return idx


def pos_rc(tokens):
    tokens = np.asarray(tokens)
    return np.stack([tokens // 64, tokens % 64], 1)


_TABLE_CACHE = {}


def get_tables(half):
    if half in _TABLE_CACHE:
        return _TABLE_CACHE[half]
    t = {}
    neg = -np.ones((CTX, 2), np.int64)
    key_pos = np.concatenate([neg, pos_rc(np.arange(SEQ))], 0)
    own_pos = pos_rc(half * SH + np.arange(SH))
    ck, sk = rope_tables(32, key_pos)
    t["cosK0"], t["sinK0"] = ck.astype(np.float32), sk.astype(np.float32)
    cq, sq = rope_tables(32, np.concatenate([own_pos, neg], 0))
    cq96 = np.ones((96, SH + CTX)); sq96 = np.zeros((96, SH + CTX))
    cq96[64:] = cq; sq96[64:] = sq
    t["cosQ0"], t["sinQ0"] = cq96.astype(np.float32), sq96.astype(np.float32)
    ck, sk = rope_tables(64, key_pos)
    t["cosK1"] = np.concatenate([ck, ck], 0).astype(np.float32)
    t["sinK1"] = np.concatenate([sk, sk], 0).astype(np.float32)
    cq, sq = rope_tables(64, own_pos)
    t["cosQ1"] = np.concatenate([cq, cq], 0).astype(np.float32)
    t["sinQ1"] = np.concatenate([sq, sq], 0).astype(np.float32)
    n2 = np.arange(64)[:, None]; k2 = np.arange(64)[None, :]
    a = 2 * np.pi * n2 * k2 / 64
    t["F64"] = np.concatenate([np.cos(a), -np.sin(a)], 1).astype(np.float32)
    n1 = np.arange(128)[:, None, None]
    kk = (64 * (64 * half + np.arange(64))[None, None, :] + np.arange(64)[None, :, None])
    a = 2 * np.pi * ((n1 * kk) % SEQ) / SEQ
    Gr, Gi = np.cos(a), -np.sin(a)
    GA = np.concatenate([Gr, Gi], 2); GB = np.concatenate([-Gi, Gr], 2)
    t["GAB"] = np.stack([GA, GB], 2).astype(np.float32)
    ch = np.arange(128)[:, None]; ch2 = np.arange(128)[None, :]
    a = 2 * np.pi * ch * ch2 / 128
    cds = np.stack([np.cos(a), np.sin(a)], 1)
    t["CdSl"] = (cds / math.sqrt(SEQ * 128)).astype(np.float32)
    t["CdSc"] = (cds / math.sqrt(CTX * 128)).astype(np.float32)
    n = (np.arange(2)[None, :, None] * 128 + np.arange(128)[:, None, None])
    k = np.arange(256)[None, None, :]
    a = 2 * np.pi * ((n * k) % 256) / 256
    t["F256"] = np.concatenate([np.cos(a), -np.sin(a)], 2).astype(np.float32)
    t["ident"] = np.eye(128, dtype=np.float32)
    sw64 = rope_swap_index(64)
    idx128 = np.concatenate([sw64, 64 + sw64])
    pm = np.zeros((128, 128), np.float32)
    pm[idx128, np.arange(128)] = 1.0
    t["permT"] = pm
    _TABLE_CACHE[half] = t
    return t


class LayerIO:
    pass


def declare_layer(nc, L, pfx, x_from_dram=None):
    io = {}

    def inp(name, shape, dt=F32):
        io[name] = nc.dram_tensor(pfx + name, list(shape), dt, kind="ExternalInput").ap()

    def scr(name, shape, dt):
        kind = "ExternalOutput" if (DEBUG_SCR and name in ("Fdl", "Fdc", "KT", "Vs", "QT", "cat")) else "Internal"
        io[name] = nc.dram_tensor(pfx + name, list(shape), dt, kind=kind).ap()

    if x_from_dram is None:
        inp("xf", [SEQ, D]); inp("xo", [SH, D]); inp("xc", [CTX, D])
    else:
        io.update(x_from_dram)
    inp("cv", [P, 8, 2])
    inp("w_mod", [D, 6 * D]); inp("b_mod", [1, 6 * D])
    inp("w_out", [D, D]); inp("w_gate", [D, FF]); inp("w_up", [D, FF]); inp("w_down", [FF, D])
    inp("ln", [4, D])
    if "ident" not in io:
        inp("ident", [P, P])
    if L == 0:
        inp("w_in", [D, 1056]); inp("w_kr_sw", [D, 32])
        inp("qn", [P, 2]); inp("kvn", [P, 2])
        inp("w_uq", [256, 768]); inp("w_uq_sw", [256, 768]); inp("w_ukv_kn", [256, 512]); inp("w_ukv_v", [256, 512])
        inp("cosK", [32, NK]); inp("sinK", [32, NK]); inp("cosQ", [96, SH + CTX]); inp("sinQ", [96, SH + CTX])
        inp("F64", [64, 128]); inp("GAB", [P, 64, 2, 128]); inp("CdSl", [P, 2, 128]); inp("CdSc", [P, 2, 128])
        inp("F256", [P, 2, 512])
        scr("Fdl", [SEQ, 512], F32); scr("Fdc", [CTX, 512], F32)
        scr("KT", [8, 96, NK], BF16); scr("Vs", [NK, 8, 65], BF16); scr("QT", [8, 96, SH + CTX], BF16)
        scr("cat", [SH + CTX, D], BF16)
    else:
        inp("w_in", [D, 3072]); inp("permT", [P, P])
        inp("lam", [4, 64]); inp("subln", [1, 128])
        inp("cosK", [P, NK]); inp("sinK", [P, NK]); inp("cosQ", [P, SH]); inp("sinQ", [P, SH])
        scr("KT", [8, 128, NK], BF16); scr("Vs", [NK, 8, 129], BF16); scr("QT", [8, 128, SH], BF16)
        scr("cat", [SH, D], BF16)
    scr("wg_bf", [11, P, 8, 256], BF16); scr("wu_bf", [11, P, 8, 256], BF16); scr("wd_bf", [FF, D], BF16)
    return io


def build_layer(nc, S, L, io, xout, xcout, consts, xbufs, b_out, b_outc):
    need_ctx = (L == 0)
    ident_f, b_idf, ident_b, b_idb, ones_f, b_ones = consts
    bin_ = Buf("ext_in")
    b_wbf = Buf("wbf%d" % L)

    with ExitStack() as lay:
        fcol, b_fcol = sbt(nc, lay, "fcol", [P, 4, 8, 2], F32)
        gbc, b_gbc = sbt(nc, lay, "gbc", [P, 2, 2, D], F32)
        lnbc, b_lnbc = sbt(nc, lay, "lnbc", [P, 4, D], F32)
        S.dma("sp", lnbc[:], io["ln"].partition_broadcast(P), bin_, b_lnbc)

        with ExitStack() as ph:
            cv, b_cv = sbt(nc, ph, "cv", [P, 8, 2], F32)
            sl, b_sl = sbt(nc, ph, "sl", [P, 8, 2], F32)
            rep, b_rep = sbt(nc, ph, "rep", [P, 2, 8, P], F32)
            wm = Ring(nc, ph, "wm", 4, [P, 8, 512], F32)
            bm = Ring(nc, ph, "bm", 4, [1, 512], F32)
            S.dma("sp", cv[:], io["cv"], bin_, b_cv)
            S.op("act", lambda e: e.activation(out=sl[:], in_=cv[:], func=AF.Silu), [b_cv], [b_sl])
            for m in range(2):
                for j in range(8):
                    S.op("dve", lambda e, m=m, j=j: e.tensor_scalar(
                        out=rep[:, m, j, :], in0=ones_f[:, :], scalar1=sl[:, j, m:m + 1], scalar2=None,
                        op0=ALU.mult), [b_sl, b_ones], [b_rep])
            vmap = {1: 0, 0: 1, 4: 2, 3: 3}
            for cb in range(12):
                c6, hf = cb // 2, cb % 2
                wt, bw = wm.next()
                bt, bb = bm.next()
                for jh in range(2):
                    S.dma("sp", wt[:, jh * 4:(jh + 1) * 4, :],
                          io["w_mod"][jh * 512:(jh + 1) * 512, cb * 512:(cb + 1) * 512].rearrange("(j p) n -> p j n", p=P),
                          bin_, bw)
                S.dma("sp", bt[:], io["b_mod"][:, cb * 512:(cb + 1) * 512], bin_, bb)
                if c6 in (2, 5):
                    gi = 0 if c6 == 2 else 1
                    for m in range(2):
                        pb, bp = S.bank()
                        for j in range(8):
                            S.op("pe", lambda e, pb=pb, m=m, j=j, wt=wt: e.matmul(
                                pb[:, :], lhsT=rep[:, m, j, :], rhs=wt[:, j, :], start=(j == 0), stop=False),
                                [b_rep, bw], [bp])
                        S.op("pe", lambda e, pb=pb, bt=bt: e.matmul(
                            pb[:, :], lhsT=ones_f[0:1, :], rhs=bt[0:1, :], start=False, stop=True),
                            [b_ones, bb], [bp])
                        S.op("dve", lambda e, pb=pb, m=m, gi=gi, hf=hf: e.tensor_copy(
                            out=gbc[:, m, gi, hf * 512:(hf + 1) * 512], in_=pb[:, :]), [bp], [b_gbc])
                else:
                    v = vmap[c6]
                    pb, bp = S.bank()
                    for q in range(4):
                        for j in range(8):
                            S.op("pe", lambda e, pb=pb, q=q, j=j, wt=wt: e.matmul(
                                pb[:, 2 * q:2 * q + 2], lhsT=wt[:, j, q * P:(q + 1) * P], rhs=sl[:, j, :],
                                start=(j == 0), stop=False), [b_sl, bw], [bp])
                        S.op("pe", lambda e, pb=pb, q=q, bt=bt: e.matmul(
                            pb[:, 2 * q:2 * q + 2], lhsT=bt[0:1, q * P:(q + 1) * P], rhs=ones_f[0:1, 0:2],
                            start=False, stop=True), [b_ones, bb], [bp])
                    S.op("dve", lambda e, pb=pb, v=v, hf=hf: e.tensor_copy(
                        out=fcol[:, v, hf * 4:hf * 4 + 4, :], in_=pb[:, 0:8].rearrange("p (q m) -> p q m", m=2)),
                        [bp], [b_fcol])
            for v in (0, 2):
                S.op("dve", lambda e, v=v: e.tensor_scalar(
                    out=fcol[:, v, :, :], in0=fcol[:, v, :, :], scalar1=1.0, scalar2=None, op0=ALU.add),
                    [b_fcol], [b_fcol])
            S.end_phase()

        def ln_rows(st_ring, xin, bx, out, bo, reads_extra=(), on_act=True):
            stt, bst = st_ring.next()
            for i in range(2):
                S.op("dve", lambda e, i=i, stt=stt: e.bn_stats(out=stt[:, 6 * i:6 * i + 6],
                                                              in_=xin[:, 512 * i:512 * i + 512]),
                     [bx] + list(reads_extra), [bst])
            S.op("dve", lambda e, stt=stt: e.bn_aggr(out=stt[:, 12:14], in_=stt[:, 0:12]), [bst], [bst])
            S.op("act", lambda e, stt=stt: e.activation(out=stt[:, 14:15], in_=stt[:, 13:14], func=AF.Sqrt,
                                                       bias=EPS, scale=1.0), [bst], [bst])
            S.op("dve", lambda e, stt=stt: e.reciprocal(out=stt[:, 14:15], in_=stt[:, 14:15]), [bst], [bst])
            if on_act:
                S.op("dve", lambda e, stt=stt: e.tensor_scalar(out=stt[:, 15:16], in0=stt[:, 12:13], scalar1=stt[:, 14:15],
                                                              scalar2=-1.0, op0=ALU.mult, op1=ALU.mult), [bst], [bst])
                S.op("act", lambda e, stt=stt: e.activation(out=out, in_=xin, func=AF.Identity, scale=stt[:, 14:15],
                                                           bias=stt[:, 15:16]), [bx, bst], [bo])
            else:
                S.op("dve", lambda e, stt=stt: e.tensor_scalar(out=out, in0=xin, scalar1=stt[:, 12:13],
                                                              scalar2=stt[:, 14:15], op0=ALU.subtract, op1=ALU.mult),
                     [bx, bst], [bo])

        def ln_block(st_ring, xb, bxb, xn_, bxn_, nt):
            stt, bst = st_ring.next()
            for t in range(nt):
                for i in range(2):
                    S.op("dve", lambda e, i=i, t=t, stt=stt: e.bn_stats(out=stt[:, 12 * t + 6 * i:12 * t + 6 * i + 6],
                                                                       in_=xb[:, t, 512 * i:512 * i + 512]), [bxb], [bst])
            for t in range(nt):
                S.op("dve", lambda e, t=t, stt=stt: e.bn_aggr(out=stt[:, 48 + 2 * t:50 + 2 * t], in_=stt[:, 12 * t:12 * t + 12]),
                     [bst], [bst])
            mv = stt[:, 48:56].rearrange("p (t c) -> p t c", c=2)
            S.op("act", lambda e, stt=stt: e.activation(out=stt[:, 56:56 + nt], in_=mv[:, 0:nt, 1], func=AF.Sqrt,
                                                       bias=EPS, scale=1.0), [bst], [bst])
            S.op("dve", lambda e, stt=stt: e.reciprocal(out=stt[:, 56:56 + nt], in_=stt[:, 56:56 + nt]), [bst], [bst])
            S.op("dve", lambda e, stt=stt: e.scalar_tensor_tensor(out=stt[:, 60:60 + nt], in0=mv[:, 0:nt, 0], scalar=-1.0,
                                                                 in1=stt[:, 56:56 + nt], op0=ALU.mult, op1=ALU.mult),
                 [bst], [bst])
            for t in range(nt):
                S.op("act", lambda e, t=t, stt=stt: e.activation(out=xn_[:, t, :], in_=xb[:, t, :], func=AF.Identity,
                                                                scale=stt[:, 56 + t:57 + t], bias=stt[:, 60 + t:61 + t]),
                     [bxb, bst], [bxn_])

        def make_hT(xn, bxn, nt, hT, bhT, vs, vb, m):
            for j in range(8):
                pb, bp = S.bank()
                for t in range(nt):
                    S.op("pe", lambda e, pb=pb, t=t, j=j: e.transpose(
                        out=pb[:, t * P:(t + 1) * P], in_=xn[:, t, j * P:(j + 1) * P], identity=ident_f[:, :]),
                        [bxn, b_idf], [bp])
                if j % 4 != 3:
                    S.op("act", lambda e, pb=pb, j=j: e.activation(
                        out=hT[:, j, 0:nt * P], in_=pb[:, 0:nt * P], func=AF.Identity,
                        scale=fcol[:, vs, j, m:m + 1], bias=fcol[:, vb, j, m:m + 1]), [bp, b_fcol], [bhT])
                else:
                    S.op("dve", lambda e, pb=pb, j=j: e.tensor_scalar(
                        out=hT[:, j, 0:nt * P], in0=pb[:, 0:nt * P], scalar1=fcol[:, vs, j, m:m + 1],
                        scalar2=fcol[:, vb, j, m:m + 1], op0=ALU.mult, op1=ALU.add), [bp, b_fcol], [bhT])

        def rms_T(src_banks, nrm, b_nrm, ntok, sq, b_sq, rb, b_rb, outT, b_out):
            for m2 in range(2):
                pbk, bpk = src_banks[m2]
                S.op("act", lambda e, pbk=pbk, m2=m2: e.activation(out=sq[:, m2, 0:ntok], in_=pbk[:, 0:ntok],
                                                                 func=AF.Square), [bpk], [b_sq])
            pb, bp = S.bank()
            for m2 in range(2):
                S.op("pe", lambda e, pb=pb, m2=m2: e.matmul(pb[:, 0:ntok], lhsT=ones_f[:, :], rhs=sq[:, m2, 0:ntok],
                                                           start=(m2 == 0), stop=(m2 == 1)), [b_sq, b_ones], [bp])
            S.op("act", lambda e, pb=pb: e.activation(out=rb[:, 0:ntok], in_=pb[:, 0:ntok], func=AF.Sqrt,
                                                     bias=EPS, scale=1.0 / 256.0), [bp], [b_rb])
            S.op("dve", lambda e: e.reciprocal(out=rb[:, 0:ntok], in_=rb[:, 0:ntok]), [b_rb], [b_rb])
            for m2 in range(2):
                pbk, bpk = src_banks[m2]
                S.op("dve", lambda e, pbk=pbk, m2=m2: e.scalar_tensor_tensor(
                    out=outT[:, m2, 0:ntok], in0=pbk[:, 0:ntok], scalar=nrm[:, m2:m2 + 1], in1=rb[:, 0:ntok],
                    op0=ALU.mult, op1=ALU.mult), [bpk, b_nrm, b_rb], [b_out])

        def rope_out(pa, bpa, pbk, bpbk, rows, ntok, cs, b_cs, sn, b_sn, t1, b_t1, t2, b_t2, out, b_o, extra=()):
            S.op("dve", lambda e: e.tensor_tensor(out=t1[0:rows, 0:ntok], in0=pa[0:rows, 0:ntok],
                                                  in1=cs[0:rows, 0:ntok], op=ALU.mult), [bpa, b_cs] + list(extra), [b_t1])
            S.op("dve", lambda e: e.tensor_tensor(out=t2[0:rows, 0:ntok], in0=pbk[0:rows, 0:ntok],
                                                  in1=sn[0:rows, 0:ntok], op=ALU.mult), [bpbk, b_sn], [b_t2])
            S.op("pool", lambda e: e.tensor_tensor(out=out, in0=t1[0:rows, 0:ntok], in1=t2[0:rows, 0:ntok],
                                                   op=ALU.add), [b_t1, b_t2], [b_o])

        kv_blocks = [(io["xc"], 2, 0, 1, xbufs["xc"])] + [(xbufs["xf_blk"](b), 4, CTX + b * 512, 0, xbufs["xf_buf"](b)) for b in range(16)]
        q_blocks = [(xbufs["xo_blk"](b), 4, b * 512, 0, xbufs["xo_buf"](b)) for b in range(8)]
        if need_ctx:
            q_blocks.append((io["xc"], 2, SH, 1, xbufs["xc"]))
        b_KT, b_Vs, b_QT, b_cat = Buf("KT"), Buf("Vs"), Buf("QT"), Buf("cat")
        b_Fdl, b_Fdc = Buf("Fdl"), Buf("Fdc")
        KT, Vs, QT, cat = io["KT"], io["Vs"], io["QT"], io["cat"]
        vw = 65 if L == 0 else 129
        R = 96 if L == 0 else 128

        with ExitStack() as ph:
            xr = Ring(nc, ph, "xr", 1, [P, 4, D], F32)
            xnr = Ring(nc, ph, "xn", 2, [P, 4, D], F32)
            stb_ring = Ring(nc, ph, "stb", 2, [P, 64], F32)
            hTr = Ring(nc, ph, "hT", 2, [P, 8, 512], BF16)
            st_ring = Ring(nc, ph, "st", 4, [P, 16], F32)
            csr = Ring(nc, ph, "csk", 2, [R if L == 1 else 32, 512], F32)
            snr = Ring(nc, ph, "snk", 2, [R if L == 1 else 32, 512], F32)
            t1, b_t1 = sbt(nc, ph, "t1", [P, 512], F32)
            t2, b_t2 = sbt(nc, ph, "t2", [P, 512], F32)
            vsr = Ring(nc, ph, "vsb", 2, [P, 4, 8, vw], BF16)
            for i in range(2):
                S.op("pool", lambda e, i=i: e.memset(vsr.t[i][:], 1.0), [], [vsr.b[i]])
            if L == 0:
                win, b_win = sbt(nc, ph, "win", [P, 8, 1056], BF16)
                wkr, b_wkr = sbt(nc, ph, "wkr", [P, 8, 32], BF16)
                wkn, b_wkn = sbt(nc, ph, "wkn", [P, 2, 512], BF16)
                wv, b_wv = sbt(nc, ph, "wv", [P, 2, 512], BF16)
                kvn, b_kvn = sbt(nc, ph, "kvn", [P, 2], F32)
                S.dma("pool", win[:], io["w_in"].rearrange("(j p) n -> p j n", p=P), bin_, b_win)
                S.dma("pool", wkr[:], io["w_kr_sw"].rearrange("(j p) n -> p j n", p=P), bin_, b_wkr)
                S.dma("pool", wkn[:], io["w_ukv_kn"].rearrange("(j p) n -> p j n", p=P), bin_, b_wkn)
                S.dma("pool", wv[:], io["w_ukv_v"].rearrange("(j p) n -> p j n", p=P), bin_, b_wv)
                S.dma("sp", kvn[:], io["kvn"], bin_, b_kvn)
                fsr = Ring(nc, ph, "fsb", 2, [P, 4, 512], F32)
                sq, b_sq = sbt(nc, ph, "sq", [P, 2, 512], F32)
                rb, b_rb = sbt(nc, ph, "rb", [P, 512], F32)
                ckvn, b_ckvn = sbt(nc, ph, "ckvn", [P, 2, 512], BF16)
                knr = Ring(nc, ph, "knT", 2, [P, 4, 512], BF16)
                krr = Ring(nc, ph, "krT", 2, [32, 512], BF16)
            else:
                win, b_win = sbt(nc, ph, "win", [P, 8, 2048], BF16)
                pmb, b_pmb = sbt(nc, ph, "pmb", [P, P], BF16)
                kbr = Ring(nc, ph, "kb", 3, [P, 512], BF16)
                for kvh in range(2):
                    S.dma("pool", win[:, :, kvh * 1024:(kvh + 1) * 1024],
                          io["w_in"][:, 1024 + kvh * 1024:2048 + kvh * 1024].rearrange("(j p) n -> p j n", p=P), bin_, b_win)
                S.dma("pool", pmb[:], io["permT"], bin_, b_pmb)
                knr = Ring(nc, ph, "knT", 2, [P, 8, 512], BF16)

            def kv_a(src, nt, koff, m, sbuf_src):
                ntok = nt * P
                xb, bxb = xr.next()
                S.dma("sp", xb[:, 0:nt, :], src.rearrange("(t p) d -> p t d", p=P), sbuf_src, bxb)
                cs, b_cs = csr.next()
                sn, b_sn = snr.next()
                S.dma("sp", cs[:, 0:ntok], io["cosK"][:, koff:koff + ntok], bin_, b_cs)
                S.dma("sp", sn[:, 0:ntok], io["sinK"][:, koff:koff + ntok], bin_, b_sn)
                xn_, bxn_ = xnr.next()
                ln_block(stb_ring, xb, bxb, xn_, bxn_, nt)
                return dict(nt=nt, koff=koff, m=m, cs=cs, b_cs=b_cs, sn=sn, b_sn=b_sn, xn=xn_, bxn=bxn_)

            def kv_b(st):
                hT, bhT = hTr.next()
                make_hT(st["xn"], st["bxn"], st["nt"], hT, bhT, 0, 1, st["m"])
                st["hT"], st["bhT"] = hT, bhT

            def kv_body(st):
                nt, koff, m = st["nt"], st["koff"], st["m"]
                cs, b_cs, sn, b_sn, hT, bhT = st["cs"], st["b_cs"], st["sn"], st["b_sn"], st["hT"], st["bhT"]
                ntok = nt * P
                vsb, bvs = vsr.next()
                if L == 0:
                    fsb, bfs = fsr.next()
                    for t in range(nt):
                        pb, bp = S.bank()
                        for j in range(8):
                            S.op("pe", lambda e, pb=pb, t=t, j=j, hT=hT: e.matmul(
                                pb[:, :], lhsT=hT[:, j, t * P:(t + 1) * P], rhs=win[:, j, 0:512],
                                start=(j == 0), stop=(j == 7)), [bhT, b_win], [bp])
                        S.op("act", lambda e, pb=pb, t=t, fsb=fsb: e.copy(out=fsb[:, t, :], in_=pb[:, :]), [bp], [bfs])
                    if m == 1:
                        S.dma("pool", io["Fdc"].rearrange("(t p) c -> p t c", p=P), fsb[:, 0:nt, :], bfs, b_Fdc)
                    else:
                        n0 = koff - CTX
                        S.dma("pool", io["Fdl"][n0:n0 + ntok, :].rearrange("(t p) c -> p t c", p=P), fsb[:, 0:nt, :],
                              bfs, b_Fdl)
                    banks = []
                    for m2 in range(2):
                        pb, bp = S.bank()
                        banks.append((pb, bp))
                        for j in range(8):
                            S.op("pe", lambda e, pb=pb, m2=m2, j=j, hT=hT: e.matmul(
                                pb[:, 0:ntok], lhsT=win[:, j, 768 + m2 * P:768 + (m2 + 1) * P], rhs=hT[:, j, 0:ntok],
                                start=(j == 0), stop=(j == 7)), [bhT, b_win], [bp])
                    rms_T(banks, kvn, b_kvn, ntok, sq, b_sq, rb, b_rb, ckvn, b_ckvn)
                    knT, bkn = knr.next()
                    for hp in range(4):
                        pb, bp = S.bank()
                        for m2 in range(2):
                            S.op("pe", lambda e, pb=pb, hp=hp, m2=m2: e.matmul(
                                pb[:, 0:ntok],
                                lhsT=wkn[:, m2, hp * P:(hp + 1) * P],
                                rhs=ckvn[:, m2, 0:ntok], start=(m2 == 0), stop=(m2 == 1)), [b_ckvn, b_wkn], [bp])
                        S.op("act", lambda e, pb=pb, hp=hp, knT=knT: e.copy(out=knT[:, hp, 0:ntok], in_=pb[:, 0:ntok]),
                             [bp], [bkn])
                    for hh in range(2):
                        S.dma("pool", KT.rearrange("(hp hh) r n -> hh r hp n", hh=2)[hh, 0:64, :, koff:koff + ntok],
                              knT[hh * 64:(hh + 1) * 64, :, 0:ntok], bkn, b_KT)
                    for t in range(nt):
                        pb, bp = S.bank()
                        for m2 in range(2):
                            S.op("pe", lambda e, pb=pb, t=t, m2=m2: e.matmul(
                                pb[:, :], lhsT=ckvn[:, m2, t * P:(t + 1) * P],
                                rhs=wv[:, m2, :],
                                start=(m2 == 0), stop=(m2 == 1)), [b_ckvn, b_wv], [bp])
                        S.op("dve", lambda e, pb=pb, t=t, vsb=vsb: e.tensor_copy(
                            out=vsb[:, t, :, 0:64], in_=pb[:, :].rearrange("p (h c) -> p h c", c=64)), [bp], [bvs])
                    pa, bpa = S.bank()
                    pbk, bpbk = S.bank()
                    for j in range(8):
                        S.op("pe", lambda e, pa=pa, j=j, hT=hT: e.matmul(
                            pa[0:32, 0:ntok], lhsT=win[:, j, 1024:1056], rhs=hT[:, j, 0:ntok],
                            start=(j == 0), stop=(j == 7)), [bhT, b_win], [bpa])
                    for j in range(8):
                        S.op("pe", lambda e, pbk=pbk, j=j, hT=hT: e.matmul(
                            pbk[0:32, 0:ntok], lhsT=wkr[:, j, :], rhs=hT[:, j, 0:ntok],
                            start=(j == 0), stop=(j == 7)), [bhT, b_wkr], [bpbk])
                    krT, bkr = krr.next()
                    rope_out(pa, bpa, pbk, bpbk, 32, ntok, cs, b_cs, sn, b_sn, t1, b_t1, t2, b_t2,
                             krT[0:32, 0:ntok], bkr)
                    for h in range(8):
                        S.dma("pool", KT[h, 64:96, koff:koff + ntok], krT[0:32, 0:ntok], bkr, b_KT)
                else:
                    knT, bkn = knr.next()
                    pend = None
                    for h in range(9):
                        cur = None
                        if h < 8:
                            pa, bpa = S.bank()
                            for j in range(8):
                                S.op("pe", lambda e, pa=pa, j=j, h=h, hT=hT: e.matmul(
                                    pa[:, 0:ntok], lhsT=win[:, j, h * P:(h + 1) * P], rhs=hT[:, j, 0:ntok],
                                    start=(j == 0), stop=(j == 7)), [bhT, b_win], [bpa])
                            kb, bkb = kbr.next()
                            S.op("act", lambda e, pa=pa, kb=kb: e.copy(out=kb[:, 0:ntok], in_=pa[:, 0:ntok]), [bpa], [bkb])
                            cur = (h, pa, bpa, kb, bkb)
                        if pend is not None:
                            h0, pa0, bpa0, kb0, bkb0 = pend
                            pbk, bpbk = S.bank()
                            S.op("pe", lambda e, pbk=pbk, kb0=kb0: e.matmul(
                                pbk[:, 0:ntok], lhsT=pmb[:, :], rhs=kb0[:, 0:ntok], start=True, stop=True),
                                [b_pmb, bkb0], [bpbk])
                            rope_out(pa0, bpa0, pbk, bpbk, P, ntok, cs, b_cs, sn, b_sn, t1, b_t1, t2, b_t2,
                                     knT[:, h0, 0:ntok], bkn, extra=[bkb0])
                        pend = cur
                    S.dma("pool", KT[:, :, koff:koff + ntok].rearrange("h r n -> r h n"), knT[:, :, 0:ntok], bkn, b_KT)
                    for t in range(nt):
                        for hf in range(2):
                            pb, bp = S.bank()
                            for j in range(8):
                                S.op("pe", lambda e, pb=pb, t=t, j=j, hf=hf, hT=hT: e.matmul(
                                    pb[:, :], lhsT=hT[:, j, t * P:(t + 1) * P],
                                    rhs=win[:, j, 1024 + hf * 512:1024 + (hf + 1) * 512],
                                    start=(j == 0), stop=(j == 7)), [bhT, b_win], [bp])
                            S.op("dve", lambda e, pb=pb, t=t, hf=hf, vsb=vsb: e.tensor_copy(
                                out=vsb[:, t, hf * 4:(hf + 1) * 4, 0:128], in_=pb[:, :].rearrange("p (h c) -> p h c", c=128)),
                                [bp], [bvs])
                S.dma("pool", Vs[koff:koff + ntok].rearrange("(t p) h c -> p t (h c)", p=P),
                      vsb[:, 0:nt].rearrange("p t h c -> p t (h c)"), bvs, b_Vs)
            sts = [None] * len(kv_blocks)
            sts[0] = kv_a(*kv_blocks[0])
            kv_b(sts[0])
            for i in range(len(kv_blocks)):
                if i + 1 < len(kv_blocks):
                    sts[i + 1] = kv_a(*kv_blocks[i + 1])
                kv_body(sts[i])
                if i + 1 < len(kv_blocks):
                    kv_b(sts[i + 1])
            S.end_phase()

        with ExitStack() as ph:
            xr = Ring(nc, ph, "xr", 1, [P, 4, D], F32)
            xnr = Ring(nc, ph, "xn", 2, [P, 4, D], F32)
            stb_ring = Ring(nc, ph, "stb", 2, [P, 64], F32)
            hTr = Ring(nc, ph, "hT", 2, [P, 8, 512], BF16)
            st_ring = Ring(nc, ph, "st", 4, [P, 16], F32)
            csr = Ring(nc, ph, "csq", 2, [R, 512], F32)
            snr = Ring(nc, ph, "snq", 2, [R, 512], F32)
            t1r = Ring(nc, ph, "t1", 2, [P, 512], F32)
            t2r = Ring(nc, ph, "t2", 2, [P, 512], F32)
            qsr = Ring(nc, ph, "qsb", 2, [R, 8, 512], BF16)
            if L == 0:
                win, b_win = sbt(nc, ph, "win", [P, 8, 256], BF16)
                wuq, b_wuq = sbt(nc, ph, "wuq", [P, 2, 768], BF16)
                wuqs, b_wuqs = sbt(nc, ph, "wuqs", [P, 2, 768], BF16)
                qn, b_qn = sbt(nc, ph, "qn", [P, 2], F32)
                S.dma("pool", win[:], io["w_in"][:, 512:768].rearrange("(j p) n -> p j n", p=P), bin_, b_win)
                S.dma("pool", wuq[:], io["w_uq"].rearrange("(j p) n -> p j n", p=P), bin_, b_wuq)
                S.dma("pool", wuqs[:], io["w_uq_sw"].rearrange("(j p) n -> p j n", p=P), bin_, b_wuqs)
                S.dma("sp", qn[:], io["qn"], bin_, b_qn)
                sq, b_sq = sbt(nc, ph, "sq", [P, 2, 512], F32)
                rb, b_rb = sbt(nc, ph, "rb", [P, 512], F32)
                cqn, b_cqn = sbt(nc, ph, "cqn", [P, 2, 512], BF16)
            else:
                win, b_win = sbt(nc, ph, "win", [P, 8, 1024], BF16)
                pmb, b_pmb = sbt(nc, ph, "pmb", [P, P], BF16)
                kbr = Ring(nc, ph, "kb", 3, [P, 512], BF16)
                S.dma("pool", win[:], io["w_in"][:, 0:1024].rearrange("(j p) n -> p j n", p=P), bin_, b_win)
                S.dma("pool", pmb[:], io["permT"], bin_, b_pmb)
            def q_a(src, nt, qoff, m, sbuf_src):
                ntok = nt * P
                xb, bxb = xr.next()
                S.dma("sp", xb[:, 0:nt, :], src.rearrange("(t p) d -> p t d", p=P), sbuf_src, bxb)
                cs, b_cs = csr.next()
                sn, b_sn = snr.next()
                S.dma("sp", cs[:, 0:ntok], io["cosQ"][:, qoff:qoff + ntok], bin_, b_cs)
                S.dma("sp", sn[:, 0:ntok], io["sinQ"][:, qoff:qoff + ntok], bin_, b_sn)
                xn_, bxn_ = xnr.next()
                ln_block(stb_ring, xb, bxb, xn_, bxn_, nt)
                return dict(nt=nt, qoff=qoff, m=m, cs=cs, b_cs=b_cs, sn=sn, b_sn=b_sn, xn=xn_, bxn=bxn_)

            def q_b(st):
                hT, bhT = hTr.next()
                make_hT(st["xn"], st["bxn"], st["nt"], hT, bhT, 0, 1, st["m"])
                st["hT"], st["bhT"] = hT, bhT

            def q_body(st):
                nt, qoff, m = st["nt"], st["qoff"], st["m"]
                cs, b_cs, sn, b_sn, hT, bhT = st["cs"], st["b_cs"], st["sn"], st["b_sn"], st["hT"], st["bhT"]
                ntok = nt * P
                qsb, bqs = qsr.next()
                if L == 0:
                    banks = []
                    for m2 in range(2):
                        pb, bp = S.bank()
                        banks.append((pb, bp))
                        for j in range(8):
                            S.op("pe", lambda e, pb=pb, m2=m2, j=j, hT=hT: e.matmul(
                                pb[:, 0:ntok], lhsT=win[:, j, m2 * P:(m2 + 1) * P], rhs=hT[:, j, 0:ntok],
                                start=(j == 0), stop=(j == 7)), [bhT, b_win], [bp])
                    rms_T(banks, qn, b_qn, ntok, sq, b_sq, rb, b_rb, cqn, b_cqn)
                pend = None
                for h in range(9 if L == 1 else 0):
                    cur = None
                    if h < 8:
                        pa, bpa = S.bank()
                        for j in range(8):
                            S.op("pe", lambda e, pa=pa, j=j, h=h, hT=hT: e.matmul(
                                pa[:, 0:ntok], lhsT=win[:, j, h * P:(h + 1) * P], rhs=hT[:, j, 0:ntok],
                                start=(j == 0), stop=(j == 7)), [bhT, b_win], [bpa])
                        kb, bkb = kbr.next()
                        S.op("act", lambda e, pa=pa, kb=kb: e.copy(out=kb[:, 0:ntok], in_=pa[:, 0:ntok]), [bpa], [bkb])
                        cur = (h, pa, bpa, kb, bkb)
                    if pend is not None:
                        h0, pa0, bpa0, kb0, bkb0 = pend
                        pbk, bpbk = S.bank()
                        S.op("pe", lambda e, pbk=pbk, kb0=kb0: e.matmul(
                            pbk[:, 0:ntok], lhsT=pmb[:, :], rhs=kb0[:, 0:ntok], start=True, stop=True),
                            [b_pmb, bkb0], [bpbk])
                        t1, b_t1 = t1r.next()
                        t2, b_t2 = t2r.next()
                        rope_out(pa0, bpa0, pbk, bpbk, R, ntok, cs, b_cs, sn, b_sn, t1, b_t1, t2, b_t2,
                                 qsb[0:R, h0, 0:ntok], bqs, extra=[bkb0])
                    pend = cur
                for h in range(8 if L == 0 else 0):
                    pa, bpa = S.bank()
                    pbk, bpbk = S.bank()
                    if L == 0:
                        for m2 in range(2):
                            S.op("pe", lambda e, pa=pa, m2=m2, h=h: e.matmul(
                                pa[0:96, 0:ntok], lhsT=wuq[:, m2, h * 96:(h + 1) * 96], rhs=cqn[:, m2, 0:ntok],
                                start=(m2 == 0), stop=(m2 == 1)), [b_cqn, b_wuq], [bpa])
                        for m2 in range(2):
                            S.op("pe", lambda e, pbk=pbk, m2=m2, h=h: e.matmul(
                                pbk[0:96, 0:ntok], lhsT=wuqs[:, m2, h * 96:(h + 1) * 96], rhs=cqn[:, m2, 0:ntok],
                                start=(m2 == 0), stop=(m2 == 1)), [b_cqn, b_wuqs], [bpbk])
                    else:
                        for j in range(8):
                            S.op("pe", lambda e, pa=pa, j=j, h=h, hT=hT: e.matmul(
                                pa[:, 0:ntok], lhsT=win[:, j, h * P:(h + 1) * P], rhs=hT[:, j, 0:ntok],
                                start=(j == 0), stop=(j == 7)), [bhT, b_win], [bpa])
                        for j in range(8):
                            S.op("pe", lambda e, pbk=pbk, j=j, h=h, hT=hT: e.matmul(
                                pbk[:, 0:ntok], lhsT=wsw[:, j, h * P:(h + 1) * P], rhs=hT[:, j, 0:ntok],
                                start=(j == 0), stop=(j == 7)), [bhT, b_wsw], [bpbk])
                    t1, b_t1 = t1r.next()
                    t2, b_t2 = t2r.next()
                    rope_out(pa, bpa, pbk, bpbk, R, ntok, cs, b_cs, sn, b_sn, t1, b_t1, t2, b_t2,
                             qsb[0:R, h, 0:ntok], bqs)
                S.dma("pool", QT[:, :, qoff:qoff + ntok].rearrange("h r n -> r h n"), qsb[0:R, :, 0:ntok], bqs, b_QT)
            sts = [None] * len(q_blocks)
            sts[0] = q_a(*q_blocks[0])
            q_b(sts[0])
            for i in range(len(q_blocks)):
                if i + 1 < len(q_blocks):
                    sts[i + 1] = q_a(*q_blocks[i + 1])
                q_body(sts[i])
                if i + 1 < len(q_blocks):
                    q_b(sts[i + 1])
            S.end_phase()

        if L == 0:
            with ExitStack() as ph:
                f64, b_f64 = sbt(nc, ph, "f64", [64, 128], F32)
                cdl, b_cdl = sbt(nc, ph, "cdl", [P, 2, 128], F32)
                cdc, b_cdc = sbt(nc, ph, "cdc", [P, 2, 128], F32)
                f256, b_f256 = sbt(nc, ph, "f256", [P, 2, 512], F32)
                S.dma("sp", f64[:], io["F64"], bin_, b_f64)
                S.dma("sp", cdl[:], io["CdSl"], bin_, b_cdl)
                S.dma("sp", cdc[:], io["CdSc"], bin_, b_cdc)
                S.dma("sp", f256[:], io["F256"], bin_, b_f256)
                Xr = Ring(nc, ph, "Xh", 2, [64, 128, 32], F32)
                T, b_T = sbt(nc, ph, "T", [P, 128, 128], F32)
                ZT, b_ZT = sbt(nc, ph, "ZT", [P, 2, 64, 64], F32)
                Gr_ = Ring(nc, ph, "G", 2, [P, 8, 2, 128], F32)
                ysr = Ring(nc, ph, "ysb", 2, [P, 4, 128], BF16)
                Fd3 = io["Fdl"].rearrange("(a b) c -> a b c", b=128)
                ZTf = ZT[:, :, :, :].rearrange("p r i q -> p r (i q)")
                for g in range(4):
                    for xq in range(4):
                        X, bX = Xr.next()
                        c0 = g * 128 + xq * 32
                        S.dma("sp", X[:], Fd3[:, :, c0:c0 + 32], b_Fdl, bX)
                        for cg in range(8):
                            pb, bp = S.bank()
                            for cc in range(4):
                                c = cg * 4 + cc
                                S.op("pe", lambda e, pb=pb, cc=cc, c=c, X=X: e.matmul(
                                    pb[:, cc * 128:(cc + 1) * 128], lhsT=X[:, :, c], rhs=f64[:, :],
                                    start=True, stop=True), [bX, b_f64], [bp])
                            ch0 = xq * 32 + cg * 4
                            if cg % 2 == 0:
                                S.op("act", lambda e, pb=pb, ch0=ch0: e.copy(
                                    out=T[:, ch0:ch0 + 4, :], in_=pb[:, :].rearrange("p (c k) -> p c k", k=128)),
                                    [bp], [b_T])
                            else:
                                S.op("dve", lambda e, pb=pb, ch0=ch0: e.tensor_copy(
                                    out=T[:, ch0:ch0 + 4, :], in_=pb[:, :].rearrange("p (c k) -> p c k", k=128)),
                                    [bp], [b_T])
                    for kc in range(8):
                        G, bG = Gr_.next()
                        S.dma("sp", G[:], io["GAB"][:, kc * 8:(kc + 1) * 8], bin_, bG)
                        for kq in range(2):
                            pb, bp = S.bank()
                            for q in range(4):
                                kl = kq * 4 + q
                                k2 = kc * 8 + kl
                                S.op("pe", lambda e, pb=pb, q=q, kl=kl, k2=k2, G=G: e.matmul(
                                    pb[:, q * 128:(q + 1) * 128], lhsT=T[:, :, k2],
                                    rhs=G[:, kl, 0, :], start=True, stop=False), [b_T, bG], [bp])
                                S.op("pe", lambda e, pb=pb, q=q, kl=kl, k2=k2, G=G: e.matmul(
                                    pb[:, q * 128:(q + 1) * 128], lhsT=T[:, :, 64 + k2],
                                    rhs=G[:, kl, 1, :], start=False, stop=True), [b_T, bG], [bp])
                            k20 = kc * 8 + kq * 4
                            S.op("dve", lambda e, pb=pb, k20=k20: e.tensor_copy(
                                out=ZT[:, :, :, k20:k20 + 4].rearrange("p r i q -> p q r i"),
                                in_=pb[:, :].rearrange("p (q r i) -> p q r i", q=4, r=2)),
                                [bp], [b_ZT])
                    for tq in range(8):
                        pb, bp = S.bank()
                        for q in range(4):
                            tt = tq * 4 + q
                            for r in range(2):
                                S.op("pe", lambda e, pb=pb, q=q, tt=tt, r=r: e.matmul(
                                    pb[:, q * 128:(q + 1) * 128], lhsT=ZTf[:, r, tt * 128:(tt + 1) * 128],
                                    rhs=cdl[:, r, :], start=(r == 0), stop=(r == 1)), [b_ZT, b_cdl], [bp])
                        ysb, bys = ysr.next()
                        S.op("act", lambda e, pb=pb, ysb=ysb: e.copy(
                            out=ysb[:, :, :], in_=pb[:, :].rearrange("p (q c) -> p q c", c=128)), [bp], [bys])
                        S.dma("pool", cat[tq * 512:(tq + 1) * 512, g * 128:(g + 1) * 128].rearrange("(t p) c -> p t c", p=P),
                              ysb[:, :, :], bys, b_cat)
                fcx, b_fcx = sbt(nc, ph, "fcx", [P, 2, 512], F32)
                zc, b_zc = sbt(nc, ph, "zc", [P, 512], F32)
                S.dma("sp", fcx[:], io["Fdc"].rearrange("(t p) c -> p t c", p=P), b_Fdc, b_fcx)
                for g in range(4):
                    pb, bp = S.bank()
                    for tl in range(2):
                        S.op("pe", lambda e, pb=pb, tl=tl, g=g: e.matmul(
                            pb[:, :], lhsT=fcx[:, tl, g * 128:(g + 1) * 128], rhs=f256[:, tl, :],
                            start=(tl == 0), stop=(tl == 1)), [b_fcx, b_f256], [bp])
                    S.op("dve", lambda e, pb=pb: e.tensor_copy(out=zc[:, :], in_=pb[:, :]), [bp], [b_zc])
                    pb2, bp2 = S.bank()
                    for tl in range(2):
                        for r in range(2):
                            S.op("pe", lambda e, pb2=pb2, tl=tl, r=r: e.matmul(
                                pb2[:, tl * 128:(tl + 1) * 128], lhsT=zc[:, r * 256 + tl * 128:r * 256 + (tl + 1) * 128],
                                rhs=cdc[:, r, :], start=(r == 0), stop=(r == 1)), [b_zc, b_cdc], [bp2])
                    ysb, bys = ysr.next()
                    S.op("act", lambda e, pb2=pb2, ysb=ysb: e.copy(
                        out=ysb[:, 0:2, :], in_=pb2[:, 0:256].rearrange("p (q c) -> p q c", c=128)), [bp2], [bys])
                    S.dma("pool", cat[SH:SH + CTX, g * 128:(g + 1) * 128].rearrange("(t p) c -> p t c", p=P),
                          ysb[:, 0:2, :], bys, b_cat)
                S.end_phase()

        with ExitStack() as ph:
            ktr = Ring(nc, ph, "kts", 2, [R, NK], BF16)
            vr = Ring(nc, ph, "vs", 2, [P, 66, vw], BF16)
            if L == 0:
                qtr = Ring(nc, ph, "qts", 2, [R, SH + CTX], BF16)
            else:
                qtr = Ring(nc, ph, "qts", 2, [P, 2, SH], BF16)
                for i in range(2):
                    S.op("pool", lambda e, i=i: e.memset(qtr.t[i][:], 0.0), [], [qtr.b[i]])
            ptr = Ring(nc, ph, "pt", 4, [P, 1024], BF16)
            rcr = Ring(nc, ph, "rc", 4, [P, 8], F32)
            asr = Ring(nc, ph, "asb", 2, [P, 4, 128 if L == 1 else 64], BF16)
            if L == 1:
                o0r = Ring(nc, ph, "o0", 2, [P, 4, 128], F32)
                o1r = Ring(nc, ph, "o1", 2, [P, 128], F32)
                sqr = Ring(nc, ph, "sqr", 2, [P, 128], F32)
                lamt, b_lam = sbt(nc, ph, "lamt", [P, 4, 64], F32)
                lamw, b_lamw = sbt(nc, ph, "lamw", [P, 8], F32)
                lamj, b_lamj = sbt(nc, ph, "lamj", [P, 64], F32)
                subl, b_subl = sbt(nc, ph, "subl", [P, 128], F32)
                epsc, b_epsc = sbt(nc, ph, "epsc", [P, 1], F32)
                S.op("dve", lambda e: e.memset(epsc[:], EPS), [], [b_epsc])
                S.dma("sp", lamt[:], io["lam"].partition_broadcast(P), bin_, b_lam)
                S.dma("sp", subl[:], io["subln"].partition_broadcast(P), bin_, b_subl)
                for i in range(2):
                    S.op("dve", lambda e, i=i: e.scalar_tensor_tensor(
                        out=lamj[:, :], in0=lamt[:, 2 * i, :], scalar=1.0, in1=lamt[:, 2 * i + 1, :],
                        op0=ALU.mult, op1=ALU.mult, accum_out=lamw[:, i:i + 1]), [b_lam], [b_lamj, b_lamw])
                S.op("act", lambda e: e.activation(out=lamw[:, 2:4], in_=lamw[:, 0:2], func=AF.Exp), [b_lamw], [b_lamw])
                S.op("dve", lambda e: e.tensor_tensor(out=lamw[:, 4:5], in0=lamw[:, 3:4], in1=lamw[:, 2:3],
                                                      op=ALU.subtract), [b_lamw], [b_lamw])
                S.op("dve", lambda e: e.tensor_scalar(out=lamw[:, 5:6], in0=lamw[:, 4:5], scalar1=-LAMBDA_INIT1,
                                                      scalar2=None, op0=ALU.add), [b_lamw], [b_lamw])
                S.op("dve", lambda e: e.tensor_scalar(out=subl[:, :], in0=subl[:, :], scalar1=1.0 - LAMBDA_INIT1,
                                                      scalar2=None, op0=ALU.mult), [b_subl], [b_subl])
            casts = []
            for c in range(11):
                casts.append(lambda c=c: S.dma("pool", io["wg_bf"][c],
                                               io["w_gate"][:, c * 256:(c + 1) * 256].rearrange("(j p) n -> p j n", p=P),
                                               bin_, b_wbf))
                casts.append(lambda c=c: S.dma("pool", io["wu_bf"][c],
                                               io["w_up"][:, c * 256:(c + 1) * 256].rearrange("(j p) n -> p j n", p=P),
                                               bin_, b_wbf))
            for r4 in range(4):
                casts.append(lambda r4=r4: S.dma("pool", io["wd_bf"][r4 * 704:(r4 + 1) * 704, :],
                                                 io["w_down"][r4 * 704:(r4 + 1) * 704, :], bin_, b_wbf,
                                                 max_dma_last_dim=4096))
            scale = MLA_SCALE if L == 0 else DIFF_SCALE
            nq_tot = SH + (CTX if need_ctx else 0)
            qblocks = [(b * 512, 512, 66) for b in range(8)]
            if need_ctx:
                qblocks.append((SH, 256, 2))
            nmaps = 1 if L == 0 else 2
            accr = Ring(nc, ph, "accs", 4, [P, 4, 132], F32)
            rc8r = Ring(nc, ph, "rc8", 4, [P, 16], F32)
            o1br = Ring(nc, ph, "o1b", 2, [P, 4, 128], F32)
            items = []
            for h in range(8):
                for qi in range(len(qblocks)):
                    for ci in range(nmaps):
                        for kp in range(qblocks[qi][2] // 2):
                            items.append((h, qi, ci, kp))
            heads = {}

            def load_head(h):
                kts, bkt = ktr.next()
                vs_, bv = vr.next()
                qts, bqt = qtr.next()
                S.dma("sp", kts[:, :], KT[h], b_KT, bkt)
                S.dma("sp", vs_[:, :, :], Vs[:, h, :].rearrange("(t p) c -> p t c", p=P), b_Vs, bv)
                if L == 0:
                    S.dma("sp", qts[:, 0:nq_tot], QT[h, :, 0:nq_tot], b_QT, bqt)
                else:
                    S.dma("sp", qts[0:64, 0, :], QT[h, 0:64, :], b_QT, bqt)
                    S.dma("sp", qts[64:128, 1, :], QT[h, 64:128, :], b_QT, bqt)
                heads[h] = (kts, bkt, vs_, bv, qts, bqt)

            qstate = {}

            def finalize(h, qi, ci):
                q0, nq, nkt = qblocks[qi]
                nj = nq // P
                stq = qstate.setdefault((h, qi), {})
                acs, bacs = accr.next()
                for j in range(nj):
                    S.op("dve", lambda e, j=j, acs=acs: e.tensor_copy(out=acs[:, j, 0:vw], in_=S.pb[4 + j][:, 0:vw]),
                         [S.bpb[4 + j]], [bacs])
                if ci == 0:
                    rc, brc = rc8r.next()
                    stq["rc"], stq["brc"] = rc, brc
                    stq["a0"], stq["ba0"] = acs, bacs
                else:
                    rc, brc = stq["rc"], stq["brc"]
                S.op("dve", lambda e, acs=acs, rc=rc, ci=ci: e.reciprocal(
                    out=rc[:, 4 * ci:4 * ci + nj], in_=acs[:, 0:nj, vw - 1]), [bacs], [brc])
                if L == 0:
                    asb, bas = asr.next()
                    for j in range(nj):
                        S.op("dve", lambda e, j=j, acs=acs, rc=rc, asb=asb: e.tensor_scalar(
                            out=asb[:, j, :], in0=acs[:, j, 0:64], scalar1=rc[:, j:j + 1], scalar2=None, op0=ALU.mult),
                            [bacs, brc], [bas])
                    S.dma("pool", cat[q0:q0 + nq, 512 + h * 64:512 + (h + 1) * 64].rearrange("(t p) c -> p t c", p=P),
                          asb[:, 0:nj, :], bas, b_cat)
                    if casts:
                        casts.pop(0)()
                elif ci == 1:
                    a0, ba0 = stq["a0"], stq["ba0"]
                    asb, bas = asr.next()
                    o1, b_o1 = o1br.next()
                    sqt, bsqt = sqr.next()
                    S.op("dve", lambda e, rc=rc: e.tensor_scalar(out=rc[:, 4:8], in0=rc[:, 4:8], scalar1=lamw[:, 5:6],
                                                                 scalar2=None, op0=ALU.mult), [brc, b_lamw], [brc])
                    for j in range(nj):
                        S.op("dve", lambda e, j=j, a0=a0, rc=rc, o1=o1: e.tensor_scalar(
                            out=o1[:, j, :], in0=a0[:, j, 0:128], scalar1=rc[:, j:j + 1], scalar2=None, op0=ALU.mult),
                            [ba0, brc], [b_o1])
                        S.op("dve", lambda e, j=j, acs=acs, rc=rc, o1=o1: e.scalar_tensor_tensor(
                            out=o1[:, j, :], in0=acs[:, j, 0:128], scalar=rc[:, 4 + j:5 + j], in1=o1[:, j, :],
                            op0=ALU.mult, op1=ALU.add), [bacs, brc, b_o1], [b_o1])
                        S.op("dve", lambda e, j=j, o1=o1, rc=rc, sqt=sqt: e.scalar_tensor_tensor(
                            out=sqt[:, :], in0=o1[:, j, :], scalar=1.0, in1=o1[:, j, :],
                            op0=ALU.mult, op1=ALU.mult, accum_out=rc[:, 8 + j:9 + j]), [b_o1], [brc, bsqt])
                    S.op("act", lambda e, rc=rc: e.activation(out=rc[:, 12:16], in_=rc[:, 8:12], func=AF.Ln,
                                                             bias=epsc[:, 0:1], scale=1.0 / 128.0), [brc, b_epsc], [brc])
                    S.op("act", lambda e, rc=rc: e.activation(out=rc[:, 12:16], in_=rc[:, 12:16], func=AF.Exp,
                                                             scale=-0.5), [brc], [brc])
                    for j in range(nj):
                        S.op("dve", lambda e, j=j, o1=o1, rc=rc, asb=asb: e.scalar_tensor_tensor(
                            out=asb[:, j, :], in0=o1[:, j, :], scalar=rc[:, 12 + j:13 + j], in1=subl[:, :],
                            op0=ALU.mult, op1=ALU.mult), [b_o1, brc, b_subl], [bas])
                    S.dma("pool", cat[q0:q0 + nq, h * 128:(h + 1) * 128].rearrange("(t p) c -> p t c", p=P),
                          asb[:, 0:nj, :], bas, b_cat)
                    if casts:
                        casts.pop(0)()

            DEPTH = 2
            pts = {}
            load_head(0)
            spair = [(S.pbig[:, 0:1024], [S.bpb[0], S.bpb[1]]), (S.pbig[:, 1024:2048], [S.bpb[2], S.bpb[3]])]
            for n in range(len(items) + DEPTH):
                if n < len(items):
                    h, qi, ci, kp = items[n]
                    q0, nq, nkt = qblocks[qi]
                    kts, bkt, vs_, bv, qts, bqt = heads[h]
                    sp2, bsp2 = spair[n % 2]
                    for u in range(2):
                        kt = 2 * kp + u
                        if L == 0:
                            S.op("pe", lambda e, sp2=sp2, u=u, kt=kt, kts=kts, qts=qts, q0=q0, nq=nq: e.matmul(
                                sp2[:, u * 512:u * 512 + nq], lhsT=kts[0:96, kt * P:(kt + 1) * P], rhs=qts[0:96, q0:q0 + nq],
                                start=True, stop=True), [bkt, bqt], [bsp2[u]])
                        else:
                            S.op("pe", lambda e, sp2=sp2, u=u, kt=kt, kts=kts, qts=qts, ci=ci, q0=q0, nq=nq: e.matmul(
                                sp2[:, u * 512:u * 512 + nq], lhsT=kts[:, kt * P:(kt + 1) * P], rhs=qts[:, ci, q0:q0 + nq],
                                start=True, stop=True), [bkt, bqt], [bsp2[u]])
                    pt, bpt = ptr.next()
                    S.op("act", lambda e, sp2=sp2, pt=pt, nq=nq: e.activation(
                        out=pt[:, :].rearrange("p (u n) -> p u n", u=2)[:, :, 0:nq],
                        in_=sp2.rearrange("p (u n) -> p u n", u=2)[:, :, 0:nq], func=AF.Exp, scale=scale), bsp2, [bpt])
                    pts[n] = (pt, bpt)
                if n >= DEPTH:
                    h, qi, ci, kp = items[n - DEPTH]
                    q0, nq, nkt = qblocks[qi]
                    if qi == 0 and ci == 0 and kp == 0 and h + 1 < 8:
                        load_head(h + 1)
                    kts, bkt, vs_, bv, qts, bqt = heads[h]
                    pt, bpt = pts.pop(n - DEPTH)
                    for u in range(2):
                        kt = 2 * kp + u
                        for j in range(nq // P):
                            S.op("pe", lambda e, j=j, u=u, pt=pt, kt=kt, vs_=vs_, nkt=nkt: e.matmul(
                                S.pb[4 + j][:, 0:vw], lhsT=pt[:, u * 512 + j * P:u * 512 + (j + 1) * P], rhs=vs_[:, kt, :],
                                start=(kt == 0), stop=(kt == nkt - 1)), [bpt, bv], [S.bpb[4 + j]])
                    if 2 * kp + 1 == nkt - 1:
                        finalize(h, qi, ci)
            while casts:
                casts.pop(0)()
            S.end_phase()

        with ExitStack() as ph:
            wout, b_wout = sbt(nc, ph, "wout", [P, 8, D], BF16)
            S.dma("pool", wout[:], io["w_out"].rearrange("(j p) n -> p j n", p=P), bin_, b_wout)
            xr = Ring(nc, ph, "xr", 2, [P, D], F32)
            ctr = Ring(nc, ph, "ct", 2, [P, D], BF16)
            cTr = Ring(nc, ph, "cT", 1, [P, 8, P], BF16)
            rr = Ring(nc, ph, "rr", 2, [P, D], F32)
            x1r = Ring(nc, ph, "x1", 2, [P, 4, D], F32)
            xn2r = Ring(nc, ph, "xn2", 1, [P, 1, D], F32)
            h2r = Ring(nc, ph, "h2T", 2, [P, 8, 512], BF16)
            actT, b_actT = sbt(nc, ph, "actT", [P, NF, 512], BF16)
            gur = Ring(nc, ph, "gu", 3, [P, 2, 8, 256], BF16)
            wdr = Ring(nc, ph, "wd", 3, [P, 2, 512], BF16)
            sgr = Ring(nc, ph, "sg", 2, [P, 512], F32)
            ygr = Ring(nc, ph, "yg", 2, [P, 512], F32)
            outr = Ring(nc, ph, "ot", 2, [P, D], F32)
            st_ring = Ring(nc, ph, "st", 4, [P, 16], F32)
            o_blocks = [(xbufs["xo_blk"](b), 4, b * 512, 0, xout(b), xbufs["xo_buf"](b), b_out(b)) for b in range(8)]
            if need_ctx:
                o_blocks.append((io["xc"], 2, SH, 1, xcout, xbufs["xc"], b_outc))

            def ln_affine(tmp, btmp, dst, bdst, g_row, b_row):
                ln_rows(st_ring, tmp, btmp, tmp, btmp)
                S.op("dve", lambda e: e.tensor_tensor(out=tmp, in0=tmp, in1=lnbc[:, g_row, :], op=ALU.mult),
                     [btmp, b_lnbc], [btmp])
                S.op("pool", lambda e: e.tensor_tensor(out=dst, in0=tmp, in1=lnbc[:, b_row, :], op=ALU.add),
                     [btmp, b_lnbc], [bdst])

            brange = [4, 8]

            def stage_a(src, nt, coff, m, dst, sbuf_src, bdst_out):
                x1s, b_x1s = x1r.next()
                h2T, b_h2T = h2r.next()
                st = dict(nt=nt, m=m, dst=dst, bdst_out=bdst_out, x1=x1s, b_x1=b_x1s, h2T=h2T, b_h2T=b_h2T)
                yield st
                for t in range(nt):
                    xt, bxt = xr.next()
                    ct, bct = ctr.next()
                    S.dma("sp", xt[:, :], src[t * P:(t + 1) * P, :], sbuf_src, bxt)
                    S.dma("sp", ct[:, :], cat[coff + t * P:coff + (t + 1) * P, :], b_cat, bct)
                    pb, bp = S.bank(*brange)
                    pbv = pb[:, :].bitcast(BF16)
                    for j in range(8):
                        S.op("pe", lambda e, pbv=pbv, j=j, ct=ct: e.transpose(
                            out=pbv[:, j * P:(j + 1) * P], in_=ct[:, j * P:(j + 1) * P], identity=ident_b[:, :]),
                            [bct, b_idb], [bp])
                    cT, bcT = cTr.next()
                    S.op("act", lambda e, pbv=pbv, cT=cT: e.copy(out=cT[:, :, :], in_=pbv.rearrange("p (j c) -> p j c", c=P)),
                         [bp], [bcT])
                    yield None
                    tmp, btmp = rr.next()
                    for hf in range(2):
                        pb, bp = S.bank(*brange)
                        for j in range(8):
                            S.op("pe", lambda e, pb=pb, j=j, hf=hf, cT=cT: e.matmul(
                                pb[:, :], lhsT=cT[:, j, :], rhs=wout[:, j, hf * 512:(hf + 1) * 512],
                                start=(j == 0), stop=(j == 7)), [bcT, b_wout], [bp])
                        S.op("dve", lambda e, pb=pb, hf=hf, tmp=tmp: e.tensor_tensor(
                            out=tmp[:, hf * 512:(hf + 1) * 512], in0=pb[:, :], in1=gbc[:, m, 0, hf * 512:(hf + 1) * 512],
                            op=ALU.mult), [bp, b_gbc], [btmp])
                    yield None
                    S.op("dve", lambda e, tmp=tmp, xt=xt: e.scalar_tensor_tensor(
                        out=tmp[:, :], in0=xt[:, :], scalar=ALPHA, in1=tmp[:, :], op0=ALU.mult, op1=ALU.add),
                        [bxt, btmp], [btmp])
                    ln_affine(tmp[:, :], btmp, x1s[:, t, :], b_x1s, 0, 1)
                    yield None
                    xn2, bxn2 = xn2r.next()
                    ln_rows(st_ring, x1s[:, t, :], b_x1s, xn2[:, 0, :], bxn2)
                    for jh in range(2):
                        pb, bp = S.bank(*brange)
                        for jj in range(4):
                            j = jh * 4 + jj
                            S.op("pe", lambda e, pb=pb, jj=jj, j=j, xn2=xn2: e.transpose(
                                out=pb[:, jj * P:(jj + 1) * P], in_=xn2[:, 0, j * P:(j + 1) * P], identity=ident_f[:, :]),
                                [bxn2, b_idf], [bp])
                        for jj in range(4):
                            j = jh * 4 + jj
                            if jj % 2 == 0:
                                S.op("act", lambda e, pb=pb, jj=jj, j=j, t=t: e.activation(
                                    out=h2T[:, j, t * P:(t + 1) * P], in_=pb[:, jj * P:(jj + 1) * P], func=AF.Identity,
                                    scale=fcol[:, 2, j, m:m + 1], bias=fcol[:, 3, j, m:m + 1]), [bp, b_fcol], [b_h2T])
                            else:
                                S.op("dve", lambda e, pb=pb, jj=jj, j=j, t=t: e.tensor_scalar(
                                    out=h2T[:, j, t * P:(t + 1) * P], in0=pb[:, jj * P:(jj + 1) * P],
                                    scalar1=fcol[:, 2, j, m:m + 1], scalar2=fcol[:, 3, j, m:m + 1],
                                    op0=ALU.mult, op1=ALU.add), [bp, b_fcol], [b_h2T])
                    yield None

            def stage_c(st):
                nt, x1s, b_x1s = st["nt"], st["x1"], st["b_x1"]
                for t in range(nt):
                    ot, bot = outr.next()
                    ln_affine(x1s[:, t, :], b_x1s, ot[:, :], bot, 2, 3)
                    S.dma("pool", st["dst"][t * P:(t + 1) * P, :], ot[:, :], bot, st["bdst_out"])
                    yield None

            def step(bg, n=1):
                for _ in range(n):
                    if bg is None:
                        return
                    try:
                        next(bg)
                    except StopIteration:
                        return

            def stage_b(st, bg):
                nt, m, x1s, b_x1s, h2T, b_h2T = st["nt"], st["m"], st["x1"], st["b_x1"], st["h2T"], st["b_h2T"]
                ntok = nt * P
                brange[0] = 0
                for c in range(11):
                    gu, bgu = gur.next()
                    S.dma("sp", gu[:, 0, :, :], io["wg_bf"][c], b_wbf, bgu)
                    S.dma("sp", gu[:, 1, :, :], io["wu_bf"][c], b_wbf, bgu)
                    for fc in range(2):
                        f = c * 2 + fc
                        pg, bpg = S.bank(*brange)
                        pu, bpu = S.bank(*brange)
                        for j in range(8):
                            S.op("pe", lambda e, pg=pg, j=j, fc=fc, gu=gu: e.matmul(
                                pg[:, 0:ntok], lhsT=gu[:, 0, j, fc * P:(fc + 1) * P], rhs=h2T[:, j, 0:ntok],
                                start=(j == 0), stop=(j == 7)), [bgu, b_h2T], [bpg])
                        for j in range(8):
                            S.op("pe", lambda e, pu=pu, j=j, fc=fc, gu=gu: e.matmul(
                                pu[:, 0:ntok], lhsT=gu[:, 1, j, fc * P:(fc + 1) * P], rhs=h2T[:, j, 0:ntok],
                                start=(j == 0), stop=(j == 7)), [bgu, b_h2T], [bpu])
                        sg, bsg = sgr.next()
                        S.op("act", lambda e, pg=pg, sg=sg: e.activation(out=sg[:, 0:ntok], in_=pg[:, 0:ntok], func=AF.Silu),
                             [bpg], [bsg])
                        S.op("dve", lambda e, pu=pu, sg=sg, f=f: e.tensor_tensor(
                            out=actT[:, f, 0:ntok], in0=pu[:, 0:ntok], in1=sg[:, 0:ntok], op=ALU.mult),
                            [bpu, bsg], [b_actT])
                    step(bg)
                brange[0] = 4
                for hf in range(2):
                    for c in range(11):
                        wd, bwd = wdr.next()
                        S.dma("sp", wd[:, :, :], io["wd_bf"][c * 256:(c + 1) * 256, hf * 512:(hf + 1) * 512]
                              .rearrange("(f p) n -> p f n", p=P), b_wbf, bwd)
                        for fc in range(2):
                            f = c * 2 + fc
                            for t in range(nt):
                                S.op("pe", lambda e, f=f, fc=fc, t=t, wd=wd: e.matmul(
                                    S.pb[t][:, :], lhsT=actT[:, f, t * P:(t + 1) * P], rhs=wd[:, fc, :],
                                    start=(f == 0), stop=(f == NF - 1)), [b_actT, bwd], [S.bpb[t]])
                        step(bg)
                    for t in range(nt):
                        yg, byg = ygr.next()
                        S.op("dve", lambda e, t=t, yg=yg, hf=hf: e.tensor_tensor(
                            out=yg[:, :], in0=S.pb[t][:, :], in1=gbc[:, m, 1, hf * 512:(hf + 1) * 512], op=ALU.mult),
                            [S.bpb[t], b_gbc], [byg])
                        S.op("dve", lambda e, t=t, yg=yg, hf=hf: e.scalar_tensor_tensor(
                            out=x1s[:, t, hf * 512:(hf + 1) * 512], in0=x1s[:, t, hf * 512:(hf + 1) * 512], scalar=ALPHA,
                            in1=yg[:, :], op0=ALU.mult, op1=ALU.add), [b_x1s, byg], [b_x1s])

            def chain(*gens):
                for g_ in gens:
                    if g_ is not None:
                        for _ in g_:
                            yield None

            nb = len(o_blocks)
            ga = stage_a(*o_blocks[0])
            st_cur = next(ga)
            for _ in ga:
                pass
            prev_c = None
            for i in range(nb):
                if i + 1 < nb:
                    ga = stage_a(*o_blocks[i + 1])
                    st_next = next(ga)
                else:
                    ga, st_next = None, None
                bg = chain(prev_c, ga)
                stage_b(st_cur, bg)
                for _ in bg:
                    pass
                prev_c = stage_c(st_cur)
                st_cur = st_next
            for _ in prev_c:
                pass
            S.end_phase()


def build_program():
    nc = bass.Bass("TRN2", target_bir_lowering=False)
    io0 = declare_layer(nc, 0, "l0_")
    x1own = [nc.dram_tensor("x1own%d" % c, [512, D], F32, kind="Internal").ap() for c in range(8)]
    x1gat = [nc.dram_tensor("x1gat%d" % c, [1024, D], F32, kind="Internal").ap() for c in range(8)]
    xc1 = nc.dram_tensor("xc1", [CTX, D], F32, kind="Internal").ap()
    io1 = declare_layer(nc, 1, "l1_", x_from_dram={"xc": xc1, "ident": io0["ident"]})
    xout = nc.dram_tensor("xout", [SH, D], F32, kind="ExternalOutput").ap()
    with ExitStack() as es:
        S = Sched(nc, es)
        ident_f = es.enter_context(nc.sbuf_tensor("ident_f", [P, P], F32)); b_idf = Buf("idf")
        ident_b = es.enter_context(nc.sbuf_tensor("ident_b", [P, P], BF16)); b_idb = Buf("idb")
        ones_f = es.enter_context(nc.sbuf_tensor("ones_f", [P, P], F32)); b_ones = Buf("ones")
        bin_ = Buf("cin")
        S.dma("sp", ident_f[:], io0["ident"], bin_, b_idf)
        S.op("dve", lambda e: e.tensor_copy(out=ident_b[:], in_=ident_f[:]), [b_idf], [b_idb])
        S.op("dve", lambda e: e.memset(ones_f[:], 1.0), [], [b_ones])
        consts = (ident_f, b_idf, ident_b, b_idb, ones_f, b_ones)
        b_xc1, b_xout = Buf("xc1"), Buf("xout")
        b_own = [Buf("x1own%d" % c) for c in range(8)]
        b_gat = [Buf("x1gat%d" % c) for c in range(8)]
        xb0 = {"xc": bin_,
               "xf_blk": lambda b: io0["xf"][b * 512:(b + 1) * 512, :], "xf_buf": lambda b: bin_,
               "xo_blk": lambda b: io0["xo"][b * 512:(b + 1) * 512, :], "xo_buf": lambda b: bin_}
        build_layer(nc, S, 0, io0, lambda b: x1own[b], xc1, consts, xb0, lambda b: b_own[b], b_xc1)
        for c in range(8):
            if NO_COLL:
                S.dma("pool", x1gat[c][0:512], x1own[c], b_own[c], b_gat[c])
                S.dma("pool", x1gat[c][512:1024], x1own[c], b_own[c], b_gat[c])
            else:
                S.coll("AllGather", [[0, 1], [2, 3], [4, 5], [6, 7]], x1own[c], x1gat[c], b_own[c], b_gat[c])
        xb1 = {"xc": b_xc1,
               "xf_blk": lambda b: x1gat[b % 8][(b // 8) * 512:(b // 8 + 1) * 512, :], "xf_buf": lambda b: b_gat[b % 8],
               "xo_blk": lambda b: x1own[b], "xo_buf": lambda b: b_own[b]}
        build_layer(nc, S, 1, io1, lambda b: xout[b * 512:(b + 1) * 512, :], None, consts, xb1, lambda b: b_xout, None)
    return nc


L0_NAMES = ["l0_w_mod", "l0_b_mod", "l0_w_in", "l0_q_norm", "l0_w_uq", "l0_kv_norm", "l0_w_ukv", "l0_w_out",
            "l0_ln1_g", "l0_ln1_b", "l0_w_gate", "l0_w_up", "l0_w_down", "l0_ln2_g", "l0_ln2_b"]
L1_NAMES = ["l1_w_mod", "l1_b_mod", "l1_w_in", "l1_lambda_q1", "l1_lambda_k1", "l1_lambda_q2", "l1_lambda_k2",
            "l1_subln", "l1_w_out", "l1_ln1_g", "l1_ln1_b", "l1_w_gate", "l1_w_up", "l1_w_down", "l1_ln2_g", "l1_ln2_b"]


def layer_inputs(L, inputs, b, half):
    pre = "l%d_" % L
    names = L0_NAMES if L == 0 else L1_NAMES
    w = {n[3:]: np.ascontiguousarray(np.asarray(inputs[n], dtype=np.float32)) for n in names}
    g = lambda n: w[n]
    t = get_tables(half)
    m = {}
    c = np.asarray(inputs["c"], np.float32)[b]
    cc = np.asarray(inputs["c_ctx"], np.float32)
    m["cv"] = np.ascontiguousarray(np.stack([c.reshape(8, P).T, cc.reshape(8, P).T], 2))
    m["w_mod"] = g("w_mod"); m["b_mod"] = g("b_mod").reshape(1, -1)
    m["w_out"] = g("w_out"); m["w_gate"] = g("w_gate"); m["w_up"] = g("w_up"); m["w_down"] = g("w_down")
    m["ln"] = np.ascontiguousarray(np.stack([g("ln1_g"), g("ln1_b"), g("ln2_g"), g("ln2_b")], 0))
    w_in = g("w_in")
    if L == 0:
        m["ident"] = t["ident"]
        m["w_in"] = w_in
        sw = rope_swap_index(32)
        m["w_kr_sw"] = np.ascontiguousarray(w_in[:, 1024 + sw])
        m["qn"] = np.ascontiguousarray(g("q_norm").reshape(2, P).T)
        m["kvn"] = np.ascontiguousarray(g("kv_norm").reshape(2, P).T)
        w_uq = g("w_uq")
        perm = np.arange(768)
        for h in range(8):
            perm[h * 96 + 64:h * 96 + 96] = h * 96 + 64 + sw
        m["w_uq"] = w_uq
        m["w_uq_sw"] = np.ascontiguousarray(w_uq[:, perm])
        w_ukv = g("w_ukv").reshape(256, 8, 128)
        m["w_ukv_kn"] = np.ascontiguousarray(w_ukv[:, :, 0:64].reshape(256, 512))
        m["w_ukv_v"] = np.ascontiguousarray(w_ukv[:, :, 64:128].reshape(256, 512))
        m["cosK"], m["sinK"], m["cosQ"], m["sinQ"] = t["cosK0"], t["sinK0"], t["cosQ0"], t["sinQ0"]
        for k in ("F64", "GAB", "CdSl", "CdSc", "F256"):
            m[k] = t[k]
    else:
        m["w_in"] = w_in
        m["permT"] = t["permT"]
        m["lam"] = np.ascontiguousarray(np.stack([g("lambda_q1"), g("lambda_k1"), g("lambda_q2"), g("lambda_k2")], 0))
        m["subln"] = g("subln").reshape(1, 128)
        m["cosK"], m["sinK"], m["cosQ"], m["sinQ"] = t["cosK1"], t["sinK1"], t["cosQ1"], t["sinQ1"]
    return {pre + k: v for k, v in m.items()}


_PROG = {}


def kernel(**inputs):
    x = np.asarray(inputs["x"], np.float32)
    xc = np.asarray(inputs["ctx"], np.float32)
    if "nc" not in _PROG:
        _PROG["nc"] = build_program()
    nc = _PROG["nc"]
    in_maps = []
    for k in range(8):
        b, half = k // 2, k % 2
        m = {}
        m.update(layer_inputs(0, inputs, b, half))
        m.update(layer_inputs(1, inputs, b, half))
        m["l0_xf"] = np.ascontiguousarray(x[b])
        m["l0_xo"] = np.ascontiguousarray(x[b, half * SH:(half + 1) * SH])
        m["l0_xc"] = np.ascontiguousarray(xc[b])
        in_maps.append(m)
    res = run_bass_kernel_spmd(nc, in_maps, core_ids=list(range(8)))
    r = res.results
    out = np.stack([np.concatenate([r[2 * b]["xout"], r[2 * b + 1]["xout"]], 0) for b in range(4)], 0)
    return out.astype(np.float32)
```

```python
import math
import numpy as np
from contextlib import ExitStack
import concourse.bass as bass
import concourse.mybir as mybir
from concourse.bass_utils import run_bass_kernel_spmd

F32 = mybir.dt.float32
BF16 = mybir.dt.bfloat16
AF = mybir.ActivationFunctionType
ALU = mybir.AluOpType

P = 128
D = 1024
SEQ = 8192
SH = 4096
CTX = 256
NK = SEQ + CTX
FF = 2816
NF = 22
ALPHA = 4.0 ** 0.25
EPS = 1e-6
MLA_SCALE = 96.0 ** -0.5
DIFF_SCALE = 64.0 ** -0.5
LAMBDA_INIT1 = 0.8 - 0.6 * math.exp(-0.3)

ENGS = ["pe", "act", "dve", "pool", "sp"]
DEBUG_SCR = False
NO_COLL = False


class Buf:
    __slots__ = ("name", "w", "r", "dsem", "dcnt", "psum")

    def __init__(self, name, psum=False):
        self.name = name
        self.w = None
        self.r = []
        self.dsem = None
        self.dcnt = 0
        self.psum = psum


class Sched:
    def __init__(self, nc, es):
        self.nc = nc
        self.es = es
        self.ops = {e: [] for e in ENGS}
        self.sem = {e: es.enter_context(nc.semaphore("s_" + e)) for e in ENGS}
        self.cnt = {e: 0 for e in ENGS}
        self.seen = {e: {} for e in ENGS}
        self.sem_pool = {"sp": [], "pool": [], "act": []}
        self.phase_bufs = []
        self.nsem = 0
        self.pbig = es.enter_context(nc.psum_tensor("pbig", [P, 4096], F32))
        self.pb = [self.pbig[:, i * 512:(i + 1) * 512] for i in range(8)]
        self.bpb = [Buf("pb%d" % i, psum=True) for i in range(8)]
        self.pbi = 0

    def bank(self, lo=0, hi=8):
        i = self.pbi
        if i < lo or i >= hi:
            i = lo
        self.pbi = i + 1
        return self.pb[i], self.bpb[i]

    def _need(self, e, ev, waits):
        if ev is None:
            return
        sem, val = ev
        if sem is self.sem[e] and e in ("pe", "sp"):
            return
        k = id(sem)
        if self.seen[e].get(k, 0) >= val:
            return
        self.seen[e][k] = val
        waits.append((sem, val))

    def _deps(self, e, reads, writes):
        waits = []
        for b in reads:
            self._need(e, b.w, waits)
            if b.psum:
                for ev in b.r:
                    if ev[0] is not self.sem[e]:
                        self._need(e, ev, waits)
        for b in writes:
            self._need(e, b.w, waits)
            for ev in b.r:
                self._need(e, ev, waits)
        return waits

    def op(self, e, fn, reads=(), writes=()):
        waits = self._deps(e, reads, writes)
        self.cnt[e] += 1
        ev = (self.sem[e], self.cnt[e])
        self.ops[e].append((waits, fn, (self.sem[e], 1)))
        for b in reads:
            b.r.append(ev)
        for b in writes:
            b.w = ev
            b.r = []
        return ev

    def dma(self, q, out_ap, in_ap, src, dst, **kw):
        srcs = src if isinstance(src, (list, tuple)) else [src]
        dsts = dst if isinstance(dst, (list, tuple)) else [dst]
        waits = self._deps(q, srcs, dsts)
        d0 = dsts[0]
        if d0.dsem is None:
            if self.sem_pool[q]:
                d0.dsem, d0.dcnt = self.sem_pool[q].pop()
            else:
                d0.dsem = self.es.enter_context(self.nc.semaphore("d%d" % self.nsem))
                self.nsem += 1
                d0.dcnt = 0
            self.phase_bufs.append((d0, q))
        else:
            assert any(b is d0 and qq == q for b, qq in self.phase_bufs), "buffer %s written by DMAs of two queues" % d0.name
        d0.dcnt += 16
        ev = (d0.dsem, d0.dcnt)
        self.ops[q].append(
            (waits, lambda eng: eng.dma_start(out=out_ap, in_=in_ap, **kw), (d0.dsem, 16)))
        for b in srcs:
            b.r.append(ev)
        for b in dsts:
            b.w = ev
            b.r = []
        return ev

    def coll(self, kind, groups, in_ap, out_ap, src, dst):
        waits = self._deps("pool", [src], [dst])
        if not hasattr(self, "cc_sem"):
            self.cc_sem = self.es.enter_context(self.nc.semaphore("cc_sem"))
            self.cc_cnt = 0
        self.cc_cnt += 1
        ev = (self.cc_sem, self.cc_cnt)
        sem = self.cc_sem
        self.ops["pool"].append(
            (waits, lambda eng: eng.collective_compute(kind, ALU.bypass, replica_groups=groups,
                                                       ins=[in_ap.opt()], outs=[out_ap.opt()]), (sem, 1)))
        src.r.append(ev)
        dst.w = ev
        dst.r = []
        return ev

    def end_phase(self):
        for b, _q in self.phase_bufs:
            waits = []
            self._need("sp", (b.dsem, b.dcnt), waits)
            if waits:
                self.ops["sp"].append((waits, None, None))
        nc = self.nc
        ops = self.ops

        def replay(e, eng):
            for waits, fn, inc in ops[e]:
                for sem, val in waits:
                    eng.wait_ge(sem, val)
                if fn is not None:
                    fn(eng).then_inc(inc[0], inc[1])

        with nc.Block() as block:
            @block.tensor
            def _(eng):
                replay("pe", eng)

            @block.scalar
            def _(eng):
                replay("act", eng)

            @block.vector
            def _(eng):
                replay("dve", eng)

            @block.gpsimd
            def _(eng):
                replay("pool", eng)

            @block.sync
            def _(eng):
                replay("sp", eng)
        self.ops = {e: [] for e in ENGS}
        for b, q in self.phase_bufs:
            self.sem_pool[q].append((b.dsem, b.dcnt))
            b.dsem = None
        self.phase_bufs = []


_UID = [0]


def _uname(name):
    _UID[0] += 1
    return "sb%d_%s" % (_UID[0], name)


class Ring:
    def __init__(self, nc, ph, name, n, shape, dt):
        self.t = [ph.enter_context(nc.sbuf_tensor(_uname("%s%d" % (name, i)), list(shape), dt)) for i in range(n)]
        self.b = [Buf("%s%d" % (name, i)) for i in range(n)]
        self.i = 0

    def next(self):
        i = self.i
        self.i = (i + 1) % len(self.t)
        return self.t[i], self.b[i]


def sbt(nc, ph, name, shape, dt):
    return ph.enter_context(nc.sbuf_tensor(_uname(name), list(shape), dt)), Buf(name)


def rope_tables(dim, positions_rc):
    q = dim // 4
    inv = 1.0 / (10000.0 ** (np.arange(q, dtype=np.float64) / q))
    n = positions_rc.shape[0]
    cos = np.ones((dim, n), np.float64)
    sin = np.zeros((dim, n), np.float64)
    valid = positions_rc[:, 0] >= 0
    for d in range(dim):
        a = d // (2 * q)
        w = d % (2 * q)
        fi = w % q
        first = w < q
        ang = positions_rc[:, a].astype(np.float64) * inv[fi]
        cos[d] = np.where(valid, np.cos(ang), 1.0)
        s = np.sin(ang)
        sin[d] = np.where(valid, -s if first else s, 0.0)
    return cos, sin


def rope_swap_index(dim):
    q = dim // 4
    idx = np.zeros(dim, np.int64)
    for d in range(dim):
        w = d % (2 * q)
        idx[d] = d + q if w < q else d - q
    return idx


def pos_rc(tokens):
    tokens = np.asarray(tokens)
    return np.stack([tokens // 64, tokens % 64], 1)


_TABLE_CACHE = {}


def get_tables(half):
    if half in _TABLE_CACHE:
        return _TABLE_CACHE[half]
    t = {}
    neg = -np.ones((CTX, 2), np.int64)
    key_pos = np.concatenate([neg, pos_rc(np.arange(SEQ))], 0)
    own_pos = pos_rc(half * SH + np.arange(SH))
    ck, sk = rope_tables(32, key_pos)
    t["cosK0"], t["sinK0"] = ck.astype(np.float32), sk.astype(np.float32)
    cq, sq = rope_tables(32, np.concatenate([own_pos, neg], 0))
    cq96 = np.ones((96, SH + CTX)); sq96 = np.zeros((96, SH + CTX))
    cq96[64:] = cq; sq96[64:] = sq
    t["cosQ0"], t["sinQ0"] = cq96.astype(np.float32), sq96.astype(np.float32)
    ck, sk = rope_tables(64, key_pos)
    t["cosK1"] = np.concatenate([ck, ck], 0).astype(np.float32)
    t["sinK1"] = np.concatenate([sk, sk], 0).astype(np.float32)
    cq, sq = rope_tables(64, own_pos)
    t["cosQ1"] = np.concatenate([cq, cq], 0).astype(np.float32)
    t["sinQ1"] = np.concatenate([sq, sq], 0).astype(np.float32)
    n2 = np.arange(64)[:, None]; k2 = np.arange(64)[None, :]
    a = 2 * np.pi * n2 * k2 / 64
    t["F64"] = np.concatenate([np.cos(a), -np.sin(a)], 1).astype(np.float32)
    n1 = np.arange(128)[:, None, None]
    kk = (64 * (64 * half + np.arange(64))[None, None, :] + np.arange(64)[None, :, None])
    a = 2 * np.pi * ((n1 * kk) % SEQ) / SEQ
    Gr, Gi = np.cos(a), -np.sin(a)
    GA = np.concatenate([Gr, Gi], 2); GB = np.concatenate([-Gi, Gr], 2)
    t["GAB"] = np.stack([GA, GB], 2).astype(np.float32)
    ch = np.arange(128)[:, None]; ch2 = np.arange(128)[None, :]
    a = 2 * np.pi * ch * ch2 / 128
    cds = np.stack([np.cos(a), np.sin(a)], 1)
    t["CdSl"] = (cds / math.sqrt(SEQ * 128)).astype(np.float32)
    t["CdSc"] = (cds / math.sqrt(CTX * 128)).astype(np.float32)
    n = (np.arange(2)[None, :, None] * 128 + np.arange(128)[:, None, None])
    k = np.arange(256)[None, None, :]
    a = 2 * np.pi * ((n * k) % 256) / 256
    t["F256"] = np.concatenate([np.cos(a), -np.sin(a)], 2).astype(np.float32)
    t["ident"] = np.eye(128, dtype=np.float32)
    sw64 = rope_swap_index(64)
    idx128 = np.concatenate([sw64, 64 + sw64])
    pm = np.zeros((128, 128), np.float32)
    pm[idx128, np.arange(128)] = 1.0
    t["permT"] = pm
    _TABLE_CACHE[half] = t
    return t


class LayerIO:
    pass


def declare_layer(nc, L, pfx, x_from_dram=None):
    io = {}

    def inp(name, shape, dt=F32):
        io[name] = nc.dram_tensor(pfx + name, list(shape), dt, kind="ExternalInput").ap()

    def scr(name, shape, dt):
        kind = "ExternalOutput" if (DEBUG_SCR and name in ("Fdl", "Fdc", "KT", "Vs", "QT", "cat")) else "Internal"
        io[name] = nc.dram_tensor(pfx + name, list(shape), dt, kind=kind).ap()

    if x_from_dram is None:
        inp("xf", [SEQ, D]); inp("xo", [SH, D]); inp("xc", [CTX, D])
    else:
        io.update(x_from_dram)
    inp("cv", [P, 8, 2])
    inp("w_mod", [D, 6 * D]); inp("b_mod", [1, 6 * D])
    inp("w_out", [D, D]); inp("w_gate", [D, FF]); inp("w_up", [D, FF]); inp("w_down", [FF, D])
    inp("ln", [4, D])
    if "ident" not in io:
        inp("ident", [P, P])
    if L == 0:
        inp("w_in", [D, 1056]); inp("w_kr_sw", [D, 32])
        inp("qn", [P, 2]); inp("kvn", [P, 2])
        inp("w_uq", [256, 768]); inp("w_uq_sw", [256, 768]); inp("w_ukv_kn", [256, 512]); inp("w_ukv_v", [256, 512])
        inp("cosK", [32, NK]); inp("sinK", [32, NK]); inp("cosQ", [96, SH + CTX]); inp("sinQ", [96, SH + CTX])
        inp("F64", [64, 128]); inp("GAB", [P, 64, 2, 128]); inp("CdSl", [P, 2, 128]); inp("CdSc", [P, 2, 128])
        inp("F256", [P, 2, 512])
        scr("Fdl", [SEQ, 512], F32); scr("Fdc", [CTX, 512], F32)
        scr("KT", [8, 96, NK], BF16); scr("Vs", [NK, 8, 65], BF16); scr("QT", [8, 96, SH + CTX], BF16)
        scr("cat", [SH + CTX, D], BF16)
    else:
        inp("w_in", [D, 3072]); inp("permT", [P, P])
        inp("lam", [4, 64]); inp("subln", [1, 128])
        inp("cosK", [P, NK]); inp("sinK", [P, NK]); inp("cosQ", [P, SH]); inp("sinQ", [P, SH])
        scr("KT", [8, 128, NK], BF16); scr("Vs", [NK, 8, 129], BF16); scr("QT", [8, 128, SH], BF16)
        scr("cat", [SH, D], BF16)
    scr("wg_bf", [11, P, 8, 256], BF16); scr("wu_bf", [11, P, 8, 256], BF16); scr("wd_bf", [FF, D], BF16)
    return io


def build_layer(nc, S, L, io, xout, xcout, consts, xbufs, b_out, b_outc):
    need_ctx = (L == 0)
    ident_f, b_idf, ident_b, b_idb, ones_f, b_ones = consts
    bin_ = Buf("ext_in")
    b_wbf = Buf("wbf%d" % L)

    with ExitStack() as lay:
        fcol, b_fcol = sbt(nc, lay, "fcol", [P, 4, 8, 2], F32)
        gbc, b_gbc = sbt(nc, lay, "gbc", [P, 2, 2, D], F32)
        lnbc, b_lnbc = sbt(nc, lay, "lnbc", [P, 4, D], F32)
        S.dma("sp", lnbc[:], io["ln"].partition_broadcast(P), bin_, b_lnbc)

        with ExitStack() as ph:
            cv, b_cv = sbt(nc, ph, "cv", [P, 8, 2], F32)
            sl, b_sl = sbt(nc, ph, "sl", [P, 8, 2], F32)
            rep, b_rep = sbt(nc, ph, "rep", [P, 2, 8, P], F32)
            wm = Ring(nc, ph, "wm", 4, [P, 8, 512], F32)
            bm = Ring(nc, ph, "bm", 4, [1, 512], F32)
            S.dma("sp", cv[:], io["cv"], bin_, b_cv)
            S.op("act", lambda e: e.activation(out=sl[:], in_=cv[:], func=AF.Silu), [b_cv], [b_sl])
            for m in range(2):
                for j in range(8):
                    S.op("dve", lambda e, m=m, j=j: e.tensor_scalar(
                        out=rep[:, m, j, :], in0=ones_f[:, :], scalar1=sl[:, j, m:m + 1], scalar2=None,
                        op0=ALU.mult), [b_sl, b_ones], [b_rep])
            vmap = {1: 0, 0: 1, 4: 2, 3: 3}
            for cb in range(12):
                c6, hf = cb // 2, cb % 2
                wt, bw = wm.next()
                bt, bb = bm.next()
                for jh in range(2):
                    S.dma("sp", wt[:, jh * 4:(jh + 1) * 4, :],
                          io["w_mod"][jh * 512:(jh + 1) * 512, cb * 512:(cb + 1) * 512].rearrange("(j p) n -> p j n", p=P),
                          bin_, bw)
                S.dma("sp", bt[:], io["b_mod"][:, cb * 512:(cb + 1) * 512], bin_, bb)
                if c6 in (2, 5):
                    gi = 0 if c6 == 2 else 1
                    for m in range(2):
                        pb, bp = S.bank()
                        for j in range(8):
                            S.op("pe", lambda e, pb=pb, m=m, j=j, wt=wt: e.matmul(
                                pb[:, :], lhsT=rep[:, m, j, :], rhs=wt[:, j, :], start=(j == 0), stop=False),
                                [b_rep, bw], [bp])
                        S.op("pe", lambda e, pb=pb, bt=bt: e.matmul(
                            pb[:, :], lhsT=ones_f[0:1, :], rhs=bt[0:1, :], start=False, stop=True),
                            [b_ones, bb], [bp])
                        S.op("dve", lambda e, pb=pb, m=m, gi=gi, hf=hf: e.tensor_copy(
                            out=gbc[:, m, gi, hf * 512:(hf + 1) * 512], in_=pb[:, :]), [bp], [b_gbc])
                else:
                    v = vmap[c6]
                    pb, bp = S.bank()
                    for q in range(4):
                        for j in range(8):
                            S.op("pe", lambda e, pb=pb, q=q, j=j, wt=wt: e.matmul(
                                pb[:, 2 * q:2 * q + 2], lhsT=wt[:, j, q * P:(q + 1) * P], rhs=sl[:, j, :],
                                start=(j == 0), stop=False), [b_sl, bw], [bp])
                        S.op("pe", lambda e, pb=pb, q=q, bt=bt: e.matmul(
                            pb[:, 2 * q:2 * q + 2], lhsT=bt[0:1, q * P:(q + 1) * P], rhs=ones_f[0:1, 0:2],
                            start=False, stop=True), [b_ones, bb], [bp])
                    S.op("dve", lambda e, pb=pb, v=v, hf=hf: e.tensor_copy(
                        out=fcol[:, v, hf * 4:hf * 4 + 4, :], in_=pb[:, 0:8].rearrange("p (q m) -> p q m", m=2)),
                        [bp], [b_fcol])
            for v in (0, 2):
                S.op("dve", lambda e, v=v: e.tensor_scalar(
                    out=fcol[:, v, :, :], in0=fcol[:, v, :, :], scalar1=1.0, scalar2=None, op0=ALU.add),
                    [b_fcol], [b_fcol])
            S.end_phase()

        def ln_rows(st_ring, xin, bx, out, bo, reads_extra=(), on_act=True):
            stt, bst = st_ring.next()
            for i in range(2):
                S.op("dve", lambda e, i=i, stt=stt: e.bn_stats(out=stt[:, 6 * i:6 * i + 6],
                                                              in_=xin[:, 512 * i:512 * i + 512]),
                     [bx] + list(reads_extra), [bst])
            S.op("dve", lambda e, stt=stt: e.bn_aggr(out=stt[:, 12:14], in_=stt[:, 0:12]), [bst], [bst])
            S.op("act", lambda e, stt=stt: e.activation(out=stt[:, 14:15], in_=stt[:, 13:14], func=AF.Sqrt,
                                                       bias=EPS, scale=1.0), [bst], [bst])
            S.op("dve", lambda e, stt=stt: e.reciprocal(out=stt[:, 14:15], in_=stt[:, 14:15]), [bst], [bst])
            if on_act:
                S.op("dve", lambda e, stt=stt: e.tensor_scalar(out=stt[:, 15:16], in0=stt[:, 12:13], scalar1=stt[:, 14:15],
                                                              scalar2=-1.0, op0=ALU.mult, op1=ALU.mult), [bst], [bst])
                S.op("act", lambda e, stt=stt: e.activation(out=out, in_=xin, func=AF.Identity, scale=stt[:, 14:15],
                                                           bias=stt[:, 15:16]), [bx, bst], [bo])
            else:
                S.op("dve", lambda e, stt=stt: e.tensor_scalar(out=out, in0=xin, scalar1=stt[:, 12:13],
                                                              scalar2=stt[:, 14:15], op0=ALU.subtract, op1=ALU.mult),
                     [bx, bst], [bo])

        def ln_block(st_ring, xb, bxb, xn_, bxn_, nt):
            stt, bst = st_ring.next()
            for t in range(nt):
                for i in range(2):
                    S.op("dve", lambda e, i=i, t=t, stt=stt: e.bn_stats(out=stt[:, 12 * t + 6 * i:12 * t + 6 * i + 6],
                                                                       in_=xb[:, t, 512 * i:512 * i + 512]), [bxb], [bst])
            for t in range(nt):
                S.op("dve", lambda e, t=t, stt=stt: e.bn_aggr(out=stt[:, 48 + 2 * t:50 + 2 * t], in_=stt[:, 12 * t:12 * t + 12]),
                     [bst], [bst])
            mv = stt[:, 48:56].rearrange("p (t c) -> p t c", c=2)
            S.op("act", lambda e, stt=stt: e.activation(out=stt[:, 56:56 + nt], in_=mv[:, 0:nt, 1], func=AF.Sqrt,
                                                       bias=EPS, scale=1.0), [bst], [bst])
            S.op("dve", lambda e, stt=stt: e.reciprocal(out=stt[:, 56:56 + nt], in_=stt[:, 56:56 + nt]), [bst], [bst])
            S.op("dve", lambda e, stt=stt: e.scalar_tensor_tensor(out=stt[:, 60:60 + nt], in0=mv[:, 0:nt, 0], scalar=-1.0,
                                                                 in1=stt[:, 56:56 + nt], op0=ALU.mult, op1=ALU.mult),
                 [bst], [bst])
            for t in range(nt):
                S.op("act", lambda e, t=t, stt=stt: e.activation(out=xn_[:, t, :], in_=xb[:, t, :], func=AF.Identity,
                                                                scale=stt[:, 56 + t:57 + t], bias=stt[:, 60 + t:61 + t]),
                     [bxb, bst], [bxn_])

        def make_hT(xn, bxn, nt, hT, bhT, vs, vb, m):
            for j in range(8):
                pb, bp = S.bank()
                for t in range(nt):
                    S.op("pe", lambda e, pb=pb, t=t, j=j: e.transpose(
                        out=pb[:, t * P:(t + 1) * P], in_=xn[:, t, j * P:(j + 1) * P], identity=ident_f[:, :]),
                        [bxn, b_idf], [bp])
                if j % 4 != 3:
                    S.op("act", lambda e, pb=pb, j=j: e.activation(
                        out=hT[:, j, 0:nt * P], in_=pb[:, 0:nt * P], func=AF.Identity,
                        scale=fcol[:, vs, j, m:m + 1], bias=fcol[:, vb, j, m:m + 1]), [bp, b_fcol], [bhT])
                else:
                    S.op("dve", lambda e, pb=pb, j=j: e.tensor_scalar(
                        out=hT[:, j, 0:nt * P], in0=pb[:, 0:nt * P], scalar1=fcol[:, vs, j, m:m + 1],
                        scalar2=fcol[:, vb, j, m:m + 1], op0=ALU.mult, op1=ALU.add), [bp, b_fcol], [bhT])

        def rms_T(src_banks, nrm, b_nrm, ntok, sq, b_sq, rb, b_rb, outT, b_out):
            for m2 in range(2):
                pbk, bpk = src_banks[m2]
                S.op("act", lambda e, pbk=pbk, m2=m2: e.activation(out=sq[:, m2, 0:ntok], in_=pbk[:, 0:ntok],
                                                                 func=AF.Square), [bpk], [b_sq])
            pb, bp = S.bank()
            for m2 in range(2):
                S.op("pe", lambda e, pb=pb, m2=m2: e.matmul(pb[:, 0:ntok], lhsT=ones_f[:, :], rhs=sq[:, m2, 0:ntok],
                                                           start=(m2 == 0), stop=(m2 == 1)), [b_sq, b_ones], [bp])
            S.op("act", lambda e, pb=pb: e.activation(out=rb[:, 0:ntok], in_=pb[:, 0:ntok], func=AF.Sqrt,
                                                     bias=EPS, scale=1.0 / 256.0), [bp], [b_rb])
            S.op("dve", lambda e: e.reciprocal(out=rb[:, 0:ntok], in_=rb[:, 0:ntok]), [b_rb], [b_rb])
            for m2 in range(2):
                pbk, bpk = src_banks[m2]
                S.op("dve", lambda e, pbk=pbk, m2=m2: e.scalar_tensor_tensor(
                    out=outT[:, m2, 0:ntok], in0=pbk[:, 0:ntok], scalar=nrm[:, m2:m2 + 1], in1=rb[:, 0:ntok],
                    op0=ALU.mult, op1=ALU.mult), [bpk, b_nrm, b_rb], [b_out])

        def rope_out(pa, bpa, pbk, bpbk, rows, ntok, cs, b_cs, sn, b_sn, t1, b_t1, t2, b_t2, out, b_o, extra=()):
            S.op("dve", lambda e: e.tensor_tensor(out=t1[0:rows, 0:ntok], in0=pa[0:rows, 0:ntok],
                                                  in1=cs[0:rows, 0:ntok], op=ALU.mult), [bpa, b_cs] + list(extra), [b_t1])
            S.op("dve", lambda e: e.tensor_tensor(out=t2[0:rows, 0:ntok], in0=pbk[0:rows, 0:ntok],
                                                  in1=sn[0:rows, 0:ntok], op=ALU.mult), [bpbk, b_sn], [b_t2])
            S.op("pool", lambda e: e.tensor_tensor(out=out, in0=t1[0:rows, 0:ntok], in1=t2[0:rows, 0:ntok],
                                                   op=ALU.add), [b_t1, b_t2], [b_o])

        kv_blocks = [(io["xc"], 2, 0, 1, xbufs["xc"])] + [(xbufs["xf_blk"](b), 4, CTX + b * 512, 0, xbufs["xf_buf"](b)) for b in range(16)]
        q_blocks = [(xbufs["xo_blk"](b), 4, b * 512, 0, xbufs["xo_buf"](b)) for b in range(8)]
        if need_ctx:
            q_blocks.append((io["xc"], 2, SH, 1, xbufs["xc"]))
        b_KT, b_Vs, b_QT, b_cat = Buf("KT"), Buf("Vs"), Buf("QT"), Buf("cat")
        b_Fdl, b_Fdc = Buf("Fdl"), Buf("Fdc")
        KT, Vs, QT, cat = io["KT"], io["Vs"], io["QT"], io["cat"]
        vw = 65 if L == 0 else 129
        R = 96 if L == 0 else 128

        with ExitStack() as ph:
            xr = Ring(nc, ph, "xr", 1, [P, 4, D], F32)
            xnr = Ring(nc, ph, "xn", 2, [P, 4, D], F32)
            stb_ring = Ring(nc, ph, "stb", 2, [P, 64], F32)
            hTr = Ring(nc, ph, "hT", 2, [P, 8, 512], BF16)
            st_ring = Ring(nc, ph, "st", 4, [P, 16], F32)
            csr = Ring(nc, ph, "csk", 2, [R if L == 1 else 32, 512], F32)
            snr = Ring(nc, ph, "snk", 2, [R if L == 1 else 32, 512], F32)
            t1, b_t1 = sbt(nc, ph, "t1", [P, 512], F32)
            t2, b_t2 = sbt(nc, ph, "t2", [P, 512], F32)
            vsr = Ring(nc, ph, "vsb", 2, [P, 4, 8, vw], BF16)
            for i in range(2):
                S.op("pool", lambda e, i=i: e.memset(vsr.t[i][:], 1.0), [], [vsr.b[i]])
            if L == 0:
                win, b_win = sbt(nc, ph, "win", [P, 8, 1056], BF16)
                wkr, b_wkr = sbt(nc, ph, "wkr", [P, 8, 32], BF16)
                wkn, b_wkn = sbt(nc, ph, "wkn", [P, 2, 512], BF16)
                wv, b_wv = sbt(nc, ph, "wv", [P, 2, 512], BF16)
                kvn, b_kvn = sbt(nc, ph, "kvn", [P, 2], F32)
                S.dma("pool", win[:], io["w_in"].rearrange("(j p) n -> p j n", p=P), bin_, b_win)
                S.dma("pool", wkr[:], io["w_kr_sw"].rearrange("(j p) n -> p j n", p=P), bin_, b_wkr)
                S.dma("pool", wkn[:], io["w_ukv_kn"].rearrange("(j p) n -> p j n", p=P), bin_, b_wkn)
                S.dma("pool", wv[:], io["w_ukv_v"].rearrange("(j p) n -> p j n", p=P), bin_, b_wv)
                S.dma("sp", kvn[:], io["kvn"], bin_, b_kvn)
                fsr = Ring(nc, ph, "fsb", 2, [P, 4, 512], F32)
                sq, b_sq = sbt(nc, ph, "sq", [P, 2, 512], F32)
                rb, b_rb = sbt(nc, ph, "rb", [P, 512], F32)
                ckvn, b_ckvn = sbt(nc, ph, "ckvn", [P, 2, 512], BF16)
                knr = Ring(nc, ph, "knT", 2, [P, 4, 512], BF16)
                krr = Ring(nc, ph, "krT", 2, [32, 512], BF16)
            else:
                win, b_win = sbt(nc, ph, "win", [P, 8, 2048], BF16)
                pmb, b_pmb = sbt(nc, ph, "pmb", [P, P], BF16)
                kbr = Ring(nc, ph, "kb", 3, [P, 512], BF16)
                for kvh in range(2):
                    S.dma("pool", win[:, :, kvh * 1024:(kvh + 1) * 1024],
                          io["w_in"][:, 1024 + kvh * 1024:2048 + kvh * 1024].rearrange("(j p) n -> p j n", p=P), bin_, b_win)
                S.dma("pool", pmb[:], io["permT"], bin_, b_pmb)
                knr = Ring(nc, ph, "knT", 2, [P, 8, 512], BF16)

            def kv_a(src, nt, koff, m, sbuf_src):
                ntok = nt * P
                xb, bxb = xr.next()
                S.dma("sp", xb[:, 0:nt, :], src.rearrange("(t p) d -> p t d", p=P), sbuf_src, bxb)
                cs, b_cs = csr.next()
                sn, b_sn = snr.next()
                S.dma("sp", cs[:, 0:ntok], io["cosK"][:, koff:koff + ntok], bin_, b_cs)
                S.dma("sp", sn[:, 0:ntok], io["sinK"][:, koff:koff + ntok], bin_, b_sn)
                xn_, bxn_ = xnr.next()
                ln_block(stb_ring, xb, bxb, xn_, bxn_, nt)
                return dict(nt=nt, koff=koff, m=m, cs=cs, b_cs=b_cs, sn=sn, b_sn=b_sn, xn=xn_, bxn=bxn_)

            def kv_b(st):
                hT, bhT = hTr.next()
                make_hT(st["xn"], st["bxn"], st["nt"], hT, bhT, 0, 1, st["m"])
                st["hT"], st["bhT"] = hT, bhT

            def kv_body(st):
                nt, koff, m = st["nt"], st["koff"], st["m"]
                cs, b_cs, sn, b_sn, hT, bhT = st["cs"], st["b_cs"], st["sn"], st["b_sn"], st["hT"], st["bhT"]
                ntok = nt * P
                vsb, bvs = vsr.next()
                if L == 0:
                    fsb, bfs = fsr.next()
                    for t in range(nt):
                        pb, bp = S.bank()
                        for j in range(8):
                            S.op("pe", lambda e, pb=pb, t=t, j=j, hT=hT: e.matmul(
                                pb[:, :], lhsT=hT[:, j, t * P:(t + 1) * P], rhs=win[:, j, 0:512],
                                start=(j == 0), stop=(j == 7)), [bhT, b_win], [bp])
                        S.op("act", lambda e, pb=pb, t=t, fsb=fsb: e.copy(out=fsb[:, t, :], in_=pb[:, :]), [bp], [bfs])
                    if m == 1:
                        S.dma("pool", io["Fdc"].rearrange("(t p) c -> p t c", p=P), fsb[:, 0:nt, :], bfs, b_Fdc)
                    else:
                        n0 = koff - CTX
                        S.dma("pool", io["Fdl"][n0:n0 + ntok, :].rearrange("(t p) c -> p t c", p=P), fsb[:, 0:nt, :],
                              bfs, b_Fdl)
                    banks = []
                    for m2 in range(2):
                        pb, bp = S.bank()
                        banks.append((pb, bp))
                        for j in range(8):
                            S.op("pe", lambda e, pb=pb, m2=m2, j=j, hT=hT: e.matmul(
                                pb[:, 0:ntok], lhsT=win[:, j, 768 + m2 * P:768 + (m2 + 1) * P], rhs=hT[:, j, 0:ntok],
                                start=(j == 0), stop=(j == 7)), [bhT, b_win], [bp])
                    rms_T(banks, kvn, b_kvn, ntok, sq, b_sq, rb, b_rb, ckvn, b_ckvn)
                    knT, bkn = knr.next()
                    for hp in range(4):
                        pb, bp = S.bank()
                        for m2 in range(2):
                            S.op("pe", lambda e, pb=pb, hp=hp, m2=m2: e.matmul(
                                pb[:, 0:ntok],
                                lhsT=wkn[:, m2, hp * P:(hp + 1) * P],
                                rhs=ckvn[:, m2, 0:ntok], start=(m2 == 0), stop=(m2 == 1)), [b_ckvn, b_wkn], [bp])
                        S.op("act", lambda e, pb=pb, hp=hp, knT=knT: e.copy(out=knT[:, hp, 0:ntok], in_=pb[:, 0:ntok]),
                             [bp], [bkn])
                    for hh in range(2):
                        S.dma("pool", KT.rearrange("(hp hh) r n -> hh r hp n", hh=2)[hh, 0:64, :, koff:koff + ntok],
                              knT[hh * 64:(hh + 1) * 64, :, 0:ntok], bkn, b_KT)
                    for t in range(nt):
                        pb, bp = S.bank()
                        for m2 in range(2):
                            S.op("pe", lambda e, pb=pb, t=t, m2=m2: e.matmul(
                                pb[:, :], lhsT=ckvn[:, m2, t * P:(t + 1) * P],
                                rhs=wv[:, m2, :],
                                start=(m2 == 0), stop=(m2 == 1)), [b_ckvn, b_wv], [bp])
                        S.op("dve", lambda e, pb=pb, t=t, vsb=vsb: e.tensor_copy(
                            out=vsb[:, t, :, 0:64], in_=pb[:, :].rearrange("p (h c) -> p h c", c=64)), [bp], [bvs])
                    pa, bpa = S.bank()
                    pbk, bpbk = S.bank()
                    for j in range(8):
                        S.op("pe", lambda e, pa=pa, j=j, hT=hT: e.matmul(
                            pa[0:32, 0:ntok], lhsT=win[:, j, 1024:1056], rhs=hT[:, j, 0:ntok],
                            start=(j == 0), stop=(j == 7)), [bhT, b_win], [bpa])
                    for j in range(8):
                        S.op("pe", lambda e, pbk=pbk, j=j, hT=hT: e.matmul(
                            pbk[0:32, 0:ntok], lhsT=wkr[:, j, :], rhs=hT[:, j, 0:ntok],
                            start=(j == 0), stop=(j == 7)), [bhT, b_wkr], [bpbk])
                    krT, bkr = krr.next()
                    rope_out(pa, bpa, pbk, bpbk, 32, ntok, cs, b_cs, sn, b_sn, t1, b_t1, t2, b_t2,
                             krT[0:32, 0:ntok], bkr)
                    for h in range(8):
                        S.dma("pool", KT[h, 64:96, koff:koff + ntok], krT[0:32, 0:ntok], bkr, b_KT)
                else:
                    knT, bkn = knr.next()
                    pend = None
                    for h in range(9):
                        cur = None
                        if h < 8:
                            pa, bpa = S.bank()
                            for j in range(8):
                                S.op("pe", lambda e, pa=pa, j=j, h=h, hT=hT: e.matmul(
                                    pa[:, 0:ntok], lhsT=win[:, j, h * P:(h + 1) * P], rhs=hT[:, j, 0:ntok],
                                    start=(j == 0), stop=(j == 7)), [bhT, b_win], [bpa])
                            kb, bkb = kbr.next()
                            S.op("act", lambda e, pa=pa, kb=kb: e.copy(out=kb[:, 0:ntok], in_=pa[:, 0:ntok]), [bpa], [bkb])
                            cur = (h, pa, bpa, kb, bkb)
                        if pend is not None:
                            h0, pa0, bpa0, kb0, bkb0 = pend
                            pbk, bpbk = S.bank()
                            S.op("pe", lambda e, pbk=pbk, kb0=kb0: e.matmul(
                                pbk[:, 0:ntok], lhsT=pmb[:, :], rhs=kb0[:, 0:ntok], start=True, stop=True),
                                [b_pmb, bkb0], [bpbk])
                            rope_out(pa0, bpa0, pbk, bpbk, P, ntok, cs, b_cs, sn, b_sn, t1, b_t1, t2, b_t2,
                                     knT[:, h0, 0:ntok], bkn, extra=[bkb0])
                        pend = cur
                    S.dma("pool", KT[:, :, koff:koff + ntok].rearrange("h r n -> r h n"), knT[:, :, 0:ntok], bkn, b_KT)
                    for t in range(nt):
                        for hf in range(2):
                            pb, bp = S.bank()
                            for j in range(8):
                                S.op("pe", lambda e, pb=pb, t=t, j=j, hf=hf, hT=hT: e.matmul(
                                    pb[:, :], lhsT=hT[:, j, t * P:(t + 1) * P],
                                    rhs=win[:, j, 1024 + hf * 512:1024 + (hf + 1) * 512],
                                    start=(j == 0), stop=(j == 7)), [bhT, b_win], [bp])
                            S.op("dve", lambda e, pb=pb, t=t, hf=hf, vsb=vsb: e.tensor_copy(
                                out=vsb[:, t, hf * 4:(hf + 1) * 4, 0:128], in_=pb[:, :].rearrange("p (h c) -> p h c", c=128)),
                                [bp], [bvs])
                S.dma("pool", Vs[koff:koff + ntok].rearrange("(t p) h c -> p t (h c)", p=P),
                      vsb[:, 0:nt].rearrange("p t h c -> p t (h c)"), bvs, b_Vs)
            sts = [None] * len(kv_blocks)
            sts[0] = kv_a(*kv_blocks[0])
            kv_b(sts[0])
            for i in range(len(kv_blocks)):
                if i + 1 < len(kv_blocks):
                    sts[i + 1] = kv_a(*kv_blocks[i + 1])
                kv_body(sts[i])
                if i + 1 < len(kv_blocks):
                    kv_b(sts[i + 1])
            S.end_phase()

        with ExitStack() as ph:
            xr = Ring(nc, ph, "xr", 1, [P, 4, D], F32)
            xnr = Ring(nc, ph, "xn", 2, [P, 4, D], F32)
            stb_ring = Ring(nc, ph, "stb", 2, [P, 64], F32)
            hTr = Ring(nc, ph, "hT", 2, [P, 8, 512], BF16)
            st_ring = Ring(nc, ph, "st", 4, [P, 16], F32)
            csr = Ring(nc, ph, "csq", 2, [R, 512], F32)
            snr = Ring(nc, ph, "snq", 2, [R, 512], F32)
            t1r = Ring(nc, ph, "t1", 2, [P, 512], F32)
            t2r = Ring(nc, ph, "t2", 2, [P, 512], F32)
            qsr = Ring(nc, ph, "qsb", 2, [R, 8, 512], BF16)
            if L == 0:
                win, b_win = sbt(nc, ph, "win", [P, 8, 256], BF16)
                wuq, b_wuq = sbt(nc, ph, "wuq", [P, 2, 768], BF16)
                wuqs, b_wuqs = sbt(nc, ph, "wuqs", [P, 2, 768], BF16)
                qn, b_qn = sbt(nc, ph, "qn", [P, 2], F32)
                S.dma("pool", win[:], io["w_in"][:, 512:768].rearrange("(j p) n -> p j n", p=P), bin_, b_win)
                S.dma("pool", wuq[:], io["w_uq"].rearrange("(j p) n -> p j n", p=P), bin_, b_wuq)
                S.dma("pool", wuqs[:], io["w_uq_sw"].rearrange("(j p) n -> p j n", p=P), bin_, b_wuqs)
                S.dma("sp", qn[:], io["qn"], bin_, b_qn)
                sq, b_sq = sbt(nc, ph, "sq", [P, 2, 512], F32)
                rb, b_rb = sbt(nc, ph, "rb", [P, 512], F32)
                cqn, b_cqn = sbt(nc, ph, "cqn", [P, 2, 512], BF16)
            else:
                win, b_win = sbt(nc, ph, "win", [P, 8, 1024], BF16)
                pmb, b_pmb = sbt(nc, ph, "pmb", [P, P], BF16)
                kbr = Ring(nc, ph, "kb", 3, [P, 512], BF16)
                S.dma("pool", win[:], io["w_in"][:, 0:1024].rearrange("(j p) n -> p j n", p=P), bin_, b_win)
                S.dma("pool", pmb[:], io["permT"], bin_, b_pmb)
            def q_a(src, nt, qoff, m, sbuf_src):
                ntok = nt * P
                xb, bxb = xr.next()
                S.dma("sp", xb[:, 0:nt, :], src.rearrange("(t p) d -> p t d", p=P), sbuf_src, bxb)
                cs, b_cs = csr.next()
                sn, b_sn = snr.next()
                S.dma("sp", cs[:, 0:ntok], io["cosQ"][:, qoff:qoff + ntok], bin_, b_cs)
                S.dma("sp", sn[:, 0:ntok], io["sinQ"][:, qoff:qoff + ntok], bin_, b_sn)
                xn_, bxn_ = xnr.next()
                ln_block(stb_ring, xb, bxb, xn_, bxn_, nt)
                return dict(nt=nt, qoff=qoff, m=m, cs=cs, b_cs=b_cs, sn=sn, b_sn=b_sn, xn=xn_, bxn=bxn_)

            def q_b(st):
                hT, bhT = hTr.next()
                make_hT(st["xn"], st["bxn"], st["nt"], hT, bhT, 0, 1, st["m"])
                st["hT"], st["bhT"] = hT, bhT

            def q_body(st):
                nt, qoff, m = st["nt"], st["qoff"], st["m"]
                cs, b_cs, sn, b_sn, hT, bhT = st["cs"], st["b_cs"], st["sn"], st["b_sn"], st["hT"], st["bhT"]
                ntok = nt * P
                qsb, bqs = qsr.next()
                if L == 0:
                    banks = []
                    for m2 in range(2):
                        pb, bp = S.bank()
                        banks.append((pb, bp))
                        for j in range(8):
                            S.op("pe", lambda e, pb=pb, m2=m2, j=j, hT=hT: e.matmul(
                                pb[:, 0:ntok], lhsT=win[:, j, m2 * P:(m2 + 1) * P], rhs=hT[:, j, 0:ntok],
                                start=(j == 0), stop=(j == 7)), [bhT, b_win], [bp])
                    rms_T(banks, qn, b_qn, ntok, sq, b_sq, rb, b_rb, cqn, b_cqn)
                pend = None
                for h in range(9 if L == 1 else 0):
                    cur = None
                    if h < 8:
                        pa, bpa = S.bank()
                        for j in range(8):
                            S.op("pe", lambda e, pa=pa, j=j, h=h, hT=hT: e.matmul(
                                pa[:, 0:ntok], lhsT=win[:, j, h * P:(h + 1) * P], rhs=hT[:, j, 0:ntok],
                                start=(j == 0), stop=(j == 7)), [bhT, b_win], [bpa])
                        kb, bkb = kbr.next()
                        S.op("act", lambda e, pa=pa, kb=kb: e.copy(out=kb[:, 0:ntok], in_=pa[:, 0:ntok]), [bpa], [bkb])
                        cur = (h, pa, bpa, kb, bkb)
                    if pend is not None:
                        h0, pa0, bpa0, kb0, bkb0 = pend
                        pbk, bpbk = S.bank()
                        S.op("pe", lambda e, pbk=pbk, kb0=kb0: e.matmul(
                            pbk[:, 0:ntok], lhsT=pmb[:, :], rhs=kb0[:, 0:ntok], start=True, stop=True),
                            [b_pmb, bkb0], [bpbk])
                        t1, b_t1 = t1r.next()
                        t2, b_t2 = t2r.next()
                        rope_out(pa0, bpa0, pbk, bpbk, R, ntok, cs, b_cs, sn, b_sn, t1, b_t1, t2, b_t2,
                                 qsb[0:R, h0, 0:ntok], bqs, extra=[bkb0])
                    pend = cur
                for h in range(8 if L == 0 else 0):
                    pa, bpa = S.bank()
                    pbk, bpbk = S.bank()
                    if L == 0:
                        for m2 in range(2):
                            S.op("pe", lambda e, pa=pa, m2=m2, h=h: e.matmul(
                                pa[0:96, 0:ntok], lhsT=wuq[:, m2, h * 96:(h + 1) * 96], rhs=cqn[:, m2, 0:ntok],
                                start=(m2 == 0), stop=(m2 == 1)), [b_cqn, b_wuq], [bpa])
                        for m2 in range(2):
                            S.op("pe", lambda e, pbk=pbk, m2=m2, h=h: e.matmul(
                                pbk[0:96, 0:ntok], lhsT=wuqs[:, m2, h * 96:(h + 1) * 96], rhs=cqn[:, m2, 0:ntok],
                                start=(m2 == 0), stop=(m2 == 1)), [b_cqn, b_wuqs], [bpbk])
                    else:
                        for j in range(8):
                            S.op("pe", lambda e, pa=pa, j=j, h=h, hT=hT: e.matmul(
                                pa[:, 0:ntok], lhsT=win[:, j, h * P:(h + 1) * P], rhs=hT[:, j, 0:ntok],
                                start=(j == 0), stop=(j == 7)), [bhT, b_win], [bpa])
                        for j in range(8):
                            S.op("pe", lambda e, pbk=pbk, j=j, h=h, hT=hT: e.matmul(
                                pbk[:, 0:ntok], lhsT=wsw[:, j, h * P:(h + 1) * P], rhs=hT[:, j, 0:ntok],
                                start=(j == 0), stop=(j == 7)), [bhT, b_wsw], [bpbk])
                    t1, b_t1 = t1r.next()
                    t2, b_t2 = t2r.next()
                    rope_out(pa, bpa, pbk, bpbk, R, ntok, cs, b_cs, sn, b_sn, t1, b_t1, t2, b_t2,
                             qsb[0:R, h, 0:ntok], bqs)
                S.dma("pool", QT[:, :, qoff:qoff + ntok].rearrange("h r n -> r h n"), qsb[0:R, :, 0:ntok], bqs, b_QT)
            sts = [None] * len(q_blocks)
            sts[0] = q_a(*q_blocks[0])
            q_b(sts[0])
            for i in range(len(q_blocks)):
                if i + 1 < len(q_blocks):
                    sts[i + 1] = q_a(*q_blocks[i + 1])
                q_body(sts[i])
                if i + 1 < len(q_blocks):
                    q_b(sts[i + 1])
            S.end_phase()

        if L == 0:
            with ExitStack() as ph:
                f64, b_f64 = sbt(nc, ph, "f64", [64, 128], F32)
                cdl, b_cdl = sbt(nc, ph, "cdl", [P, 2, 128], F32)
                cdc, b_cdc = sbt(nc, ph, "cdc", [P, 2, 128], F32)
                f256, b_f256 = sbt(nc, ph, "f256", [P, 2, 512], F32)
                S.dma("sp", f64[:], io["F64"], bin_, b_f64)
                S.dma("sp", cdl[:], io["CdSl"], bin_, b_cdl)
                S.dma("sp", cdc[:], io["CdSc"], bin_, b_cdc)
                S.dma("sp", f256[:], io["F256"], bin_, b_f256)
                Xr = Ring(nc, ph, "Xh", 2, [64, 128, 32], F32)
                T, b_T = sbt(nc, ph, "T", [P, 128, 128], F32)
                ZT, b_ZT = sbt(nc, ph, "ZT", [P, 2, 64, 64], F32)
                Gr_ = Ring(nc, ph, "G", 2, [P, 8, 2, 128], F32)
                ysr = Ring(nc, ph, "ysb", 2, [P, 4, 128], BF16)
                Fd3 = io["Fdl"].rearrange("(a b) c -> a b c", b=128)
                ZTf = ZT[:, :, :, :].rearrange("p r i q -> p r (i q)")
                for g in range(4):
                    for xq in range(4):
                        X, bX = Xr.next()
                        c0 = g * 128 + xq * 32
                        S.dma("sp", X[:], Fd3[:, :, c0:c0 + 32], b_Fdl, bX)
                        for cg in range(8):
                            pb, bp = S.bank()
                            for cc in range(4):
                                c = cg * 4 + cc
                                S.op("pe", lambda e, pb=pb, cc=cc, c=c, X=X: e.matmul(
                                    pb[:, cc * 128:(cc + 1) * 128], lhsT=X[:, :, c], rhs=f64[:, :],
                                    start=True, stop=True), [bX, b_f64], [bp])
                            ch0 = xq * 32 + cg * 4
                            if cg % 2 == 0:
                                S.op("act", lambda e, pb=pb, ch0=ch0: e.copy(
                                    out=T[:, ch0:ch0 + 4, :], in_=pb[:, :].rearrange("p (c k) -> p c k", k=128)),
                                    [bp], [b_T])
                            else:
                                S.op("dve", lambda e, pb=pb, ch0=ch0: e.tensor_copy(
                                    out=T[:, ch0:ch0 + 4, :], in_=pb[:, :].rearrange("p (c k) -> p c k", k=128)),
                                    [bp], [b_T])
                    for kc in range(8):
                        G, bG = Gr_.next()
                        S.dma("sp", G[:], io["GAB"][:, kc * 8:(kc + 1) * 8], bin_, bG)
                        for kq in range(2):
                            pb, bp = S.bank()
                            for q in range(4):
                                kl = kq * 4 + q
                                k2 = kc * 8 + kl
                                S.op("pe", lambda e, pb=pb, q=q, kl=kl, k2=k2, G=G: e.matmul(
                                    pb[:, q * 128:(q + 1) * 128], lhsT=T[:, :, k2],
                                    rhs=G[:, kl, 0, :], start=True, stop=False), [b_T, bG], [bp])
                                S.op("pe", lambda e, pb=pb, q=q, kl=kl, k2=k2, G=G: e.matmul(
                                    pb[:, q * 128:(q + 1) * 128], lhsT=T[:, :, 64 + k2],
                                    rhs=G[:, kl, 1, :], start=False, stop=True), [b_T, bG], [bp])
                            k20 = kc * 8 + kq * 4
                            S.op("dve", lambda e, pb=pb, k20=k20: e.tensor_copy(
                                out=ZT[:, :, :, k20:k20 + 4].rearrange("p r i q -> p q r i"),
                                in_=pb[:, :].rearrange("p (q r i) -> p q r i", q=4, r=2)),
                                [bp], [b_ZT])
                    for tq in range(8):
                        pb, bp = S.bank()
                        for q in range(4):
                            tt = tq * 4 + q
                            for r in range(2):
                                S.op("pe", lambda e, pb=pb, q=q, tt=tt, r=r: e.matmul(
                                    pb[:, q * 128:(q + 1) * 128], lhsT=ZTf[:, r, tt * 128:(tt + 1) * 128],
                                    rhs=cdl[:, r, :], start=(r == 0), stop=(r == 1)), [b_ZT, b_cdl], [bp])
                        ysb, bys = ysr.next()
                        S.op("act", lambda e, pb=pb, ysb=ysb: e.copy(
                            out=ysb[:, :, :], in_=pb[:, :].rearrange("p (q c) -> p q c", c=128)), [bp], [bys])
                        S.dma("pool", cat[tq * 512:(tq + 1) * 512, g * 128:(g + 1) * 128].rearrange("(t p) c -> p t c", p=P),
                              ysb[:, :, :], bys, b_cat)
                fcx, b_fcx = sbt(nc, ph, "fcx", [P, 2, 512], F32)
                zc, b_zc = sbt(nc, ph, "zc", [P, 512], F32)
                S.dma("sp", fcx[:], io["Fdc"].rearrange("(t p) c -> p t c", p=P), b_Fdc, b_fcx)
                for g in range(4):
                    pb, bp = S.bank()
                    for tl in range(2):
                        S.op("pe", lambda e, pb=pb, tl=tl, g=g: e.matmul(
                            pb[:, :], lhsT=fcx[:, tl, g * 128:(g + 1) * 128], rhs=f256[:, tl, :],
                            start=(tl == 0), stop=(tl == 1)), [b_fcx, b_f256], [bp])
                    S.op("dve", lambda e, pb=pb: e.tensor_copy(out=zc[:, :], in_=pb[:, :]), [bp], [b_zc])
                    pb2, bp2 = S.bank()
                    for tl in range(2):
                        for r in range(2):
                            S.op("pe", lambda e, pb2=pb2, tl=tl, r=r: e.matmul(
                                pb2[:, tl * 128:(tl + 1) * 128], lhsT=zc[:, r * 256 + tl * 128:r * 256 + (tl + 1) * 128],
                                rhs=cdc[:, r, :], start=(r == 0), stop=(r == 1)), [b_zc, b_cdc], [bp2])
                    ysb, bys = ysr.next()
                    S.op("act", lambda e, pb2=pb2, ysb=ysb: e.copy(
                        out=ysb[:, 0:2, :], in_=pb2[:, 0:256].rearrange("p (q c) -> p q c", c=128)), [bp2], [bys])
                    S.dma("pool", cat[SH:SH + CTX, g * 128:(g + 1) * 128].rearrange("(t p) c -> p t c", p=P),
                          ysb[:, 0:2, :], bys, b_cat)
                S.end_phase()

        with ExitStack() as ph:
            ktr = Ring(nc, ph, "kts", 2, [R, NK], BF16)
            vr = Ring(nc, ph, "vs", 2, [P, 66, vw], BF16)
            if L == 0:
                qtr = Ring(nc, ph, "qts", 2, [R, SH + CTX], BF16)
            else:
                qtr = Ring(nc, ph, "qts", 2, [P, 2, SH], BF16)
                for i in range(2):
                    S.op("pool", lambda e, i=i: e.memset(qtr.t[i][:], 0.0), [], [qtr.b[i]])
            ptr = Ring(nc, ph, "pt", 4, [P, 1024], BF16)
            rcr = Ring(nc, ph, "rc", 4, [P, 8], F32)
            asr = Ring(nc, ph, "asb", 2, [P, 4, 128 if L == 1 else 64], BF16)
            if L == 1:
                o0r = Ring(nc, ph, "o0", 2, [P, 4, 128], F32)
                o1r = Ring(nc, ph, "o1", 2, [P, 128], F32)
                sqr = Ring(nc, ph, "sqr", 2, [P, 128], F32)
                lamt, b_lam = sbt(nc, ph, "lamt", [P, 4, 64], F32)
                lamw, b_lamw = sbt(nc, ph, "lamw", [P, 8], F32)
                lamj, b_lamj = sbt(nc, ph, "lamj", [P, 64], F32)
                subl, b_subl = sbt(nc, ph, "subl", [P, 128], F32)
                epsc, b_epsc = sbt(nc, ph, "epsc", [P, 1], F32)
                S.op("dve", lambda e: e.memset(epsc[:], EPS), [], [b_epsc])
                S.dma("sp", lamt[:], io["lam"].partition_broadcast(P), bin_, b_lam)
                S.dma("sp", subl[:], io["subln"].partition_broadcast(P), bin_, b_subl)
                for i in range(2):
                    S.op("dve", lambda e, i=i: e.scalar_tensor_tensor(
                        out=lamj[:, :], in0=lamt[:, 2 * i, :], scalar=1.0, in1=lamt[:, 2 * i + 1, :],
                        op0=ALU.mult, op1=ALU.mult, accum_out=lamw[:, i:i + 1]), [b_lam], [b_lamj, b_lamw])
                S.op("act", lambda e: e.activation(out=lamw[:, 2:4], in_=lamw[:, 0:2], func=AF.Exp), [b_lamw], [b_lamw])
                S.op("dve", lambda e: e.tensor_tensor(out=lamw[:, 4:5], in0=lamw[:, 3:4], in1=lamw[:, 2:3],
                                                      op=ALU.subtract), [b_lamw], [b_lamw])
                S.op("dve", lambda e: e.tensor_scalar(out=lamw[:, 5:6], in0=lamw[:, 4:5], scalar1=-LAMBDA_INIT1,
                                                      scalar2=None, op0=ALU.add), [b_lamw], [b_lamw])
                S.op("dve", lambda e: e.tensor_scalar(out=subl[:, :], in0=subl[:, :], scalar1=1.0 - LAMBDA_INIT1,
                                                      scalar2=None, op0=ALU.mult), [b_subl], [b_subl])
            casts = []
            for c in range(11):
                casts.append(lambda c=c: S.dma("pool", io["wg_bf"][c],
                                               io["w_gate"][:, c * 256:(c + 1) * 256].rearrange("(j p) n -> p j n", p=P),
                                               bin_, b_wbf))
                casts.append(lambda c=c: S.dma("pool", io["wu_bf"][c],
                                               io["w_up"][:, c * 256:(c + 1) * 256].rearrange("(j p) n -> p j n", p=P),
                                               bin_, b_wbf))
            for r4 in range(4):
                casts.append(lambda r4=r4: S.dma("pool", io["wd_bf"][r4 * 704:(r4 + 1) * 704, :],
                                                 io["w_down"][r4 * 704:(r4 + 1) * 704, :], bin_, b_wbf,
                                                 max_dma_last_dim=4096))
            scale = MLA_SCALE if L == 0 else DIFF_SCALE
            nq_tot = SH + (CTX if need_ctx else 0)
            qblocks = [(b * 512, 512, 66) for b in range(8)]
            if need_ctx:
                qblocks.append((SH, 256, 2))
            nmaps = 1 if L == 0 else 2
            accr = Ring(nc, ph, "accs", 4, [P, 4, 132], F32)
            rc8r = Ring(nc, ph, "rc8", 4, [P, 16], F32)
            o1br = Ring(nc, ph, "o1b", 2, [P, 4, 128], F32)
            items = []
            for h in range(8):
                for qi in range(len(qblocks)):
                    for ci in range(nmaps):
                        for kp in range(qblocks[qi][2] // 2):
                            items.append((h, qi, ci, kp))
            heads = {}

            def load_head(h):
                kts, bkt = ktr.next()
                vs_, bv = vr.next()
                qts, bqt = qtr.next()
                S.dma("sp", kts[:, :], KT[h], b_KT, bkt)
                S.dma("sp", vs_[:, :, :], Vs[:, h, :].rearrange("(t p) c -> p t c", p=P), b_Vs, bv)
                if L == 0:
                    S.dma("sp", qts[:, 0:nq_tot], QT[h, :, 0:nq_tot], b_QT, bqt)
                else:
                    S.dma("sp", qts[0:64, 0, :], QT[h, 0:64, :], b_QT, bqt)
                    S.dma("sp", qts[64:128, 1, :], QT[h, 64:128, :], b_QT, bqt)
                heads[h] = (kts, bkt, vs_, bv, qts, bqt)

            qstate = {}

            def finalize(h, qi, ci):
                q0, nq, nkt = qblocks[qi]
                nj = nq // P
                stq = qstate.setdefault((h, qi), {})
                acs, bacs = accr.next()
                for j in range(nj):
                    S.op("dve", lambda e, j=j, acs=acs: e.tensor_copy(out=acs[:, j, 0:vw], in_=S.pb[4 + j][:, 0:vw]),
                         [S.bpb[4 + j]], [bacs])
                if ci == 0:
                    rc, brc = rc8r.next()
                    stq["rc"], stq["brc"] = rc, brc
                    stq["a0"], stq["ba0"] = acs, bacs
                else:
                    rc, brc = stq["rc"], stq["brc"]
                S.op("dve", lambda e, acs=acs, rc=rc, ci=ci: e.reciprocal(
                    out=rc[:, 4 * ci:4 * ci + nj], in_=acs[:, 0:nj, vw - 1]), [bacs], [brc])
                if L == 0:
                    asb, bas = asr.next()
                    for j in range(nj):
                        S.op("dve", lambda e, j=j, acs=acs, rc=rc, asb=asb: e.tensor_scalar(
                            out=asb[:, j, :], in0=acs[:, j, 0:64], scalar1=rc[:, j:j + 1], scalar2=None, op0=ALU.mult),
                            [bacs, brc], [bas])
                    S.dma("pool", cat[q0:q0 + nq, 512 + h * 64:512 + (h + 1) * 64].rearrange("(t p) c -> p t c", p=P),
                          asb[:, 0:nj, :], bas, b_cat)
                    if casts:
                        casts.pop(0)()
                elif ci == 1:
                    a0, ba0 = stq["a0"], stq["ba0"]
                    asb, bas = asr.next()
                    o1, b_o1 = o1br.next()
                    sqt, bsqt = sqr.next()
                    S.op("dve", lambda e, rc=rc: e.tensor_scalar(out=rc[:, 4:8], in0=rc[:, 4:8], scalar1=lamw[:, 5:6],
                                                                 scalar2=None, op0=ALU.mult), [brc, b_lamw], [brc])
                    for j in range(nj):
                        S.op("dve", lambda e, j=j, a0=a0, rc=rc, o1=o1: e.tensor_scalar(
                            out=o1[:, j, :], in0=a0[:, j, 0:128], scalar1=rc[:, j:j + 1], scalar2=None, op0=ALU.mult),
                            [ba0, brc], [b_o1])
                        S.op("dve", lambda e, j=j, acs=acs, rc=rc, o1=o1: e.scalar_tensor_tensor(
                            out=o1[:, j, :], in0=acs[:, j, 0:128], scalar=rc[:, 4 + j:5 + j], in1=o1[:, j, :],
                            op0=ALU.mult, op1=ALU.add), [bacs, brc, b_o1], [b_o1])
                        S.op("dve", lambda e, j=j, o1=o1, rc=rc, sqt=sqt: e.scalar_tensor_tensor(
                            out=sqt[:, :], in0=o1[:, j, :], scalar=1.0, in1=o1[:, j, :],
                            op0=ALU.mult, op1=ALU.mult, accum_out=rc[:, 8 + j:9 + j]), [b_o1], [brc, bsqt])
                    S.op("act", lambda e, rc=rc: e.activation(out=rc[:, 12:16], in_=rc[:, 8:12], func=AF.Ln,
                                                             bias=epsc[:, 0:1], scale=1.0 / 128.0), [brc, b_epsc], [brc])
                    S.op("act", lambda e, rc=rc: e.activation(out=rc[:, 12:16], in_=rc[:, 12:16], func=AF.Exp,
                                                             scale=-0.5), [brc], [brc])
                    for j in range(nj):
                        S.op("dve", lambda e, j=j, o1=o1, rc=rc, asb=asb: e.scalar_tensor_tensor(
                            out=asb[:, j, :], in0=o1[:, j, :], scalar=rc[:, 12 + j:13 + j], in1=subl[:, :],
                            op0=ALU.mult, op1=ALU.mult), [b_o1, brc, b_subl], [bas])
                    S.dma("pool", cat[q0:q0 + nq, h * 128:(h + 1) * 128].rearrange("(t p) c -> p t c", p=P),
                          asb[:, 0:nj, :], bas, b_cat)
                    if casts:
                        casts.pop(0)()

            DEPTH = 2
            pts = {}
            load_head(0)
            spair = [(S.pbig[:, 0:1024], [S.bpb[0], S.bpb[1]]), (S.pbig[:, 1024:2048], [S.bpb[2], S.bpb[3]])]
            for n in range(len(items) + DEPTH):
                if n < len(items):
                    h, qi, ci, kp = items[n]
                    q0, nq, nkt = qblocks[qi]
                    kts, bkt, vs_, bv, qts, bqt = heads[h]
                    sp2, bsp2 = spair[n % 2]
                    for u in range(2):
                        kt = 2 * kp + u
                        if L == 0:
                            S.op("pe", lambda e, sp2=sp2, u=u, kt=kt, kts=kts, qts=qts, q0=q0, nq=nq: e.matmul(
                                sp2[:, u * 512:u * 512 + nq], lhsT=kts[0:96, kt * P:(kt + 1) * P], rhs=qts[0:96, q0:q0 + nq],
                                start=True, stop=True), [bkt, bqt], [bsp2[u]])
                        else:
                            S.op("pe", lambda e, sp2=sp2, u=u, kt=kt, kts=kts, qts=qts, ci=ci, q0=q0, nq=nq: e.matmul(
                                sp2[:, u * 512:u * 512 + nq], lhsT=kts[:, kt * P:(kt + 1) * P], rhs=qts[:, ci, q0:q0 + nq],
                                start=True, stop=True), [bkt, bqt], [bsp2[u]])
                    pt, bpt = ptr.next()
                    S.op("act", lambda e, sp2=sp2, pt=pt, nq=nq: e.activation(
                        out=pt[:, :].rearrange("p (u n) -> p u n", u=2)[:, :, 0:nq],
                        in_=sp2.rearrange("p (u n) -> p u n", u=2)[:, :, 0:nq], func=AF.Exp, scale=scale), bsp2, [bpt])
                    pts[n] = (pt, bpt)
                if n >= DEPTH:
                    h, qi, ci, kp = items[n - DEPTH]
                    q0, nq, nkt = qblocks[qi]
                    if qi == 0 and ci == 0 and kp == 0 and h + 1 < 8:
                        load_head(h + 1)
                    kts, bkt, vs_, bv, qts, bqt = heads[h]
                    pt, bpt = pts.pop(n - DEPTH)
                    for u in range(2):
                        kt = 2 * kp + u
                        for j in range(nq // P):
                            S.op("pe", lambda e, j=j, u=u, pt=pt, kt=kt, vs_=vs_, nkt=nkt: e.matmul(
                                S.pb[4 + j][:, 0:vw], lhsT=pt[:, u * 512 + j * P:u * 512 + (j + 1) * P], rhs=vs_[:, kt, :],
                                start=(kt == 0), stop=(kt == nkt - 1)), [bpt, bv], [S.bpb[4 + j]])
                    if 2 * kp + 1 == nkt - 1:
                        finalize(h, qi, ci)
            while casts:
                casts.pop(0)()
            S.end_phase()

        with ExitStack() as ph:
            wout, b_wout = sbt(nc, ph, "wout", [P, 8, D], BF16)
            S.dma("pool", wout[:], io["w_out"].rearrange("(j p) n -> p j n", p=P), bin_, b_wout)
            xr = Ring(nc, ph, "xr", 2, [P, D], F32)
            ctr = Ring(nc, ph, "ct", 2, [P, D], BF16)
            cTr = Ring(nc, ph, "cT", 1, [P, 8, P], BF16)
            rr = Ring(nc, ph, "rr", 2, [P, D], F32)
            x1r = Ring(nc, ph, "x1", 2, [P, 4, D], F32)
            xn2r = Ring(nc, ph, "xn2", 1, [P, 1, D], F32)
            h2r = Ring(nc, ph, "h2T", 2, [P, 8, 512], BF16)
            actT, b_actT = sbt(nc, ph, "actT", [P, NF, 512], BF16)
            gur = Ring(nc, ph, "gu", 3, [P, 2, 8, 256], BF16)
            wdr = Ring(nc, ph, "wd", 3, [P, 2, 512], BF16)
            sgr = Ring(nc, ph, "sg", 2, [P, 512], F32)
            ygr = Ring(nc, ph, "yg", 2, [P, 512], F32)
            outr = Ring(nc, ph, "ot", 2, [P, D], F32)
            st_ring = Ring(nc, ph, "st", 4, [P, 16], F32)
            o_blocks = [(xbufs["xo_blk"](b), 4, b * 512, 0, xout(b), xbufs["xo_buf"](b), b_out(b)) for b in range(8)]
            if need_ctx:
                o_blocks.append((io["xc"], 2, SH, 1, xcout, xbufs["xc"], b_outc))

            def ln_affine(tmp, btmp, dst, bdst, g_row, b_row):
                ln_rows(st_ring, tmp, btmp, tmp, btmp)
                S.op("dve", lambda e: e.tensor_tensor(out=tmp, in0=tmp, in1=lnbc[:, g_row, :], op=ALU.mult),
                     [btmp, b_lnbc], [btmp])
                S.op("pool", lambda e: e.tensor_tensor(out=dst, in0=tmp, in1=lnbc[:, b_row, :], op=ALU.add),
                     [btmp, b_lnbc], [bdst])

            brange = [4, 8]

            def stage_a(src, nt, coff, m, dst, sbuf_src, bdst_out):
                x1s, b_x1s = x1r.next()
                h2T, b_h2T = h2r.next()
                st = dict(nt=nt, m=m, dst=dst, bdst_out=bdst_out, x1=x1s, b_x1=b_x1s, h2T=h2T, b_h2T=b_h2T)
                yield st
                for t in range(nt):
                    xt, bxt = xr.next()
                    ct, bct = ctr.next()
                    S.dma("sp", xt[:, :], src[t * P:(t + 1) * P, :], sbuf_src, bxt)
                    S.dma("sp", ct[:, :], cat[coff + t * P:coff + (t + 1) * P, :], b_cat, bct)
                    pb, bp = S.bank(*brange)
                    pbv = pb[:, :].bitcast(BF16)
                    for j in range(8):
                        S.op("pe", lambda e, pbv=pbv, j=j, ct=ct: e.transpose(
                            out=pbv[:, j * P:(j + 1) * P], in_=ct[:, j * P:(j + 1) * P], identity=ident_b[:, :]),
                            [bct, b_idb], [bp])
                    cT, bcT = cTr.next()
                    S.op("act", lambda e, pbv=pbv, cT=cT: e.copy(out=cT[:, :, :], in_=pbv.rearrange("p (j c) -> p j c", c=P)),
                         [bp], [bcT])
                    yield None
                    tmp, btmp = rr.next()
                    for hf in range(2):
                        pb, bp = S.bank(*brange)
                        for j in range(8):
                            S.op("pe", lambda e, pb=pb, j=j, hf=hf, cT=cT: e.matmul(
                                pb[:, :], lhsT=cT[:, j, :], rhs=wout[:, j, hf * 512:(hf + 1) * 512],
                                start=(j == 0), stop=(j == 7)), [bcT, b_wout], [bp])
                        S.op("dve", lambda e, pb=pb, hf=hf, tmp=tmp: e.tensor_tensor(
                            out=tmp[:, hf * 512:(hf + 1) * 512], in0=pb[:, :], in1=gbc[:, m, 0, hf * 512:(hf + 1) * 512],
                            op=ALU.mult), [bp, b_gbc], [btmp])
                    yield None
                    S.op("dve", lambda e, tmp=tmp, xt=xt: e.scalar_tensor_tensor(
                        out=tmp[:, :], in0=xt[:, :], scalar=ALPHA, in1=tmp[:, :], op0=ALU.mult, op1=ALU.add),
                        [bxt, btmp], [btmp])
                    ln_affine(tmp[:, :], btmp, x1s[:, t, :], b_x1s, 0, 1)
                    yield None
                    xn2, bxn2 = xn2r.next()
                    ln_rows(st_ring, x1s[:, t, :], b_x1s, xn2[:, 0, :], bxn2)
                    for jh in range(2):
                        pb, bp = S.bank(*brange)
                        for jj in range(4):
                            j = jh * 4 + jj
                            S.op("pe", lambda e, pb=pb, jj=jj, j=j, xn2=xn2: e.transpose(
                                out=pb[:, jj * P:(jj + 1) * P], in_=xn2[:, 0, j * P:(j + 1) * P], identity=ident_f[:, :]),
                                [bxn2, b_idf], [bp])
                        for jj in range(4):
                            j = jh * 4 + jj
                            if jj % 2 == 0:
                                S.op("act", lambda e, pb=pb, jj=jj, j=j, t=t: e.activation(
                                    out=h2T[:, j, t * P:(t + 1) * P], in_=pb[:, jj * P:(jj + 1) * P], func=AF.Identity,
                                    scale=fcol[:, 2, j, m:m + 1], bias=fcol[:, 3, j, m:m + 1]), [bp, b_fcol], [b_h2T])
                            else:
                                S.op("dve", lambda e, pb=pb, jj=jj, j=j, t=t: e.tensor_scalar(
                                    out=h2T[:, j, t * P:(t + 1) * P], in0=pb[:, jj * P:(jj + 1) * P],
                                    scalar1=fcol[:, 2, j, m:m + 1], scalar2=fcol[:, 3, j, m:m + 1],
                                    op0=ALU.mult, op1=ALU.add), [bp, b_fcol], [b_h2T])
                    yield None

            def stage_c(st):
                nt, x1s, b_x1s = st["nt"], st["x1"], st["b_x1"]
                for t in range(nt):
                    ot, bot = outr.next()
                    ln_affine(x1s[:, t, :], b_x1s, ot[:, :], bot, 2, 3)
                    S.dma("pool", st["dst"][t * P:(t + 1) * P, :], ot[:, :], bot, st["bdst_out"])
                    yield None

            def step(bg, n=1):
                for _ in range(n):
                    if bg is None:
                        return
                    try:
                        next(bg)
                    except StopIteration:
                        return

            def stage_b(st, bg):
                nt, m, x1s, b_x1s, h2T, b_h2T = st["nt"], st["m"], st["x1"], st["b_x1"], st["h2T"], st["b_h2T"]
                ntok = nt * P
                brange[0] = 0
                for c in range(11):
                    gu, bgu = gur.next()
                    S.dma("sp", gu[:, 0, :, :], io["wg_bf"][c], b_wbf, bgu)
                    S.dma("sp", gu[:, 1, :, :], io["wu_bf"][c], b_wbf, bgu)
                    for fc in range(2):
                        f = c * 2 + fc
                        pg, bpg = S.bank(*brange)
                        pu, bpu = S.bank(*brange)
                        for j in range(8):
                            S.op("pe", lambda e, pg=pg, j=j, fc=fc, gu=gu: e.matmul(
                                pg[:, 0:ntok], lhsT=gu[:, 0, j, fc * P:(fc + 1) * P], rhs=h2T[:, j, 0:ntok],
                                start=(j == 0), stop=(j == 7)), [bgu, b_h2T], [bpg])
                        for j in range(8):
                            S.op("pe", lambda e, pu=pu, j=j, fc=fc, gu=gu: e.matmul(
                                pu[:, 0:ntok], lhsT=gu[:, 1, j, fc * P:(fc + 1) * P], rhs=h2T[:, j, 0:ntok],
                                start=(j == 0), stop=(j == 7)), [bgu, b_h2T], [bpu])
                        sg, bsg = sgr.next()
                        S.op("act", lambda e, pg=pg, sg=sg: e.activation(out=sg[:, 0:ntok], in_=pg[:, 0:ntok], func=AF.Silu),
                             [bpg], [bsg])
                        S.op("dve", lambda e, pu=pu, sg=sg, f=f: e.tensor_tensor(
                            out=actT[:, f, 0:ntok], in0=pu[:, 0:ntok], in1=sg[:, 0:ntok], op=ALU.mult),
                            [bpu, bsg], [b_actT])
                    step(bg)
                brange[0] = 4
                for hf in range(2):
                    for c in range(11):
                        wd, bwd = wdr.next()
                        S.dma("sp", wd[:, :, :], io["wd_bf"][c * 256:(c + 1) * 256, hf * 512:(hf + 1) * 512]
                              .rearrange("(f p) n -> p f n", p=P), b_wbf, bwd)
                        for fc in range(2):
                            f = c * 2 + fc
                            for t in range(nt):
                                S.op("pe", lambda e, f=f, fc=fc, t=t, wd=wd: e.matmul(
                                    S.pb[t][:, :], lhsT=actT[:, f, t * P:(t + 1) * P], rhs=wd[:, fc, :],
                                    start=(f == 0), stop=(f == NF - 1)), [b_actT, bwd], [S.bpb[t]])
                        step(bg)
                    for t in range(nt):
                        yg, byg = ygr.next()
                        S.op("dve", lambda e, t=t, yg=yg, hf=hf: e.tensor_tensor(
                            out=yg[:, :], in0=S.pb[t][:, :], in1=gbc[:, m, 1, hf * 512:(hf + 1) * 512], op=ALU.mult),
                            [S.bpb[t], b_gbc], [byg])
                        S.op("dve", lambda e, t=t, yg=yg, hf=hf: e.scalar_tensor_tensor(
                            out=x1s[:, t, hf * 512:(hf + 1) * 512], in0=x1s[:, t, hf * 512:(hf + 1) * 512], scalar=ALPHA,
                            in1=yg[:, :], op0=ALU.mult, op1=ALU.add), [b_x1s, byg], [b_x1s])

            def chain(*gens):
                for g_ in gens:
                    if g_ is not None:
                        for _ in g_:
                            yield None

            nb = len(o_blocks)
            ga = stage_a(*o_blocks[0])
            st_cur = next(ga)
            for _ in ga:
                pass
            prev_c = None
            for i in range(nb):
                if i + 1 < nb:
                    ga = stage_a(*o_blocks[i + 1])
                    st_next = next(ga)
                else:
                    ga, st_next = None, None
                bg = chain(prev_c, ga)
                stage_b(st_cur, bg)
                for _ in bg:
                    pass
                prev_c = stage_c(st_cur)
                st_cur = st_next
            for _ in prev_c:
                pass
            S.end_phase()


def build_program():
    nc = bass.Bass("TRN2", target_bir_lowering=False)
    io0 = declare_layer(nc, 0, "l0_")
    x1own = [nc.dram_tensor("x1own%d" % c, [512, D], F32, kind="Internal").ap() for c in range(8)]
    x1gat = [nc.dram_tensor("x1gat%d" % c, [1024, D], F32, kind="Internal").ap() for c in range(8)]
    xc1 = nc.dram_tensor("xc1", [CTX, D], F32, kind="Internal").ap()
    io1 = declare_layer(nc, 1, "l1_", x_from_dram={"xc": xc1, "ident": io0["ident"]})
    xout = nc.dram_tensor("xout", [SH, D], F32, kind="ExternalOutput").ap()
    with ExitStack() as es:
        S = Sched(nc, es)
        ident_f = es.enter_context(nc.sbuf_tensor("ident_f", [P, P], F32)); b_idf = Buf("idf")
        ident_b = es.enter_context(nc.sbuf_tensor("ident_b", [P, P], BF16)); b_idb = Buf("idb")
        ones_f = es.enter_context(nc.sbuf_tensor("ones_f", [P, P], F32)); b_ones = Buf("ones")
        bin_ = Buf("cin")
        S.dma("sp", ident_f[:], io0["ident"], bin_, b_idf)
        S.op("dve", lambda e: e.tensor_copy(out=ident_b[:], in_=ident_f[:]), [b_idf], [b_idb])
        S.op("dve", lambda e: e.memset(ones_f[:], 1.0), [], [b_ones])
        consts = (ident_f, b_idf, ident_b, b_idb, ones_f, b_ones)
        b_xc1, b_xout = Buf("xc1"), Buf("xout")
        b_own = [Buf("x1own%d" % c) for c in range(8)]
        b_gat = [Buf("x1gat%d" % c) for c in range(8)]
        xb0 = {"xc": bin_,
               "xf_blk": lambda b: io0["xf"][b * 512:(b + 1) * 512, :], "xf_buf": lambda b: bin_,
               "xo_blk": lambda b: io0["xo"][b * 512:(b + 1) * 512, :], "xo_buf": lambda b: bin_}
        build_layer(nc, S, 0, io0, lambda b: x1own[b], xc1, consts, xb0, lambda b: b_own[b], b_xc1)
        for c in range(8):
            if NO_COLL:
                S.dma("pool", x1gat[c][0:512], x1own[c], b_own[c], b_gat[c])
                S.dma("pool", x1gat[c][512:1024], x1own[c], b_own[c], b_gat[c])
            else:
                S.coll("AllGather", [[0, 1], [2, 3], [4, 5], [6, 7]], x1own[c], x1gat[c], b_own[c], b_gat[c])
        xb1 = {"xc": b_xc1,
               "xf_blk": lambda b: x1gat[b % 8][(b // 8) * 512:(b // 8 + 1) * 512, :], "xf_buf": lambda b: b_gat[b % 8],
               "xo_blk": lambda b: x1own[b], "xo_buf": lambda b: b_own[b]}
        build_layer(nc, S, 1, io1, lambda b: xout[b * 512:(b + 1) * 512, :], None, consts, xb1, lambda b: b_xout, None)
    return nc


L0_NAMES = ["l0_w_mod", "l0_b_mod", "l0_w_in", "l0_q_norm", "l0_w_uq", "l0_kv_norm", "l0_w_ukv", "l0_w_out",
            "l0_ln1_g", "l0_ln1_b", "l0_w_gate", "l0_w_up", "l0_w_down", "l0_ln2_g", "l0_ln2_b"]
L1_NAMES = ["l1_w_mod", "l1_b_mod", "l1_w_in", "l1_lambda_q1", "l1_lambda_k1", "l1_lambda_q2", "l1_lambda_k2",
            "l1_subln", "l1_w_out", "l1_ln1_g", "l1_ln1_b", "l1_w_gate", "l1_w_up", "l1_w_down", "l1_ln2_g", "l1_ln2_b"]


def layer_inputs(L, inputs, b, half):
    pre = "l%d_" % L
    names = L0_NAMES if L == 0 else L1_NAMES
    w = {n[3:]: np.ascontiguousarray(np.asarray(inputs[n], dtype=np.float32)) for n in names}
    g = lambda n: w[n]
    t = get_tables(half)
    m = {}
    c = np.asarray(inputs["c"], np.float32)[b]
    cc = np.asarray(inputs["c_ctx"], np.float32)
    m["cv"] = np.ascontiguousarray(np.stack([c.reshape(8, P).T, cc.reshape(8, P).T], 2))
    m["w_mod"] = g("w_mod"); m["b_mod"] = g("b_mod").reshape(1, -1)
    m["w_out"] = g("w_out"); m["w_gate"] = g("w_gate"); m["w_up"] = g("w_up"); m["w_down"] = g("w_down")
    m["ln"] = np.ascontiguousarray(np.stack([g("ln1_g"), g("ln1_b"), g("ln2_g"), g("ln2_b")], 0))
    w_in = g("w_in")
    if L == 0:
        m["ident"] = t["ident"]
        m["w_in"] = w_in
        sw = rope_swap_index(32)
        m["w_kr_sw"] = np.ascontiguousarray(w_in[:, 1024 + sw])
        m["qn"] = np.ascontiguousarray(g("q_norm").reshape(2, P).T)
        m["kvn"] = np.ascontiguousarray(g("kv_norm").reshape(2, P).T)
        w_uq = g("w_uq")
        perm = np.arange(768)
        for h in range(8):
            perm[h * 96 + 64:h * 96 + 96] = h * 96 + 64 + sw
        m["w_uq"] = w_uq
        m["w_uq_sw"] = np.ascontiguousarray(w_uq[:, perm])
        w_ukv = g("w_ukv").reshape(256, 8, 128)
        m["w_ukv_kn"] = np.ascontiguousarray(w_ukv[:, :, 0:64].reshape(256, 512))
        m["w_ukv_v"] = np.ascontiguousarray(w_ukv[:, :, 64:128].reshape(256, 512))
        m["cosK"], m["sinK"], m["cosQ"], m["sinQ"] = t["cosK0"], t["sinK0"], t["cosQ0"], t["sinQ0"]
        for k in ("F64", "GAB", "CdSl", "CdSc", "F256"):
            m[k] = t[k]
    else:
        m["w_in"] = w_in
        m["permT"] = t["permT"]
        m["lam"] = np.ascontiguousarray(np.stack([g("lambda_q1"), g("lambda_k1"), g("lambda_q2"), g("lambda_k2")], 0))
        m["subln"] = g("subln").reshape(1, 128)
        m["cosK"], m["sinK"], m["cosQ"], m["sinQ"] = t["cosK1"], t["sinK1"], t["cosQ1"], t["sinQ1"]
    return {pre + k: v for k, v in m.items()}


_PROG = {}


def kernel(**inputs):
    x = np.asarray(inputs["x"], np.float32)
    xc = np.asarray(inputs["ctx"], np.float32)
    if "nc" not in _PROG:
        _PROG["nc"] = build_program()
    nc = _PROG["nc"]
    in_maps = []
    for k in range(8):
        b, half = k // 2, k % 2
        m = {}
        m.update(layer_inputs(0, inputs, b, half))
        m.update(layer_inputs(1, inputs, b, half))
        m["l0_xf"] = np.ascontiguousarray(x[b])
        m["l0_xo"] = np.ascontiguousarray(x[b, half * SH:(half + 1) * SH])
        m["l0_xc"] = np.ascontiguousarray(xc[b])
        in_maps.append(m)
    res = run_bass_kernel_spmd(nc, in_maps, core_ids=list(range(8)))
    r = res.results
    out = np.stack([np.concatenate([r[2 * b]["xout"], r[2 * b + 1]["xout"]], 0) for b in range(4)], 0)
    return out.astype(np.float32)
```
